# Optimizing a Trainium2 kernel written in Bass

```python
import jax, jax.numpy as jnp
from jax import lax
import numpy as np

D_MODEL = 1024
BATCH = 16
SEQ = 4096
DEPTH = 2

N_A_LAYERS = DEPTH // 2
N_B_LAYERS = DEPTH - N_A_LAYERS

ROPE_THETA = 10000.0
EPS = 1e-6
Q_BLOCK = 128
ADA_INIT = 0.5

A_HEADS = 16
A_LATENT = 128
A_ROPE = 32
A_VDIM = 128
A_WIDTH = A_HEADS * A_VDIM
IDX_HEADS = 8
IDX_DIM = 64
TOPK_MAX = 256
A_SIZES = (A_HEADS * A_LATENT, A_HEADS * A_ROPE, A_LATENT, A_ROPE, A_WIDTH,
           IDX_HEADS * IDX_DIM, IDX_DIM, IDX_HEADS)
A_IN = sum(A_SIZES)

B_HEADS = 8
B_HDIM = 128
B_GROUPS = ((128, 1), (512, 4), (2048, 16))
N_GROUPS = len(B_GROUPS)
B_WIDTH = B_HEADS * B_HDIM
B_IN = N_GROUPS * B_WIDTH + B_WIDTH
KV_OUT = 2 * B_WIDTH

kernel_name = "yoco_dsa_longnet_hybrid"


def rmsnorm(x, g):
    xf = x.astype(jnp.float32)
    y = xf * lax.rsqrt(jnp.mean(xf * xf, axis=-1, keepdims=True) + EPS)
    return (y * g.astype(jnp.float32)).astype(x.dtype)


def rope_tables(positions, dim):
    inv = ROPE_THETA ** (-jnp.arange(0, dim, 2, dtype=jnp.float32) / dim)
    ang = positions.astype(jnp.float32)[..., None] * inv
    return jnp.cos(ang)[:, :, None, :], jnp.sin(ang)[:, :, None, :]


def apply_rope(x, cs):
    cos, sin = cs
    xf = x.astype(jnp.float32)
    x1, x2 = jnp.split(xf, 2, axis=-1)
    return jnp.concatenate([x1 * cos - x2 * sin, x2 * cos + x1 * sin], axis=-1).astype(x.dtype)


def split_cols(t, sizes):
    out, o = [], 0
    for s in sizes:
        out.append(t[..., o:o + s])
        o += s
    return out


def ada_prenorm(h, c, g, ada_w, ada_b):
    mod = (jax.nn.silu(c) @ ada_w + ada_b)[:, None, :]
    shift, scale, gate = jnp.split(mod, 3, axis=-1)
    return rmsnorm(h, g) * (1 + scale) + shift, gate


def dsa_mixer(h, positions, w_in, kv_norm_g, w_uv, w_out):
    B_, S, _ = h.shape
    q_lat, q_rope, c_kv, k_rope, gate, q_idx, k_idx, w_idx = split_cols(h @ w_in, A_SIZES)
    rope_a = rope_tables(positions, A_ROPE)
    q_lat = q_lat.reshape(B_, S, A_HEADS, A_LATENT)
    q_rope = apply_rope(q_rope.reshape(B_, S, A_HEADS, A_ROPE), rope_a)
    c_kv = rmsnorm(c_kv, kv_norm_g)
    k_rope = apply_rope(k_rope[:, :, None, :], rope_a)[:, :, 0]
    keys = jnp.concatenate([c_kv, k_rope], axis=-1)
    qcat = jnp.concatenate([q_lat, q_rope], axis=-1)
    rope_i = rope_tables(positions, IDX_DIM)
    q_idx = apply_rope(q_idx.reshape(B_, S, IDX_HEADS, IDX_DIM), rope_i)
    k_idx = apply_rope(k_idx[:, :, None, :], rope_i)[:, :, 0]
    w_idx = w_idx * IDX_HEADS ** -0.5
    topk = min(TOPK_MAX, S // 4)
    nblk = S // Q_BLOCK
    key_pos = jnp.arange(S)
    scale = (A_LATENT + A_ROPE) ** -0.5

    def to_blocks(t):
        return t.reshape(B_, nblk, Q_BLOCK, *t.shape[2:]).swapaxes(0, 1)

    def block(args):
        qb, qib, wb, start = args
        qpos = start + jnp.arange(Q_BLOCK)
        rel = jax.nn.relu(jnp.einsum('bqhd,bsd->bqhs', qib, k_idx))
        iscore = jnp.einsum('bqh,bqhs->bqs', wb, rel).astype(jnp.float32)
        causal = key_pos[None, :] <= qpos[:, None]
        iscore = jnp.where(causal[None], iscore, -jnp.inf)
        _, sel = lax.top_k(iscore, topk)
        gk = jax.vmap(lambda kb, ib: kb[ib])(keys, sel)
        s = jnp.einsum('bqhd,bqkd->bqhk', qb, gk).astype(jnp.float32) * scale
        valid = (sel <= qpos[None, :, None])[:, :, None, :]
        p = jax.nn.softmax(jnp.where(valid, s, -jnp.inf), axis=-1)
        return jnp.einsum('bqhk,bqkc->bqhc', p.astype(gk.dtype), gk[..., :A_LATENT])

    starts = jnp.arange(nblk, dtype=jnp.int32) * Q_BLOCK
    o_lat = lax.map(block, (to_blocks(qcat), to_blocks(q_idx), to_blocks(w_idx), starts))
    o_lat = o_lat.swapaxes(0, 1).reshape(B_, S, A_HEADS, A_LATENT)
    o = jnp.einsum('bshc,hcv->bshv', o_lat, w_uv).reshape(B_, S, A_WIDTH)
    return (o * jax.nn.silu(gate)) @ w_out


def shared_kv(h, positions, g, w_kv):
    B_, S, _ = h.shape
    k, v = jnp.split(rmsnorm(h, g) @ w_kv, 2, axis=-1)
    k = apply_rope(k.reshape(B_, S, B_HEADS, B_HDIM), rope_tables(positions, B_HDIM))
    return k, v.reshape(B_, S, B_HEADS, B_HDIM)


def dilated_band_attention(q, k, v, dilation, window):
    B_, S, H, Dh = q.shape
    W = window // dilation
    n = S // dilation
    nb = -(-n // W)
    n_pad = nb * W

    def sub(t):
        t = t.reshape(B_, n, dilation, H, Dh).swapaxes(1, 2).reshape(B_ * dilation, n, H, Dh)
        return jnp.pad(t, ((0, 0), (0, n_pad - n), (0, 0), (0, 0)))

    def with_prev(t):
        tb = t.reshape(-1, nb, W, H, Dh)
        prev = jnp.pad(tb, ((0, 0), (1, 0), (0, 0), (0, 0), (0, 0)))[:, :-1]
        return jnp.concatenate([prev, tb], axis=2)

    qb = sub(q).reshape(-1, nb, W, H, Dh)
    kb, vb = with_prev(sub(k)), with_prev(sub(v))
    s = jnp.einsum('znqhd,znkhd->znhqk', qb, kb).astype(jnp.float32) * Dh ** -0.5
    qpos = jnp.arange(nb)[:, None] * W + jnp.arange(W)[None, :]
    kpos = jnp.arange(nb)[:, None] * W - W + jnp.arange(2 * W)[None, :]
    dist = qpos[:, :, None] - kpos[:, None, :]
    mask = (dist >= 0) & (dist <= W) & (kpos[:, None, :] >= 0)
    s = jnp.where(mask[None, :, None], s, -jnp.inf)
    lse = jax.nn.logsumexp(s, axis=-1)
    p = jnp.exp(s - lse[..., None])
    o = jnp.einsum('znhqk,znkhd->znqhd', p.astype(v.dtype), vb)
    o = o.reshape(B_, dilation, n_pad, H, Dh)[:, :, :n].swapaxes(1, 2).reshape(B_, S, H, Dh)
    lse = lse.transpose(0, 1, 3, 2).reshape(B_, dilation, n_pad, H)[:, :, :n]
    lse = lse.swapaxes(1, 2).reshape(B_, S, H)
    return o, lse


def dilated_mixer(h, positions, k_sh, v_sh, w_in, w_out):
    B_, S, _ = h.shape
    proj = h @ w_in
    parts = split_cols(proj, (B_WIDTH,) * N_GROUPS + (B_WIDTH,))
    gate = parts[-1]
    rope_b = rope_tables(positions, B_HDIM)
    outs, lses = [], []
    for qg, (window, dil) in zip(parts[:-1], B_GROUPS):
        q = apply_rope(qg.reshape(B_, S, B_HEADS, B_HDIM), rope_b)
        o, lse = dilated_band_attention(q, k_sh, v_sh, dil, window)
        outs.append(o.astype(jnp.float32))
        lses.append(lse)
    alpha = jax.nn.softmax(jnp.stack(lses, axis=-1), axis=-1)
    o = jnp.einsum('bshg,gbshd->bshd', alpha, jnp.stack(outs)).astype(h.dtype)
    return (o.reshape(B_, S, B_WIDTH) * jax.nn.silu(gate)) @ w_out


def setup_inputs(seed: int = 0) -> dict:
    key = jax.random.key(seed)
    ks = jax.random.split(key, 18)
    D = D_MODEL

    def nrm(k, shape, fan_in):
        return jax.random.normal(k, shape, jnp.float32) * fan_in ** -0.5

    def gain(k, shape):
        return 1.0 + 0.05 * jax.random.normal(k, shape, jnp.float32)

    positions = (jax.random.randint(ks[2], (BATCH, 1), 0, 1024, dtype=jnp.int32)
                 + jnp.arange(SEQ, dtype=jnp.int32)[None, :])
    return {
        "x": jax.random.normal(ks[0], (BATCH, SEQ, D), jnp.float32),
        "c": jax.random.normal(ks[1], (BATCH, D), jnp.float32),
        "positions": positions,
        "a_norm": gain(ks[3], (N_A_LAYERS, D)),
        "a_ada_w": nrm(ks[4], (N_A_LAYERS, D, 3 * D), D) * ADA_INIT,
        "a_ada_b": 0.02 * jax.random.normal(ks[5], (N_A_LAYERS, 3 * D), jnp.float32),
        "a_w_in": nrm(ks[6], (N_A_LAYERS, D, A_IN), D),
        "a_kv_norm": gain(ks[7], (N_A_LAYERS, A_LATENT)),
        "a_w_uv": nrm(ks[8], (N_A_LAYERS, A_HEADS, A_LATENT, A_VDIM), A_LATENT),
        "a_w_out": nrm(ks[9], (N_A_LAYERS, A_WIDTH, D), A_WIDTH),
        "kv_norm": gain(ks[10], (D,)),
        "w_kv": nrm(ks[11], (D, KV_OUT), D),
        "b_norm": gain(ks[12], (N_B_LAYERS, D)),
        "b_ada_w": nrm(ks[13], (N_B_LAYERS, D, 3 * D), D) * ADA_INIT,
        "b_ada_b": 0.02 * jax.random.normal(ks[14], (N_B_LAYERS, 3 * D), jnp.float32),
        "b_w_in": nrm(ks[15], (N_B_LAYERS, D, B_IN), D),
        "b_w_out": nrm(ks[16], (N_B_LAYERS, B_WIDTH, D), B_WIDTH),
        "final_norm": gain(ks[17], (D,)),
    }


def reference(x, c, positions, a_norm, a_ada_w, a_ada_b, a_w_in, a_kv_norm, a_w_uv, a_w_out,
              kv_norm, w_kv, b_norm, b_ada_w, b_ada_b, b_w_in, b_w_out, final_norm):
    h = x
    k_sh = v_sh = None
    for i in range(DEPTH):
        if i < N_A_LAYERS:
            hn, g = ada_prenorm(h, c, a_norm[i], a_ada_w[i], a_ada_b[i])
            h = h + g * dsa_mixer(hn, positions, a_w_in[i], a_kv_norm[i], a_w_uv[i], a_w_out[i])
        else:
            if i == N_A_LAYERS:
                k_sh, v_sh = shared_kv(h, positions, kv_norm, w_kv)
            j = i - N_A_LAYERS
            hn, g = ada_prenorm(h, c, b_norm[j], b_ada_w[j], b_ada_b[j])
            h = h + g * dilated_mixer(hn, positions, k_sh, v_sh, b_w_in[j], b_w_out[j])
    return rmsnorm(h, final_norm)
```

```python
import math
from contextlib import ExitStack
import numpy as np
import concourse.bass as bass
import concourse.mybir as mybir
from concourse.bass_utils import run_bass_kernel_spmd

F32 = mybir.dt.float32
BF16 = mybir.dt.bfloat16
I32 = mybir.dt.int32
AF = mybir.ActivationFunctionType
ALU = mybir.AluOpType
AX = mybir.AxisListType

ENGS = ("pe", "act", "dve", "pool", "sp")

D = 1024
S = 4096
NB = 2
NCORES = 8
NQB = S // 128
EPS = 1e-6
THETA = 10000.0
NIT = 22
TOPK = 256
NEG = -1.0e30
WA_COLS = 5768
WB_COLS = 6144


class Op:
    __slots__ = ("eng", "fn", "deps", "is_dma", "chan", "val", "flag")

    def __init__(self, eng, fn, is_dma=False, chan=None):
        self.eng = eng
        self.fn = fn
        self.deps = ()
        self.is_dma = is_dma
        self.chan = chan
        self.val = 0
        self.flag = False


class Prog:
    def __init__(self, nc):
        self.nc = nc
        self.chan_sem = {}
        self.chan_cnt = {}
        self.free_chan = []
        self.eng_sem = {}
        self.eng_cnt = {e: 0 for e in ENGS}
        self.phase_no = 0
        self.total_ops = 0
        self._reset()

    def _reset(self):
        self.ops = []
        self.last_w = {}
        self.readers = {}

    def add(self, eng, fn, reads=(), writes=(), chan=None):
        is_dma = chan is not None
        op = Op(eng, fn, is_dma, chan)
        psr = [k for k in reads if isinstance(k, str) and k.startswith("ps")]
        if psr:
            reads = [k for k in reads if k not in psr]
            writes = list(writes) + [k for k in psr if k not in writes]
        deps = {}
        for k in reads:
            w = self.last_w.get(k)
            if w is not None:
                deps[id(w)] = w
        for k in writes:
            w = self.last_w.get(k)
            if w is not None:
                deps[id(w)] = w
            for r in self.readers.get(k, ()):
                deps[id(r)] = r
        dl = []
        for d in deps.values():
            if d is op:
                continue
            if (not is_dma) and eng == "pe" and d.eng == "pe" and not d.is_dma:
                continue
            dl.append(d)
        op.deps = dl
        for k in reads:
            self.readers.setdefault(k, []).append(op)
        for k in writes:
            self.last_w[k] = op
            self.readers[k] = []
        self.ops.append(op)
        return op

    def op(self, eng, meth, reads, writes, *args, **kw):
        return self.add(eng, lambda e: getattr(e, meth)(*args, **kw), reads, writes)

    def dma(self, q, out, in_, reads, writes, chan):
        return self.add(q, lambda e: e.dma_start(out=out, in_=in_), reads, writes, chan=chan)

    def emit(self):
        nc = self.nc
        self.phase_no += 1
        ops = self.ops
        for op in ops:
            for d in op.deps:
                d.flag = True
        per_eng = {e: [] for e in ENGS}
        for op in ops:
            per_eng[op.eng].append(op)
        last_compute = {}
        for e in ENGS:
            for op in reversed(per_eng[e]):
                if not op.is_dma:
                    op.flag = True
                    last_compute[e] = op
                    break
        for e in last_compute:
            if e not in self.eng_sem:
                self.eng_sem[e] = nc.alloc_semaphore("eng_%s" % e)
        eng_sem = self.eng_sem
        cnt = self.eng_cnt
        for op in ops:
            if op.is_dma:
                if op.chan not in self.chan_sem:
                    if self.free_chan:
                        self.chan_sem[op.chan], self.chan_cnt[op.chan] = self.free_chan.pop()
                    else:
                        self.chan_sem[op.chan] = nc.alloc_semaphore("ch%d" % len(self.chan_sem))
                        self.chan_cnt[op.chan] = 0
                self.chan_cnt[op.chan] += 16
                op.val = self.chan_cnt[op.chan]
            elif op.flag:
                cnt[op.eng] += 1
                op.val = cnt[op.eng]
        final_eng = {e: (eng_sem[e], last_compute[e].val) for e in last_compute}
        final_chan = {c: (self.chan_sem[c], self.chan_cnt[c]) for c in self.chan_sem}

        def sem_of(d):
            return self.chan_sem[d.chan] if d.is_dma else eng_sem[d.eng]

        def run(e, engine):
            waited = {}
            for op in per_eng[e]:
                need = {}
                for d in op.deps:
                    s = sem_of(d)
                    k = id(s)
                    if k not in need or need[k][1] < d.val:
                        need[k] = (s, d.val)
                for k, (s, v) in need.items():
                    if waited.get(k, 0) >= v:
                        continue
                    engine.wait_ge(s, v)
                    waited[k] = v
                ins = op.fn(engine)
                if op.is_dma:
                    ins.then_inc(self.chan_sem[op.chan], 16)
                elif op.flag:
                    ins.then_inc(eng_sem[op.eng], 1)
            for e2, (s, v) in final_eng.items():
                if waited.get(id(s), 0) < v:
                    engine.wait_ge(s, v)
            for c, (s, v) in final_chan.items():
                if v > 0 and waited.get(id(s), 0) < v:
                    engine.wait_ge(s, v)

        with nc.Block() as block:
            @block.tensor
            def _(eng):
                run("pe", eng)

            @block.scalar
            def _(eng):
                run("act", eng)

            @block.vector
            def _(eng):
                run("dve", eng)

            @block.gpsimd
            def _(eng):
                run("pool", eng)

            @block.sync
            def _(eng):
                run("sp", eng)
        self.total_ops += len(ops)
        for c in list(self.chan_sem):
            self.free_chan.append((self.chan_sem.pop(c), self.chan_cnt.pop(c)))
        self._reset()


class Rot:
    def __init__(self, items):
        self.items = items
        self.i = 0

    def next(self):
        it = self.items[self.i % len(self.items)]
        self.i += 1
        return it


def build_program(debug=False, stop_after=99):
    nc = bass.Bass("TRN2", target_bir_lowering=False)
    P = Prog(nc)

    def din(name, shape, dt=F32):
        return nc.dram_tensor(name, list(shape), dt, kind="ExternalInput").ap()

    def dscr(name, shape, dt=BF16):
        return nc.dram_tensor(name, list(shape), dt).ap()

    x = din("x", [NB * S, D])
    pos = din("pos", [NB, S], I32)
    cT = din("cT", [128, 8 * NB])
    normsT = din("normsT", [128, 24])
    ada_w = [din("a_ada_w", [D, 3 * D]), din("b_ada_w", [D, 3 * D])]
    ada_bT = din("ada_bT", [128, 48])
    ada_bg = din("ada_bg", [1, 2 * D])
    WA = din("WA", [D, WA_COLS])
    WB = din("WB", [D, WB_COLS])
    akvg = din("akvg", [1, 128])
    wuv = din("wuv", [128, 16 * 128])
    woa = din("woa", [2 * D, D])
    wob = din("wob", [D, D])
    fng = din("fng", [1, D])
    cst = din("cst", [128, 640 + 3 + NIT])
    out = nc.dram_tensor("out", [NB * S, D], F32, kind="ExternalOutput").ap()

    QL = dscr("QL", [NB, NQB, 128, 2048])
    GT = dscr("GT", [NB, NQB, 128, 2048])
    QR = dscr("QR", [NB, NQB, 2, 128, 128])
    QI = dscr("QI", [NB, NQB, 2, 128, 128])
    QI2 = dscr("QI2", [NB, NQB, 2, 128, 128])
    QR2 = dscr("QR2", [NB, NQB, 2, 128, 128])
    KR1 = dscr("KR1", [NB, 128, S]); KR2 = dscr("KR2", [NB, 128, S])
    KI1 = dscr("KI1", [NB, 128, S]); KI2 = dscr("KI2", [NB, 128, S])
    VA = dscr("VA", [NB, S, 128])
    KL = dscr("KL", [NB, 128, S])
    WI = dscr("WI", [NB, S, 8], F32)
    H1 = dscr("H1", [NB * S, D], F32)
    KT2 = dscr("KT2", [NB, 4, 2, 128, S])
    QT2 = dscr("QT2", [NB, 3, 4, 2, 128, S])
    GB = dscr("GB", [NB, 8, 128, S])
    VB = dscr("VB", [NB, S, D])
    YT = dscr("YT", [NB, 8, 128, S])

    dbg = {}
    if debug:
        for nm, shp, dt in (("d_QL", [128, 2048], BF16), ("d_H1", [NB * S, D], F32), ("d_mod", [128, 64], F32),
                            ("d_KL", [128, S], BF16), ("d_IS", [128, S], F32), ("d_lo", [128, 8], F32),
                            ("d_YT", [128, S], BF16), ("d_KT", [128, S], BF16)):
            dbg[nm] = nc.dram_tensor(nm, shp, dt, kind="ExternalOutput").ap()

    def sb(name, shape, dt=F32):
        return nc.alloc_sbuf_tensor(name, list(shape), dt)

    CST = sb("CST", [128, 640 + 3 + NIT])
    IDF = CST[:, 0:128]
    MPREV_F = CST[:, 128:256]
    MCUR_F = CST[:, 256:384]
    CAUS = CST[:, 384:512]
    INV = CST[:, 640:643]
    POW2 = CST[:, 643:643 + NIT]
    IDB = sb("IDB", [128, 128], BF16)
    ONESB = sb("ONESB", [128, 128], BF16)
    ONESF = sb("ONESF", [1, 128], F32)
    HALFPI = sb("HALFPI", [128, 1], F32)
    MASKB = sb("MASKB", [128, 256], BF16)
    NRM = sb("NRM", [128, 24])
    ASCL = [sb("ASCL%d" % l, [128, 8 * NB]) for l in range(2)]
    ASFT = [sb("ASFT%d" % l, [128, 8 * NB]) for l in range(2)]
    GBC = [sb("GBC%d" % l, [128, NB * D]) for l in range(2)]
    AKVG = sb("AKVG", [128, 128])
    FNG = sb("FNG", [128, D])

    ps = [nc.alloc_psum_tensor("ps%d" % i, [128, 512], F32) for i in range(8)]

    P.dma("sp", CST[:], cst, [], ["CST"], "l0")
    P.dma("sp", NRM[:], normsT, [], ["NRM"], "l1")
    P.dma("sp", AKVG[:], akvg.partition_broadcast(128), [], ["AKVG"], "l2")
    P.dma("sp", FNG[:], fng.partition_broadcast(128), [], ["FNG"], "l3")
    P.add("dve", lambda e: e.tensor_copy(out=IDB[:], in_=IDF), ["CST"], ["IDB"])
    P.add("dve", lambda e: e.memset(ONESB[:], 1.0), [], ["ONESB"])
    P.add("dve", lambda e: e.memset(ONESF[:], 1.0), [], ["ONESF"])
    P.add("dve", lambda e: e.memset(HALFPI[:], math.pi / 2), [], ["HALFPI"])
    P.add("dve", lambda e: e.tensor_copy(out=MASKB[:], in_=CST[:, 128:384]), ["CST"], ["MASKB"])

    with ExitStack() as es:
        W0 = es.enter_context(nc.sbuf_tensor("p0_w", [128, 8, 3 * D], F32))
        C0 = es.enter_context(nc.sbuf_tensor("p0_c", [128, 8 * NB], F32))
        SC0 = es.enter_context(nc.sbuf_tensor("p0_sc", [128, 8 * NB], F32))
        SCB = es.enter_context(nc.sbuf_tensor("p0_scb", [128, 8 * NB, 128], F32))
        BT0 = es.enter_context(nc.sbuf_tensor("p0_bT", [128, 48], F32))
        BG0 = es.enter_context(nc.sbuf_tensor("p0_bg", [128, 2 * D], F32))
        M0 = es.enter_context(nc.sbuf_tensor("p0_m", [128, 16 * NB], F32))
        P.dma("sp", C0[:], cT, [], ["C0"], "l4")
        P.dma("sp", BT0[:], ada_bT, [], ["BT0"], "l5")
        P.dma("sp", BG0[:], ada_bg.partition_broadcast(128), [], ["BG0"], "l6")
        P.add("act", lambda e: e.activation(out=SC0[:], in_=C0[:], func=AF.Silu), ["C0"], ["SC0"])
        P.add("dve", lambda e: e.tensor_copy(out=SCB[:], in_=SC0[:].unsqueeze(2).to_broadcast([128, 8 * NB, 128])),
              ["SC0"], ["SCB"])
        for l in range(2):
            for kc in range(8):
                P.dma("sp", W0[:, kc, :], ada_w[l][kc * 128:(kc + 1) * 128, :], [], [("W0", kc)], "w%d" % kc)
            mps = ps[0]
            for j in range(16):
                for kc in range(8):
                    P.add("pe", lambda e, j=j, kc=kc: e.matmul(
                        mps[:, j * NB:(j + 1) * NB], lhsT=W0[:, kc, j * 128:(j + 1) * 128],
                        rhs=SC0[:, kc * NB:(kc + 1) * NB], start=(kc == 0), stop=(kc == 7)),
                        [("W0", kc), "SC0"], ["psmps"])
            P.add("dve", lambda e, l=l: e.tensor_tensor(
                out=M0[:].rearrange("p (j b) -> p j b", b=NB), in0=mps[:, 0:16 * NB].rearrange("p (j b) -> p j b", b=NB),
                in1=BT0[:, l * 24:l * 24 + 16].unsqueeze(2).to_broadcast([128, 16, NB]), op=ALU.add),
                ["psmps", "BT0"], ["M0"])
            P.add("dve", lambda e, l=l: e.tensor_copy(out=ASFT[l][:], in_=M0[:, 0:8 * NB]), ["M0"], ["ASFT%d" % l])
            P.add("dve", lambda e, l=l: e.scalar_tensor_tensor(
                out=ASCL[l][:].rearrange("p (j b) -> p j b", b=NB), in0=M0[:, 8 * NB:16 * NB].rearrange("p (j b) -> p j b", b=NB),
                scalar=1.0, in1=NRM[:, l * 8:(l + 1) * 8].unsqueeze(2).to_broadcast([128, 8, NB]),
                op0=ALU.add, op1=ALU.mult), ["M0", "NRM"], ["ASCL%d" % l])
            for b in range(NB):
                for nh in range(2):
                    gps = ps[1 + (b * 2 + nh) % 2]
                    gkey = "psg%d" % ((b * 2 + nh) % 2)
                    for kc in range(8):
                        P.add("pe", lambda e, b=b, nh=nh, kc=kc, gps=gps: e.matmul(
                            gps[:], lhsT=SCB[:, kc * NB + b, :], rhs=W0[:, kc, 2 * D + nh * 512:2 * D + (nh + 1) * 512],
                            start=(kc == 0), stop=(kc == 7)), [("W0", kc), "SCB"], [gkey])
                    P.add("dve", lambda e, l=l, b=b, nh=nh, gps=gps: e.tensor_tensor(
                        out=GBC[l][:, b * D + nh * 512:b * D + (nh + 1) * 512], in0=gps[:],
                        in1=BG0[:, l * D + nh * 512:l * D + (nh + 1) * 512], op=ALU.add), [gkey, "BG0"], ["GBC%d" % l])
        if debug:
            P.dma("pool", dbg["d_mod"][:, 0:16], ASCL[0][:], ["ASCL0"], [], "s0")
            P.dma("pool", dbg["d_mod"][:, 16:32], ASFT[0][:], ["ASFT0"], [], "s0")
            P.dma("pool", dbg["d_mod"][:, 32:48], ASCL[1][:], ["ASCL1"], [], "s0")
            P.dma("pool", dbg["d_mod"][:, 48:64], GBC[0][:, 0:16], ["GBC0"], [], "s0")
        P.emit()
    if stop_after == 0:
        return nc, dbg, P

    def load_weights(Wd, Wsb, ncols, stg, wkey):
        npc = 4
        pw = ncols // npc
        engs = ("dve", "act")
        i = 0
        for kc in range(8):
            for pc in range(npc):
                st, sk = stg.next()
                P.dma("sp", st[:, 0:pw], Wd[kc * 128:(kc + 1) * 128, pc * pw:(pc + 1) * pw], [], [sk], "ws%d" % (i % 2))
                en = engs[i % 2]
                if en == "act":
                    P.add("act", lambda e, st=st, kc=kc, pc=pc: e.copy(out=Wsb[:, kc, pc * pw:(pc + 1) * pw], in_=st[:, 0:pw]),
                          [sk], [(wkey, kc, pc)])
                else:
                    P.add(en, lambda e, st=st, kc=kc, pc=pc: e.tensor_copy(out=Wsb[:, kc, pc * pw:(pc + 1) * pw], in_=st[:, 0:pw]),
                          [sk], [(wkey, kc, pc)])
                i += 1
        return [[(wkey, kc, pc) for pc in range(npc)] for kc in range(8)]

    def rope_tables(b, t0, specs, POSF, tabs, tmps):
        import os as _os
        lvl = int(_os.environ.get("K_TABLVL", 9))
        for (icol, ti) in specs:
            ct, st_, tk = tabs[ti]
            for which, tile_, off in ((0, ct, 0.25), (1, st_, 0.0)):
                ki, kf, kk = tmps.next()
                P.add("pool", lambda e, ki=ki, icol=icol, off=off: e.tensor_scalar(
                    out=ki[:], in0=POSF[:], scalar1=INV[:, icol:icol + 1], scalar2=off, op0=ALU.mult, op1=ALU.add),
                    ["POSF", "CST"], [kk + "i"])
                if lvl < 2:
                    continue
                P.add("pool", lambda e, ki=ki, kf=kf: e.tensor_copy(out=kf[:], in_=ki[:]), [kk + "i"], [kk + "f"])
                if lvl < 3:
                    continue
                P.add("dve", lambda e, kf=kf, icol=icol: e.scalar_tensor_tensor(
                    out=kf[:], in0=POSF[:], scalar=INV[:, icol:icol + 1], in1=kf[:], op0=ALU.mult, op1=ALU.subtract),
                    ["POSF", "CST", kk + "f"], [kk + "f"])
                if lvl < 4:
                    continue
                _sb = _os.environ.get("K_SINB", "")
                if _sb == "zero":
                    P.add("act", lambda e, kf=kf, tile_=tile_, off=off: e.activation(
                        out=tile_[:], in_=kf[:], func=AF.Sin, scale=2 * math.pi), [kk + "f"], [(tk, which)])
                elif _sb == "noscale":
                    P.add("act", lambda e, kf=kf, tile_=tile_, off=off: e.activation(
                        out=tile_[:], in_=kf[:], func=AF.Sin), [kk + "f"], [(tk, which)])
                elif off == 0.0:
                    P.add("act", lambda e, kf=kf, tile_=tile_, off=off: e.activation(
                        out=tile_[:], in_=kf[:], func=AF.Sin, scale=2 * math.pi), [kk + "f"], [(tk, which)])
                else:
                    P.add("act", lambda e, kf=kf, tile_=tile_, off=off: e.activation(
                        out=tile_[:], in_=kf[:], func=AF.Sin, scale=2 * math.pi, bias=HALFPI[:]),
                        [kk + "f", "HALFPI"], [(tk, which)])

    def norm_transpose(src_rows, XT, xkey, sub, SS, evacs, psT_pair, junk):
        import os as _os
        nlvl = int(_os.environ.get("K_NTLVL", 9))
        ssk = xkey + "ss"
        P.add("act", lambda e: e.activation(out=junk[:], in_=XT[:], func=AF.Square, accum_out=SS[:, 0:1]),
              [xkey], ["junk", ssk])
        P.add("act", lambda e: e.activation(out=SS[:, 1:2], in_=SS[:, 0:1], func=AF.Sqrt, scale=1.0 / D, bias=EPS),
              [ssk], [ssk + "b"])
        P.add("dve", lambda e: e.reciprocal(out=SS[:, 2:3], in_=SS[:, 1:2]), [ssk + "b"], [ssk + "c"])
        P.add("dve", lambda e: e.tensor_scalar(out=XT[:], in0=XT[:], scalar1=SS[:, 2:3], scalar2=None, op0=ALU.mult),
              [xkey, ssk + "c"], [xkey])
        if nlvl < 2:
            return
        for half in range(2):
            pt, pk = psT_pair[half]
            for q in range(4):
                kc = half * 4 + q
                P.add("pe", lambda e, pt=pt, q=q, kc=kc: e.transpose(
                    out=pt[:, q * 128:(q + 1) * 128], in_=XT[:, kc * 128:(kc + 1) * 128], identity=IDF),
                    [xkey, "CST"], [pk])
            for q in range(4):
                kc = half * 4 + q
                if nlvl >= 3:
                    evacs(kc, pt[:, q * 128:(q + 1) * 128], pk)

    with ExitStack() as es:
        W1 = es.enter_context(nc.sbuf_tensor("p1_w", [128, 8, WA_COLS], BF16))
        STG = es.enter_context(nc.sbuf_tensor("p1_stg", [128, 2, WA_COLS // 4], F32))
        X1 = es.enter_context(nc.sbuf_tensor("p1_x", [128, 2, D], F32))
        JUNK = es.enter_context(nc.sbuf_tensor("p1_junk", [128, D], BF16))
        SS1 = es.enter_context(nc.sbuf_tensor("p1_ss", [128, 2, 4], F32))
        HN = es.enter_context(nc.sbuf_tensor("p1_hn", [128, 2, 8, 512], BF16))
        POSI = es.enter_context(nc.sbuf_tensor("p1_posi", [128, 512], I32))
        POSF = es.enter_context(nc.sbuf_tensor("p1_posf", [128, 512], F32))
        TAB = es.enter_context(nc.sbuf_tensor("p1_tab", [128, 4, 512], F32))
        TKI = es.enter_context(nc.sbuf_tensor("p1_ki", [128, 1, 512], I32))
        TKF = es.enter_context(nc.sbuf_tensor("p1_kf", [128, 1, 512], F32))
        STQ = es.enter_context(nc.sbuf_tensor("p1_sq", [128, 2, 4, 8, 128], BF16))
        RO = es.enter_context(nc.sbuf_tensor("p1_ro", [128, 4, 512], BF16))
        RT = es.enter_context(nc.sbuf_tensor("p1_rt", [128, 4, 512], F32))
        VST = es.enter_context(nc.sbuf_tensor("p1_v", [128, 2, 4, 128], BF16))
        KLST = es.enter_context(nc.sbuf_tensor("p1_kl", [128, 2, 512], BF16))
        WIST = es.enter_context(nc.sbuf_tensor("p1_wi", [128, 2, 4, 8], F32))
        KVS = es.enter_context(nc.sbuf_tensor("p1_kv", [128, 8], F32))
        stg = Rot([(STG[:, i, :], "stg%d" % i) for i in range(2)])
        wkeys = load_weights(WA, W1, WA_COLS, stg, "W1")
        allw = [k for kc in range(8) for k in wkeys[kc]]
        tabs = [(TAB[:, 0, :], TAB[:, 1, :], "tab32"), (TAB[:, 2, :], TAB[:, 3, :], "tab64")]
        tmps = Rot([(TKI[:, i, :], TKF[:, i, :], "tk%d" % i) for i in range(1)])
        fm = Rot([(ps[i], "ps%d" % i) for i in range(4)])
        psT_pair = [(ps[4], "ps4"), (ps[5], "ps5")]
        psTok = (ps[6], "ps6")
        psVT = ps[7][:].bitcast(BF16)
        stq = Rot([(STQ[:, i], "stq%d" % i) for i in range(2)])
        ro = Rot([(RO[:, i, :], "ro%d" % i) for i in range(4)])
        rt = Rot([(RT[:, i, :], "rt%d" % i) for i in range(4)])
        import os as _os
        ntile = int(_os.environ.get('K_NT', NB * S // 512))
        _skip = _os.environ.get('K_SKIP', '')
        for tt in range(ntile):
            b = tt // 8
            t0 = (tt % 8) * 512
            qb0 = t0 // 128
            hn = HN[:, tt % 2]
            hk = "hn%d" % (tt % 2)
            P.dma("sp", POSI[:], pos[b:b + 1, t0:t0 + 512].partition_broadcast(128), [], ["POSI"], "POSI")
            P.add("pool", lambda e: e.tensor_copy(out=POSF[:], in_=POSI[:]), ["POSI"], ["POSF"])
            if 'tab' not in _skip:
                rope_tables(b, t0, [(0, 0), (1, 1)], POSF, tabs, tmps)
            vst = VST[:, tt % 2]; vk = "vst%d" % (tt % 2)
            klst = KLST[:, tt % 2, :]; klk = "klst%d" % (tt % 2)
            wist = WIST[:, tt % 2]; wik = "wist%d" % (tt % 2)
            for sub in range(4):
                xi = (tt * 4 + sub) % 2
                XT = X1[:, xi, :]
                xkey = "x1_%d" % xi
                r0 = b * S + t0 + sub * 128
                P.dma("sp", XT, x[r0:r0 + 128, :], [], [xkey], "lx%d" % xi)

                def evacs(kc, pap, pk, sub=sub, b=b, hn=hn, hk=hk):
                    dst = hn[:, kc, sub * 128:(sub + 1) * 128]
                    _ev = _os.environ.get('K_EV', '')
                    if (kc < 4 and _ev != 'dve') or _ev == 'act':
                        P.add("act", lambda e: e.activation(
                            out=dst, in_=pap, func=AF.Identity, scale=ASCL[0][:, kc * NB + b:kc * NB + b + 1],
                            bias=ASFT[0][:, kc * NB + b:kc * NB + b + 1]), [pk, "ASCL0", "ASFT0"], [(hk, kc)])
                    else:
                        P.add("dve", lambda e: e.tensor_scalar(
                            out=dst, in0=pap, scalar1=ASCL[0][:, kc * NB + b:kc * NB + b + 1],
                            scalar2=ASFT[0][:, kc * NB + b:kc * NB + b + 1], op0=ALU.mult, op1=ALU.add),
                            [pk, "ASCL0", "ASFT0"], [(hk, kc)])
                if 'nt' not in _skip:
                    norm_transpose(None, XT, xkey, sub, SS1[:, xi, :], evacs, psT_pair, JUNK)
            hkeys = [(hk, kc) for kc in range(8)]
            for sub in (range(0) if 'tok' in _skip else range(4)):
                pt, pk = psTok
                for kc in range(8):
                    P.add("pe", lambda e, kc=kc, sub=sub, pt=pt, hn=hn: e.matmul(
                        pt[:, 0:136], lhsT=hn[:, kc, sub * 128:(sub + 1) * 128], rhs=W1[:, kc, 5632:5768],
                        start=(kc == 0), stop=(kc == 7)), [(hk, kc)] + wkeys[kc], [pk])
                P.add("act", lambda e, pt=pt: e.activation(out=JUNK[:, 0:128], in_=pt[:, 0:128], func=AF.Square,
                                                           accum_out=KVS[:, 0:1]), [pk], ["junk", "kvs0"])
                P.add("act", lambda e: e.activation(out=KVS[:, 1:2], in_=KVS[:, 0:1], func=AF.Sqrt, scale=1.0 / 128, bias=EPS),
                      ["kvs0"], ["kvs1"])
                P.add("dve", lambda e: e.reciprocal(out=KVS[:, 2:3], in_=KVS[:, 1:2]), ["kvs1"], ["kvs2"])
                P.add("dve", lambda e, pt=pt, sub=sub, vst=vst: e.scalar_tensor_tensor(
                    out=vst[:, sub, :], in0=pt[:, 0:128], scalar=KVS[:, 2:3], in1=AKVG[:], op0=ALU.mult, op1=ALU.mult),
                    [pk, "kvs2", "AKVG"], [(vk, sub)])
                P.add("act", lambda e, pt=pt, sub=sub, wist=wist: e.mul(out=wist[:, sub, :], in_=pt[:, 128:136], mul=8.0 ** -0.5),
                      [pk], [(wik, sub)])
                P.add("pe", lambda e, sub=sub, vst=vst: e.transpose(out=psVT[:, sub * 128:(sub + 1) * 128], in_=vst[:, sub, :],
                                                                    identity=IDB[:]), [(vk, sub), "IDB"], ["psvt"])
            if 'tail' not in _skip:
                P.add("act", lambda e, klst=klst: e.copy(out=klst, in_=psVT[:, 0:512]), ["psvt"], [klk])
                P.dma("pool", VA[b, t0:t0 + 512, :].rearrange("(s p) c -> p s c", p=128), vst[:], [(vk, s_) for s_ in range(4)], [], vk)
                P.dma("pool", WI[b, t0:t0 + 512, :].rearrange("(s p) c -> p s c", p=128), wist[:], [(wik, s_) for s_ in range(4)], [], wik)
                P.dma("pool", KL[b, :, t0:t0 + 512], klst, [klk], [], klk)
            if debug and tt == 0:
                P.dma("pool", dbg["d_KL"][:, 0:512], klst, [klk], [], klk)

            def fm_chunk(ci, hn=hn, hk=hk):
                pt, pk = fm.next()
                for kc in range(8):
                    P.add("pe", lambda e, kc=kc, pt=pt, ci=ci, hn=hn: e.matmul(
                        pt[:], lhsT=W1[:, kc, ci * 128:(ci + 1) * 128], rhs=hn[:, kc, :], start=(kc == 0), stop=(kc == 7)),
                        [(hk, kc)] + wkeys[kc], [pk])
                return pt, pk
            for grp, func, dst in (() if 'fm' in _skip else ((0, None, QL), (1, AF.Silu, GT))):
              for hg in range(2):
                sq, sqk = stq.next()
                for hl in range(8):
                    h = hg * 8 + hl
                    pt, pk = fm_chunk(grp * 16 + h)
                    o = sq[:, :, hl, :]
                    i_ = pt[:].rearrange("p (a q) -> p a q", q=128)
                    if func is None:
                        if h % 2 == 0:
                            P.add("act", lambda e, o=o, i_=i_: e.copy(out=o, in_=i_), [pk], [(sqk, hl)])
                        else:
                            P.add("dve", lambda e, o=o, i_=i_: e.tensor_copy(out=o, in_=i_), [pk], [(sqk, hl)])
                    else:
                        P.add("act", lambda e, o=o, i_=i_: e.activation(out=o, in_=i_, func=AF.Silu), [pk], [(sqk, hl)])
                P.dma("pool", dst[b, qb0:qb0 + 4, :, hg * 1024:(hg + 1) * 1024].rearrange("a p f -> p a f"),
                      sq.rearrange("p a h q -> p a (h q)"), [(sqk, hl) for hl in range(8)], [], sqk)
                if debug and tt == 0 and grp == 0:
                    P.dma("pool", dbg["d_QL"][:, hg * 1024:(hg + 1) * 1024], sq[:, 0].rearrange("p h q -> p (h q)"),
                          [(sqk, hl) for hl in range(8)], [], sqk)
            pairs = [(32, 0, "qr", 0), (34, 0, "qr", 1), (36, 1, "qi", 0), (38, 1, "qi", 1), (40, 0, "kr", 0), (42, 1, "ki", 0)]
            for (c0, ti, kind, half) in ([] if 'rope' in _skip else pairs):
                ct, st_, tk = tabs[ti]
                p1, k1 = fm_chunk(c0)
                p2, k2 = fm_chunk(c0 + 1)
                ta, tak = rt.next(); tb, tbk = rt.next()
                oa, oak = ro.next(); ob, obk = ro.next()
                P.add("dve", lambda e, ta=ta, p1=p1, ct=ct: e.tensor_tensor(out=ta, in0=p1[:], in1=ct, op=ALU.mult), [k1, (tk, 0)], [tak])
                P.add("dve", lambda e, tb=tb, p2=p2, st_=st_: e.tensor_tensor(out=tb, in0=p2[:], in1=st_, op=ALU.mult), [k2, (tk, 1)], [tbk])
                P.add("dve", lambda e, oa=oa, ta=ta, tb=tb: e.tensor_tensor(out=oa, in0=ta, in1=tb, op=ALU.subtract), [tak, tbk], [oak])
                tc_, tck = rt.next(); td, tdk = rt.next()
                P.add("dve", lambda e, tc_=tc_, p2=p2, ct=ct: e.tensor_tensor(out=tc_, in0=p2[:], in1=ct, op=ALU.mult), [k2, (tk, 0)], [tck])
                P.add("dve", lambda e, td=td, p1=p1, st_=st_: e.tensor_tensor(out=td, in0=p1[:], in1=st_, op=ALU.mult), [k1, (tk, 1)], [tdk])
                P.add("dve", lambda e, ob=ob, tc_=tc_, td=td: e.tensor_tensor(out=ob, in0=tc_, in1=td, op=ALU.add), [tck, tdk], [obk])
                if kind == "qr":
                    P.dma("pool", QR[b, qb0:qb0 + 4, half].rearrange("a p q -> p a q"), oa.rearrange("p (a q) -> p a q", q=128), [oak], [], oak)
                    P.dma("pool", QR2[b, qb0:qb0 + 4, half].rearrange("a p q -> p a q"), ob.rearrange("p (a q) -> p a q", q=128), [obk], [], obk)
                elif kind == "qi":
                    P.dma("pool", QI[b, qb0:qb0 + 4, half].rearrange("a p q -> p a q"), oa.rearrange("p (a q) -> p a q", q=128), [oak], [], oak)
                    P.dma("pool", QI2[b, qb0:qb0 + 4, half].rearrange("a p q -> p a q"), ob.rearrange("p (a q) -> p a q", q=128), [obk], [], obk)
                elif kind == "kr":
                    P.dma("pool", KR1[b, :, t0:t0 + 512], oa, [oak], [], oak)
                    P.dma("pool", KR2[b, :, t0:t0 + 512], ob, [obk], [], obk)
                else:
                    P.dma("pool", KI1[b, :, t0:t0 + 512], oa, [oak], [], oak)
                    P.dma("pool", KI2[b, :, t0:t0 + 512], ob, [obk], [], obk)
        P.emit()

    if stop_after == 1:
        return nc, dbg, P

    import os as _os
    SCALE_A = (128 + 32) ** -0.5
    BIGM = 30000.0
    with ExitStack() as es:
        def T(name, shape, dt=F32):
            return es.enter_context(nc.sbuf_tensor(name, list(shape), dt))
        WOA = T("p2_woa", [128, 16, D], BF16)
        WUV = T("p2_wuv", [128, 16 * 128], BF16)
        STG2 = T("p2_stg", [128, 2, 1024], F32)
        KIs = T("p2_ki", [128, S], BF16)
        KLs = T("p2_kl", [128, S], BF16)
        KRs = T("p2_kr", [128, S], BF16)
        Vs = T("p2_v", [128, NQB, 128], BF16)
        QIq = T("p2_qi", [128, 2, 4, 128], BF16)
        WIq = T("p2_wi", [128, 2, 8], F32)
        WAB = T("p2_wab", [128, 2, 16], F32)
        QLq = T("p2_ql", [128, 2, 16, 128], BF16)
        QRq = T("p2_qr", [128, 2, 16, 128], BF16)
        GTq = T("p2_gt", [128, 2, 16, 128], BF16)
        Xq = T("p2_x", [128, 2, D], F32)
        IS = T("p2_is", [128, S], F32)
        JNK = T("p2_jnk", [128, S], BF16)
        MS = T("p2_ms", [128, S], BF16)
        MT = T("p2_mt", [128, NQB, 128], BF16)
        TMPI = T("p2_tmpi", [128, 3, 512], F32)
        PT = T("p2_pt", [128, 3, 512], BF16)
        RD = T("p2_rd", [128, 1024], F32)
        ON = T("p2_on", [128, 1024], BF16)
        Y = T("p2_y", [128, 16, 128], BF16)
        H1q = T("p2_h1", [128, 2, D], F32)
        TMPH = T("p2_tmph", [128, D], F32)
        BS = T("p2_bs", [128, 8 + NIT], F32)
        NEG30 = T("p2_neg", [128, 1], F32)

        P.op("dve", "memset", [], ["NEG30"], NEG30[:], -BIGM)
        P.op("dve", "memset", [], ["KRs"], KRs[:], 0.0)
        P.op("pool", "memset", [], ["QRq0", "QRq1"], QRq[:], 0.0)
        for h in range(16):
            st = STG2[:, h % 2, :]
            P.dma("sp", st, woa[h * 128:(h + 1) * 128, :], [], ["stg2_%d" % (h % 2)], "stg2_%d" % (h % 2))
            if h % 2 == 0:
                P.op("dve", "tensor_copy", ["stg2_0"], [("WOA", h)], out=WOA[:, h, :], in_=st)
            else:
                P.op("act", "copy", ["stg2_1"], [("WOA", h)], out=WOA[:, h, :], in_=st)
        for i in range(2):
            st = STG2[:, i, :]
            P.dma("sp", st, wuv[:, i * 1024:(i + 1) * 1024], [], ["stg2_%d" % i], "stg2_%d" % i)
            P.op("dve", "tensor_copy", ["stg2_%d" % i], [("WUV", i)], out=WUV[:, i * 1024:(i + 1) * 1024], in_=st)
        woa_keys = [("WOA", h) for h in range(16)]
        wpool = Rot([(ps[i], "ps%d" % i) for i in range(4)])
        psO = [(ps[4], "ps4"), (ps[5], "ps5")]
        psD = [(ps[6], "ps6"), (ps[7], "ps7")]
        tmpi = Rot([(TMPI[:, i, :], "tmpi%d" % i) for i in range(3)])
        ptr = Rot([(PT[:, i, :], "pt%d" % i) for i in range(3)])
        nblk = int(_os.environ.get("K_NBLK", NB * NQB))
        dbg_qb = int(_os.environ.get("K_DBGQB", 3))
        for blk in range(nblk):
            b = blk // NQB
            qb = blk % NQB
            sl = blk % 2
            nk = (qb + 1) * 128
            nkc = qb + 1
            if qb == 0:
                ki1 = KI1[b].rearrange("(i r) t -> r i t", r=4)[0]
                ki2 = KI2[b].rearrange("(i r) t -> r i t", r=4)[0]
                P.dma("sp", KIs[0:32, :], ki1, [], ["KIs"], "KIs")
                P.dma("sp", KIs[32:64, :], ki2, [], ["KIs"], "KIs")
                P.dma("sp", KIs[64:96, :], ki1, [], ["KIs"], "KIs")
                P.dma("sp", KIs[96:128, :], ki2, [], ["KIs"], "KIs")
                P.dma("sp", KLs[:], KL[b], [], ["KLs"], "KLs")
                P.dma("sp", KRs[0:16, :], KR1[b].rearrange("(i r) t -> r i t", r=8)[0], [], ["KRs"], "KRs")
                P.dma("sp", KRs[16:32, :], KR2[b].rearrange("(i r) t -> r i t", r=8)[0], [], ["KRs"], "KRs")
                P.dma("sp", Vs[:], VA[b].rearrange("(c p) d -> p c d", p=128), [], ["Vs"], "Vs")
            qik = "QIq%d" % sl
            for h2 in range(2):
                for xp, src in ((0, QI), (1, QI2)):
                    for half in range(2):
                        sv = src[b, qb, half].rearrange("(i pp h2) q -> h2 i pp q", pp=2, h2=2)[h2]
                        dv = QIq[h2 * 64 + xp * 32:h2 * 64 + xp * 32 + 32, sl, half * 2:half * 2 + 2, :]
                        P.dma("sp", dv, sv, [], [qik], qik)
            P.dma("sp", WIq[:, sl, :], WI[b, qb * 128:(qb + 1) * 128, :], [], ["WIq%d" % sl], "WIq%d" % sl)
            P.dma("sp", QLq[:, sl].rearrange("p h q -> p (h q)"), QL[b, qb], [], ["QLq%d" % sl], "QLq%d" % sl)
            for half in range(2):
                P.dma("sp", QRq[0:16, sl, half * 8:(half + 1) * 8, :], QR[b, qb, half].rearrange("(i hh) q -> i hh q", hh=8),
                      [], ["QRq%d" % sl], "QRq%d" % sl)
                P.dma("sp", QRq[16:32, sl, half * 8:(half + 1) * 8, :], QR2[b, qb, half].rearrange("(i hh) q -> i hh q", hh=8),
                      [], ["QRq%d" % sl], "QRq%d" % sl)
            P.dma("sp", GTq[:, sl].rearrange("p h q -> p (h q)"), GT[b, qb], [], ["GTq%d" % sl], "GTq%d" % sl)
            r0 = b * S + qb * 128
            P.dma("sp", Xq[:, sl, :], x[r0:r0 + 128, :], [], ["Xq%d" % sl], "Xq%d" % sl)
            wabk = "WAB%d" % sl
            P.op("act", "activation", ["WIq%d" % sl], [wabk + "a"], out=WAB[:, sl, 0:8], in_=WIq[:, sl, :], func=AF.Abs)
            P.op("act", "activation", ["WIq%d" % sl], [wabk + "s"], out=WAB[:, sl, 8:16], in_=WIq[:, sl, :], func=AF.Sign)
            nc5 = (nk + 511) // 512
            iskeys = [("IS", c5) for c5 in range(nc5)]
            for c5 in range(nc5):
                w = min(512, nk - c5 * 512)
                cs = slice(c5 * 512, c5 * 512 + w)
                for h in range(8):
                    pt_, pk = wpool.next()
                    pb = (h % 2) * 64
                    P.op("pe", "matmul", [qik, "KIs"], [pk], pt_[:, 0:w], lhsT=QIq[pb:pb + 64, sl, h // 2, :], rhs=KIs[pb:pb + 64, cs],
                         start=True, stop=True)
                    tm, tmk = tmpi.next()
                    P.op("act", "activation", [pk, wabk + "a"], [tmk], out=tm[:, 0:w], in_=pt_[:, 0:w], func=AF.Relu, scale=WAB[:, sl, h:h + 1])
                    if h == 0:
                        P.op("dve", "tensor_scalar", [tmk, wabk + "s"], [("IS", c5)], out=IS[:, cs], in0=tm[:, 0:w],
                             scalar1=WAB[:, sl, 8:9], scalar2=None, op0=ALU.mult)
                    else:
                        P.op("dve", "scalar_tensor_tensor", [tmk, wabk + "s", ("IS", c5)], [("IS", c5)], out=IS[:, cs], in0=tm[:, 0:w],
                             scalar=WAB[:, sl, 8 + h:9 + h], in1=IS[:, cs], op0=ALU.mult, op1=ALU.add)
            P.op("dve", "tensor_reduce", iskeys, ["bs_mx"], out=BS[:, 0:1], in_=IS[:, 0:nk], axis=AX.X, op=ALU.max)
            P.op("dve", "tensor_reduce", iskeys, ["bs_mn"], out=BS[:, 1:2], in_=IS[:, 0:nk], axis=AX.X, op=ALU.min)
            P.op("dve", "tensor_tensor", ["bs_mx", "bs_mn"], ["bs_w0"], out=BS[:, 2:3], in0=BS[:, 0:1], in1=BS[:, 1:2], op=ALU.subtract)
            P.op("dve", "tensor_scalar", ["bs_w0", "CST"], ["bs_steps"], out=BS[:, 8:8 + NIT], in0=POW2, scalar1=BS[:, 2:3], scalar2=None, op0=ALU.mult)
            P.op("dve", "tensor_copy", ["bs_mn"], ["bs_lo"], out=BS[:, 3:4], in_=BS[:, 1:2])
            dk = ("IS", (nk - 128) // 512)
            P.op("dve", "tensor_tensor", [dk, "CST"], [dk], out=IS[:, nk - 128:nk], in0=IS[:, nk - 128:nk], in1=CAUS, op=ALU.add)
            for it in range(NIT):
                P.op("dve", "tensor_tensor", ["bs_lo", "bs_steps"], ["bs_mid"], out=BS[:, 4:5], in0=BS[:, 3:4], in1=BS[:, 8 + it:9 + it], op=ALU.add)
                P.op("dve", "tensor_scalar", iskeys + ["bs_mid"], ["JNK", "bs_cnt"], out=JNK[:, 0:nk], in0=IS[:, 0:nk], scalar1=BS[:, 4:5],
                     scalar2=0.0, op0=ALU.is_ge, op1=ALU.add, accum_out=BS[:, 5:6])
                P.op("dve", "scalar_tensor_tensor", ["bs_cnt", "bs_steps"], ["bs_t"], out=BS[:, 6:7], in0=BS[:, 5:6], scalar=TOPK - 0.5,
                     in1=BS[:, 8 + it:9 + it], op0=ALU.is_ge, op1=ALU.mult)
                P.op("dve", "tensor_tensor", ["bs_lo", "bs_t"], ["bs_lo"], out=BS[:, 3:4], in0=BS[:, 3:4], in1=BS[:, 6:7], op=ALU.add)
            P.op("dve", "tensor_scalar", iskeys + ["bs_lo"], ["MS"], out=MS[:, 0:nk], in0=IS[:, 0:nk], scalar1=BS[:, 3:4], scalar2=None, op0=ALU.is_ge)
            if debug and b == 0 and qb == dbg_qb:
                P.dma("pool", dbg["d_IS"][:, 0:nk], IS[:, 0:nk], iskeys, [], "dbgis")
                P.dma("pool", dbg["d_lo"], BS[:, 0:8], ["bs_lo", "bs_cnt", "bs_mx", "bs_mn"], [], "dbglo")
            for kc0 in range(0, nkc, 4):
                n4 = min(4, nkc - kc0)
                pt_, pk = wpool.next()
                pv = pt_[:].bitcast(BF16)
                for j in range(n4):
                    kc = kc0 + j
                    P.op("pe", "transpose", ["MS", "IDB"], [pk], out=pv[:, j * 128:(j + 1) * 128], in_=MS[:, kc * 128:(kc + 1) * 128], identity=IDB[:])
                P.op("act", "activation", [pk, "NEG30"], [("MT", kc0 // 4)], out=MT[:, kc0:kc0 + n4, :].rearrange("p c q -> p (c q)"),
                     in_=pv[:, 0:n4 * 128], func=AF.Identity, scale=BIGM, bias=NEG30[:])
            for hg in range(2):
                for kc in range(nkc):
                    ks = slice(kc * 128, (kc + 1) * 128)
                    for g in range(2):
                        h0 = hg * 8 + g * 4
                        pt_, pk = wpool.next()
                        P.op("pe", "matmul", ["KLs", "QLq%d" % sl], [pk], pt_[:], lhsT=KLs[:, ks],
                             rhs=QLq[:, sl, h0:h0 + 4, :].rearrange("p h q -> p (h q)"), start=True, stop=False)
                        P.op("pe", "matmul", ["KRs", "QRq%d" % sl], [pk], pt_[:], lhsT=KRs[:, ks],
                             rhs=QRq[:, sl, h0:h0 + 4, :].rearrange("p h q -> p (h q)"), start=False, stop=False)
                        P.op("pe", "matmul", ["IDB", ("MT", kc // 4)], [pk], pt_[:], lhsT=IDB[:],
                             rhs=MT[:, kc, :].unsqueeze(1).to_broadcast([128, 4, 128]), start=False, stop=True)
                        pr, prk = ptr.next()
                        P.op("act", "activation", [pk], [prk], out=pr, in_=pt_[:], func=AF.Exp, scale=SCALE_A)
                        P.op("pe", "matmul", ["Vs", prk], [psO[g][1]], psO[g][0][:], lhsT=Vs[:, kc, :], rhs=pr, start=(kc == 0), stop=(kc == nkc - 1))
                        P.op("pe", "matmul", ["ONESB", prk], [psD[g][1]], psD[g][0][:], lhsT=ONESB[:], rhs=pr, start=(kc == 0), stop=(kc == nkc - 1))
                for g in range(2):
                    h0 = hg * 8 + g * 4
                    gs = slice(g * 512, (g + 1) * 512)
                    P.op("dve", "reciprocal", [psD[g][1]], [("RD", g)], out=RD[:, gs], in_=psD[g][0][:])
                    P.op("dve", "tensor_tensor", [psO[g][1], ("RD", g)], [("ON", g)], out=ON[:, gs], in0=psO[g][0][:], in1=RD[:, gs], op=ALU.mult)
                    pt_, pk = wpool.next()
                    for hl in range(4):
                        h = h0 + hl
                        P.op("pe", "matmul", [("WUV", h // 8), ("ON", g)], [pk], pt_[:, hl * 128:(hl + 1) * 128], lhsT=WUV[:, h * 128:(h + 1) * 128],
                             rhs=ON[:, g * 512 + hl * 128:g * 512 + (hl + 1) * 128], start=True, stop=True)
                    P.op("dve", "tensor_tensor", [pk, "GTq%d" % sl], [("Y", h0 // 4)], out=Y[:, h0:h0 + 4, :].rearrange("p h q -> p (h q)"),
                         in0=pt_[:], in1=GTq[:, sl, h0:h0 + 4, :].rearrange("p h q -> p (h q)"), op=ALU.mult)
            ykeys = [("Y", i) for i in range(4)]
            for nh in range(2):
                pt_, pk = wpool.next()
                for h in range(16):
                    P.op("pe", "matmul", ykeys + [("WOA", h)], [pk], pt_[:], lhsT=Y[:, h, :], rhs=WOA[:, h, nh * 512:(nh + 1) * 512],
                         start=(h == 0), stop=(h == 15))
                P.op("dve", "tensor_tensor", [pk, "GBC0"], [("TMPH", nh)], out=TMPH[:, nh * 512:(nh + 1) * 512], in0=pt_[:],
                     in1=GBC[0][:, b * D + nh * 512:b * D + (nh + 1) * 512], op=ALU.mult)
            h1k = "H1q%d" % sl
            P.op("pool", "tensor_tensor", [("TMPH", 0), ("TMPH", 1), "Xq%d" % sl], [h1k], out=H1q[:, sl, :], in0=TMPH[:], in1=Xq[:, sl, :], op=ALU.add)
            P.dma("pool", H1[r0:r0 + 128, :], H1q[:, sl, :], [h1k], [], h1k)
            if debug:
                P.dma("pool", dbg["d_H1"][r0:r0 + 128, :], H1q[:, sl, :], [h1k], [], h1k)
        P.emit()

    if stop_after == 2:
        return nc, dbg, P

    with ExitStack() as es:
        def T(name, shape, dt=F32):
            return es.enter_context(nc.sbuf_tensor(name, list(shape), dt))
        W3 = T("p3_w", [128, 8, WB_COLS], BF16)
        STG3 = T("p3_stg", [128, 2, WB_COLS // 4], F32)
        X3 = T("p3_x", [128, 2, D], F32)
        JUNK3 = T("p3_junk", [128, D], BF16)
        SS3 = T("p3_ss", [128, 2, 4], F32)
        KVT = T("p3_kvt", [128, 2, 8, 512], BF16)
        HBT = T("p3_hbt", [128, 2, 8, 512], BF16)
        POSI3 = T("p3_posi", [128, 512], I32)
        POSF3 = T("p3_posf", [128, 512], F32)
        TAB3 = T("p3_tab", [128, 2, 512], F32)
        TKI3 = T("p3_ki", [128, 1, 512], I32)
        TKF3 = T("p3_kf", [128, 1, 512], F32)
        RT3 = T("p3_rt", [128, 4, 512], F32)
        RO3 = T("p3_ro", [128, 4, 512], BF16)
        VSTG = T("p3_vst", [128, 2, D], BF16)
        GST = T("p3_gst", [128, 2, 512], BF16)
        stg = Rot([(STG3[:, i, :], "stg3_%d" % i) for i in range(2)])
        wkeys = load_weights(WB, W3, WB_COLS, stg, "W3")
        tabs = [(TAB3[:, 0, :], TAB3[:, 1, :], "tab128")]
        tmps = Rot([(TKI3[:, 0, :], TKF3[:, 0, :], "tk3")])
        fm = Rot([(ps[i], "ps%d" % i) for i in range(4)])
        psT_pair = [(ps[4], "ps4"), (ps[5], "ps5")]
        vps = Rot([(ps[6], "ps6"), (ps[7], "ps7")])
        ro = Rot([(RO3[:, i, :], "ro3_%d" % i) for i in range(4)])
        rt = Rot([(RT3[:, i, :], "rt3_%d" % i) for i in range(4)])
        gst = Rot([(GST[:, i, :], "gst%d" % i) for i in range(2)])
        ntile3 = int(_os.environ.get("K_NT3", NB * S // 512))
        for tt in range(ntile3):
            b = tt // 8
            t0 = (tt % 8) * 512
            kvt = KVT[:, tt % 2]; kvk = "kvt%d" % (tt % 2)
            hbt = HBT[:, tt % 2]; hbk = "hbt%d" % (tt % 2)
            P.dma("sp", POSI3[:], pos[b:b + 1, t0:t0 + 512].partition_broadcast(128), [], ["POSI"], "POSI")
            P.op("pool", "tensor_copy", ["POSI"], ["POSF"], out=POSF3[:], in_=POSI3[:])
            rope_tables(b, t0, [(2, 0)], POSF3, tabs, tmps)
            for sub in range(4):
                xi = (tt * 4 + sub) % 2
                XT = X3[:, xi, :]
                xkey = "x3_%d" % xi
                r0 = b * S + t0 + sub * 128
                P.dma("sp", XT, H1[r0:r0 + 128, :], [], [xkey], xkey)

                def evacs(kc, pap, pk, sub=sub, b=b, kvt=kvt, hbt=hbt, kvk=kvk, hbk=hbk):
                    d1 = kvt[:, kc, sub * 128:(sub + 1) * 128]
                    d2 = hbt[:, kc, sub * 128:(sub + 1) * 128]
                    if kc < 4:
                        P.op("act", "activation", [pk, "NRM"], [(kvk, kc)], out=d1, in_=pap, func=AF.Copy, scale=NRM[:, 16 + kc:17 + kc])
                        P.op("act", "activation", [pk, "ASCL1", "ASFT1"], [(hbk, kc)], out=d2, in_=pap, func=AF.Identity,
                             scale=ASCL[1][:, kc * NB + b:kc * NB + b + 1], bias=ASFT[1][:, kc * NB + b:kc * NB + b + 1])
                    else:
                        P.op("dve", "tensor_scalar", [pk, "NRM"], [(kvk, kc)], out=d1, in0=pap, scalar1=NRM[:, 16 + kc:17 + kc], scalar2=None, op0=ALU.mult)
                        P.op("dve", "tensor_scalar", [pk, "ASCL1", "ASFT1"], [(hbk, kc)], out=d2, in0=pap,
                             scalar1=ASCL[1][:, kc * NB + b:kc * NB + b + 1], scalar2=ASFT[1][:, kc * NB + b:kc * NB + b + 1], op0=ALU.mult, op1=ALU.add)
                norm_transpose(None, XT, xkey, sub, SS3[:, xi, :], evacs, psT_pair, JUNK3)

            def fm3(ci, src, sk):
                pt_, pk = fm.next()
                for kc in range(8):
                    P.op("pe", "matmul", [(sk, kc)] + wkeys[kc], [pk], pt_[:], lhsT=W3[:, kc, ci * 128:(ci + 1) * 128], rhs=src[:, kc, :],
                         start=(kc == 0), stop=(kc == 7))
                return pt_, pk
            ct, st_, tk = tabs[0]
            pair_list = [(2 * j, kvt, kvk, KT2[b, j]) for j in range(4)]
            pair_list += [(8 + g * 8 + 2 * j, hbt, hbk, QT2[b, g, j]) for g in range(3) for j in range(4)]
            for (c0, src, sk, dst) in pair_list:
                p1, k1 = fm3(c0, src, sk)
                p2_, k2 = fm3(c0 + 1, src, sk)
                ta, tak = rt.next(); tb, tbk = rt.next()
                oa, oak = ro.next(); ob, obk = ro.next()
                P.op("dve", "tensor_tensor", [k1, (tk, 0)], [tak], out=ta, in0=p1[:], in1=ct, op=ALU.mult)
                P.op("dve", "tensor_tensor", [k2, (tk, 1)], [tbk], out=tb, in0=p2_[:], in1=st_, op=ALU.mult)
                P.op("pool", "tensor_tensor", [tak, tbk], [oak], out=oa, in0=ta, in1=tb, op=ALU.subtract)
                tc_, tck = rt.next(); td, tdk = rt.next()
                P.op("dve", "tensor_tensor", [k2, (tk, 0)], [tck], out=tc_, in0=p2_[:], in1=ct, op=ALU.mult)
                P.op("dve", "tensor_tensor", [k1, (tk, 1)], [tdk], out=td, in0=p1[:], in1=st_, op=ALU.mult)
                P.op("pool", "tensor_tensor", [tck, tdk], [obk], out=ob, in0=tc_, in1=td, op=ALU.add)
                P.dma("pool", dst[0, :, t0:t0 + 512], oa, [oak], [], oak)
                P.dma("pool", dst[1, :, t0:t0 + 512], ob, [obk], [], obk)
                if debug and tt == 0 and c0 == 0:
                    P.dma("pool", dbg["d_KT"][:, 0:512], oa, [oak], [], oak)
            for h in range(8):
                p1, k1 = fm3(32 + h, hbt, hbk)
                g_, gk = gst.next()
                P.op("act", "activation", [k1], [gk], out=g_, in_=p1[:], func=AF.Silu)
                P.dma("pool", GB[b, h, :, t0:t0 + 512], g_, [gk], [], gk)
            for sub in range(4):
                vs_ = VSTG[:, sub % 2, :]
                vk_ = "vstg%d" % (sub % 2)
                for nh in range(2):
                    pt_, pk = vps.next()
                    for kc in range(8):
                        P.op("pe", "matmul", [(kvk, kc)] + wkeys[kc], [pk], pt_[:], lhsT=kvt[:, kc, sub * 128:(sub + 1) * 128],
                             rhs=W3[:, kc, 5120 + nh * 512:5120 + (nh + 1) * 512], start=(kc == 0), stop=(kc == 7))
                    if nh == 0:
                        P.op("act", "copy", [pk], [(vk_, nh)], out=vs_[:, nh * 512:(nh + 1) * 512], in_=pt_[:])
                    else:
                        P.op("dve", "tensor_copy", [pk], [(vk_, nh)], out=vs_[:, nh * 512:(nh + 1) * 512], in_=pt_[:])
                r0 = t0 + sub * 128
                P.dma("pool", VB[b, r0:r0 + 128, :], vs_, [(vk_, 0), (vk_, 1)], [], vk_)
        P.emit()
    if stop_after == 3:
        return nc, dbg, P

    SCALE_B = 128 ** -0.5
    with ExitStack() as es:
        def T(name, shape, dt=F32):
            return es.enter_context(nc.sbuf_tensor(name, list(shape), dt))
        KTh = T("p4_k", [128, 2, S], BF16)
        QTh = T("p4_q", [128, 2, 3, S], BF16)
        Vh = T("p4_v", [128, 2, 3, NQB, 128], BF16)
        GBh = T("p4_g", [128, 2, S], BF16)
        OD = T("p4_od", [128, 2, S], F32)
        PTb = T("p4_pt", [128, 3, 256], BF16)
        YTo = T("p4_y", [128, S], BF16)
        sp_ = Rot([(ps[i], "ps%d" % i) for i in range(4)])
        op_ = Rot([(ps[i], "ps%d" % i) for i in range(4, 8)])
        ptb = Rot([(PTb[:, i, :], "ptb%d" % i) for i in range(3)])
        nhead4 = int(_os.environ.get("K_NH4", NB * 8))
        for idx in range(nhead4):
            b = idx // 8
            h = idx % 8
            sl = idx % 2
            j = h // 2
            hh = h % 2
            kk_ = "KTh%d" % sl; qk_ = "QTh%d" % sl; vk_ = "Vh%d" % sl; gk_ = "GBh%d" % sl
            for xx in range(2):
                P.dma("sp", KTh[xx * 64:(xx + 1) * 64, sl, :], KT2[b, j, xx, hh * 64:(hh + 1) * 64, :], [], [kk_], kk_)
                for g in range(3):
                    P.dma("sp", QTh[xx * 64:(xx + 1) * 64, sl, g, :], QT2[b, g, j, xx, hh * 64:(hh + 1) * 64, :], [], [qk_], qk_)
            for g, d in enumerate((1, 4, 16)):
                nch = NQB // d
                vv = VB[b, :, h * 128:(h + 1) * 128].rearrange("(c a r) f -> r a c f", a=128, r=d)
                for r in range(d):
                    P.dma("sp", Vh[:, sl, g, r * nch:(r + 1) * nch, :], vv[r], [], [vk_], vk_)
            P.dma("sp", GBh[:, sl, :], GB[b, h], [], [gk_], gk_)
            for g, d in enumerate((1, 4, 16)):
                nch = NQB // d
                qv = QTh[:, sl, g, :].rearrange("p (c a r) -> p r c a", a=128, r=d)
                kv_ = KTh[:, sl, :].rearrange("p (c a r) -> p r c a", a=128, r=d)
                ov = OD[:].rearrange("p x (c a r) -> p r c x a", a=128, r=d)
                for r in range(d):
                    for c in range(nch):
                        ti = r * nch + c
                        lo = 0 if c > 0 else 128
                        pS, pSk = sp_.next()
                        if c > 0:
                            P.op("pe", "matmul", [kk_, qk_], [pSk], pS[:, 0:128], lhsT=kv_[:, r, c - 1, :], rhs=qv[:, r, c, :], start=True, stop=True)
                        P.op("pe", "matmul", [kk_, qk_], [pSk], pS[:, 128:256], lhsT=kv_[:, r, c, :], rhs=qv[:, r, c, :], start=True, stop=True)
                        pt_, ptk = ptb.next()
                        P.op("act", "activation", [pSk], [ptk], out=pt_[:, lo:256], in_=pS[:, lo:256], func=AF.Exp, scale=SCALE_B)
                        P.op("pool", "tensor_tensor", [ptk, "MASKB"], [ptk], out=pt_[:, lo:256], in0=pt_[:, lo:256], in1=MASKB[:, lo:256], op=ALU.mult)
                        pO, pOk = op_.next()
                        for half, lhs_of in ((0, lambda t: Vh[:, sl, g, t, :]), (1, lambda t: ONESB[:])):
                            oc = slice(half * 128, (half + 1) * 128)
                            if c > 0:
                                P.op("pe", "matmul", [vk_, "ONESB", ptk], [pOk], pO[:, oc], lhsT=lhs_of(ti - 1), rhs=pt_[:, 0:128], start=True, stop=False)
                                P.op("pe", "matmul", [vk_, "ONESB", ptk], [pOk], pO[:, oc], lhsT=lhs_of(ti), rhs=pt_[:, 128:256], start=False, stop=True)
                            else:
                                P.op("pe", "matmul", [vk_, "ONESB", ptk], [pOk], pO[:, oc], lhsT=lhs_of(ti), rhs=pt_[:, 128:256], start=True, stop=True)
                        dst = ov[:, r, c, :, :]
                        src = pO[:, 0:256].rearrange("p (x a) -> p x a", a=128)
                        if g == 0:
                            P.op("act", "copy", [pOk], [("OD", c // 4)], out=dst, in_=src)
                        else:
                            odk = [("OD", i) for i in range(8)]
                            P.op("dve", "tensor_tensor", [pOk] + odk, odk, out=dst, in0=src, in1=dst, op=ALU.add)
            odk = [("OD", i) for i in range(8)]
            P.op("dve", "reciprocal", odk, odk, out=OD[:, 1, :], in_=OD[:, 1, :])
            P.op("pool", "tensor_tensor", odk, odk, out=OD[:, 0, :], in0=OD[:, 0, :], in1=OD[:, 1, :], op=ALU.mult)
            P.op("dve", "tensor_tensor", odk + [gk_], ["YTo"], out=YTo[:], in0=OD[:, 0, :], in1=GBh[:, sl, :], op=ALU.mult)
            P.dma("pool", YT[b, h], YTo[:], ["YTo"], [], "YTo")
            if debug and idx == 0:
                P.dma("pool", dbg["d_YT"], YTo[:], ["YTo"], [], "YTo")
        P.emit()
    if stop_after == 4:
        return nc, dbg, P

    with ExitStack() as es:
        def T(name, shape, dt=F32):
            return es.enter_context(nc.sbuf_tensor(name, list(shape), dt))
        WOB = T("p5_w", [128, 8, D], BF16)
        STG5 = T("p5_stg", [128, 2, D], F32)
        Yq = T("p5_y", [128, 2, 8, 128], BF16)
        H1t = T("p5_h1", [128, 2, D], F32)
        TMP5 = T("p5_tmp", [128, D], F32)
        H2 = T("p5_h2", [128, D], F32)
        OUTt = T("p5_out", [128, 2, D], F32)
        JUNK5 = T("p5_junk", [128, D], BF16)
        SS5 = T("p5_ss", [128, 4], F32)
        for h in range(8):
            st = STG5[:, h % 2, :]
            P.dma("sp", st, wob[h * 128:(h + 1) * 128, :], [], ["stg5_%d" % (h % 2)], "stg5_%d" % (h % 2))
            if h % 2 == 0:
                P.op("dve", "tensor_copy", ["stg5_0"], [("WOB", h)], out=WOB[:, h, :], in_=st)
            else:
                P.op("act", "copy", ["stg5_1"], [("WOB", h)], out=WOB[:, h, :], in_=st)
        pp = Rot([(ps[i], "ps%d" % i) for i in range(4)])
        ntile5 = int(_os.environ.get("K_NT5", NB * NQB))
        for tt in range(ntile5):
            b = tt // NQB
            qb = tt % NQB
            sl = tt % 2
            r0 = b * S + qb * 128
            yk = "Yq%d" % sl; hk_ = "H1t%d" % sl; ok_ = "OUTt%d" % sl
            P.dma("sp", Yq[:, sl], YT[b, :, :, qb * 128:(qb + 1) * 128].rearrange("h p t -> p h t"), [], [yk], yk)
            P.dma("sp", H1t[:, sl, :], H1[r0:r0 + 128, :], [], [hk_], hk_)
            for nh in range(2):
                pt_, pk = pp.next()
                for h in range(8):
                    P.op("pe", "matmul", [yk, ("WOB", h)], [pk], pt_[:], lhsT=Yq[:, sl, h, :], rhs=WOB[:, h, nh * 512:(nh + 1) * 512],
                         start=(h == 0), stop=(h == 7))
                P.op("dve", "tensor_tensor", [pk, "GBC1"], [("TMP5", nh)], out=TMP5[:, nh * 512:(nh + 1) * 512], in0=pt_[:],
                     in1=GBC[1][:, b * D + nh * 512:b * D + (nh + 1) * 512], op=ALU.mult)
            P.op("pool", "tensor_tensor", [("TMP5", 0), ("TMP5", 1), hk_], ["H2"], out=H2[:], in0=TMP5[:], in1=H1t[:, sl, :], op=ALU.add)
            P.op("act", "activation", ["H2"], ["junk5", "ss5a"], out=JUNK5[:], in_=H2[:], func=AF.Square, accum_out=SS5[:, 0:1])
            P.op("act", "activation", ["ss5a"], ["ss5b"], out=SS5[:, 1:2], in_=SS5[:, 0:1], func=AF.Sqrt, scale=1.0 / D, bias=EPS)
            P.op("dve", "reciprocal", ["ss5b"], ["ss5c"], out=SS5[:, 2:3], in_=SS5[:, 1:2])
            P.op("dve", "scalar_tensor_tensor", ["H2", "ss5c", "FNG"], [ok_], out=OUTt[:, sl, :], in0=H2[:], scalar=SS5[:, 2:3], in1=FNG[:],
                 op0=ALU.mult, op1=ALU.mult)
            P.dma("pool", out[r0:r0 + 128, :], OUTt[:, sl, :], [ok_], [], ok_)
        P.emit()

    return nc, dbg, P


def _perm_a():
    idx = []
    idx += list(range(0, 2048))
    idx += list(range(2720, 4768))
    for hb in (0, 8):
        for part in (0, 16):
            idx += [2048 + (hb + (p % 8)) * 32 + part + p // 8 for p in range(128)]
    for hb in (0, 4):
        for part in (0, 32):
            idx += [4768 + (hb + (p % 4)) * 64 + part + p // 4 for p in range(128)]
    for part in (0, 16):
        idx += [2688 + part + p // 8 for p in range(128)]
    for part in (0, 32):
        idx += [5280 + part + p // 4 for p in range(128)]
    idx += list(range(2560, 2688))
    idx += list(range(5344, 5352))
    assert len(idx) == WA_COLS
    return np.array(idx)


def _consts():
    c = np.zeros((128, 640 + 3 + NIT), np.float32)
    c[:, 0:128] = np.eye(128, dtype=np.float32)
    a = np.arange(128)[:, None]
    q = np.arange(128)[None, :]
    c[:, 128:256] = (a >= q).astype(np.float32)
    c[:, 256:384] = (a <= q).astype(np.float32)
    qq = np.arange(128)[:, None]
    kk = np.arange(128)[None, :]
    c[:, 384:512] = np.where(kk <= qq, 0.0, NEG).astype(np.float32)
    p = np.arange(128)
    two_pi = 2 * math.pi
    inv32 = (np.float32(THETA) ** (-(np.arange(0, 32, 2, dtype=np.float32)) / np.float32(32))).astype(np.float32)
    inv64 = (np.float32(THETA) ** (-(np.arange(0, 64, 2, dtype=np.float32)) / np.float32(64))).astype(np.float32)
    inv128 = (np.float32(THETA) ** (-(np.arange(0, 128, 2, dtype=np.float32)) / np.float32(128))).astype(np.float32)
    c[:, 640] = inv32[p // 8].astype(np.float64) / two_pi
    c[:, 641] = inv64[p // 4].astype(np.float64) / two_pi
    c[:, 642] = inv128[p % 64].astype(np.float64) / two_pi
    c[:, 643:643 + NIT] = (0.5 ** np.arange(1, NIT + 1))[None, :]
    return c


def _fm(v, nch):
    return np.ascontiguousarray(np.asarray(v, np.float32).reshape(nch, 128).T)


def prepare_inputs(x, c, positions, a_norm, a_ada_w, a_ada_b, a_w_in, a_kv_norm, a_w_uv, a_w_out,
                   kv_norm, w_kv, b_norm, b_ada_w, b_ada_b, b_w_in, b_w_out, final_norm):
    f = lambda a: np.ascontiguousarray(np.asarray(a, np.float32))
    x = f(x); c = f(c)
    positions = np.ascontiguousarray(np.asarray(positions, np.int32))
    WA = np.ascontiguousarray(f(a_w_in)[0][:, _perm_a()])
    wkv = f(w_kv); bw = f(b_w_in)[0]
    cols = []
    def pair_cols(base):
        out_ = []
        for j in range(4):
            for part in (0, 64):
                out_.append([base + (2 * j + p // 64) * 128 + part + (p % 64) for p in range(128)])
        return out_
    kcols = np.concatenate([wkv[:, ci] for ci in pair_cols(0)], axis=1)
    qcols = np.concatenate([bw[:, ci] for g in range(3) for ci in pair_cols(g * 1024)], axis=1)
    WB = np.ascontiguousarray(np.concatenate([kcols, qcols, bw[:, 3072:4096], wkv[:, 1024:2048]], axis=1))
    assert WB.shape[1] == WB_COLS
    normsT = np.concatenate([_fm(f(a_norm)[0], 8), _fm(f(b_norm)[0], 8), _fm(f(kv_norm), 8)], axis=1)
    ada_bT = np.concatenate([_fm(f(a_ada_b)[0], 24), _fm(f(b_ada_b)[0], 24)], axis=1)
    ada_bg = np.concatenate([f(a_ada_b)[0][2 * D:], f(b_ada_b)[0][2 * D:]])[None, :]
    wuv = np.ascontiguousarray(f(a_w_uv)[0].transpose(1, 0, 2).reshape(128, 16 * 128))
    shared = {
        "normsT": np.ascontiguousarray(normsT), "a_ada_w": f(a_ada_w)[0], "b_ada_w": f(b_ada_w)[0],
        "ada_bT": np.ascontiguousarray(ada_bT), "ada_bg": np.ascontiguousarray(ada_bg),
        "WA": WA, "WB": WB, "akvg": f(a_kv_norm)[0][None, :], "wuv": wuv, "woa": f(a_w_out)[0], "wob": f(b_w_out)[0],
        "fng": f(final_norm)[None, :], "cst": _consts(),
    }
    in_maps = []
    for core in range(NCORES):
        bs = slice(core * NB, (core + 1) * NB)
        cc = c[bs]
        cTl = np.ascontiguousarray(cc.reshape(NB, 8, 128).transpose(2, 1, 0).reshape(128, 8 * NB))
        m = dict(shared)
        m["x"] = np.ascontiguousarray(x[bs].reshape(NB * S, D))
        m["pos"] = np.ascontiguousarray(positions[bs])
        m["cT"] = cTl
        in_maps.append(m)
    return in_maps


def kernel(**inputs):
    in_maps = prepare_inputs(**inputs)
    nc, _, _ = build_program(debug=False)
    res = run_bass_kernel_spmd(nc, in_maps, core_ids=list(range(NCORES)))
    outs = [np.asarray(r["out"]).reshape(NB, S, D) for r in res.results]
    return np.concatenate(outs, axis=0).astype(np.float32)
```

```python
import math
from contextlib import ExitStack
import numpy as np
import concourse.bass as bass
import concourse.mybir as mybir
from concourse.bass_utils import run_bass_kernel_spmd

F32 = mybir.dt.float32
BF16 = mybir.dt.bfloat16
I32 = mybir.dt.int32
AF = mybir.ActivationFunctionType
ALU = mybir.AluOpType
AX = mybir.AxisListType

ENGS = ("pe", "act", "dve", "pool", "sp")

D = 1024
S = 4096
NB = 2
NCORES = 8
NQB = S // 128
EPS = 1e-6
THETA = 10000.0
NIT = 22
TOPK = 256
NEG = -1.0e30
WA_COLS = 5768
WB_COLS = 6144


class Op:
    __slots__ = ("eng", "fn", "deps", "is_dma", "chan", "val", "flag")

    def __init__(self, eng, fn, is_dma=False, chan=None):
        self.eng = eng
        self.fn = fn
        self.deps = ()
        self.is_dma = is_dma
        self.chan = chan
        self.val = 0
        self.flag = False


class Prog:
    def __init__(self, nc):
        self.nc = nc
        self.chan_sem = {}
        self.chan_cnt = {}
        self.free_chan = []
        self.eng_sem = {}
        self.eng_cnt = {e: 0 for e in ENGS}
        self.phase_no = 0
        self.total_ops = 0
        self._reset()

    def _reset(self):
        self.ops = []
        self.last_w = {}
        self.readers = {}

    def add(self, eng, fn, reads=(), writes=(), chan=None):
        is_dma = chan is not None
        op = Op(eng, fn, is_dma, chan)
        psr = [k for k in reads if isinstance(k, str) and k.startswith("ps")]
        if psr:
            reads = [k for k in reads if k not in psr]
            writes = list(writes) + [k for k in psr if k not in writes]
        deps = {}
        for k in reads:
            w = self.last_w.get(k)
            if w is not None:
                deps[id(w)] = w
        for k in writes:
            w = self.last_w.get(k)
            if w is not None:
                deps[id(w)] = w
            for r in self.readers.get(k, ()):
                deps[id(r)] = r
        dl = []
        for d in deps.values():
            if d is op:
                continue
            if (not is_dma) and eng == "pe" and d.eng == "pe" and not d.is_dma:
                continue
            dl.append(d)
        op.deps = dl
        for k in reads:
            self.readers.setdefault(k, []).append(op)
        for k in writes:
            self.last_w[k] = op
            self.readers[k] = []
        self.ops.append(op)
        return op

    def op(self, eng, meth, reads, writes, *args, **kw):
        return self.add(eng, lambda e: getattr(e, meth)(*args, **kw), reads, writes)

    def dma(self, q, out, in_, reads, writes, chan):
        return self.add(q, lambda e: e.dma_start(out=out, in_=in_), reads, writes, chan=chan)

    def emit(self):
        nc = self.nc
        self.phase_no += 1
        ops = self.ops
        for op in ops:
            for d in op.deps:
                d.flag = True
        per_eng = {e: [] for e in ENGS}
        for op in ops:
            per_eng[op.eng].append(op)
        last_compute = {}
        for e in ENGS:
            for op in reversed(per_eng[e]):
                if not op.is_dma:
                    op.flag = True
                    last_compute[e] = op
                    break
        for e in last_compute:
            if e not in self.eng_sem:
                self.eng_sem[e] = nc.alloc_semaphore("eng_%s" % e)
        eng_sem = self.eng_sem
        cnt = self.eng_cnt
        for op in ops:
            if op.is_dma:
                if op.chan not in self.chan_sem:
                    if self.free_chan:
                        self.chan_sem[op.chan], self.chan_cnt[op.chan] = self.free_chan.pop()
                    else:
                        self.chan_sem[op.chan] = nc.alloc_semaphore("ch%d" % len(self.chan_sem))
                        self.chan_cnt[op.chan] = 0
                self.chan_cnt[op.chan] += 16
                op.val = self.chan_cnt[op.chan]
            elif op.flag:
                cnt[op.eng] += 1
                op.val = cnt[op.eng]
        final_eng = {e: (eng_sem[e], last_compute[e].val) for e in last_compute}
        final_chan = {c: (self.chan_sem[c], self.chan_cnt[c]) for c in self.chan_sem}

        def sem_of(d):
            return self.chan_sem[d.chan] if d.is_dma else eng_sem[d.eng]

        def run(e, engine):
            waited = {}
            for op in per_eng[e]:
                need = {}
                for d in op.deps:
                    s = sem_of(d)
                    k = id(s)
                    if k not in need or need[k][1] < d.val:
                        need[k] = (s, d.val)
                for k, (s, v) in need.items():
                    if waited.get(k, 0) >= v:
                        continue
                    engine.wait_ge(s, v)
                    waited[k] = v
                ins = op.fn(engine)
                if op.is_dma:
                    ins.then_inc(self.chan_sem[op.chan], 16)
                elif op.flag:
                    ins.then_inc(eng_sem[op.eng], 1)
            for e2, (s, v) in final_eng.items():
                if waited.get(id(s), 0) < v:
                    engine.wait_ge(s, v)
            for c, (s, v) in final_chan.items():
                if v > 0 and waited.get(id(s), 0) < v:
                    engine.wait_ge(s, v)

        with nc.Block() as block:
            @block.tensor
            def _(eng):
                run("pe", eng)

            @block.scalar
            def _(eng):
                run("act", eng)

            @block.vector
            def _(eng):
                run("dve", eng)

            @block.gpsimd
            def _(eng):
                run("pool", eng)

            @block.sync
            def _(eng):
                run("sp", eng)
        self.total_ops += len(ops)
        for c in list(self.chan_sem):
            self.free_chan.append((self.chan_sem.pop(c), self.chan_cnt.pop(c)))
        self._reset()


class Rot:
    def __init__(self, items):
        self.items = items
        self.i = 0

    def next(self):
        it = self.items[self.i % len(self.items)]
        self.i += 1
        return it


def build_program(debug=False, stop_after=99):
    nc = bass.Bass("TRN2", target_bir_lowering=False)
    P = Prog(nc)

    def din(name, shape, dt=F32):
        return nc.dram_tensor(name, list(shape), dt, kind="ExternalInput").ap()

    def dscr(name, shape, dt=BF16):
        return nc.dram_tensor(name, list(shape), dt).ap()

    x = din("x", [NB * S, D])
    pos = din("pos", [NB, S], I32)
    cT = din("cT", [128, 8 * NB])
    normsT = din("normsT", [128, 24])
    ada_w = [din("a_ada_w", [D, 3 * D]), din("b_ada_w", [D, 3 * D])]
    ada_bT = din("ada_bT", [128, 48])
    ada_bg = din("ada_bg", [1, 2 * D])
    WA = din("WA", [D, WA_COLS])
    WB = din("WB", [D, WB_COLS])
    akvg = din("akvg", [1, 128])
    wuv = din("wuv", [128, 16 * 128])
    woa = din("woa", [2 * D, D])
    wob = din("wob", [D, D])
    fng = din("fng", [1, D])
    cst = din("cst", [128, 640 + 3 + NIT])
    out = nc.dram_tensor("out", [NB * S, D], F32, kind="ExternalOutput").ap()

    QL = dscr("QL", [NB, NQB, 128, 2048])
    GT = dscr("GT", [NB, NQB, 128, 2048])
    QR = dscr("QR", [NB, NQB, 2, 128, 128])
    QI = dscr("QI", [NB, NQB, 2, 128, 128])
    QI2 = dscr("QI2", [NB, NQB, 2, 128, 128])
    QR2 = dscr("QR2", [NB, NQB, 2, 128, 128])
    KR1 = dscr("KR1", [NB, 128, S]); KR2 = dscr("KR2", [NB, 128, S])
    KI1 = dscr("KI1", [NB, 128, S]); KI2 = dscr("KI2", [NB, 128, S])
    VA = dscr("VA", [NB, S, 128])
    KL = dscr("KL", [NB, 128, S])
    WI = dscr("WI", [NB, S, 8], F32)
    H1 = dscr("H1", [NB * S, D], F32)
    KT2 = dscr("KT2", [NB, 4, 2, 128, S])
    QT2 = dscr("QT2", [NB, 3, 4, 2, 128, S])
    GB = dscr("GB", [NB, 8, 128, S])
    VB = dscr("VB", [NB, S, D])
    YT = dscr("YT", [NB, 8, 128, S])

    dbg = {}
    if debug:
        for nm, shp, dt in (("d_QL", [128, 2048], BF16), ("d_H1", [NB * S, D], F32), ("d_mod", [128, 64], F32),
                            ("d_KL", [128, S], BF16), ("d_IS", [128, S], F32), ("d_lo", [128, 8], F32),
                            ("d_YT", [128, S], BF16), ("d_KT", [128, S], BF16)):
            dbg[nm] = nc.dram_tensor(nm, shp, dt, kind="ExternalOutput").ap()

    def sb(name, shape, dt=F32):
        return nc.alloc_sbuf_tensor(name, list(shape), dt)

    CST = sb("CST", [128, 640 + 3 + NIT])
    IDF = CST[:, 0:128]
    MPREV_F = CST[:, 128:256]
    MCUR_F = CST[:, 256:384]
    CAUS = CST[:, 384:512]
    INV = CST[:, 640:643]
    POW2 = CST[:, 643:643 + NIT]
    IDB = sb("IDB", [128, 128], BF16)
    ONESB = sb("ONESB", [128, 128], BF16)
    ONESF = sb("ONESF", [1, 128], F32)
    HALFPI = sb("HALFPI", [128, 1], F32)
    MASKB = sb("MASKB", [128, 256], BF16)
    NRM = sb("NRM", [128, 24])
    ASCL = [sb("ASCL%d" % l, [128, 8 * NB]) for l in range(2)]
    ASFT = [sb("ASFT%d" % l, [128, 8 * NB]) for l in range(2)]
    GBC = [sb("GBC%d" % l, [128, NB * D]) for l in range(2)]
    AKVG = sb("AKVG", [128, 128])
    FNG = sb("FNG", [128, D])

    ps = [nc.alloc_psum_tensor("ps%d" % i, [128, 512], F32) for i in range(8)]

    P.dma("sp", CST[:], cst, [], ["CST"], "l0")
    P.dma("sp", NRM[:], normsT, [], ["NRM"], "l1")
    P.dma("sp", AKVG[:], akvg.partition_broadcast(128), [], ["AKVG"], "l2")
    P.dma("sp", FNG[:], fng.partition_broadcast(128), [], ["FNG"], "l3")
    P.add("dve", lambda e: e.tensor_copy(out=IDB[:], in_=IDF), ["CST"], ["IDB"])
    P.add("dve", lambda e: e.memset(ONESB[:], 1.0), [], ["ONESB"])
    P.add("dve", lambda e: e.memset(ONESF[:], 1.0), [], ["ONESF"])
    P.add("dve", lambda e: e.memset(HALFPI[:], math.pi / 2), [], ["HALFPI"])
    P.add("dve", lambda e: e.tensor_copy(out=MASKB[:], in_=CST[:, 128:384]), ["CST"], ["MASKB"])

    with ExitStack() as es:
        W0 = es.enter_context(nc.sbuf_tensor("p0_w", [128, 8, 3 * D], F32))
        C0 = es.enter_context(nc.sbuf_tensor("p0_c", [128, 8 * NB], F32))
        SC0 = es.enter_context(nc.sbuf_tensor("p0_sc", [128, 8 * NB], F32))
        SCB = es.enter_context(nc.sbuf_tensor("p0_scb", [128, 8 * NB, 128], F32))
        BT0 = es.enter_context(nc.sbuf_tensor("p0_bT", [128, 48], F32))
        BG0 = es.enter_context(nc.sbuf_tensor("p0_bg", [128, 2 * D], F32))
        M0 = es.enter_context(nc.sbuf_tensor("p0_m", [128, 16 * NB], F32))
        P.dma("sp", C0[:], cT, [], ["C0"], "l4")
        P.dma("sp", BT0[:], ada_bT, [], ["BT0"], "l5")
        P.dma("sp", BG0[:], ada_bg.partition_broadcast(128), [], ["BG0"], "l6")
        P.add("act", lambda e: e.activation(out=SC0[:], in_=C0[:], func=AF.Silu), ["C0"], ["SC0"])
        P.add("dve", lambda e: e.tensor_copy(out=SCB[:], in_=SC0[:].unsqueeze(2).to_broadcast([128, 8 * NB, 128])),
              ["SC0"], ["SCB"])
        for l in range(2):
            for kc in range(8):
                P.dma("sp", W0[:, kc, :], ada_w[l][kc * 128:(kc + 1) * 128, :], [], [("W0", kc)], "w%d" % kc)
            mps = ps[0]
            for j in range(16):
                for kc in range(8):
                    P.add("pe", lambda e, j=j, kc=kc: e.matmul(
                        mps[:, j * NB:(j + 1) * NB], lhsT=W0[:, kc, j * 128:(j + 1) * 128],
                        rhs=SC0[:, kc * NB:(kc + 1) * NB], start=(kc == 0), stop=(kc == 7)),
                        [("W0", kc), "SC0"], ["psmps"])
            P.add("dve", lambda e, l=l: e.tensor_tensor(
                out=M0[:].rearrange("p (j b) -> p j b", b=NB), in0=mps[:, 0:16 * NB].rearrange("p (j b) -> p j b", b=NB),
                in1=BT0[:, l * 24:l * 24 + 16].unsqueeze(2).to_broadcast([128, 16, NB]), op=ALU.add),
                ["psmps", "BT0"], ["M0"])
            P.add("dve", lambda e, l=l: e.tensor_copy(out=ASFT[l][:], in_=M0[:, 0:8 * NB]), ["M0"], ["ASFT%d" % l])
            P.add("dve", lambda e, l=l: e.scalar_tensor_tensor(
                out=ASCL[l][:].rearrange("p (j b) -> p j b", b=NB), in0=M0[:, 8 * NB:16 * NB].rearrange("p (j b) -> p j b", b=NB),
                scalar=1.0, in1=NRM[:, l * 8:(l + 1) * 8].unsqueeze(2).to_broadcast([128, 8, NB]),
                op0=ALU.add, op1=ALU.mult), ["M0", "NRM"], ["ASCL%d" % l])
            for b in range(NB):
                for nh in range(2):
                    gps = ps[1 + (b * 2 + nh) % 2]
                    gkey = "psg%d" % ((b * 2 + nh) % 2)
                    for kc in range(8):
                        P.add("pe", lambda e, b=b, nh=nh, kc=kc, gps=gps: e.matmul(
                            gps[:], lhsT=SCB[:, kc * NB + b, :], rhs=W0[:, kc, 2 * D + nh * 512:2 * D + (nh + 1) * 512],
                            start=(kc == 0), stop=(kc == 7)), [("W0", kc), "SCB"], [gkey])
                    P.add("dve", lambda e, l=l, b=b, nh=nh, gps=gps: e.tensor_tensor(
                        out=GBC[l][:, b * D + nh * 512:b * D + (nh + 1) * 512], in0=gps[:],
                        in1=BG0[:, l * D + nh * 512:l * D + (nh + 1) * 512], op=ALU.add), [gkey, "BG0"], ["GBC%d" % l])
        if debug:
            P.dma("pool", dbg["d_mod"][:, 0:16], ASCL[0][:], ["ASCL0"], [], "s0")
            P.dma("pool", dbg["d_mod"][:, 16:32], ASFT[0][:], ["ASFT0"], [], "s0")
            P.dma("pool", dbg["d_mod"][:, 32:48], ASCL[1][:], ["ASCL1"], [], "s0")
            P.dma("pool", dbg["d_mod"][:, 48:64], GBC[0][:, 0:16], ["GBC0"], [], "s0")
        P.emit()
    if stop_after == 0:
        return nc, dbg, P

    def load_weights(Wd, Wsb, ncols, stg, wkey):
        npc = 4
        pw = ncols // npc
        engs = ("dve", "act")
        i = 0
        for kc in range(8):
            for pc in range(npc):
                st, sk = stg.next()
                P.dma("sp", st[:, 0:pw], Wd[kc * 128:(kc + 1) * 128, pc * pw:(pc + 1) * pw], [], [sk], "ws%d" % (i % 2))
                en = engs[i % 2]
                if en == "act":
                    P.add("act", lambda e, st=st, kc=kc, pc=pc: e.copy(out=Wsb[:, kc, pc * pw:(pc + 1) * pw], in_=st[:, 0:pw]),
                          [sk], [(wkey, kc, pc)])
                else:
                    P.add(en, lambda e, st=st, kc=kc, pc=pc: e.tensor_copy(out=Wsb[:, kc, pc * pw:(pc + 1) * pw], in_=st[:, 0:pw]),
                          [sk], [(wkey, kc, pc)])
                i += 1
        return [[(wkey, kc, pc) for pc in range(npc)] for kc in range(8)]

    def rope_tables(b, t0, specs, POSF, tabs, tmps):
        import os as _os
        lvl = int(_os.environ.get("K_TABLVL", 9))
        for (icol, ti) in specs:
            ct, st_, tk = tabs[ti]
            for which, tile_, off in ((0, ct, 0.25), (1, st_, 0.0)):
                ki, kf, kk = tmps.next()
                P.add("pool", lambda e, ki=ki, icol=icol, off=off: e.tensor_scalar(
                    out=ki[:], in0=POSF[:], scalar1=INV[:, icol:icol + 1], scalar2=off, op0=ALU.mult, op1=ALU.add),
                    ["POSF", "CST"], [kk + "i"])
                if lvl < 2:
                    continue
                P.add("pool", lambda e, ki=ki, kf=kf: e.tensor_copy(out=kf[:], in_=ki[:]), [kk + "i"], [kk + "f"])
                if lvl < 3:
                    continue
                P.add("dve", lambda e, kf=kf, icol=icol: e.scalar_tensor_tensor(
                    out=kf[:], in0=POSF[:], scalar=INV[:, icol:icol + 1], in1=kf[:], op0=ALU.mult, op1=ALU.subtract),
                    ["POSF", "CST", kk + "f"], [kk + "f"])
                if lvl < 4:
                    continue
                _sb = _os.environ.get("K_SINB", "")
                if _sb == "zero":
                    P.add("act", lambda e, kf=kf, tile_=tile_, off=off: e.activation(
                        out=tile_[:], in_=kf[:], func=AF.Sin, scale=2 * math.pi), [kk + "f"], [(tk, which)])
                elif _sb == "noscale":
                    P.add("act", lambda e, kf=kf, tile_=tile_, off=off: e.activation(
                        out=tile_[:], in_=kf[:], func=AF.Sin), [kk + "f"], [(tk, which)])
                elif off == 0.0:
                    P.add("act", lambda e, kf=kf, tile_=tile_, off=off: e.activation(
                        out=tile_[:], in_=kf[:], func=AF.Sin, scale=2 * math.pi), [kk + "f"], [(tk, which)])
                else:
                    P.add("act", lambda e, kf=kf, tile_=tile_, off=off: e.activation(
                        out=tile_[:], in_=kf[:], func=AF.Sin, scale=2 * math.pi, bias=HALFPI[:]),
                        [kk + "f", "HALFPI"], [(tk, which)])

    def norm_transpose(src_rows, XT, xkey, sub, SS, evacs, psT_pair, junk):
        import os as _os
        nlvl = int(_os.environ.get("K_NTLVL", 9))
        ssk = xkey + "ss"
        P.add("act", lambda e: e.activation(out=junk[:], in_=XT[:], func=AF.Square, accum_out=SS[:, 0:1]),
              [xkey], ["junk", ssk])
        P.add("act", lambda e: e.activation(out=SS[:, 1:2], in_=SS[:, 0:1], func=AF.Sqrt, scale=1.0 / D, bias=EPS),
              [ssk], [ssk + "b"])
        P.add("dve", lambda e: e.reciprocal(out=SS[:, 2:3], in_=SS[:, 1:2]), [ssk + "b"], [ssk + "c"])
        P.add("dve", lambda e: e.tensor_scalar(out=XT[:], in0=XT[:], scalar1=SS[:, 2:3], scalar2=None, op0=ALU.mult),
              [xkey, ssk + "c"], [xkey])
        if nlvl < 2:
            return
        for half in range(2):
            pt, pk = psT_pair[half]
            for q in range(4):
                kc = half * 4 + q
                P.add("pe", lambda e, pt=pt, q=q, kc=kc: e.transpose(
                    out=pt[:, q * 128:(q + 1) * 128], in_=XT[:, kc * 128:(kc + 1) * 128], identity=IDF),
                    [xkey, "CST"], [pk])
            for q in range(4):
                kc = half * 4 + q
                if nlvl >= 3:
                    evacs(kc, pt[:, q * 128:(q + 1) * 128], pk)

    with ExitStack() as es:
        W1 = es.enter_context(nc.sbuf_tensor("p1_w", [128, 8, WA_COLS], BF16))
        STG = es.enter_context(nc.sbuf_tensor("p1_stg", [128, 2, WA_COLS // 4], F32))
        X1 = es.enter_context(nc.sbuf_tensor("p1_x", [128, 2, D], F32))
        JUNK = es.enter_context(nc.sbuf_tensor("p1_junk", [128, D], BF16))
        SS1 = es.enter_context(nc.sbuf_tensor("p1_ss", [128, 2, 4], F32))
        HN = es.enter_context(nc.sbuf_tensor("p1_hn", [128, 2, 8, 512], BF16))
        POSI = es.enter_context(nc.sbuf_tensor("p1_posi", [128, 512], I32))
        POSF = es.enter_context(nc.sbuf_tensor("p1_posf", [128, 512], F32))
        TAB = es.enter_context(nc.sbuf_tensor("p1_tab", [128, 4, 512], F32))
        TKI = es.enter_context(nc.sbuf_tensor("p1_ki", [128, 1, 512], I32))
        TKF = es.enter_context(nc.sbuf_tensor("p1_kf", [128, 1, 512], F32))
        STQ = es.enter_context(nc.sbuf_tensor("p1_sq", [128, 2, 4, 8, 128], BF16))
        RO = es.enter_context(nc.sbuf_tensor("p1_ro", [128, 4, 512], BF16))
        RT = es.enter_context(nc.sbuf_tensor("p1_rt", [128, 4, 512], F32))
        VST = es.enter_context(nc.sbuf_tensor("p1_v", [128, 2, 4, 128], BF16))
        KLST = es.enter_context(nc.sbuf_tensor("p1_kl", [128, 2, 512], BF16))
        WIST = es.enter_context(nc.sbuf_tensor("p1_wi", [128, 2, 4, 8], F32))
        KVS = es.enter_context(nc.sbuf_tensor("p1_kv", [128, 8], F32))
        stg = Rot([(STG[:, i, :], "stg%d" % i) for i in range(2)])
        wkeys = load_weights(WA, W1, WA_COLS, stg, "W1")
        allw = [k for kc in range(8) for k in wkeys[kc]]
        tabs = [(TAB[:, 0, :], TAB[:, 1, :], "tab32"), (TAB[:, 2, :], TAB[:, 3, :], "tab64")]
        tmps = Rot([(TKI[:, i, :], TKF[:, i, :], "tk%d" % i) for i in range(1)])
        fm = Rot([(ps[i], "ps%d" % i) for i in range(4)])
        psT_pair = [(ps[4], "ps4"), (ps[5], "ps5")]
        psTok = (ps[6], "ps6")
        psVT = ps[7][:].bitcast(BF16)
        stq = Rot([(STQ[:, i], "stq%d" % i) for i in range(2)])
        ro = Rot([(RO[:, i, :], "ro%d" % i) for i in range(4)])
        rt = Rot([(RT[:, i, :], "rt%d" % i) for i in range(4)])
        import os as _os
        ntile = int(_os.environ.get('K_NT', NB * S // 512))
        _skip = _os.environ.get('K_SKIP', '')
        for tt in range(ntile):
            b = tt // 8
            t0 = (tt % 8) * 512
            qb0 = t0 // 128
            hn = HN[:, tt % 2]
            hk = "hn%d" % (tt % 2)
            P.dma("sp", POSI[:], pos[b:b + 1, t0:t0 + 512].partition_broadcast(128), [], ["POSI"], "POSI")
            P.add("pool", lambda e: e.tensor_copy(out=POSF[:], in_=POSI[:]), ["POSI"], ["POSF"])
            if 'tab' not in _skip:
                rope_tables(b, t0, [(0, 0), (1, 1)], POSF, tabs, tmps)
            vst = VST[:, tt % 2]; vk = "vst%d" % (tt % 2)
            klst = KLST[:, tt % 2, :]; klk = "klst%d" % (tt % 2)
            wist = WIST[:, tt % 2]; wik = "wist%d" % (tt % 2)
            for sub in range(4):
                xi = (tt * 4 + sub) % 2
                XT = X1[:, xi, :]
                xkey = "x1_%d" % xi
                r0 = b * S + t0 + sub * 128
                P.dma("sp", XT, x[r0:r0 + 128, :], [], [xkey], "lx%d" % xi)

                def evacs(kc, pap, pk, sub=sub, b=b, hn=hn, hk=hk):
                    dst = hn[:, kc, sub * 128:(sub + 1) * 128]
                    _ev = _os.environ.get('K_EV', '')
                    if (kc < 4 and _ev != 'dve') or _ev == 'act':
                        P.add("act", lambda e: e.activation(
                            out=dst, in_=pap, func=AF.Identity, scale=ASCL[0][:, kc * NB + b:kc * NB + b + 1],
                            bias=ASFT[0][:, kc * NB + b:kc * NB + b + 1]), [pk, "ASCL0", "ASFT0"], [(hk, kc)])
                    else:
                        P.add("dve", lambda e: e.tensor_scalar(
                            out=dst, in0=pap, scalar1=ASCL[0][:, kc * NB + b:kc * NB + b + 1],
                            scalar2=ASFT[0][:, kc * NB + b:kc * NB + b + 1], op0=ALU.mult, op1=ALU.add),
                            [pk, "ASCL0", "ASFT0"], [(hk, kc)])
                if 'nt' not in _skip:
                    norm_transpose(None, XT, xkey, sub, SS1[:, xi, :], evacs, psT_pair, JUNK)
            hkeys = [(hk, kc) for kc in range(8)]
            for sub in (range(0) if 'tok' in _skip else range(4)):
                pt, pk = psTok
                for kc in range(8):
                    P.add("pe", lambda e, kc=kc, sub=sub, pt=pt, hn=hn: e.matmul(
                        pt[:, 0:136], lhsT=hn[:, kc, sub * 128:(sub + 1) * 128], rhs=W1[:, kc, 5632:5768],
                        start=(kc == 0), stop=(kc == 7)), [(hk, kc)] + wkeys[kc], [pk])
                P.add("act", lambda e, pt=pt: e.activation(out=JUNK[:, 0:128], in_=pt[:, 0:128], func=AF.Square,
                                                           accum_out=KVS[:, 0:1]), [pk], ["junk", "kvs0"])
                P.add("act", lambda e: e.activation(out=KVS[:, 1:2], in_=KVS[:, 0:1], func=AF.Sqrt, scale=1.0 / 128, bias=EPS),
                      ["kvs0"], ["kvs1"])
                P.add("dve", lambda e: e.reciprocal(out=KVS[:, 2:3], in_=KVS[:, 1:2]), ["kvs1"], ["kvs2"])
                P.add("dve", lambda e, pt=pt, sub=sub, vst=vst: e.scalar_tensor_tensor(
                    out=vst[:, sub, :], in0=pt[:, 0:128], scalar=KVS[:, 2:3], in1=AKVG[:], op0=ALU.mult, op1=ALU.mult),
                    [pk, "kvs2", "AKVG"], [(vk, sub)])
                P.add("act", lambda e, pt=pt, sub=sub, wist=wist: e.mul(out=wist[:, sub, :], in_=pt[:, 128:136], mul=8.0 ** -0.5),
                      [pk], [(wik, sub)])
                P.add("pe", lambda e, sub=sub, vst=vst: e.transpose(out=psVT[:, sub * 128:(sub + 1) * 128], in_=vst[:, sub, :],
                                                                    identity=IDB[:]), [(vk, sub), "IDB"], ["psvt"])
            if 'tail' not in _skip:
                P.add("act", lambda e, klst=klst: e.copy(out=klst, in_=psVT[:, 0:512]), ["psvt"], [klk])
                P.dma("pool", VA[b, t0:t0 + 512, :].rearrange("(s p) c -> p s c", p=128), vst[:], [(vk, s_) for s_ in range(4)], [], vk)
                P.dma("pool", WI[b, t0:t0 + 512, :].rearrange("(s p) c -> p s c", p=128), wist[:], [(wik, s_) for s_ in range(4)], [], wik)
                P.dma("pool", KL[b, :, t0:t0 + 512], klst, [klk], [], klk)
            if debug and tt == 0:
                P.dma("pool", dbg["d_KL"][:, 0:512], klst, [klk], [], klk)

            def fm_chunk(ci, hn=hn, hk=hk):
                pt, pk = fm.next()
                for kc in range(8):
                    P.add("pe", lambda e, kc=kc, pt=pt, ci=ci, hn=hn: e.matmul(
                        pt[:], lhsT=W1[:, kc, ci * 128:(ci + 1) * 128], rhs=hn[:, kc, :], start=(kc == 0), stop=(kc == 7)),
                        [(hk, kc)] + wkeys[kc], [pk])
                return pt, pk
            for grp, func, dst in (() if 'fm' in _skip else ((0, None, QL), (1, AF.Silu, GT))):
              for hg in range(2):
                sq, sqk = stq.next()
                for hl in range(8):
                    h = hg * 8 + hl
                    pt, pk = fm_chunk(grp * 16 + h)
                    o = sq[:, :, hl, :]
                    i_ = pt[:].rearrange("p (a q) -> p a q", q=128)
                    if func is None:
                        if h % 2 == 0:
                            P.add("act", lambda e, o=o, i_=i_: e.copy(out=o, in_=i_), [pk], [(sqk, hl)])
                        else:
                            P.add("dve", lambda e, o=o, i_=i_: e.tensor_copy(out=o, in_=i_), [pk], [(sqk, hl)])
                    else:
                        P.add("act", lambda e, o=o, i_=i_: e.activation(out=o, in_=i_, func=AF.Silu), [pk], [(sqk, hl)])
                P.dma("pool", dst[b, qb0:qb0 + 4, :, hg * 1024:(hg + 1) * 1024].rearrange("a p f -> p a f"),
                      sq.rearrange("p a h q -> p a (h q)"), [(sqk, hl) for hl in range(8)], [], sqk)
                if debug and tt == 0 and grp == 0:
                    P.dma("pool", dbg["d_QL"][:, hg * 1024:(hg + 1) * 1024], sq[:, 0].rearrange("p h q -> p (h q)"),
                          [(sqk, hl) for hl in range(8)], [], sqk)
            pairs = [(32, 0, "qr", 0), (34, 0, "qr", 1), (36, 1, "qi", 0), (38, 1, "qi", 1), (40, 0, "kr", 0), (42, 1, "ki", 0)]
            for (c0, ti, kind, half) in ([] if 'rope' in _skip else pairs):
                ct, st_, tk = tabs[ti]
                p1, k1 = fm_chunk(c0)
                p2, k2 = fm_chunk(c0 + 1)
                ta, tak = rt.next(); tb, tbk = rt.next()
                oa, oak = ro.next(); ob, obk = ro.next()
                P.add("dve", lambda e, ta=ta, p1=p1, ct=ct: e.tensor_tensor(out=ta, in0=p1[:], in1=ct, op=ALU.mult), [k1, (tk, 0)], [tak])
                P.add("dve", lambda e, tb=tb, p2=p2, st_=st_: e.tensor_tensor(out=tb, in0=p2[:], in1=st_, op=ALU.mult), [k2, (tk, 1)], [tbk])
                P.add("dve", lambda e, oa=oa, ta=ta, tb=tb: e.tensor_tensor(out=oa, in0=ta, in1=tb, op=ALU.subtract), [tak, tbk], [oak])
                tc_, tck = rt.next(); td, tdk = rt.next()
                P.add("dve", lambda e, tc_=tc_, p2=p2, ct=ct: e.tensor_tensor(out=tc_, in0=p2[:], in1=ct, op=ALU.mult), [k2, (tk, 0)], [tck])
                P.add("dve", lambda e, td=td, p1=p1, st_=st_: e.tensor_tensor(out=td, in0=p1[:], in1=st_, op=ALU.mult), [k1, (tk, 1)], [tdk])
                P.add("dve", lambda e, ob=ob, tc_=tc_, td=td: e.tensor_tensor(out=ob, in0=tc_, in1=td, op=ALU.add), [tck, tdk], [obk])
                if kind == "qr":
                    P.dma("pool", QR[b, qb0:qb0 + 4, half].rearrange("a p q -> p a q"), oa.rearrange("p (a q) -> p a q", q=128), [oak], [], oak)
                    P.dma("pool", QR2[b, qb0:qb0 + 4, half].rearrange("a p q -> p a q"), ob.rearrange("p (a q) -> p a q", q=128), [obk], [], obk)
                elif kind == "qi":
                    P.dma("pool", QI[b, qb0:qb0 + 4, half].rearrange("a p q -> p a q"), oa.rearrange("p (a q) -> p a q", q=128), [oak], [], oak)
                    P.dma("pool", QI2[b, qb0:qb0 + 4, half].rearrange("a p q -> p a q"), ob.rearrange("p (a q) -> p a q", q=128), [obk], [], obk)
                elif kind == "kr":
                    P.dma("pool", KR1[b, :, t0:t0 + 512], oa, [oak], [], oak)
                    P.dma("pool", KR2[b, :, t0:t0 + 512], ob, [obk], [], obk)
                else:
                    P.dma("pool", KI1[b, :, t0:t0 + 512], oa, [oak], [], oak)
                    P.dma("pool", KI2[b, :, t0:t0 + 512], ob, [obk], [], obk)
        P.emit()

    if stop_after == 1:
        return nc, dbg, P

    import os as _os
    SCALE_A = (128 + 32) ** -0.5
    BIGM = 30000.0
    with ExitStack() as es:
        def T(name, shape, dt=F32):
            return es.enter_context(nc.sbuf_tensor(name, list(shape), dt))
        WOA = T("p2_woa", [128, 16, D], BF16)
        WUV = T("p2_wuv", [128, 16 * 128], BF16)
        STG2 = T("p2_stg", [128, 2, 1024], F32)
        KIs = T("p2_ki", [128, S], BF16)
        KLs = T("p2_kl", [128, S], BF16)
        KRs = T("p2_kr", [128, S], BF16)
        Vs = T("p2_v", [128, NQB, 128], BF16)
        QIq = T("p2_qi", [128, 2, 4, 128], BF16)
        WIq = T("p2_wi", [128, 2, 8], F32)
        WAB = T("p2_wab", [128, 2, 16], F32)
        QLq = T("p2_ql", [128, 2, 16, 128], BF16)
        QRq = T("p2_qr", [128, 2, 16, 128], BF16)
        GTq = T("p2_gt", [128, 2, 16, 128], BF16)
        Xq = T("p2_x", [128, 2, D], F32)
        IS = T("p2_is", [128, S], F32)
        MS = T("p2_ms", [128, S], BF16)
        MT = T("p2_mt", [128, 2, NQB, 128], BF16)
        TMPI = T("p2_tmpi", [128, 3, 512], F32)
        PT = T("p2_pt", [128, 4, 512], BF16)
        RD = T("p2_rd", [128, 1024], F32)
        ON = T("p2_on", [128, 1024], BF16)
        Y = T("p2_y", [128, 16, 128], BF16)
        H1q = T("p2_h1", [128, 2, D], F32)
        TMPH = T("p2_tmph", [128, D], F32)
        BS = T("p2_bs", [128, 8 + NIT], F32)
        NEG30 = T("p2_neg", [128, 1], F32)

        P.op("dve", "memset", [], ["NEG30"], NEG30[:], -BIGM)
        P.op("dve", "memset", [], ["KRs"], KRs[:], 0.0)
        P.op("pool", "memset", [], ["QRq0", "QRq1"], QRq[:], 0.0)
        for h in range(16):
            st = STG2[:, h % 2, :]
            P.dma("sp", st, woa[h * 128:(h + 1) * 128, :], [], ["stg2_%d" % (h % 2)], "stg2_%d" % (h % 2))
            if h % 2 == 0:
                P.op("dve", "tensor_copy", ["stg2_0"], [("WOA", h)], out=WOA[:, h, :], in_=st)
            else:
                P.op("act", "copy", ["stg2_1"], [("WOA", h)], out=WOA[:, h, :], in_=st)
        for i in range(2):
            st = STG2[:, i, :]
            P.dma("sp", st, wuv[:, i * 1024:(i + 1) * 1024], [], ["stg2_%d" % i], "stg2_%d" % i)
            P.op("dve", "tensor_copy", ["stg2_%d" % i], [("WUV", i)], out=WUV[:, i * 1024:(i + 1) * 1024], in_=st)
        woa_keys = [("WOA", h) for h in range(16)]
        wpool = Rot([(ps[i], "ps%d" % i) for i in range(4)])
        psO = [(ps[4], "ps4"), (ps[5], "ps5")]
        psD = [(ps[6], "ps6"), (ps[7], "ps7")]
        tmpi = Rot([(TMPI[:, i, :], "tmpi%d" % i) for i in range(3)])
        ptr = Rot([(PT[:, i, :], "pt%d" % i) for i in range(4)])
        nblk = int(_os.environ.get("K_NBLK", NB * NQB))
        dbg_qb = int(_os.environ.get("K_DBGQB", 3))
        def idx_part(blk):
            b = blk // NQB
            qb = blk % NQB
            sl = blk % 2
            nk = (qb + 1) * 128
            nkc = qb + 1
            if qb == 0:
                ki1 = KI1[b].rearrange("(i r) t -> r i t", r=4)[0]
                ki2 = KI2[b].rearrange("(i r) t -> r i t", r=4)[0]
                P.dma("sp", KIs[0:32, :], ki1, [], ["KIs"], "KIs")
                P.dma("sp", KIs[32:64, :], ki2, [], ["KIs"], "KIs")
                P.dma("sp", KIs[64:96, :], ki1, [], ["KIs"], "KIs")
                P.dma("sp", KIs[96:128, :], ki2, [], ["KIs"], "KIs")
            qik = "QIq%d" % sl
            for h2 in range(2):
                for xp, src in ((0, QI), (1, QI2)):
                    for half in range(2):
                        sv = src[b, qb, half].rearrange("(i pp h2) q -> h2 i pp q", pp=2, h2=2)[h2]
                        dv = QIq[h2 * 64 + xp * 32:h2 * 64 + xp * 32 + 32, sl, half * 2:half * 2 + 2, :]
                        P.dma("sp", dv, sv, [], [qik], qik)
            P.dma("sp", WIq[:, sl, :], WI[b, qb * 128:(qb + 1) * 128, :], [], ["WIq%d" % sl], "WIq%d" % sl)
            P.dma("sp", QLq[:, sl].rearrange("p h q -> p (h q)"), QL[b, qb], [], ["QLq%d" % sl], "QLq%d" % sl)
            for half in range(2):
                P.dma("sp", QRq[0:16, sl, half * 8:(half + 1) * 8, :], QR[b, qb, half].rearrange("(i hh) q -> i hh q", hh=8),
                      [], ["QRq%d" % sl], "QRq%d" % sl)
                P.dma("sp", QRq[16:32, sl, half * 8:(half + 1) * 8, :], QR2[b, qb, half].rearrange("(i hh) q -> i hh q", hh=8),
                      [], ["QRq%d" % sl], "QRq%d" % sl)
            P.dma("sp", GTq[:, sl].rearrange("p h q -> p (h q)"), GT[b, qb], [], ["GTq%d" % sl], "GTq%d" % sl)
            r0 = b * S + qb * 128
            P.dma("sp", Xq[:, sl, :], x[r0:r0 + 128, :], [], ["Xq%d" % sl], "Xq%d" % sl)
            wabk = "WAB%d" % sl
            P.op("act", "activation", ["WIq%d" % sl], [wabk + "a"], out=WAB[:, sl, 0:8], in_=WIq[:, sl, :], func=AF.Abs)
            P.op("act", "activation", ["WIq%d" % sl], [wabk + "s"], out=WAB[:, sl, 8:16], in_=WIq[:, sl, :], func=AF.Sign)
            nc5 = (nk + 511) // 512
            iskeys = [("IS", c5) for c5 in range(nc5)]
            for c5 in range(nc5):
                w = min(512, nk - c5 * 512)
                cs = slice(c5 * 512, c5 * 512 + w)
                for h in range(8):
                    pt_, pk = wpool.next()
                    pb = (h % 2) * 64
                    P.op("pe", "matmul", [qik, "KIs"], [pk], pt_[:, 0:w], lhsT=QIq[pb:pb + 64, sl, h // 2, :], rhs=KIs[pb:pb + 64, cs],
                         start=True, stop=True)
                    tm, tmk = tmpi.next()
                    P.op("act", "activation", [pk, wabk + "a"], [tmk], out=tm[:, 0:w], in_=pt_[:, 0:w], func=AF.Relu, scale=WAB[:, sl, h:h + 1])
                    if h == 0:
                        P.op("dve", "tensor_scalar", [tmk, wabk + "s"], [("IS", c5)], out=IS[:, cs], in0=tm[:, 0:w],
                             scalar1=WAB[:, sl, 8:9], scalar2=None, op0=ALU.mult)
                    else:
                        P.op("dve", "scalar_tensor_tensor", [tmk, wabk + "s", ("IS", c5)], [("IS", c5)], out=IS[:, cs], in0=tm[:, 0:w],
                             scalar=WAB[:, sl, 8 + h:9 + h], in1=IS[:, cs], op0=ALU.mult, op1=ALU.add)
            P.op("dve", "tensor_reduce", iskeys, ["bs_mx"], out=BS[:, 0:1], in_=IS[:, 0:nk], axis=AX.X, op=ALU.max)
            P.op("dve", "tensor_reduce", iskeys, ["bs_mn"], out=BS[:, 1:2], in_=IS[:, 0:nk], axis=AX.X, op=ALU.min)
            P.op("dve", "tensor_tensor", ["bs_mx", "bs_mn"], ["bs_w0"], out=BS[:, 2:3], in0=BS[:, 0:1], in1=BS[:, 1:2], op=ALU.subtract)
            P.op("dve", "tensor_scalar", ["bs_w0", "CST"], ["bs_steps"], out=BS[:, 8:8 + NIT], in0=POW2, scalar1=BS[:, 2:3], scalar2=None, op0=ALU.mult)
            P.op("dve", "tensor_copy", ["bs_mn"], ["bs_lo"], out=BS[:, 3:4], in_=BS[:, 1:2])
            dk = ("IS", (nk - 128) // 512)
            P.op("dve", "tensor_tensor", [dk, "CST"], [dk], out=IS[:, nk - 128:nk], in0=IS[:, nk - 128:nk], in1=CAUS, op=ALU.add)
            for it in range(NIT):
                P.op("dve", "tensor_tensor", ["bs_lo", "bs_steps"], ["bs_mid"], out=BS[:, 4:5], in0=BS[:, 3:4], in1=BS[:, 8 + it:9 + it], op=ALU.add)
                P.op("dve", "tensor_scalar", iskeys + ["bs_mid"], ["MS", "bs_cnt"], out=MS[:, 0:nk], in0=IS[:, 0:nk], scalar1=BS[:, 4:5],
                     scalar2=0.0, op0=ALU.is_ge, op1=ALU.add, accum_out=BS[:, 5:6])
                P.op("dve", "scalar_tensor_tensor", ["bs_cnt", "bs_steps"], ["bs_t"], out=BS[:, 6:7], in0=BS[:, 5:6], scalar=TOPK - 0.5,
                     in1=BS[:, 8 + it:9 + it], op0=ALU.is_ge, op1=ALU.mult)
                P.op("dve", "tensor_tensor", ["bs_lo", "bs_t"], ["bs_lo"], out=BS[:, 3:4], in0=BS[:, 3:4], in1=BS[:, 6:7], op=ALU.add)
            P.op("dve", "tensor_scalar", iskeys + ["bs_lo"], ["MS"], out=MS[:, 0:nk], in0=IS[:, 0:nk], scalar1=BS[:, 3:4], scalar2=None, op0=ALU.is_ge)
            if debug and b == 0 and qb == dbg_qb:
                P.dma("pool", dbg["d_IS"][:, 0:nk], IS[:, 0:nk], iskeys, [], "dbgis")
                P.dma("pool", dbg["d_lo"], BS[:, 0:8], ["bs_lo", "bs_cnt", "bs_mx", "bs_mn"], [], "dbglo")
            for kc0 in range(0, nkc, 4):
                n4 = min(4, nkc - kc0)
                pt_, pk = wpool.next()
                pv = pt_[:].bitcast(BF16)
                for j in range(n4):
                    kc = kc0 + j
                    P.op("pe", "transpose", ["MS", "IDB"], [pk], out=pv[:, j * 128:(j + 1) * 128], in_=MS[:, kc * 128:(kc + 1) * 128], identity=IDB[:])
                P.op("act", "activation", [pk, "NEG30"], [("MT", sl, kc0 // 4)], out=MT[:, sl, kc0:kc0 + n4, :].rearrange("p c q -> p (c q)"),
                     in_=pv[:, 0:n4 * 128], func=AF.Identity, scale=BIGM, bias=NEG30[:])

        def att_part(blk):
            b = blk // NQB
            qb = blk % NQB
            sl = blk % 2
            nk = (qb + 1) * 128
            nkc = qb + 1
            r0 = b * S + qb * 128
            if qb == 0:
                P.dma("sp", KLs[:], KL[b], [], ["KLs"], "KLs")
                P.dma("sp", KRs[0:16, :], KR1[b].rearrange("(i r) t -> r i t", r=8)[0], [], ["KRs"], "KRs")
                P.dma("sp", KRs[16:32, :], KR2[b].rearrange("(i r) t -> r i t", r=8)[0], [], ["KRs"], "KRs")
                P.dma("sp", Vs[:], VA[b].rearrange("(c p) d -> p c d", p=128), [], ["Vs"], "Vs")
            for hg in range(2):
                groups = [(kc, g) for kc in range(nkc) for g in range(2)]

                def qk(kc, g):
                    ks = slice(kc * 128, (kc + 1) * 128)
                    h0 = hg * 8 + g * 4
                    pt_, pk = wpool.next()
                    P.op("pe", "matmul", ["KLs", "QLq%d" % sl], [pk], pt_[:], lhsT=KLs[:, ks],
                         rhs=QLq[:, sl, h0:h0 + 4, :].rearrange("p h q -> p (h q)"), start=True, stop=False)
                    P.op("pe", "matmul", ["KRs", "QRq%d" % sl], [pk], pt_[:], lhsT=KRs[:, ks],
                         rhs=QRq[:, sl, h0:h0 + 4, :].rearrange("p h q -> p (h q)"), start=False, stop=False)
                    P.op("pe", "matmul", ["IDB", ("MT", sl, kc // 4)], [pk], pt_[:], lhsT=IDB[:],
                         rhs=MT[:, sl, kc, :].unsqueeze(1).to_broadcast([128, 4, 128]), start=False, stop=True)
                    return pt_, pk
                pend = [qk(*groups[0])]
                if len(groups) > 1:
                    pend.append(qk(*groups[1]))
                for gi, (kc, g) in enumerate(groups):
                    if gi + 2 < len(groups):
                        pend.append(qk(*groups[gi + 2]))
                    pt_, pk = pend.pop(0)
                    pr, prk = ptr.next()
                    P.op("act", "activation", [pk], [prk], out=pr, in_=pt_[:], func=AF.Exp, scale=SCALE_A)
                    P.op("pe", "matmul", ["Vs", prk], [psO[g][1]], psO[g][0][:], lhsT=Vs[:, kc, :], rhs=pr, start=(kc == 0), stop=(kc == nkc - 1))
                    P.op("pe", "matmul", ["ONESB", prk], [psD[g][1]], psD[g][0][:], lhsT=ONESB[:], rhs=pr, start=(kc == 0), stop=(kc == nkc - 1))
                for g in range(2):
                    h0 = hg * 8 + g * 4
                    gs = slice(g * 512, (g + 1) * 512)
                    P.op("dve", "reciprocal", [psD[g][1]], [("RD", g)], out=RD[:, gs], in_=psD[g][0][:])
                    P.op("dve", "tensor_tensor", [psO[g][1], ("RD", g)], [("ON", g)], out=ON[:, gs], in0=psO[g][0][:], in1=RD[:, gs], op=ALU.mult)
                    pt_, pk = wpool.next()
                    for hl in range(4):
                        h = h0 + hl
                        P.op("pe", "matmul", [("WUV", h // 8), ("ON", g)], [pk], pt_[:, hl * 128:(hl + 1) * 128], lhsT=WUV[:, h * 128:(h + 1) * 128],
                             rhs=ON[:, g * 512 + hl * 128:g * 512 + (hl + 1) * 128], start=True, stop=True)
                    P.op("dve", "tensor_tensor", [pk, "GTq%d" % sl], [("Y", h0 // 4)], out=Y[:, h0:h0 + 4, :].rearrange("p h q -> p (h q)"),
                         in0=pt_[:], in1=GTq[:, sl, h0:h0 + 4, :].rearrange("p h q -> p (h q)"), op=ALU.mult)
            ykeys = [("Y", i) for i in range(4)]
            for nh in range(2):
                pt_, pk = wpool.next()
                for h in range(16):
                    P.op("pe", "matmul", ykeys + [("WOA", h)], [pk], pt_[:], lhsT=Y[:, h, :], rhs=WOA[:, h, nh * 512:(nh + 1) * 512],
                         start=(h == 0), stop=(h == 15))
                P.op("dve", "tensor_tensor", [pk, "GBC0"], [("TMPH", nh)], out=TMPH[:, nh * 512:(nh + 1) * 512], in0=pt_[:],
                     in1=GBC[0][:, b * D + nh * 512:b * D + (nh + 1) * 512], op=ALU.mult)
            h1k = "H1q%d" % sl
            P.op("pool", "tensor_tensor", [("TMPH", 0), ("TMPH", 1), "Xq%d" % sl], [h1k], out=H1q[:, sl, :], in0=TMPH[:], in1=Xq[:, sl, :], op=ALU.add)
            P.dma("pool", H1[r0:r0 + 128, :], H1q[:, sl, :], [h1k], [], h1k)
            if debug:
                P.dma("pool", dbg["d_H1"][r0:r0 + 128, :], H1q[:, sl, :], [h1k], [], h1k)

        idx_part(0)
        for blk in range(nblk):
            if blk + 1 < nblk:
                idx_part(blk + 1)
            att_part(blk)
        P.emit()

    if stop_after == 2:
        return nc, dbg, P

    with ExitStack() as es:
        def T(name, shape, dt=F32):
            return es.enter_context(nc.sbuf_tensor(name, list(shape), dt))
        W3 = T("p3_w", [128, 8, WB_COLS], BF16)
        STG3 = T("p3_stg", [128, 2, WB_COLS // 4], F32)
        X3 = T("p3_x", [128, 2, D], F32)
        JUNK3 = T("p3_junk", [128, D], BF16)
        SS3 = T("p3_ss", [128, 2, 4], F32)
        KVT = T("p3_kvt", [128, 2, 8, 512], BF16)
        HBT = T("p3_hbt", [128, 2, 8, 512], BF16)
        POSI3 = T("p3_posi", [128, 512], I32)
        POSF3 = T("p3_posf", [128, 512], F32)
        TAB3 = T("p3_tab", [128, 2, 512], F32)
        TKI3 = T("p3_ki", [128, 1, 512], I32)
        TKF3 = T("p3_kf", [128, 1, 512], F32)
        RT3 = T("p3_rt", [128, 4, 512], F32)
        RO3 = T("p3_ro", [128, 4, 512], BF16)
        VSTG = T("p3_vst", [128, 2, D], BF16)
        GST = T("p3_gst", [128, 2, 512], BF16)
        stg = Rot([(STG3[:, i, :], "stg3_%d" % i) for i in range(2)])
        wkeys = load_weights(WB, W3, WB_COLS, stg, "W3")
        tabs = [(TAB3[:, 0, :], TAB3[:, 1, :], "tab128")]
        tmps = Rot([(TKI3[:, 0, :], TKF3[:, 0, :], "tk3")])
        fm = Rot([(ps[i], "ps%d" % i) for i in range(4)])
        psT_pair = [(ps[4], "ps4"), (ps[5], "ps5")]
        vps = Rot([(ps[6], "ps6"), (ps[7], "ps7")])
        ro = Rot([(RO3[:, i, :], "ro3_%d" % i) for i in range(4)])
        rt = Rot([(RT3[:, i, :], "rt3_%d" % i) for i in range(4)])
        gst = Rot([(GST[:, i, :], "gst%d" % i) for i in range(2)])
        ntile3 = int(_os.environ.get("K_NT3", NB * S // 512))
        for tt in range(ntile3):
            b = tt // 8
            t0 = (tt % 8) * 512
            kvt = KVT[:, tt % 2]; kvk = "kvt%d" % (tt % 2)
            hbt = HBT[:, tt % 2]; hbk = "hbt%d" % (tt % 2)
            P.dma("sp", POSI3[:], pos[b:b + 1, t0:t0 + 512].partition_broadcast(128), [], ["POSI"], "POSI")
            P.op("pool", "tensor_copy", ["POSI"], ["POSF"], out=POSF3[:], in_=POSI3[:])
            rope_tables(b, t0, [(2, 0)], POSF3, tabs, tmps)
            for sub in range(4):
                xi = (tt * 4 + sub) % 2
                XT = X3[:, xi, :]
                xkey = "x3_%d" % xi
                r0 = b * S + t0 + sub * 128
                P.dma("sp", XT, H1[r0:r0 + 128, :], [], [xkey], xkey)

                def evacs(kc, pap, pk, sub=sub, b=b, kvt=kvt, hbt=hbt, kvk=kvk, hbk=hbk):
                    d1 = kvt[:, kc, sub * 128:(sub + 1) * 128]
                    d2 = hbt[:, kc, sub * 128:(sub + 1) * 128]
                    if kc < 4:
                        P.op("act", "activation", [pk, "NRM"], [(kvk, kc)], out=d1, in_=pap, func=AF.Copy, scale=NRM[:, 16 + kc:17 + kc])
                        P.op("act", "activation", [pk, "ASCL1", "ASFT1"], [(hbk, kc)], out=d2, in_=pap, func=AF.Identity,
                             scale=ASCL[1][:, kc * NB + b:kc * NB + b + 1], bias=ASFT[1][:, kc * NB + b:kc * NB + b + 1])
                    else:
                        P.op("dve", "tensor_scalar", [pk, "NRM"], [(kvk, kc)], out=d1, in0=pap, scalar1=NRM[:, 16 + kc:17 + kc], scalar2=None, op0=ALU.mult)
                        P.op("dve", "tensor_scalar", [pk, "ASCL1", "ASFT1"], [(hbk, kc)], out=d2, in0=pap,
                             scalar1=ASCL[1][:, kc * NB + b:kc * NB + b + 1], scalar2=ASFT[1][:, kc * NB + b:kc * NB + b + 1], op0=ALU.mult, op1=ALU.add)
                norm_transpose(None, XT, xkey, sub, SS3[:, xi, :], evacs, psT_pair, JUNK3)

            def fm3(ci, src, sk):
                pt_, pk = fm.next()
                for kc in range(8):
                    P.op("pe", "matmul", [(sk, kc)] + wkeys[kc], [pk], pt_[:], lhsT=W3[:, kc, ci * 128:(ci + 1) * 128], rhs=src[:, kc, :],
                         start=(kc == 0), stop=(kc == 7))
                return pt_, pk
            ct, st_, tk = tabs[0]
            pair_list = [(2 * j, kvt, kvk, KT2[b, j]) for j in range(4)]
            pair_list += [(8 + g * 8 + 2 * j, hbt, hbk, QT2[b, g, j]) for g in range(3) for j in range(4)]
            for (c0, src, sk, dst) in pair_list:
                p1, k1 = fm3(c0, src, sk)
                p2_, k2 = fm3(c0 + 1, src, sk)
                ta, tak = rt.next(); tb, tbk = rt.next()
                oa, oak = ro.next(); ob, obk = ro.next()
                P.op("dve", "tensor_tensor", [k1, (tk, 0)], [tak], out=ta, in0=p1[:], in1=ct, op=ALU.mult)
                P.op("dve", "tensor_tensor", [k2, (tk, 1)], [tbk], out=tb, in0=p2_[:], in1=st_, op=ALU.mult)
                P.op("pool", "tensor_tensor", [tak, tbk], [oak], out=oa, in0=ta, in1=tb, op=ALU.subtract)
                tc_, tck = rt.next(); td, tdk = rt.next()
                P.op("dve", "tensor_tensor", [k2, (tk, 0)], [tck], out=tc_, in0=p2_[:], in1=ct, op=ALU.mult)
                P.op("dve", "tensor_tensor", [k1, (tk, 1)], [tdk], out=td, in0=p1[:], in1=st_, op=ALU.mult)
                P.op("pool", "tensor_tensor", [tck, tdk], [obk], out=ob, in0=tc_, in1=td, op=ALU.add)
                P.dma("pool", dst[0, :, t0:t0 + 512], oa, [oak], [], oak)
                P.dma("pool", dst[1, :, t0:t0 + 512], ob, [obk], [], obk)
                if debug and tt == 0 and c0 == 0:
                    P.dma("pool", dbg["d_KT"][:, 0:512], oa, [oak], [], oak)
            for h in range(8):
                p1, k1 = fm3(32 + h, hbt, hbk)
                g_, gk = gst.next()
                P.op("act", "activation", [k1], [gk], out=g_, in_=p1[:], func=AF.Silu)
                P.dma("pool", GB[b, h, :, t0:t0 + 512], g_, [gk], [], gk)
            for sub in range(4):
                vs_ = VSTG[:, sub % 2, :]
                vk_ = "vstg%d" % (sub % 2)
                for nh in range(2):
                    pt_, pk = vps.next()
                    for kc in range(8):
                        P.op("pe", "matmul", [(kvk, kc)] + wkeys[kc], [pk], pt_[:], lhsT=kvt[:, kc, sub * 128:(sub + 1) * 128],
                             rhs=W3[:, kc, 5120 + nh * 512:5120 + (nh + 1) * 512], start=(kc == 0), stop=(kc == 7))
                    if nh == 0:
                        P.op("act", "copy", [pk], [(vk_, nh)], out=vs_[:, nh * 512:(nh + 1) * 512], in_=pt_[:])
                    else:
                        P.op("dve", "tensor_copy", [pk], [(vk_, nh)], out=vs_[:, nh * 512:(nh + 1) * 512], in_=pt_[:])
                r0 = t0 + sub * 128
                P.dma("pool", VB[b, r0:r0 + 128, :], vs_, [(vk_, 0), (vk_, 1)], [], vk_)
        P.emit()
    if stop_after == 3:
        return nc, dbg, P

    SCALE_B = 128 ** -0.5
    with ExitStack() as es:
        def T(name, shape, dt=F32):
            return es.enter_context(nc.sbuf_tensor(name, list(shape), dt))
        KTh = T("p4_k", [128, 2, S], BF16)
        QTh = T("p4_q", [128, 2, 3, S], BF16)
        Vh = T("p4_v", [128, 2, 3, NQB, 128], BF16)
        GBh = T("p4_g", [128, 2, S], BF16)
        OD = T("p4_od", [128, 2, S], F32)
        PTb = T("p4_pt", [128, 4, 256], BF16)
        YTo = T("p4_y", [128, S], BF16)
        sp_ = Rot([(ps[i], "ps%d" % i) for i in range(4)])
        op_ = Rot([(ps[i], "ps%d" % i) for i in range(4, 8)])
        ptb = Rot([(PTb[:, i, :], "ptb%d" % i) for i in range(4)])
        nhead4 = int(_os.environ.get("K_NH4", NB * 8))
        for idx in range(nhead4):
            b = idx // 8
            h = idx % 8
            sl = idx % 2
            j = h // 2
            hh = h % 2
            kk_ = "KTh%d" % sl; qk_ = "QTh%d" % sl; vk_ = "Vh%d" % sl; gk_ = "GBh%d" % sl
            for xx in range(2):
                P.dma("sp", KTh[xx * 64:(xx + 1) * 64, sl, :], KT2[b, j, xx, hh * 64:(hh + 1) * 64, :], [], [kk_], kk_)
                for g in range(3):
                    P.dma("sp", QTh[xx * 64:(xx + 1) * 64, sl, g, :], QT2[b, g, j, xx, hh * 64:(hh + 1) * 64, :], [], [qk_], qk_)
            for g, d in enumerate((1, 4, 16)):
                nch = NQB // d
                vv = VB[b, :, h * 128:(h + 1) * 128].rearrange("(c a r) f -> r a c f", a=128, r=d)
                for r in range(d):
                    P.dma("sp", Vh[:, sl, g, r * nch:(r + 1) * nch, :], vv[r], [], [vk_], vk_)
            P.dma("sp", GBh[:, sl, :], GB[b, h], [], [gk_], gk_)
            units = [(g, d, r, c) for g, d in enumerate((1, 4, 16)) for r in range(d) for c in range(NQB // d)]

            def qk4(g, d, r, c):
                qv = QTh[:, sl, g, :].rearrange("p (c a r) -> p r c a", a=128, r=d)
                kv_ = KTh[:, sl, :].rearrange("p (c a r) -> p r c a", a=128, r=d)
                pS, pSk = sp_.next()
                if c > 0:
                    P.op("pe", "matmul", [kk_, qk_], [pSk], pS[:, 0:128], lhsT=kv_[:, r, c - 1, :], rhs=qv[:, r, c, :], start=True, stop=True)
                P.op("pe", "matmul", [kk_, qk_], [pSk], pS[:, 128:256], lhsT=kv_[:, r, c, :], rhs=qv[:, r, c, :], start=True, stop=True)
                return pS, pSk
            pend = [qk4(*units[0]), qk4(*units[1])]
            for ui, (g, d, r, c) in enumerate(units):
                if ui + 2 < len(units):
                    pend.append(qk4(*units[ui + 2]))
                pS, pSk = pend.pop(0)
                nch = NQB // d
                ov = OD[:].rearrange("p x (c a r) -> p r c x a", a=128, r=d)
                ti = r * nch + c
                lo = 0 if c > 0 else 128
                pt_, ptk = ptb.next()
                P.op("act", "activation", [pSk], [ptk], out=pt_[:, lo:256], in_=pS[:, lo:256], func=AF.Exp, scale=SCALE_B)
                P.op("pool", "tensor_tensor", [ptk, "MASKB"], [ptk], out=pt_[:, lo:256], in0=pt_[:, lo:256], in1=MASKB[:, lo:256], op=ALU.mult)
                pO, pOk = op_.next()
                for half, lhs_of in ((0, lambda t: Vh[:, sl, g, t, :]), (1, lambda t: ONESB[:])):
                    oc = slice(half * 128, (half + 1) * 128)
                    if c > 0:
                        P.op("pe", "matmul", [vk_, "ONESB", ptk], [pOk], pO[:, oc], lhsT=lhs_of(ti - 1), rhs=pt_[:, 0:128], start=True, stop=False)
                        P.op("pe", "matmul", [vk_, "ONESB", ptk], [pOk], pO[:, oc], lhsT=lhs_of(ti), rhs=pt_[:, 128:256], start=False, stop=True)
                    else:
                        P.op("pe", "matmul", [vk_, "ONESB", ptk], [pOk], pO[:, oc], lhsT=lhs_of(ti), rhs=pt_[:, 128:256], start=True, stop=True)
                dst = ov[:, r, c, :, :]
                src = pO[:, 0:256].rearrange("p (x a) -> p x a", a=128)
                if g == 0:
                    P.op("act", "copy", [pOk], [("OD", c // 4)], out=dst, in_=src)
                else:
                    odk = [("OD", i) for i in range(8)]
                    P.op("dve", "tensor_tensor", [pOk] + odk, odk, out=dst, in0=src, in1=dst, op=ALU.add)
            odk = [("OD", i) for i in range(8)]
            P.op("dve", "reciprocal", odk, odk, out=OD[:, 1, :], in_=OD[:, 1, :])
            P.op("pool", "tensor_tensor", odk, odk, out=OD[:, 0, :], in0=OD[:, 0, :], in1=OD[:, 1, :], op=ALU.mult)
            P.op("dve", "tensor_tensor", odk + [gk_], ["YTo"], out=YTo[:], in0=OD[:, 0, :], in1=GBh[:, sl, :], op=ALU.mult)
            P.dma("pool", YT[b, h], YTo[:], ["YTo"], [], "YTo")
            if debug and idx == 0:
                P.dma("pool", dbg["d_YT"], YTo[:], ["YTo"], [], "YTo")
        P.emit()
    if stop_after == 4:
        return nc, dbg, P

    with ExitStack() as es:
        def T(name, shape, dt=F32):
            return es.enter_context(nc.sbuf_tensor(name, list(shape), dt))
        WOB = T("p5_w", [128, 8, D], BF16)
        STG5 = T("p5_stg", [128, 2, D], F32)
        Yq = T("p5_y", [128, 2, 8, 128], BF16)
        H1t = T("p5_h1", [128, 2, D], F32)
        TMP5 = T("p5_tmp", [128, D], F32)
        H2 = T("p5_h2", [128, D], F32)
        OUTt = T("p5_out", [128, 2, D], F32)
        JUNK5 = T("p5_junk", [128, D], BF16)
        SS5 = T("p5_ss", [128, 4], F32)
        for h in range(8):
            st = STG5[:, h % 2, :]
            P.dma("sp", st, wob[h * 128:(h + 1) * 128, :], [], ["stg5_%d" % (h % 2)], "stg5_%d" % (h % 2))
            if h % 2 == 0:
                P.op("dve", "tensor_copy", ["stg5_0"], [("WOB", h)], out=WOB[:, h, :], in_=st)
            else:
                P.op("act", "copy", ["stg5_1"], [("WOB", h)], out=WOB[:, h, :], in_=st)
        pp = Rot([(ps[i], "ps%d" % i) for i in range(4)])
        ntile5 = int(_os.environ.get("K_NT5", NB * NQB))
        for tt in range(ntile5):
            b = tt // NQB
            qb = tt % NQB
            sl = tt % 2
            r0 = b * S + qb * 128
            yk = "Yq%d" % sl; hk_ = "H1t%d" % sl; ok_ = "OUTt%d" % sl
            P.dma("sp", Yq[:, sl], YT[b, :, :, qb * 128:(qb + 1) * 128].rearrange("h p t -> p h t"), [], [yk], yk)
            P.dma("sp", H1t[:, sl, :], H1[r0:r0 + 128, :], [], [hk_], hk_)
            for nh in range(2):
                pt_, pk = pp.next()
                for h in range(8):
                    P.op("pe", "matmul", [yk, ("WOB", h)], [pk], pt_[:], lhsT=Yq[:, sl, h, :], rhs=WOB[:, h, nh * 512:(nh + 1) * 512],
                         start=(h == 0), stop=(h == 7))
                P.op("dve", "tensor_tensor", [pk, "GBC1"], [("TMP5", nh)], out=TMP5[:, nh * 512:(nh + 1) * 512], in0=pt_[:],
                     in1=GBC[1][:, b * D + nh * 512:b * D + (nh + 1) * 512], op=ALU.mult)
            P.op("pool", "tensor_tensor", [("TMP5", 0), ("TMP5", 1), hk_], ["H2"], out=H2[:], in0=TMP5[:], in1=H1t[:, sl, :], op=ALU.add)
            P.op("act", "activation", ["H2"], ["junk5", "ss5a"], out=JUNK5[:], in_=H2[:], func=AF.Square, accum_out=SS5[:, 0:1])
            P.op("act", "activation", ["ss5a"], ["ss5b"], out=SS5[:, 1:2], in_=SS5[:, 0:1], func=AF.Sqrt, scale=1.0 / D, bias=EPS)
            P.op("dve", "reciprocal", ["ss5b"], ["ss5c"], out=SS5[:, 2:3], in_=SS5[:, 1:2])
            P.op("dve", "scalar_tensor_tensor", ["H2", "ss5c", "FNG"], [ok_], out=OUTt[:, sl, :], in0=H2[:], scalar=SS5[:, 2:3], in1=FNG[:],
                 op0=ALU.mult, op1=ALU.mult)
            P.dma("pool", out[r0:r0 + 128, :], OUTt[:, sl, :], [ok_], [], ok_)
        P.emit()

    return nc, dbg, P


def _perm_a():
    idx = []
    idx += list(range(0, 2048))
    idx += list(range(2720, 4768))
    for hb in (0, 8):
        for part in (0, 16):
            idx += [2048 + (hb + (p % 8)) * 32 + part + p // 8 for p in range(128)]
    for hb in (0, 4):
        for part in (0, 32):
            idx += [4768 + (hb + (p % 4)) * 64 + part + p // 4 for p in range(128)]
    for part in (0, 16):
        idx += [2688 + part + p // 8 for p in range(128)]
    for part in (0, 32):
        idx += [5280 + part + p // 4 for p in range(128)]
    idx += list(range(2560, 2688))
    idx += list(range(5344, 5352))
    assert len(idx) == WA_COLS
    return np.array(idx)


def _consts():
    c = np.zeros((128, 640 + 3 + NIT), np.float32)
    c[:, 0:128] = np.eye(128, dtype=np.float32)
    a = np.arange(128)[:, None]
    q = np.arange(128)[None, :]
    c[:, 128:256] = (a >= q).astype(np.float32)
    c[:, 256:384] = (a <= q).astype(np.float32)
    qq = np.arange(128)[:, None]
    kk = np.arange(128)[None, :]
    c[:, 384:512] = np.where(kk <= qq, 0.0, NEG).astype(np.float32)
    p = np.arange(128)
    two_pi = 2 * math.pi
    inv32 = (np.float32(THETA) ** (-(np.arange(0, 32, 2, dtype=np.float32)) / np.float32(32))).astype(np.float32)
    inv64 = (np.float32(THETA) ** (-(np.arange(0, 64, 2, dtype=np.float32)) / np.float32(64))).astype(np.float32)
    inv128 = (np.float32(THETA) ** (-(np.arange(0, 128, 2, dtype=np.float32)) / np.float32(128))).astype(np.float32)
    c[:, 640] = inv32[p // 8].astype(np.float64) / two_pi
    c[:, 641] = inv64[p // 4].astype(np.float64) / two_pi
    c[:, 642] = inv128[p % 64].astype(np.float64) / two_pi
    c[:, 643:643 + NIT] = (0.5 ** np.arange(1, NIT + 1))[None, :]
    return c


def _fm(v, nch):
    return np.ascontiguousarray(np.asarray(v, np.float32).reshape(nch, 128).T)


def prepare_inputs(x, c, positions, a_norm, a_ada_w, a_ada_b, a_w_in, a_kv_norm, a_w_uv, a_w_out,
                   kv_norm, w_kv, b_norm, b_ada_w, b_ada_b, b_w_in, b_w_out, final_norm):
    f = lambda a: np.ascontiguousarray(np.asarray(a, np.float32))
    x = f(x); c = f(c)
    positions = np.ascontiguousarray(np.asarray(positions, np.int32))
    WA = np.ascontiguousarray(f(a_w_in)[0][:, _perm_a()])
    wkv = f(w_kv); bw = f(b_w_in)[0]
    cols = []
    def pair_cols(base):
        out_ = []
        for j in range(4):
            for part in (0, 64):
                out_.append([base + (2 * j + p // 64) * 128 + part + (p % 64) for p in range(128)])
        return out_
    kcols = np.concatenate([wkv[:, ci] for ci in pair_cols(0)], axis=1)
    qcols = np.concatenate([bw[:, ci] for g in range(3) for ci in pair_cols(g * 1024)], axis=1)
    WB = np.ascontiguousarray(np.concatenate([kcols, qcols, bw[:, 3072:4096], wkv[:, 1024:2048]], axis=1))
    assert WB.shape[1] == WB_COLS
    normsT = np.concatenate([_fm(f(a_norm)[0], 8), _fm(f(b_norm)[0], 8), _fm(f(kv_norm), 8)], axis=1)
    ada_bT = np.concatenate([_fm(f(a_ada_b)[0], 24), _fm(f(b_ada_b)[0], 24)], axis=1)
    ada_bg = np.concatenate([f(a_ada_b)[0][2 * D:], f(b_ada_b)[0][2 * D:]])[None, :]
    wuv = np.ascontiguousarray(f(a_w_uv)[0].transpose(1, 0, 2).reshape(128, 16 * 128))
    shared = {
        "normsT": np.ascontiguousarray(normsT), "a_ada_w": f(a_ada_w)[0], "b_ada_w": f(b_ada_w)[0],
        "ada_bT": np.ascontiguousarray(ada_bT), "ada_bg": np.ascontiguousarray(ada_bg),
        "WA": WA, "WB": WB, "akvg": f(a_kv_norm)[0][None, :], "wuv": wuv, "woa": f(a_w_out)[0], "wob": f(b_w_out)[0],
        "fng": f(final_norm)[None, :], "cst": _consts(),
    }
    in_maps = []
    for core in range(NCORES):
        bs = slice(core * NB, (core + 1) * NB)
        cc = c[bs]
        cTl = np.ascontiguousarray(cc.reshape(NB, 8, 128).transpose(2, 1, 0).reshape(128, 8 * NB))
        m = dict(shared)
        m["x"] = np.ascontiguousarray(x[bs].reshape(NB * S, D))
        m["pos"] = np.ascontiguousarray(positions[bs])
        m["cT"] = cTl
        in_maps.append(m)
    return in_maps


def kernel(**inputs):
    in_maps = prepare_inputs(**inputs)
    nc, _, _ = build_program(debug=False)
    res = run_bass_kernel_spmd(nc, in_maps, core_ids=list(range(NCORES)))
    outs = [np.asarray(r["out"]).reshape(NB, S, D) for r in res.results]
    return np.concatenate(outs, axis=0).astype(np.float32)
```

```python
import math
from contextlib import ExitStack
import numpy as np
import concourse.bass as bass
import concourse.mybir as mybir
from concourse.bass_utils import run_bass_kernel_spmd

F32 = mybir.dt.float32
BF16 = mybir.dt.bfloat16
I32 = mybir.dt.int32
AF = mybir.ActivationFunctionType
ALU = mybir.AluOpType
AX = mybir.AxisListType

ENGS = ("pe", "act", "dve", "pool", "sp")

D = 1024
S = 4096
NB = 2
NCORES = 8
NQB = S // 128
EPS = 1e-6
THETA = 10000.0
NIT = 18
TOPK = 256
NEG = -1.0e30
WA_COLS = 5768
WB_COLS = 6144


class Op:
    __slots__ = ("eng", "fn", "deps", "is_dma", "chan", "val", "flag")

    def __init__(self, eng, fn, is_dma=False, chan=None):
        self.eng = eng
        self.fn = fn
        self.deps = ()
        self.is_dma = is_dma
        self.chan = chan
        self.val = 0
        self.flag = False


class Prog:
    def __init__(self, nc):
        self.nc = nc
        self.chan_sem = {}
        self.chan_cnt = {}
        self.free_chan = []
        self.eng_sem = {}
        self.eng_cnt = {e: 0 for e in ENGS}
        self.phase_no = 0
        self.total_ops = 0
        self._reset()

    def _reset(self):
        self.ops = []
        self.last_w = {}
        self.readers = {}

    def add(self, eng, fn, reads=(), writes=(), chan=None):
        is_dma = chan is not None
        op = Op(eng, fn, is_dma, chan)
        psr = [k for k in reads if isinstance(k, str) and k.startswith("ps")]
        if psr:
            reads = [k for k in reads if k not in psr]
            writes = list(writes) + [k for k in psr if k not in writes]
        deps = {}
        for k in reads:
            w = self.last_w.get(k)
            if w is not None:
                deps[id(w)] = w
        for k in writes:
            w = self.last_w.get(k)
            if w is not None:
                deps[id(w)] = w
            for r in self.readers.get(k, ()):
                deps[id(r)] = r
        dl = []
        for d in deps.values():
            if d is op:
                continue
            if (not is_dma) and eng == "pe" and d.eng == "pe" and not d.is_dma:
                continue
            dl.append(d)
        op.deps = dl
        for k in reads:
            self.readers.setdefault(k, []).append(op)
        for k in writes:
            self.last_w[k] = op
            self.readers[k] = []
        self.ops.append(op)
        return op

    def op(self, eng, meth, reads, writes, *args, **kw):
        return self.add(eng, lambda e: getattr(e, meth)(*args, **kw), reads, writes)

    def dma(self, q, out, in_, reads, writes, chan):
        return self.add(q, lambda e: e.dma_start(out=out, in_=in_), reads, writes, chan=chan)

    def emit(self):
        nc = self.nc
        self.phase_no += 1
        ops = self.ops
        for op in ops:
            for d in op.deps:
                d.flag = True
        per_eng = {e: [] for e in ENGS}
        for op in ops:
            per_eng[op.eng].append(op)
        last_compute = {}
        for e in ENGS:
            for op in reversed(per_eng[e]):
                if not op.is_dma:
                    op.flag = True
                    last_compute[e] = op
                    break
        for e in last_compute:
            if e not in self.eng_sem:
                self.eng_sem[e] = nc.alloc_semaphore("eng_%s" % e)
        eng_sem = self.eng_sem
        cnt = self.eng_cnt
        for op in ops:
            if op.is_dma:
                if op.chan not in self.chan_sem:
                    if self.free_chan:
                        self.chan_sem[op.chan], self.chan_cnt[op.chan] = self.free_chan.pop()
                    else:
                        self.chan_sem[op.chan] = nc.alloc_semaphore("ch%d" % len(self.chan_sem))
                        self.chan_cnt[op.chan] = 0
                self.chan_cnt[op.chan] += 16
                op.val = self.chan_cnt[op.chan]
            elif op.flag:
                cnt[op.eng] += 1
                op.val = cnt[op.eng]
        final_eng = {e: (eng_sem[e], last_compute[e].val) for e in last_compute}
        final_chan = {c: (self.chan_sem[c], self.chan_cnt[c]) for c in self.chan_sem}

        def sem_of(d):
            return self.chan_sem[d.chan] if d.is_dma else eng_sem[d.eng]

        def run(e, engine):
            waited = {}
            for op in per_eng[e]:
                need = {}
                for d in op.deps:
                    s = sem_of(d)
                    k = id(s)
                    if k not in need or need[k][1] < d.val:
                        need[k] = (s, d.val)
                for k, (s, v) in need.items():
                    if waited.get(k, 0) >= v:
                        continue
                    engine.wait_ge(s, v)
                    waited[k] = v
                ins = op.fn(engine)
                if op.is_dma:
                    ins.then_inc(self.chan_sem[op.chan], 16)
                elif op.flag:
                    ins.then_inc(eng_sem[op.eng], 1)
            for e2, (s, v) in final_eng.items():
                if waited.get(id(s), 0) < v:
                    engine.wait_ge(s, v)
            for c, (s, v) in final_chan.items():
                if v > 0 and waited.get(id(s), 0) < v:
                    engine.wait_ge(s, v)

        with nc.Block() as block:
            @block.tensor
            def _(eng):
                run("pe", eng)

            @block.scalar
            def _(eng):
                run("act", eng)

            @block.vector
            def _(eng):
                run("dve", eng)

            @block.gpsimd
            def _(eng):
                run("pool", eng)

            @block.sync
            def _(eng):
                run("sp", eng)
        self.total_ops += len(ops)
        for c in list(self.chan_sem):
            self.free_chan.append((self.chan_sem.pop(c), self.chan_cnt.pop(c)))
        self._reset()


class Rot:
    def __init__(self, items):
        self.items = items
        self.i = 0

    def next(self):
        it = self.items[self.i % len(self.items)]
        self.i += 1
        return it


def build_program(debug=False, stop_after=99):
    nc = bass.Bass("TRN2", target_bir_lowering=False)
    P = Prog(nc)

    def din(name, shape, dt=F32):
        return nc.dram_tensor(name, list(shape), dt, kind="ExternalInput").ap()

    def dscr(name, shape, dt=BF16):
        return nc.dram_tensor(name, list(shape), dt).ap()

    x = din("x", [NB * S, D])
    pos = din("pos", [NB, S], I32)
    cT = din("cT", [128, 8 * NB])
    normsT = din("normsT", [128, 24])
    ada_w = [din("a_ada_w", [D, 3 * D]), din("b_ada_w", [D, 3 * D])]
    ada_bT = din("ada_bT", [128, 48])
    ada_bg = din("ada_bg", [1, 2 * D])
    WA = din("WA", [D, WA_COLS])
    WB = din("WB", [D, WB_COLS])
    akvg = din("akvg", [1, 128])
    wuv = din("wuv", [128, 16 * 128])
    woa = din("woa", [2 * D, D])
    wob = din("wob", [D, D])
    fng = din("fng", [1, D])
    cst = din("cst", [128, 640 + 3 + NIT])
    out = nc.dram_tensor("out", [NB * S, D], F32, kind="ExternalOutput").ap()

    QL = dscr("QL", [NB, NQB, 128, 2048])
    GT = dscr("GT", [NB, NQB, 128, 2048])
    QR = dscr("QR", [NB, NQB, 2, 128, 128])
    QI = dscr("QI", [NB, NQB, 2, 128, 128])
    QI2 = dscr("QI2", [NB, NQB, 2, 128, 128])
    QR2 = dscr("QR2", [NB, NQB, 2, 128, 128])
    KR1 = dscr("KR1", [NB, 128, S]); KR2 = dscr("KR2", [NB, 128, S])
    KI1 = dscr("KI1", [NB, 128, S]); KI2 = dscr("KI2", [NB, 128, S])
    VA = dscr("VA", [NB, S, 128])
    KL = dscr("KL", [NB, 128, S])
    WI = dscr("WI", [NB, S, 8], F32)
    H1 = dscr("H1", [NB * S, D], F32)
    KT2 = dscr("KT2", [NB, 4, 2, 128, S])
    QT2 = dscr("QT2", [NB, 3, 4, 2, 128, S])
    GB = dscr("GB", [NB, 8, 128, S])
    VB = dscr("VB", [NB, S, D])
    YT = dscr("YT", [NB, 8, 128, S])

    dbg = {}
    if debug:
        for nm, shp, dt in (("d_QL", [128, 2048], BF16), ("d_H1", [NB * S, D], F32), ("d_mod", [128, 64], F32),
                            ("d_KL", [128, S], BF16), ("d_IS", [128, S], F32), ("d_lo", [128, 8], F32),
                            ("d_YT", [128, S], BF16), ("d_KT", [128, S], BF16)):
            dbg[nm] = nc.dram_tensor(nm, shp, dt, kind="ExternalOutput").ap()

    def sb(name, shape, dt=F32):
        return nc.alloc_sbuf_tensor(name, list(shape), dt)

    CST = sb("CST", [128, 640 + 3 + NIT])
    IDF = CST[:, 0:128]
    MPREV_F = CST[:, 128:256]
    MCUR_F = CST[:, 256:384]
    CAUS = CST[:, 384:512]
    INV = CST[:, 640:643]
    POW2 = CST[:, 643:643 + NIT]
    IDB = sb("IDB", [128, 128], BF16)
    ONESB = sb("ONESB", [128, 128], BF16)
    ONESF = sb("ONESF", [1, 128], F32)
    HALFPI = sb("HALFPI", [128, 1], F32)
    MASKB = sb("MASKB", [128, 256], BF16)
    NRM = sb("NRM", [128, 24])
    ASCL = [sb("ASCL%d" % l, [128, 8 * NB]) for l in range(2)]
    ASFT = [sb("ASFT%d" % l, [128, 8 * NB]) for l in range(2)]
    GBC = [sb("GBC%d" % l, [128, NB * D]) for l in range(2)]
    AKVG = sb("AKVG", [128, 128])
    FNG = sb("FNG", [128, D])

    ps = [nc.alloc_psum_tensor("ps%d" % i, [128, 512], F32) for i in range(8)]

    P.dma("sp", CST[:], cst, [], ["CST"], "l0")
    P.dma("sp", NRM[:], normsT, [], ["NRM"], "l1")
    P.dma("sp", AKVG[:], akvg.partition_broadcast(128), [], ["AKVG"], "l2")
    P.dma("sp", FNG[:], fng.partition_broadcast(128), [], ["FNG"], "l3")
    P.add("dve", lambda e: e.tensor_copy(out=IDB[:], in_=IDF), ["CST"], ["IDB"])
    P.add("dve", lambda e: e.memset(ONESB[:], 1.0), [], ["ONESB"])
    P.add("dve", lambda e: e.memset(ONESF[:], 1.0), [], ["ONESF"])
    P.add("dve", lambda e: e.memset(HALFPI[:], math.pi / 2), [], ["HALFPI"])
    P.add("dve", lambda e: e.tensor_copy(out=MASKB[:], in_=CST[:, 128:384]), ["CST"], ["MASKB"])

    with ExitStack() as es:
        W0 = es.enter_context(nc.sbuf_tensor("p0_w", [128, 8, 3 * D], F32))
        C0 = es.enter_context(nc.sbuf_tensor("p0_c", [128, 8 * NB], F32))
        SC0 = es.enter_context(nc.sbuf_tensor("p0_sc", [128, 8 * NB], F32))
        SCB = es.enter_context(nc.sbuf_tensor("p0_scb", [128, 8 * NB, 128], F32))
        BT0 = es.enter_context(nc.sbuf_tensor("p0_bT", [128, 48], F32))
        BG0 = es.enter_context(nc.sbuf_tensor("p0_bg", [128, 2 * D], F32))
        M0 = es.enter_context(nc.sbuf_tensor("p0_m", [128, 16 * NB], F32))
        P.dma("sp", C0[:], cT, [], ["C0"], "l4")
        P.dma("sp", BT0[:], ada_bT, [], ["BT0"], "l5")
        P.dma("sp", BG0[:], ada_bg.partition_broadcast(128), [], ["BG0"], "l6")
        P.add("act", lambda e: e.activation(out=SC0[:], in_=C0[:], func=AF.Silu), ["C0"], ["SC0"])
        P.add("dve", lambda e: e.tensor_copy(out=SCB[:], in_=SC0[:].unsqueeze(2).to_broadcast([128, 8 * NB, 128])),
              ["SC0"], ["SCB"])
        for l in range(2):
            for kc in range(8):
                P.dma("sp", W0[:, kc, :], ada_w[l][kc * 128:(kc + 1) * 128, :], [], [("W0", kc)], "w%d" % kc)
            mps = ps[0]
            for j in range(16):
                for kc in range(8):
                    P.add("pe", lambda e, j=j, kc=kc: e.matmul(
                        mps[:, j * NB:(j + 1) * NB], lhsT=W0[:, kc, j * 128:(j + 1) * 128],
                        rhs=SC0[:, kc * NB:(kc + 1) * NB], start=(kc == 0), stop=(kc == 7)),
                        [("W0", kc), "SC0"], ["psmps"])
            P.add("dve", lambda e, l=l: e.tensor_tensor(
                out=M0[:].rearrange("p (j b) -> p j b", b=NB), in0=mps[:, 0:16 * NB].rearrange("p (j b) -> p j b", b=NB),
                in1=BT0[:, l * 24:l * 24 + 16].unsqueeze(2).to_broadcast([128, 16, NB]), op=ALU.add),
                ["psmps", "BT0"], ["M0"])
            P.add("dve", lambda e, l=l: e.tensor_copy(out=ASFT[l][:], in_=M0[:, 0:8 * NB]), ["M0"], ["ASFT%d" % l])
            P.add("dve", lambda e, l=l: e.scalar_tensor_tensor(
                out=ASCL[l][:].rearrange("p (j b) -> p j b", b=NB), in0=M0[:, 8 * NB:16 * NB].rearrange("p (j b) -> p j b", b=NB),
                scalar=1.0, in1=NRM[:, l * 8:(l + 1) * 8].unsqueeze(2).to_broadcast([128, 8, NB]),
                op0=ALU.add, op1=ALU.mult), ["M0", "NRM"], ["ASCL%d" % l])
            for b in range(NB):
                for nh in range(2):
                    gps = ps[1 + (b * 2 + nh) % 2]
                    gkey = "psg%d" % ((b * 2 + nh) % 2)
                    for kc in range(8):
                        P.add("pe", lambda e, b=b, nh=nh, kc=kc, gps=gps: e.matmul(
                            gps[:], lhsT=SCB[:, kc * NB + b, :], rhs=W0[:, kc, 2 * D + nh * 512:2 * D + (nh + 1) * 512],
                            start=(kc == 0), stop=(kc == 7)), [("W0", kc), "SCB"], [gkey])
                    P.add("dve", lambda e, l=l, b=b, nh=nh, gps=gps: e.tensor_tensor(
                        out=GBC[l][:, b * D + nh * 512:b * D + (nh + 1) * 512], in0=gps[:],
                        in1=BG0[:, l * D + nh * 512:l * D + (nh + 1) * 512], op=ALU.add), [gkey, "BG0"], ["GBC%d" % l])
        if debug:
            P.dma("pool", dbg["d_mod"][:, 0:16], ASCL[0][:], ["ASCL0"], [], "s0")
            P.dma("pool", dbg["d_mod"][:, 16:32], ASFT[0][:], ["ASFT0"], [], "s0")
            P.dma("pool", dbg["d_mod"][:, 32:48], ASCL[1][:], ["ASCL1"], [], "s0")
            P.dma("pool", dbg["d_mod"][:, 48:64], GBC[0][:, 0:16], ["GBC0"], [], "s0")
        P.emit()
    if stop_after == 0:
        return nc, dbg, P

    def load_weights(Wd, Wsb, ncols, stg, wkey):
        npc = 4
        pw = ncols // npc
        engs = ("dve", "act")
        i = 0
        for kc in range(8):
            for pc in range(npc):
                st, sk = stg.next()
                P.dma("sp", st[:, 0:pw], Wd[kc * 128:(kc + 1) * 128, pc * pw:(pc + 1) * pw], [], [sk], "ws%d" % (i % 2))
                en = engs[i % 2]
                if en == "act":
                    P.add("act", lambda e, st=st, kc=kc, pc=pc: e.copy(out=Wsb[:, kc, pc * pw:(pc + 1) * pw], in_=st[:, 0:pw]),
                          [sk], [(wkey, kc, pc)])
                else:
                    P.add(en, lambda e, st=st, kc=kc, pc=pc: e.tensor_copy(out=Wsb[:, kc, pc * pw:(pc + 1) * pw], in_=st[:, 0:pw]),
                          [sk], [(wkey, kc, pc)])
                i += 1
        return [[(wkey, kc, pc) for pc in range(npc)] for kc in range(8)]

    def rope_tables(b, t0, specs, POSF, tabs, tmps):
        import os as _os
        lvl = int(_os.environ.get("K_TABLVL", 9))
        for (icol, ti) in specs:
            ct, st_, tk = tabs[ti]
            for which, tile_, off in ((0, ct, 0.25), (1, st_, 0.0)):
                ki, kf, kk = tmps.next()
                P.add("pool", lambda e, ki=ki, icol=icol, off=off: e.tensor_scalar(
                    out=ki[:], in0=POSF[:], scalar1=INV[:, icol:icol + 1], scalar2=off, op0=ALU.mult, op1=ALU.add),
                    ["POSF", "CST"], [kk + "i"])
                if lvl < 2:
                    continue
                P.add("pool", lambda e, ki=ki, kf=kf: e.tensor_copy(out=kf[:], in_=ki[:]), [kk + "i"], [kk + "f"])
                if lvl < 3:
                    continue
                P.add("dve", lambda e, kf=kf, icol=icol: e.scalar_tensor_tensor(
                    out=kf[:], in0=POSF[:], scalar=INV[:, icol:icol + 1], in1=kf[:], op0=ALU.mult, op1=ALU.subtract),
                    ["POSF", "CST", kk + "f"], [kk + "f"])
                if lvl < 4:
                    continue
                _sb = _os.environ.get("K_SINB", "")
                if _sb == "zero":
                    P.add("act", lambda e, kf=kf, tile_=tile_, off=off: e.activation(
                        out=tile_[:], in_=kf[:], func=AF.Sin, scale=2 * math.pi), [kk + "f"], [(tk, which)])
                elif _sb == "noscale":
                    P.add("act", lambda e, kf=kf, tile_=tile_, off=off: e.activation(
                        out=tile_[:], in_=kf[:], func=AF.Sin), [kk + "f"], [(tk, which)])
                elif off == 0.0:
                    P.add("act", lambda e, kf=kf, tile_=tile_, off=off: e.activation(
                        out=tile_[:], in_=kf[:], func=AF.Sin, scale=2 * math.pi), [kk + "f"], [(tk, which)])
                else:
                    P.add("act", lambda e, kf=kf, tile_=tile_, off=off: e.activation(
                        out=tile_[:], in_=kf[:], func=AF.Sin, scale=2 * math.pi, bias=HALFPI[:]),
                        [kk + "f", "HALFPI"], [(tk, which)])

    def norm_transpose(src_rows, XT, xkey, sub, SS, evacs, psT_pair, junk):
        import os as _os
        nlvl = int(_os.environ.get("K_NTLVL", 9))
        ssk = xkey + "ss"
        P.add("act", lambda e: e.activation(out=junk[:], in_=XT[:], func=AF.Square, accum_out=SS[:, 0:1]),
              [xkey], ["junk", ssk])
        P.add("act", lambda e: e.activation(out=SS[:, 1:2], in_=SS[:, 0:1], func=AF.Sqrt, scale=1.0 / D, bias=EPS),
              [ssk], [ssk + "b"])
        P.add("dve", lambda e: e.reciprocal(out=SS[:, 2:3], in_=SS[:, 1:2]), [ssk + "b"], [ssk + "c"])
        P.add("dve", lambda e: e.tensor_scalar(out=XT[:], in0=XT[:], scalar1=SS[:, 2:3], scalar2=None, op0=ALU.mult),
              [xkey, ssk + "c"], [xkey])
        if nlvl < 2:
            return
        for half in range(2):
            pt, pk = psT_pair[half]
            for q in range(4):
                kc = half * 4 + q
                P.add("pe", lambda e, pt=pt, q=q, kc=kc: e.transpose(
                    out=pt[:, q * 128:(q + 1) * 128], in_=XT[:, kc * 128:(kc + 1) * 128], identity=IDF),
                    [xkey, "CST"], [pk])
            for q in range(4):
                kc = half * 4 + q
                if nlvl >= 3:
                    evacs(kc, pt[:, q * 128:(q + 1) * 128], pk)

    with ExitStack() as es:
        W1 = es.enter_context(nc.sbuf_tensor("p1_w", [128, 8, WA_COLS], BF16))
        STG = es.enter_context(nc.sbuf_tensor("p1_stg", [128, 2, WA_COLS // 4], F32))
        X1 = es.enter_context(nc.sbuf_tensor("p1_x", [128, 2, D], F32))
        JUNK = es.enter_context(nc.sbuf_tensor("p1_junk", [128, D], BF16))
        SS1 = es.enter_context(nc.sbuf_tensor("p1_ss", [128, 2, 4], F32))
        HN = es.enter_context(nc.sbuf_tensor("p1_hn", [128, 2, 8, 512], BF16))
        POSI = es.enter_context(nc.sbuf_tensor("p1_posi", [128, 512], I32))
        POSF = es.enter_context(nc.sbuf_tensor("p1_posf", [128, 512], F32))
        TAB = es.enter_context(nc.sbuf_tensor("p1_tab", [128, 4, 512], F32))
        TKI = es.enter_context(nc.sbuf_tensor("p1_ki", [128, 1, 512], I32))
        TKF = es.enter_context(nc.sbuf_tensor("p1_kf", [128, 1, 512], F32))
        STQ = es.enter_context(nc.sbuf_tensor("p1_sq", [128, 2, 4, 8, 128], BF16))
        RO = es.enter_context(nc.sbuf_tensor("p1_ro", [128, 4, 512], BF16))
        RT = es.enter_context(nc.sbuf_tensor("p1_rt", [128, 4, 512], F32))
        VST = es.enter_context(nc.sbuf_tensor("p1_v", [128, 2, 4, 128], BF16))
        KLST = es.enter_context(nc.sbuf_tensor("p1_kl", [128, 2, 512], BF16))
        WIST = es.enter_context(nc.sbuf_tensor("p1_wi", [128, 2, 4, 8], F32))
        KVS = es.enter_context(nc.sbuf_tensor("p1_kv", [128, 8], F32))
        stg = Rot([(STG[:, i, :], "stg%d" % i) for i in range(2)])
        wkeys = load_weights(WA, W1, WA_COLS, stg, "W1")
        allw = [k for kc in range(8) for k in wkeys[kc]]
        tabs = [(TAB[:, 0, :], TAB[:, 1, :], "tab32"), (TAB[:, 2, :], TAB[:, 3, :], "tab64")]
        tmps = Rot([(TKI[:, i, :], TKF[:, i, :], "tk%d" % i) for i in range(1)])
        fm = Rot([(ps[i], "ps%d" % i) for i in range(4)])
        psT_pair = [(ps[4], "ps4"), (ps[5], "ps5")]
        psTok = (ps[6], "ps6")
        psVT = ps[7][:].bitcast(BF16)
        stq = Rot([(STQ[:, i], "stq%d" % i) for i in range(2)])
        ro = Rot([(RO[:, i, :], "ro%d" % i) for i in range(4)])
        rt = Rot([(RT[:, i, :], "rt%d" % i) for i in range(4)])
        import os as _os
        ntile = int(_os.environ.get('K_NT', NB * S // 512))
        _skip = _os.environ.get('K_SKIP', '')
        for tt in range(ntile):
            b = tt // 8
            t0 = (tt % 8) * 512
            qb0 = t0 // 128
            hn = HN[:, tt % 2]
            hk = "hn%d" % (tt % 2)
            P.dma("sp", POSI[:], pos[b:b + 1, t0:t0 + 512].partition_broadcast(128), [], ["POSI"], "POSI")
            P.add("pool", lambda e: e.tensor_copy(out=POSF[:], in_=POSI[:]), ["POSI"], ["POSF"])
            if 'tab' not in _skip:
                rope_tables(b, t0, [(0, 0), (1, 1)], POSF, tabs, tmps)
            vst = VST[:, tt % 2]; vk = "vst%d" % (tt % 2)
            klst = KLST[:, tt % 2, :]; klk = "klst%d" % (tt % 2)
            wist = WIST[:, tt % 2]; wik = "wist%d" % (tt % 2)
            for sub in range(4):
                xi = (tt * 4 + sub) % 2
                XT = X1[:, xi, :]
                xkey = "x1_%d" % xi
                r0 = b * S + t0 + sub * 128
                P.dma("sp", XT, x[r0:r0 + 128, :], [], [xkey], "lx%d" % xi)

                def evacs(kc, pap, pk, sub=sub, b=b, hn=hn, hk=hk):
                    dst = hn[:, kc, sub * 128:(sub + 1) * 128]
                    _ev = _os.environ.get('K_EV', '')
                    if (kc < 4 and _ev != 'dve') or _ev == 'act':
                        P.add("act", lambda e: e.activation(
                            out=dst, in_=pap, func=AF.Identity, scale=ASCL[0][:, kc * NB + b:kc * NB + b + 1],
                            bias=ASFT[0][:, kc * NB + b:kc * NB + b + 1]), [pk, "ASCL0", "ASFT0"], [(hk, kc)])
                    else:
                        P.add("dve", lambda e: e.tensor_scalar(
                            out=dst, in0=pap, scalar1=ASCL[0][:, kc * NB + b:kc * NB + b + 1],
                            scalar2=ASFT[0][:, kc * NB + b:kc * NB + b + 1], op0=ALU.mult, op1=ALU.add),
                            [pk, "ASCL0", "ASFT0"], [(hk, kc)])
                if 'nt' not in _skip:
                    norm_transpose(None, XT, xkey, sub, SS1[:, xi, :], evacs, psT_pair, JUNK)
            hkeys = [(hk, kc) for kc in range(8)]
            for sub in (range(0) if 'tok' in _skip else range(4)):
                pt, pk = psTok
                for kc in range(8):
                    P.add("pe", lambda e, kc=kc, sub=sub, pt=pt, hn=hn: e.matmul(
                        pt[:, 0:136], lhsT=hn[:, kc, sub * 128:(sub + 1) * 128], rhs=W1[:, kc, 5632:5768],
                        start=(kc == 0), stop=(kc == 7)), [(hk, kc)] + wkeys[kc], [pk])
                P.add("act", lambda e, pt=pt: e.activation(out=JUNK[:, 0:128], in_=pt[:, 0:128], func=AF.Square,
                                                           accum_out=KVS[:, 0:1]), [pk], ["junk", "kvs0"])
                P.add("act", lambda e: e.activation(out=KVS[:, 1:2], in_=KVS[:, 0:1], func=AF.Sqrt, scale=1.0 / 128, bias=EPS),
                      ["kvs0"], ["kvs1"])
                P.add("dve", lambda e: e.reciprocal(out=KVS[:, 2:3], in_=KVS[:, 1:2]), ["kvs1"], ["kvs2"])
                P.add("dve", lambda e, pt=pt, sub=sub, vst=vst: e.scalar_tensor_tensor(
                    out=vst[:, sub, :], in0=pt[:, 0:128], scalar=KVS[:, 2:3], in1=AKVG[:], op0=ALU.mult, op1=ALU.mult),
                    [pk, "kvs2", "AKVG"], [(vk, sub)])
                P.add("act", lambda e, pt=pt, sub=sub, wist=wist: e.mul(out=wist[:, sub, :], in_=pt[:, 128:136], mul=8.0 ** -0.5),
                      [pk], [(wik, sub)])
                P.add("pe", lambda e, sub=sub, vst=vst: e.transpose(out=psVT[:, sub * 128:(sub + 1) * 128], in_=vst[:, sub, :],
                                                                    identity=IDB[:]), [(vk, sub), "IDB"], ["psvt"])
            if 'tail' not in _skip:
                P.add("act", lambda e, klst=klst: e.copy(out=klst, in_=psVT[:, 0:512]), ["psvt"], [klk])
                P.dma("pool", VA[b, t0:t0 + 512, :].rearrange("(s p) c -> p s c", p=128), vst[:], [(vk, s_) for s_ in range(4)], [], vk)
                P.dma("pool", WI[b, t0:t0 + 512, :].rearrange("(s p) c -> p s c", p=128), wist[:], [(wik, s_) for s_ in range(4)], [], wik)
                P.dma("pool", KL[b, :, t0:t0 + 512], klst, [klk], [], klk)
            if debug and tt == 0:
                P.dma("pool", dbg["d_KL"][:, 0:512], klst, [klk], [], klk)

            def fm_chunk(ci, hn=hn, hk=hk):
                pt, pk = fm.next()
                for kc in range(8):
                    P.add("pe", lambda e, kc=kc, pt=pt, ci=ci, hn=hn: e.matmul(
                        pt[:], lhsT=W1[:, kc, ci * 128:(ci + 1) * 128], rhs=hn[:, kc, :], start=(kc == 0), stop=(kc == 7)),
                        [(hk, kc)] + wkeys[kc], [pk])
                return pt, pk
            for grp, func, dst in (() if 'fm' in _skip else ((0, None, QL), (1, AF.Silu, GT))):
              for hg in range(2):
                sq, sqk = stq.next()
                for hl in range(8):
                    h = hg * 8 + hl
                    pt, pk = fm_chunk(grp * 16 + h)
                    o = sq[:, :, hl, :]
                    i_ = pt[:].rearrange("p (a q) -> p a q", q=128)
                    if func is None:
                        if h % 2 == 0:
                            P.add("act", lambda e, o=o, i_=i_: e.copy(out=o, in_=i_), [pk], [(sqk, hl)])
                        else:
                            P.add("dve", lambda e, o=o, i_=i_: e.tensor_copy(out=o, in_=i_), [pk], [(sqk, hl)])
                    else:
                        P.add("act", lambda e, o=o, i_=i_: e.activation(out=o, in_=i_, func=AF.Silu), [pk], [(sqk, hl)])
                P.dma("pool", dst[b, qb0:qb0 + 4, :, hg * 1024:(hg + 1) * 1024].rearrange("a p f -> p a f"),
                      sq.rearrange("p a h q -> p a (h q)"), [(sqk, hl) for hl in range(8)], [], sqk)
                if debug and tt == 0 and grp == 0:
                    P.dma("pool", dbg["d_QL"][:, hg * 1024:(hg + 1) * 1024], sq[:, 0].rearrange("p h q -> p (h q)"),
                          [(sqk, hl) for hl in range(8)], [], sqk)
            pairs = [(32, 0, "qr", 0), (34, 0, "qr", 1), (36, 1, "qi", 0), (38, 1, "qi", 1), (40, 0, "kr", 0), (42, 1, "ki", 0)]
            for (c0, ti, kind, half) in ([] if 'rope' in _skip else pairs):
                ct, st_, tk = tabs[ti]
                p1, k1 = fm_chunk(c0)
                p2, k2 = fm_chunk(c0 + 1)
                ta, tak = rt.next(); tb, tbk = rt.next()
                oa, oak = ro.next(); ob, obk = ro.next()
                P.add("dve", lambda e, ta=ta, p1=p1, ct=ct: e.tensor_tensor(out=ta, in0=p1[:], in1=ct, op=ALU.mult), [k1, (tk, 0)], [tak])
                P.add("dve", lambda e, tb=tb, p2=p2, st_=st_: e.tensor_tensor(out=tb, in0=p2[:], in1=st_, op=ALU.mult), [k2, (tk, 1)], [tbk])
                P.add("dve", lambda e, oa=oa, ta=ta, tb=tb: e.tensor_tensor(out=oa, in0=ta, in1=tb, op=ALU.subtract), [tak, tbk], [oak])
                tc_, tck = rt.next(); td, tdk = rt.next()
                P.add("dve", lambda e, tc_=tc_, p2=p2, ct=ct: e.tensor_tensor(out=tc_, in0=p2[:], in1=ct, op=ALU.mult), [k2, (tk, 0)], [tck])
                P.add("dve", lambda e, td=td, p1=p1, st_=st_: e.tensor_tensor(out=td, in0=p1[:], in1=st_, op=ALU.mult), [k1, (tk, 1)], [tdk])
                P.add("dve", lambda e, ob=ob, tc_=tc_, td=td: e.tensor_tensor(out=ob, in0=tc_, in1=td, op=ALU.add), [tck, tdk], [obk])
                if kind == "qr":
                    P.dma("pool", QR[b, qb0:qb0 + 4, half].rearrange("a p q -> p a q"), oa.rearrange("p (a q) -> p a q", q=128), [oak], [], oak)
                    P.dma("pool", QR2[b, qb0:qb0 + 4, half].rearrange("a p q -> p a q"), ob.rearrange("p (a q) -> p a q", q=128), [obk], [], obk)
                elif kind == "qi":
                    P.dma("pool", QI[b, qb0:qb0 + 4, half].rearrange("a p q -> p a q"), oa.rearrange("p (a q) -> p a q", q=128), [oak], [], oak)
                    P.dma("pool", QI2[b, qb0:qb0 + 4, half].rearrange("a p q -> p a q"), ob.rearrange("p (a q) -> p a q", q=128), [obk], [], obk)
                elif kind == "kr":
                    P.dma("pool", KR1[b, :, t0:t0 + 512], oa, [oak], [], oak)
                    P.dma("pool", KR2[b, :, t0:t0 + 512], ob, [obk], [], obk)
                else:
                    P.dma("pool", KI1[b, :, t0:t0 + 512], oa, [oak], [], oak)
                    P.dma("pool", KI2[b, :, t0:t0 + 512], ob, [obk], [], obk)
        P.emit()

    if stop_after == 1:
        return nc, dbg, P

    import os as _os
    SCALE_A = (128 + 32) ** -0.5
    BIGM = 30000.0
    with ExitStack() as es:
        def T(name, shape, dt=F32):
            return es.enter_context(nc.sbuf_tensor(name, list(shape), dt))
        WOA = T("p2_woa", [128, 16, D], BF16)
        WUV = T("p2_wuv", [128, 16 * 128], BF16)
        KIs = T("p2_ki", [128, S], BF16)
        KLs = T("p2_kl", [128, S], BF16)
        KRs = T("p2_kr", [128, S], BF16)
        Vs = T("p2_v", [128, NQB, 128], BF16)
        QIq = T("p2_qi", [128, 2, 4, 128], BF16)
        WIq = T("p2_wi", [128, 2, 8], F32)
        WAB = T("p2_wab", [128, 2, 16], F32)
        QLq = T("p2_ql", [128, 2, 16, 128], BF16)
        QRq = T("p2_qr", [128, 2, 16, 128], BF16)
        GTq = T("p2_gt", [128, 16, 128], BF16)
        Xq = T("p2_x", [128, D], F32)
        IS = T("p2_is", [128, S], F32)
        MS = T("p2_ms", [128, 2, S], BF16)
        MT = T("p2_mt", [128, 2, NQB, 128], BF16)
        TMPI = T("p2_tmpi", [128, 3, 512], F32)
        PT = T("p2_pt", [128, 4, 512], BF16)
        RD = T("p2_rd", [128, 1024], F32)
        ON = T("p2_on", [128, 1024], BF16)
        Y = T("p2_y", [128, 16, 128], BF16)
        H1q = T("p2_h1", [128, D], F32)
        OC = T("p2_oc", [128, 2, 512], F32)
        TMPH = T("p2_tmph", [128, D], F32)
        BS = T("p2_bs", [128, 8 + NIT], F32)
        NEG30 = T("p2_neg", [128, 1], F32)

        P.op("dve", "memset", [], ["NEG30"], NEG30[:], -BIGM)
        P.op("dve", "memset", [], ["KRs"], KRs[:], 0.0)
        P.op("pool", "memset", [], ["QRq0", "QRq1"], QRq[:], 0.0)
        for h in range(16):
            st = IS[:, (h % 2) * 1024:(h % 2 + 1) * 1024]
            sk2 = [("IS", 2 * (h % 2)), ("IS", 2 * (h % 2) + 1)]
            P.dma("sp", st, woa[h * 128:(h + 1) * 128, :], [], sk2, "stg2_%d" % (h % 2))
            if h % 2 == 0:
                P.op("dve", "tensor_copy", sk2, [("WOA", h)], out=WOA[:, h, :], in_=st)
            else:
                P.op("act", "copy", sk2, [("WOA", h)], out=WOA[:, h, :], in_=st)
        for i in range(2):
            st = IS[:, i * 1024:(i + 1) * 1024]
            sk2 = [("IS", 2 * i), ("IS", 2 * i + 1)]
            P.dma("sp", st, wuv[:, i * 1024:(i + 1) * 1024], [], sk2, "stg2_%d" % i)
            P.op("dve", "tensor_copy", sk2, [("WUV", i)], out=WUV[:, i * 1024:(i + 1) * 1024], in_=st)
        woa_keys = [("WOA", h) for h in range(16)]
        wpool = Rot([(ps[i], "ps%d" % i) for i in range(4)])
        psO = [(ps[4], "ps4"), (ps[5], "ps5")]
        psD = [(ps[6], "ps6"), (ps[7], "ps7")]
        tmpi = Rot([(TMPI[:, i, :], "tmpi%d" % i) for i in range(3)])
        ptr = Rot([(PT[:, i, :], "pt%d" % i) for i in range(4)])
        ocr = Rot([(OC[:, i, :], "oc%d" % i) for i in range(2)])
        nblk = int(_os.environ.get("K_NBLK", NB * NQB))
        dbg_qb = int(_os.environ.get("K_DBGQB", 3))
        def idx_part(blk):
            b = blk // NQB
            qb = blk % NQB
            sl = blk % 2
            nk = (qb + 1) * 128
            nkc = qb + 1
            if qb == 0:
                ki1 = KI1[b].rearrange("(i r) t -> r i t", r=4)[0]
                ki2 = KI2[b].rearrange("(i r) t -> r i t", r=4)[0]
                P.dma("sp", KIs[0:32, :], ki1, [], ["KIs"], "KIs")
                P.dma("sp", KIs[32:64, :], ki2, [], ["KIs"], "KIs")
                P.dma("sp", KIs[64:96, :], ki1, [], ["KIs"], "KIs")
                P.dma("sp", KIs[96:128, :], ki2, [], ["KIs"], "KIs")
            qik = "QIq%d" % sl
            for h2 in range(2):
                for xp, src in ((0, QI), (1, QI2)):
                    for half in range(2):
                        sv = src[b, qb, half].rearrange("(i pp h2) q -> h2 i pp q", pp=2, h2=2)[h2]
                        dv = QIq[h2 * 64 + xp * 32:h2 * 64 + xp * 32 + 32, sl, half * 2:half * 2 + 2, :]
                        P.dma("sp", dv, sv, [], [qik], qik)
            P.dma("sp", WIq[:, sl, :], WI[b, qb * 128:(qb + 1) * 128, :], [], ["WIq%d" % sl], "WIq%d" % sl)
            P.dma("sp", QLq[:, sl].rearrange("p h q -> p (h q)"), QL[b, qb], [], ["QLq%d" % sl], "QLq%d" % sl)
            for half in range(2):
                P.dma("sp", QRq[0:16, sl, half * 8:(half + 1) * 8, :], QR[b, qb, half].rearrange("(i hh) q -> i hh q", hh=8),
                      [], ["QRq%d" % sl], "QRq%d" % sl)
                P.dma("sp", QRq[16:32, sl, half * 8:(half + 1) * 8, :], QR2[b, qb, half].rearrange("(i hh) q -> i hh q", hh=8),
                      [], ["QRq%d" % sl], "QRq%d" % sl)
            wabk = "WAB%d" % sl
            P.op("act", "activation", ["WIq%d" % sl], [wabk + "a"], out=WAB[:, sl, 0:8], in_=WIq[:, sl, :], func=AF.Abs)
            P.op("act", "activation", ["WIq%d" % sl], [wabk + "s"], out=WAB[:, sl, 8:16], in_=WIq[:, sl, :], func=AF.Sign)
            nc5 = (nk + 511) // 512
            iskeys = [("IS", c5) for c5 in range(nc5)]
            for c5 in range(nc5):
                w = min(512, nk - c5 * 512)
                cs = slice(c5 * 512, c5 * 512 + w)
                for h in range(8):
                    pt_, pk = wpool.next()
                    pb = (h % 2) * 64
                    P.op("pe", "matmul", [qik, "KIs"], [pk], pt_[:, 0:w], lhsT=QIq[pb:pb + 64, sl, h // 2, :], rhs=KIs[pb:pb + 64, cs],
                         start=True, stop=True)
                    tm, tmk = tmpi.next()
                    P.op("act", "activation", [pk, wabk + "a"], [tmk], out=tm[:, 0:w], in_=pt_[:, 0:w], func=AF.Relu, scale=WAB[:, sl, h:h + 1])
                    if h == 0:
                        P.op("dve", "tensor_scalar", [tmk, wabk + "s"], [("IS", c5)], out=IS[:, cs], in0=tm[:, 0:w],
                             scalar1=WAB[:, sl, 8:9], scalar2=None, op0=ALU.mult)
                    else:
                        P.op("dve", "scalar_tensor_tensor", [tmk, wabk + "s", ("IS", c5)], [("IS", c5)], out=IS[:, cs], in0=tm[:, 0:w],
                             scalar=WAB[:, sl, 8 + h:9 + h], in1=IS[:, cs], op0=ALU.mult, op1=ALU.add)
            P.op("dve", "tensor_reduce", iskeys, ["bs_mx"], out=BS[:, 0:1], in_=IS[:, 0:nk], axis=AX.X, op=ALU.max)
            P.op("dve", "tensor_reduce", iskeys, ["bs_mn"], out=BS[:, 1:2], in_=IS[:, 0:nk], axis=AX.X, op=ALU.min)
            P.op("dve", "tensor_tensor", ["bs_mx", "bs_mn"], ["bs_w0"], out=BS[:, 2:3], in0=BS[:, 0:1], in1=BS[:, 1:2], op=ALU.subtract)
            P.op("dve", "tensor_scalar", ["bs_w0", "CST"], ["bs_steps"], out=BS[:, 8:8 + NIT], in0=POW2, scalar1=BS[:, 2:3], scalar2=None, op0=ALU.mult)
            P.op("dve", "tensor_copy", ["bs_mn"], ["bs_lo"], out=BS[:, 3:4], in_=BS[:, 1:2])
            dk = ("IS", (nk - 128) // 512)
            P.op("dve", "tensor_tensor", [dk, "CST"], [dk], out=IS[:, nk - 128:nk], in0=IS[:, nk - 128:nk], in1=CAUS, op=ALU.add)
            for it in range(NIT):
                P.op("dve", "tensor_tensor", ["bs_lo", "bs_steps"], ["bs_mid"], out=BS[:, 4:5], in0=BS[:, 3:4], in1=BS[:, 8 + it:9 + it], op=ALU.add)
                P.op("dve", "tensor_scalar", iskeys + ["bs_mid"], ["MS%d" % sl, "bs_cnt"], out=MS[:, sl, 0:nk], in0=IS[:, 0:nk], scalar1=BS[:, 4:5],
                     scalar2=0.0, op0=ALU.is_ge, op1=ALU.add, accum_out=BS[:, 5:6])
                P.op("dve", "scalar_tensor_tensor", ["bs_cnt", "bs_steps"], ["bs_t"], out=BS[:, 6:7], in0=BS[:, 5:6], scalar=TOPK - 0.5,
                     in1=BS[:, 8 + it:9 + it], op0=ALU.is_ge, op1=ALU.mult)
                P.op("dve", "tensor_tensor", ["bs_lo", "bs_t"], ["bs_lo"], out=BS[:, 3:4], in0=BS[:, 3:4], in1=BS[:, 6:7], op=ALU.add)
            P.op("dve", "tensor_scalar", iskeys + ["bs_lo"], ["MS%d" % sl], out=MS[:, sl, 0:nk], in0=IS[:, 0:nk], scalar1=BS[:, 3:4], scalar2=None, op0=ALU.is_ge)
            if debug and b == 0 and qb == dbg_qb:
                P.dma("pool", dbg["d_IS"][:, 0:nk], IS[:, 0:nk], iskeys, [], "dbgis")
                P.dma("pool", dbg["d_lo"], BS[:, 0:8], ["bs_lo", "bs_cnt", "bs_mx", "bs_mn"], [], "dbglo")

        def att_part(blk):
            b = blk // NQB
            qb = blk % NQB
            sl = blk % 2
            nk = (qb + 1) * 128
            nkc = qb + 1
            r0 = b * S + qb * 128
            if qb == 0:
                P.dma("sp", KLs[:], KL[b], [], ["KLs"], "KLs")
                P.dma("sp", KRs[0:16, :], KR1[b].rearrange("(i r) t -> r i t", r=8)[0], [], ["KRs"], "KRs")
                P.dma("sp", KRs[16:32, :], KR2[b].rearrange("(i r) t -> r i t", r=8)[0], [], ["KRs"], "KRs")
                P.dma("sp", Vs[:], VA[b].rearrange("(c p) d -> p c d", p=128), [], ["Vs"], "Vs")
            P.dma("sp", GTq[:].rearrange("p h q -> p (h q)"), GT[b, qb], [], ["GTq"], "GTq")
            P.dma("sp", Xq[:], x[r0:r0 + 128, :], [], ["Xq"], "Xq")
            for kc0 in range(0, nkc, 4):
                n4 = min(4, nkc - kc0)
                pt_, pk = wpool.next()
                pv = pt_[:].bitcast(BF16)
                for j in range(n4):
                    kc = kc0 + j
                    P.op("pe", "transpose", ["MS%d" % sl, "IDB"], [pk], out=pv[:, j * 128:(j + 1) * 128], in_=MS[:, sl, kc * 128:(kc + 1) * 128], identity=IDB[:])
                P.op("act", "activation", [pk, "NEG30"], [("MT", sl, kc0 // 4)], out=MT[:, sl, kc0:kc0 + n4, :].rearrange("p c q -> p (c q)"),
                     in_=pv[:, 0:n4 * 128], func=AF.Identity, scale=BIGM, bias=NEG30[:])

            for hg in range(2):
                groups = [(kc, g) for kc in range(nkc) for g in range(2)]

                def qk(kc, g):
                    ks = slice(kc * 128, (kc + 1) * 128)
                    h0 = hg * 8 + g * 4
                    pt_, pk = wpool.next()
                    P.op("pe", "matmul", ["KLs", "QLq%d" % sl], [pk], pt_[:], lhsT=KLs[:, ks],
                         rhs=QLq[:, sl, h0:h0 + 4, :].rearrange("p h q -> p (h q)"), start=True, stop=False)
                    P.op("pe", "matmul", ["KRs", "QRq%d" % sl], [pk], pt_[:], lhsT=KRs[:, ks],
                         rhs=QRq[:, sl, h0:h0 + 4, :].rearrange("p h q -> p (h q)"), start=False, stop=False)
                    P.op("pe", "matmul", ["IDB", ("MT", sl, kc // 4)], [pk], pt_[:], lhsT=IDB[:],
                         rhs=MT[:, sl, kc, :].unsqueeze(1).to_broadcast([128, 4, 128]), start=False, stop=True)
                    return pt_, pk
                pend = [qk(*groups[0])]
                if len(groups) > 1:
                    pend.append(qk(*groups[1]))
                for gi, (kc, g) in enumerate(groups):
                    if gi + 2 < len(groups):
                        pend.append(qk(*groups[gi + 2]))
                    pt_, pk = pend.pop(0)
                    pr, prk = ptr.next()
                    P.op("act", "activation", [pk], [prk], out=pr, in_=pt_[:], func=AF.Exp, scale=SCALE_A)
                    P.op("pe", "matmul", ["Vs", prk], [psO[g][1]], psO[g][0][:], lhsT=Vs[:, kc, :], rhs=pr, start=(kc == 0), stop=(kc == nkc - 1))
                    P.op("pe", "matmul", ["ONESB", prk], [psD[g][1]], psD[g][0][:], lhsT=ONESB[:], rhs=pr, start=(kc == 0), stop=(kc == nkc - 1))
                for g in range(2):
                    h0 = hg * 8 + g * 4
                    gs = slice(g * 512, (g + 1) * 512)
                    oc, ock = ocr.next()
                    P.op("act", "activation", [psD[g][1]], [("RD", g)], out=RD[:, gs], in_=psD[g][0][:], func=AF.Ln)
                    P.op("act", "activation", [("RD", g)], [("RD", g)], out=RD[:, gs], in_=RD[:, gs], func=AF.Exp, scale=-1.0)
                    P.op("act", "copy", [psO[g][1]], [ock], out=oc, in_=psO[g][0][:])
                    P.op("pool", "tensor_tensor", [ock, ("RD", g)], [("ON", g)], out=ON[:, gs], in0=oc, in1=RD[:, gs], op=ALU.mult)
                    pt_, pk = wpool.next()
                    for hl in range(4):
                        h = h0 + hl
                        P.op("pe", "matmul", [("WUV", h // 8), ("ON", g)], [pk], pt_[:, hl * 128:(hl + 1) * 128], lhsT=WUV[:, h * 128:(h + 1) * 128],
                             rhs=ON[:, g * 512 + hl * 128:g * 512 + (hl + 1) * 128], start=True, stop=True)
                    oc2, ock2 = ocr.next()
                    P.op("act", "copy", [pk], [ock2], out=oc2, in_=pt_[:])
                    P.op("pool", "tensor_tensor", [ock2, "GTq"], [("Y", h0 // 4)], out=Y[:, h0:h0 + 4, :].rearrange("p h q -> p (h q)"),
                         in0=oc2, in1=GTq[:, h0:h0 + 4, :].rearrange("p h q -> p (h q)"), op=ALU.mult)
            ykeys = [("Y", i) for i in range(4)]
            for nh in range(2):
                pt_, pk = wpool.next()
                for h in range(16):
                    P.op("pe", "matmul", ykeys + [("WOA", h)], [pk], pt_[:], lhsT=Y[:, h, :], rhs=WOA[:, h, nh * 512:(nh + 1) * 512],
                         start=(h == 0), stop=(h == 15))
                P.op("act", "copy", [pk], [("TMPH", nh)], out=TMPH[:, nh * 512:(nh + 1) * 512], in_=pt_[:])
                P.op("pool", "tensor_tensor", [("TMPH", nh), "GBC0"], [("TMPH", nh)], out=TMPH[:, nh * 512:(nh + 1) * 512],
                     in0=TMPH[:, nh * 512:(nh + 1) * 512], in1=GBC[0][:, b * D + nh * 512:b * D + (nh + 1) * 512], op=ALU.mult)
            P.op("pool", "tensor_tensor", [("TMPH", 0), ("TMPH", 1), "Xq"], ["H1q"], out=H1q[:], in0=TMPH[:], in1=Xq[:], op=ALU.add)
            P.dma("pool", H1[r0:r0 + 128, :], H1q[:], ["H1q"], [], "H1q")
            if debug:
                P.dma("pool", dbg["d_H1"][r0:r0 + 128, :], H1q[:], ["H1q"], [], "H1q")

        idx_part(0)
        for blk in range(nblk):
            if blk + 1 < nblk:
                idx_part(blk + 1)
            att_part(blk)
        P.emit()

    if stop_after == 2:
        return nc, dbg, P

    with ExitStack() as es:
        def T(name, shape, dt=F32):
            return es.enter_context(nc.sbuf_tensor(name, list(shape), dt))
        W3 = T("p3_w", [128, 8, WB_COLS], BF16)
        STG3 = T("p3_stg", [128, 2, WB_COLS // 4], F32)
        X3 = T("p3_x", [128, 2, D], F32)
        JUNK3 = T("p3_junk", [128, D], BF16)
        SS3 = T("p3_ss", [128, 2, 4], F32)
        KVT = T("p3_kvt", [128, 2, 8, 512], BF16)
        HBT = T("p3_hbt", [128, 2, 8, 512], BF16)
        POSI3 = T("p3_posi", [128, 512], I32)
        POSF3 = T("p3_posf", [128, 512], F32)
        TAB3 = T("p3_tab", [128, 2, 512], F32)
        TKI3 = T("p3_ki", [128, 1, 512], I32)
        TKF3 = T("p3_kf", [128, 1, 512], F32)
        RT3 = T("p3_rt", [128, 4, 512], F32)
        RO3 = T("p3_ro", [128, 4, 512], BF16)
        VSTG = T("p3_vst", [128, 2, D], BF16)
        GST = T("p3_gst", [128, 2, 512], BF16)
        stg = Rot([(STG3[:, i, :], "stg3_%d" % i) for i in range(2)])
        wkeys = load_weights(WB, W3, WB_COLS, stg, "W3")
        tabs = [(TAB3[:, 0, :], TAB3[:, 1, :], "tab128")]
        tmps = Rot([(TKI3[:, 0, :], TKF3[:, 0, :], "tk3")])
        fm = Rot([(ps[i], "ps%d" % i) for i in range(4)])
        psT_pair = [(ps[4], "ps4"), (ps[5], "ps5")]
        vps = Rot([(ps[6], "ps6"), (ps[7], "ps7")])
        ro = Rot([(RO3[:, i, :], "ro3_%d" % i) for i in range(4)])
        rt = Rot([(RT3[:, i, :], "rt3_%d" % i) for i in range(4)])
        gst = Rot([(GST[:, i, :], "gst%d" % i) for i in range(2)])
        ntile3 = int(_os.environ.get("K_NT3", NB * S // 512))
        for tt in range(ntile3):
            b = tt // 8
            t0 = (tt % 8) * 512
            kvt = KVT[:, tt % 2]; kvk = "kvt%d" % (tt % 2)
            hbt = HBT[:, tt % 2]; hbk = "hbt%d" % (tt % 2)
            P.dma("sp", POSI3[:], pos[b:b + 1, t0:t0 + 512].partition_broadcast(128), [], ["POSI"], "POSI")
            P.op("pool", "tensor_copy", ["POSI"], ["POSF"], out=POSF3[:], in_=POSI3[:])
            rope_tables(b, t0, [(2, 0)], POSF3, tabs, tmps)
            for sub in range(4):
                xi = (tt * 4 + sub) % 2
                XT = X3[:, xi, :]
                xkey = "x3_%d" % xi
                r0 = b * S + t0 + sub * 128
                P.dma("sp", XT, H1[r0:r0 + 128, :], [], [xkey], xkey)

                def evacs(kc, pap, pk, sub=sub, b=b, kvt=kvt, hbt=hbt, kvk=kvk, hbk=hbk):
                    d1 = kvt[:, kc, sub * 128:(sub + 1) * 128]
                    d2 = hbt[:, kc, sub * 128:(sub + 1) * 128]
                    if kc < 4:
                        P.op("act", "activation", [pk, "NRM"], [(kvk, kc)], out=d1, in_=pap, func=AF.Copy, scale=NRM[:, 16 + kc:17 + kc])
                        P.op("act", "activation", [pk, "ASCL1", "ASFT1"], [(hbk, kc)], out=d2, in_=pap, func=AF.Identity,
                             scale=ASCL[1][:, kc * NB + b:kc * NB + b + 1], bias=ASFT[1][:, kc * NB + b:kc * NB + b + 1])
                    else:
                        P.op("dve", "tensor_scalar", [pk, "NRM"], [(kvk, kc)], out=d1, in0=pap, scalar1=NRM[:, 16 + kc:17 + kc], scalar2=None, op0=ALU.mult)
                        P.op("dve", "tensor_scalar", [pk, "ASCL1", "ASFT1"], [(hbk, kc)], out=d2, in0=pap,
                             scalar1=ASCL[1][:, kc * NB + b:kc * NB + b + 1], scalar2=ASFT[1][:, kc * NB + b:kc * NB + b + 1], op0=ALU.mult, op1=ALU.add)
                norm_transpose(None, XT, xkey, sub, SS3[:, xi, :], evacs, psT_pair, JUNK3)

            def fm3(ci, src, sk):
                pt_, pk = fm.next()
                for kc in range(8):
                    P.op("pe", "matmul", [(sk, kc)] + wkeys[kc], [pk], pt_[:], lhsT=W3[:, kc, ci * 128:(ci + 1) * 128], rhs=src[:, kc, :],
                         start=(kc == 0), stop=(kc == 7))
                return pt_, pk
            ct, st_, tk = tabs[0]
            pair_list = [(2 * j, kvt, kvk, KT2[b, j]) for j in range(4)]
            pair_list += [(8 + g * 8 + 2 * j, hbt, hbk, QT2[b, g, j]) for g in range(3) for j in range(4)]
            for (c0, src, sk, dst) in pair_list:
                p1, k1 = fm3(c0, src, sk)
                p2_, k2 = fm3(c0 + 1, src, sk)
                ta, tak = rt.next(); tb, tbk = rt.next()
                oa, oak = ro.next(); ob, obk = ro.next()
                P.op("dve", "tensor_tensor", [k1, (tk, 0)], [tak], out=ta, in0=p1[:], in1=ct, op=ALU.mult)
                P.op("dve", "tensor_tensor", [k2, (tk, 1)], [tbk], out=tb, in0=p2_[:], in1=st_, op=ALU.mult)
                P.op("pool", "tensor_tensor", [tak, tbk], [oak], out=oa, in0=ta, in1=tb, op=ALU.subtract)
                tc_, tck = rt.next(); td, tdk = rt.next()
                P.op("dve", "tensor_tensor", [k2, (tk, 0)], [tck], out=tc_, in0=p2_[:], in1=ct, op=ALU.mult)
                P.op("dve", "tensor_tensor", [k1, (tk, 1)], [tdk], out=td, in0=p1[:], in1=st_, op=ALU.mult)
                P.op("pool", "tensor_tensor", [tck, tdk], [obk], out=ob, in0=tc_, in1=td, op=ALU.add)
                P.dma("pool", dst[0, :, t0:t0 + 512], oa, [oak], [], oak)
                P.dma("pool", dst[1, :, t0:t0 + 512], ob, [obk], [], obk)
                if debug and tt == 0 and c0 == 0:
                    P.dma("pool", dbg["d_KT"][:, 0:512], oa, [oak], [], oak)
            for h in range(8):
                p1, k1 = fm3(32 + h, hbt, hbk)
                g_, gk = gst.next()
                P.op("act", "activation", [k1], [gk], out=g_, in_=p1[:], func=AF.Silu)
                P.dma("pool", GB[b, h, :, t0:t0 + 512], g_, [gk], [], gk)
            for sub in range(4):
                vs_ = VSTG[:, sub % 2, :]
                vk_ = "vstg%d" % (sub % 2)
                for nh in range(2):
                    pt_, pk = vps.next()
                    for kc in range(8):
                        P.op("pe", "matmul", [(kvk, kc)] + wkeys[kc], [pk], pt_[:], lhsT=kvt[:, kc, sub * 128:(sub + 1) * 128],
                             rhs=W3[:, kc, 5120 + nh * 512:5120 + (nh + 1) * 512], start=(kc == 0), stop=(kc == 7))
                    if nh == 0:
                        P.op("act", "copy", [pk], [(vk_, nh)], out=vs_[:, nh * 512:(nh + 1) * 512], in_=pt_[:])
                    else:
                        P.op("dve", "tensor_copy", [pk], [(vk_, nh)], out=vs_[:, nh * 512:(nh + 1) * 512], in_=pt_[:])
                r0 = t0 + sub * 128
                P.dma("pool", VB[b, r0:r0 + 128, :], vs_, [(vk_, 0), (vk_, 1)], [], vk_)
        P.emit()
    if stop_after == 3:
        return nc, dbg, P

    SCALE_B = 128 ** -0.5
    with ExitStack() as es:
        def T(name, shape, dt=F32):
            return es.enter_context(nc.sbuf_tensor(name, list(shape), dt))
        KTh = T("p4_k", [128, 2, S], BF16)
        QTh = T("p4_q", [128, 2, 3, S], BF16)
        Vh = T("p4_v", [128, 2, 3, NQB, 128], BF16)
        GBh = T("p4_g", [128, 2, S], BF16)
        OD = T("p4_od", [128, 2, S], F32)
        PTb = T("p4_pt", [128, 4, 256], BF16)
        YTo = T("p4_y", [128, S], BF16)
        sp_ = Rot([(ps[i], "ps%d" % i) for i in range(4)])
        op_ = Rot([(ps[i], "ps%d" % i) for i in range(4, 8)])
        ptb = Rot([(PTb[:, i, :], "ptb%d" % i) for i in range(4)])
        nhead4 = int(_os.environ.get("K_NH4", NB * 8))
        for idx in range(nhead4):
            b = idx // 8
            h = idx % 8
            sl = idx % 2
            j = h // 2
            hh = h % 2
            kk_ = "KTh%d" % sl; qk_ = "QTh%d" % sl; vk_ = "Vh%d" % sl; gk_ = "GBh%d" % sl
            for xx in range(2):
                P.dma("sp", KTh[xx * 64:(xx + 1) * 64, sl, :], KT2[b, j, xx, hh * 64:(hh + 1) * 64, :], [], [kk_], kk_)
                for g in range(3):
                    P.dma("sp", QTh[xx * 64:(xx + 1) * 64, sl, g, :], QT2[b, g, j, xx, hh * 64:(hh + 1) * 64, :], [], [qk_], qk_)
            for g, d in enumerate((1, 4, 16)):
                nch = NQB // d
                vv = VB[b, :, h * 128:(h + 1) * 128].rearrange("(c a r) f -> r a c f", a=128, r=d)
                for r in range(d):
                    P.dma("sp", Vh[:, sl, g, r * nch:(r + 1) * nch, :], vv[r], [], [vk_], vk_)
            P.dma("sp", GBh[:, sl, :], GB[b, h], [], [gk_], gk_)
            units = [(g, d, r, c) for g, d in enumerate((1, 4, 16)) for r in range(d) for c in range(NQB // d)]

            def qk4(g, d, r, c):
                qv = QTh[:, sl, g, :].rearrange("p (c a r) -> p r c a", a=128, r=d)
                kv_ = KTh[:, sl, :].rearrange("p (c a r) -> p r c a", a=128, r=d)
                pS, pSk = sp_.next()
                if c > 0:
                    P.op("pe", "matmul", [kk_, qk_], [pSk], pS[:, 0:128], lhsT=kv_[:, r, c - 1, :], rhs=qv[:, r, c, :], start=True, stop=True)
                P.op("pe", "matmul", [kk_, qk_], [pSk], pS[:, 128:256], lhsT=kv_[:, r, c, :], rhs=qv[:, r, c, :], start=True, stop=True)
                return pS, pSk
            pend = [qk4(*units[0]), qk4(*units[1])]
            for ui, (g, d, r, c) in enumerate(units):
                if ui + 2 < len(units):
                    pend.append(qk4(*units[ui + 2]))
                pS, pSk = pend.pop(0)
                nch = NQB // d
                ov = OD[:].rearrange("p x (c a r) -> p r c x a", a=128, r=d)
                ti = r * nch + c
                lo = 0 if c > 0 else 128
                pt_, ptk = ptb.next()
                P.op("act", "activation", [pSk], [ptk], out=pt_[:, lo:256], in_=pS[:, lo:256], func=AF.Exp, scale=SCALE_B)
                P.op("pool", "tensor_tensor", [ptk, "MASKB"], [ptk], out=pt_[:, lo:256], in0=pt_[:, lo:256], in1=MASKB[:, lo:256], op=ALU.mult)
                pO, pOk = op_.next()
                for half, lhs_of in ((0, lambda t: Vh[:, sl, g, t, :]), (1, lambda t: ONESB[:])):
                    oc = slice(half * 128, (half + 1) * 128)
                    if c > 0:
                        P.op("pe", "matmul", [vk_, "ONESB", ptk], [pOk], pO[:, oc], lhsT=lhs_of(ti - 1), rhs=pt_[:, 0:128], start=True, stop=False)
                        P.op("pe", "matmul", [vk_, "ONESB", ptk], [pOk], pO[:, oc], lhsT=lhs_of(ti), rhs=pt_[:, 128:256], start=False, stop=True)
                    else:
                        P.op("pe", "matmul", [vk_, "ONESB", ptk], [pOk], pO[:, oc], lhsT=lhs_of(ti), rhs=pt_[:, 128:256], start=True, stop=True)
                dst = ov[:, r, c, :, :]
                src = pO[:, 0:256].rearrange("p (x a) -> p x a", a=128)
                if g == 0:
                    P.op("act", "copy", [pOk], [("OD", c // 4)], out=dst, in_=src)
                else:
                    odk = [("OD", i) for i in range(8)]
                    P.op("dve", "tensor_tensor", [pOk] + odk, odk, out=dst, in0=src, in1=dst, op=ALU.add)
            odk = [("OD", i) for i in range(8)]
            P.op("dve", "reciprocal", odk, odk, out=OD[:, 1, :], in_=OD[:, 1, :])
            P.op("pool", "tensor_tensor", odk, odk, out=OD[:, 0, :], in0=OD[:, 0, :], in1=OD[:, 1, :], op=ALU.mult)
            P.op("dve", "tensor_tensor", odk + [gk_], ["YTo"], out=YTo[:], in0=OD[:, 0, :], in1=GBh[:, sl, :], op=ALU.mult)
            P.dma("pool", YT[b, h], YTo[:], ["YTo"], [], "YTo")
            if debug and idx == 0:
                P.dma("pool", dbg["d_YT"], YTo[:], ["YTo"], [], "YTo")
        P.emit()
    if stop_after == 4:
        return nc, dbg, P

    with ExitStack() as es:
        def T(name, shape, dt=F32):
            return es.enter_context(nc.sbuf_tensor(name, list(shape), dt))
        WOB = T("p5_w", [128, 8, D], BF16)
        STG5 = T("p5_stg", [128, 2, D], F32)
        Yq = T("p5_y", [128, 2, 8, 128], BF16)
        H1t = T("p5_h1", [128, 2, D], F32)
        TMP5 = T("p5_tmp", [128, D], F32)
        H2 = T("p5_h2", [128, D], F32)
        OUTt = T("p5_out", [128, 2, D], F32)
        JUNK5 = T("p5_junk", [128, D], BF16)
        SS5 = T("p5_ss", [128, 4], F32)
        for h in range(8):
            st = STG5[:, h % 2, :]
            P.dma("sp", st, wob[h * 128:(h + 1) * 128, :], [], ["stg5_%d" % (h % 2)], "stg5_%d" % (h % 2))
            if h % 2 == 0:
                P.op("dve", "tensor_copy", ["stg5_0"], [("WOB", h)], out=WOB[:, h, :], in_=st)
            else:
                P.op("act", "copy", ["stg5_1"], [("WOB", h)], out=WOB[:, h, :], in_=st)
        pp = Rot([(ps[i], "ps%d" % i) for i in range(4)])
        ntile5 = int(_os.environ.get("K_NT5", NB * NQB))
        for tt in range(ntile5):
            b = tt // NQB
            qb = tt % NQB
            sl = tt % 2
            r0 = b * S + qb * 128
            yk = "Yq%d" % sl; hk_ = "H1t%d" % sl; ok_ = "OUTt%d" % sl
            P.dma("sp", Yq[:, sl], YT[b, :, :, qb * 128:(qb + 1) * 128].rearrange("h p t -> p h t"), [], [yk], yk)
            P.dma("sp", H1t[:, sl, :], H1[r0:r0 + 128, :], [], [hk_], hk_)
            for nh in range(2):
                pt_, pk = pp.next()
                for h in range(8):
                    P.op("pe", "matmul", [yk, ("WOB", h)], [pk], pt_[:], lhsT=Yq[:, sl, h, :], rhs=WOB[:, h, nh * 512:(nh + 1) * 512],
                         start=(h == 0), stop=(h == 7))
                P.op("dve", "tensor_tensor", [pk, "GBC1"], [("TMP5", nh)], out=TMP5[:, nh * 512:(nh + 1) * 512], in0=pt_[:],
                     in1=GBC[1][:, b * D + nh * 512:b * D + (nh + 1) * 512], op=ALU.mult)
            P.op("pool", "tensor_tensor", [("TMP5", 0), ("TMP5", 1), hk_], ["H2"], out=H2[:], in0=TMP5[:], in1=H1t[:, sl, :], op=ALU.add)
            P.op("act", "activation", ["H2"], ["junk5", "ss5a"], out=JUNK5[:], in_=H2[:], func=AF.Square, accum_out=SS5[:, 0:1])
            P.op("act", "activation", ["ss5a"], ["ss5b"], out=SS5[:, 1:2], in_=SS5[:, 0:1], func=AF.Sqrt, scale=1.0 / D, bias=EPS)
            P.op("dve", "reciprocal", ["ss5b"], ["ss5c"], out=SS5[:, 2:3], in_=SS5[:, 1:2])
            P.op("dve", "scalar_tensor_tensor", ["H2", "ss5c", "FNG"], [ok_], out=OUTt[:, sl, :], in0=H2[:], scalar=SS5[:, 2:3], in1=FNG[:],
                 op0=ALU.mult, op1=ALU.mult)
            P.dma("pool", out[r0:r0 + 128, :], OUTt[:, sl, :], [ok_], [], ok_)
        P.emit()

    return nc, dbg, P


def _perm_a():
    idx = []
    idx += list(range(0, 2048))
    idx += list(range(2720, 4768))
    for hb in (0, 8):
        for part in (0, 16):
            idx += [2048 + (hb + (p % 8)) * 32 + part + p // 8 for p in range(128)]
    for hb in (0, 4):
        for part in (0, 32):
            idx += [4768 + (hb + (p % 4)) * 64 + part + p // 4 for p in range(128)]
    for part in (0, 16):
        idx += [2688 + part + p // 8 for p in range(128)]
    for part in (0, 32):
        idx += [5280 + part + p // 4 for p in range(128)]
    idx += list(range(2560, 2688))
    idx += list(range(5344, 5352))
    assert len(idx) == WA_COLS
    return np.array(idx)


def _consts():
    c = np.zeros((128, 640 + 3 + NIT), np.float32)
    c[:, 0:128] = np.eye(128, dtype=np.float32)
    a = np.arange(128)[:, None]
    q = np.arange(128)[None, :]
    c[:, 128:256] = (a >= q).astype(np.float32)
    c[:, 256:384] = (a <= q).astype(np.float32)
    qq = np.arange(128)[:, None]
    kk = np.arange(128)[None, :]
    c[:, 384:512] = np.where(kk <= qq, 0.0, NEG).astype(np.float32)
    p = np.arange(128)
    two_pi = 2 * math.pi
    inv32 = (np.float32(THETA) ** (-(np.arange(0, 32, 2, dtype=np.float32)) / np.float32(32))).astype(np.float32)
    inv64 = (np.float32(THETA) ** (-(np.arange(0, 64, 2, dtype=np.float32)) / np.float32(64))).astype(np.float32)
    inv128 = (np.float32(THETA) ** (-(np.arange(0, 128, 2, dtype=np.float32)) / np.float32(128))).astype(np.float32)
    c[:, 640] = inv32[p // 8].astype(np.float64) / two_pi
    c[:, 641] = inv64[p // 4].astype(np.float64) / two_pi
    c[:, 642] = inv128[p % 64].astype(np.float64) / two_pi
    c[:, 643:643 + NIT] = (0.5 ** np.arange(1, NIT + 1))[None, :]
    return c


def _fm(v, nch):
    return np.ascontiguousarray(np.asarray(v, np.float32).reshape(nch, 128).T)


def prepare_inputs(x, c, positions, a_norm, a_ada_w, a_ada_b, a_w_in, a_kv_norm, a_w_uv, a_w_out,
                   kv_norm, w_kv, b_norm, b_ada_w, b_ada_b, b_w_in, b_w_out, final_norm):
    f = lambda a: np.ascontiguousarray(np.asarray(a, np.float32))
    x = f(x); c = f(c)
    positions = np.ascontiguousarray(np.asarray(positions, np.int32))
    WA = np.ascontiguousarray(f(a_w_in)[0][:, _perm_a()])
    wkv = f(w_kv); bw = f(b_w_in)[0]
    cols = []
    def pair_cols(base):
        out_ = []
        for j in range(4):
            for part in (0, 64):
                out_.append([base + (2 * j + p // 64) * 128 + part + (p % 64) for p in range(128)])
        return out_
    kcols = np.concatenate([wkv[:, ci] for ci in pair_cols(0)], axis=1)
    qcols = np.concatenate([bw[:, ci] for g in range(3) for ci in pair_cols(g * 1024)], axis=1)
    WB = np.ascontiguousarray(np.concatenate([kcols, qcols, bw[:, 3072:4096], wkv[:, 1024:2048]], axis=1))
    assert WB.shape[1] == WB_COLS
    normsT = np.concatenate([_fm(f(a_norm)[0], 8), _fm(f(b_norm)[0], 8), _fm(f(kv_norm), 8)], axis=1)
    ada_bT = np.concatenate([_fm(f(a_ada_b)[0], 24), _fm(f(b_ada_b)[0], 24)], axis=1)
    ada_bg = np.concatenate([f(a_ada_b)[0][2 * D:], f(b_ada_b)[0][2 * D:]])[None, :]
    wuv = np.ascontiguousarray(f(a_w_uv)[0].transpose(1, 0, 2).reshape(128, 16 * 128))
    shared = {
        "normsT": np.ascontiguousarray(normsT), "a_ada_w": f(a_ada_w)[0], "b_ada_w": f(b_ada_w)[0],
        "ada_bT": np.ascontiguousarray(ada_bT), "ada_bg": np.ascontiguousarray(ada_bg),
        "WA": WA, "WB": WB, "akvg": f(a_kv_norm)[0][None, :], "wuv": wuv, "woa": f(a_w_out)[0], "wob": f(b_w_out)[0],
        "fng": f(final_norm)[None, :], "cst": _consts(),
    }
    in_maps = []
    for core in range(NCORES):
        bs = slice(core * NB, (core + 1) * NB)
        cc = c[bs]
        cTl = np.ascontiguousarray(cc.reshape(NB, 8, 128).transpose(2, 1, 0).reshape(128, 8 * NB))
        m = dict(shared)
        m["x"] = np.ascontiguousarray(x[bs].reshape(NB * S, D))
        m["pos"] = np.ascontiguousarray(positions[bs])
        m["cT"] = cTl
        in_maps.append(m)
    return in_maps


def kernel(**inputs):
    in_maps = prepare_inputs(**inputs)
    nc, _, _ = build_program(debug=False)
    res = run_bass_kernel_spmd(nc, in_maps, core_ids=list(range(NCORES)))
    outs = [np.asarray(r["out"]).reshape(NB, S, D) for r in res.results]
    return np.concatenate(outs, axis=0).astype(np.float32)
```

```python
import math
from contextlib import ExitStack
import numpy as np
import concourse.bass as bass
import concourse.mybir as mybir
from concourse.bass_utils import run_bass_kernel_spmd

F32 = mybir.dt.float32
BF16 = mybir.dt.bfloat16
I32 = mybir.dt.int32
AF = mybir.ActivationFunctionType
ALU = mybir.AluOpType
AX = mybir.AxisListType

ENGS = ("pe", "act", "dve", "pool", "sp")

D = 1024
S = 4096
NB = 2
NCORES = 8
NQB = S // 128
EPS = 1e-6
THETA = 10000.0
NIT = 16
TOPK = 256
NEG = -1.0e30
WA_COLS = 5768
WB_COLS = 6144


class Op:
    __slots__ = ("eng", "fn", "deps", "is_dma", "chan", "val", "flag")

    def __init__(self, eng, fn, is_dma=False, chan=None):
        self.eng = eng
        self.fn = fn
        self.deps = ()
        self.is_dma = is_dma
        self.chan = chan
        self.val = 0
        self.flag = False


class Prog:
    def __init__(self, nc):
        self.nc = nc
        self.chan_sem = {}
        self.chan_cnt = {}
        self.free_chan = []
        self.eng_sem = {}
        self.eng_cnt = {e: 0 for e in ENGS}
        self.phase_no = 0
        self.total_ops = 0
        self._reset()

    def _reset(self):
        self.ops = []
        self.last_w = {}
        self.readers = {}

    def add(self, eng, fn, reads=(), writes=(), chan=None):
        is_dma = chan is not None
        op = Op(eng, fn, is_dma, chan)
        psr = [k for k in reads if isinstance(k, str) and k.startswith("ps")]
        if psr:
            reads = [k for k in reads if k not in psr]
            writes = list(writes) + [k for k in psr if k not in writes]
        deps = {}
        for k in reads:
            w = self.last_w.get(k)
            if w is not None:
                deps[id(w)] = w
        for k in writes:
            w = self.last_w.get(k)
            if w is not None:
                deps[id(w)] = w
            for r in self.readers.get(k, ()):
                deps[id(r)] = r
        dl = []
        for d in deps.values():
            if d is op:
                continue
            if (not is_dma) and eng == "pe" and d.eng == "pe" and not d.is_dma:
                continue
            dl.append(d)
        op.deps = dl
        for k in reads:
            self.readers.setdefault(k, []).append(op)
        for k in writes:
            self.last_w[k] = op
            self.readers[k] = []
        self.ops.append(op)
        return op

    def op(self, eng, meth, reads, writes, *args, **kw):
        return self.add(eng, lambda e: getattr(e, meth)(*args, **kw), reads, writes)

    def dma(self, q, out, in_, reads, writes, chan):
        return self.add(q, lambda e: e.dma_start(out=out, in_=in_), reads, writes, chan=chan)

    def emit(self):
        nc = self.nc
        self.phase_no += 1
        ops = self.ops
        for op in ops:
            for d in op.deps:
                d.flag = True
        per_eng = {e: [] for e in ENGS}
        for op in ops:
            per_eng[op.eng].append(op)
        last_compute = {}
        for e in ENGS:
            for op in reversed(per_eng[e]):
                if not op.is_dma:
                    op.flag = True
                    last_compute[e] = op
                    break
        for e in last_compute:
            if e not in self.eng_sem:
                self.eng_sem[e] = nc.alloc_semaphore("eng_%s" % e)
        eng_sem = self.eng_sem
        cnt = self.eng_cnt
        for op in ops:
            if op.is_dma:
                if op.chan not in self.chan_sem:
                    if self.free_chan:
                        self.chan_sem[op.chan], self.chan_cnt[op.chan] = self.free_chan.pop()
                    else:
                        self.chan_sem[op.chan] = nc.alloc_semaphore("ch%d" % len(self.chan_sem))
                        self.chan_cnt[op.chan] = 0
                self.chan_cnt[op.chan] += 16
                op.val = self.chan_cnt[op.chan]
            elif op.flag:
                cnt[op.eng] += 1
                op.val = cnt[op.eng]
        final_eng = {e: (eng_sem[e], last_compute[e].val) for e in last_compute}
        final_chan = {c: (self.chan_sem[c], self.chan_cnt[c]) for c in self.chan_sem}

        def sem_of(d):
            return self.chan_sem[d.chan] if d.is_dma else eng_sem[d.eng]

        def run(e, engine):
            waited = {}
            for op in per_eng[e]:
                need = {}
                for d in op.deps:
                    s = sem_of(d)
                    k = id(s)
                    if k not in need or need[k][1] < d.val:
                        need[k] = (s, d.val)
                for k, (s, v) in need.items():
                    if waited.get(k, 0) >= v:
                        continue
                    engine.wait_ge(s, v)
                    waited[k] = v
                ins = op.fn(engine)
                if op.is_dma:
                    ins.then_inc(self.chan_sem[op.chan], 16)
                elif op.flag:
                    ins.then_inc(eng_sem[op.eng], 1)
            for e2, (s, v) in final_eng.items():
                if waited.get(id(s), 0) < v:
                    engine.wait_ge(s, v)
            for c, (s, v) in final_chan.items():
                if v > 0 and waited.get(id(s), 0) < v:
                    engine.wait_ge(s, v)

        with nc.Block() as block:
            @block.tensor
            def _(eng):
                run("pe", eng)

            @block.scalar
            def _(eng):
                run("act", eng)

            @block.vector
            def _(eng):
                run("dve", eng)

            @block.gpsimd
            def _(eng):
                run("pool", eng)

            @block.sync
            def _(eng):
                run("sp", eng)
        self.total_ops += len(ops)
        for c in list(self.chan_sem):
            self.free_chan.append((self.chan_sem.pop(c), self.chan_cnt.pop(c)))
        self._reset()


class Rot:
    def __init__(self, items):
        self.items = items
        self.i = 0

    def next(self):
        it = self.items[self.i % len(self.items)]
        self.i += 1
        return it


def build_program(debug=False, stop_after=99):
    nc = bass.Bass("TRN2", target_bir_lowering=False)
    P = Prog(nc)

    def din(name, shape, dt=F32):
        return nc.dram_tensor(name, list(shape), dt, kind="ExternalInput").ap()

    def dscr(name, shape, dt=BF16):
        return nc.dram_tensor(name, list(shape), dt).ap()

    x = din("x", [NB * S, D])
    pos = din("pos", [NB, S], I32)
    cT = din("cT", [128, 8 * NB])
    normsT = din("normsT", [128, 24])
    ada_w = [din("a_ada_w", [D, 3 * D]), din("b_ada_w", [D, 3 * D])]
    ada_bT = din("ada_bT", [128, 48])
    ada_bg = din("ada_bg", [1, 2 * D])
    WA = din("WA", [D, WA_COLS])
    WB = din("WB", [D, WB_COLS])
    akvg = din("akvg", [1, 128])
    wuv = din("wuv", [128, 16 * 128])
    woa = din("woa", [2 * D, D])
    wob = din("wob", [D, D])
    fng = din("fng", [1, D])
    cst = din("cst", [128, 640 + 4 + NIT])
    out = nc.dram_tensor("out", [NB * S, D], F32, kind="ExternalOutput").ap()

    QL = dscr("QL", [NB, NQB, 128, 2048])
    GT = dscr("GT", [NB, NQB, 128, 2048])
    QR = dscr("QR", [NB, NQB, 2, 128, 128])
    QI = dscr("QI", [NB, NQB, 2, 128, 128])
    QI2 = dscr("QI2", [NB, NQB, 2, 128, 128])
    QR2 = dscr("QR2", [NB, NQB, 2, 128, 128])
    KR1 = dscr("KR1", [NB, 128, S]); KR2 = dscr("KR2", [NB, 128, S])
    KI1 = dscr("KI1", [NB, 128, S]); KI2 = dscr("KI2", [NB, 128, S])
    VA = dscr("VA", [NB, S, 128])
    KL = dscr("KL", [NB, 128, S])
    WI = dscr("WI", [NB, S, 8], F32)
    H1 = dscr("H1", [NB * S, D], F32)
    KT2 = dscr("KT2", [NB, 4, 2, 128, S])
    QT2 = dscr("QT2", [NB, 3, 4, 2, 128, S])
    GB = dscr("GB", [NB, 8, 128, S])
    VB = dscr("VB", [NB, S, D])
    YT = dscr("YT", [NB, 8, 128, S])

    dbg = {}
    if debug:
        for nm, shp, dt in (("d_QL", [128, 2048], BF16), ("d_H1", [NB * S, D], F32), ("d_mod", [128, 64], F32),
                            ("d_KL", [128, S], BF16), ("d_IS", [128, S], F32), ("d_lo", [128, 8], F32),
                            ("d_YT", [128, S], BF16), ("d_KT", [128, S], BF16)):
            dbg[nm] = nc.dram_tensor(nm, shp, dt, kind="ExternalOutput").ap()

    def sb(name, shape, dt=F32):
        return nc.alloc_sbuf_tensor(name, list(shape), dt)

    CST = sb("CST", [128, 640 + 4 + NIT])
    IDF = CST[:, 0:128]
    MPREV_F = CST[:, 128:256]
    MCUR_F = CST[:, 256:384]
    CAUS = CST[:, 384:512]
    INV = CST[:, 640:643]
    POW2 = CST[:, 643:644 + NIT]
    IDB = sb("IDB", [128, 128], BF16)
    ONESB = sb("ONESB", [128, 128], BF16)
    ONESF = sb("ONESF", [1, 128], F32)
    HALFPI = sb("HALFPI", [128, 1], F32)
    MASKB = sb("MASKB", [128, 256], BF16)
    NRM = sb("NRM", [128, 24])
    ASCL = [sb("ASCL%d" % l, [128, 8 * NB]) for l in range(2)]
    ASFT = [sb("ASFT%d" % l, [128, 8 * NB]) for l in range(2)]
    GBC = [sb("GBC%d" % l, [128, NB * D]) for l in range(2)]
    AKVG = sb("AKVG", [128, 128])
    FNG = sb("FNG", [128, D])

    ps = [nc.alloc_psum_tensor("ps%d" % i, [128, 512], F32) for i in range(8)]

    P.dma("sp", CST[:], cst, [], ["CST"], "l0")
    P.dma("sp", NRM[:], normsT, [], ["NRM"], "l1")
    P.dma("sp", AKVG[:], akvg.partition_broadcast(128), [], ["AKVG"], "l2")
    P.dma("sp", FNG[:], fng.partition_broadcast(128), [], ["FNG"], "l3")
    P.add("dve", lambda e: e.tensor_copy(out=IDB[:], in_=IDF), ["CST"], ["IDB"])
    P.add("dve", lambda e: e.memset(ONESB[:], 1.0), [], ["ONESB"])
    P.add("dve", lambda e: e.memset(ONESF[:], 1.0), [], ["ONESF"])
    P.add("dve", lambda e: e.memset(HALFPI[:], math.pi / 2), [], ["HALFPI"])
    P.add("dve", lambda e: e.tensor_copy(out=MASKB[:], in_=CST[:, 128:384]), ["CST"], ["MASKB"])

    with ExitStack() as es:
        W0 = es.enter_context(nc.sbuf_tensor("p0_w", [128, 8, 3 * D], F32))
        C0 = es.enter_context(nc.sbuf_tensor("p0_c", [128, 8 * NB], F32))
        SC0 = es.enter_context(nc.sbuf_tensor("p0_sc", [128, 8 * NB], F32))
        SCB = es.enter_context(nc.sbuf_tensor("p0_scb", [128, 8 * NB, 128], F32))
        BT0 = es.enter_context(nc.sbuf_tensor("p0_bT", [128, 48], F32))
        BG0 = es.enter_context(nc.sbuf_tensor("p0_bg", [128, 2 * D], F32))
        M0 = es.enter_context(nc.sbuf_tensor("p0_m", [128, 16 * NB], F32))
        P.dma("sp", C0[:], cT, [], ["C0"], "l4")
        P.dma("sp", BT0[:], ada_bT, [], ["BT0"], "l5")
        P.dma("sp", BG0[:], ada_bg.partition_broadcast(128), [], ["BG0"], "l6")
        P.add("act", lambda e: e.activation(out=SC0[:], in_=C0[:], func=AF.Silu), ["C0"], ["SC0"])
        P.add("dve", lambda e: e.tensor_copy(out=SCB[:], in_=SC0[:].unsqueeze(2).to_broadcast([128, 8 * NB, 128])),
              ["SC0"], ["SCB"])
        for l in range(2):
            for kc in range(8):
                P.dma("sp", W0[:, kc, :], ada_w[l][kc * 128:(kc + 1) * 128, :], [], [("W0", kc)], "w%d" % kc)
            mps = ps[0]
            for j in range(16):
                for kc in range(8):
                    P.add("pe", lambda e, j=j, kc=kc: e.matmul(
                        mps[:, j * NB:(j + 1) * NB], lhsT=W0[:, kc, j * 128:(j + 1) * 128],
                        rhs=SC0[:, kc * NB:(kc + 1) * NB], start=(kc == 0), stop=(kc == 7)),
                        [("W0", kc), "SC0"], ["psmps"])
            P.add("dve", lambda e, l=l: e.tensor_tensor(
                out=M0[:].rearrange("p (j b) -> p j b", b=NB), in0=mps[:, 0:16 * NB].rearrange("p (j b) -> p j b", b=NB),
                in1=BT0[:, l * 24:l * 24 + 16].unsqueeze(2).to_broadcast([128, 16, NB]), op=ALU.add),
                ["psmps", "BT0"], ["M0"])
            P.add("dve", lambda e, l=l: e.tensor_copy(out=ASFT[l][:], in_=M0[:, 0:8 * NB]), ["M0"], ["ASFT%d" % l])
            P.add("dve", lambda e, l=l: e.scalar_tensor_tensor(
                out=ASCL[l][:].rearrange("p (j b) -> p j b", b=NB), in0=M0[:, 8 * NB:16 * NB].rearrange("p (j b) -> p j b", b=NB),
                scalar=1.0, in1=NRM[:, l * 8:(l + 1) * 8].unsqueeze(2).to_broadcast([128, 8, NB]),
                op0=ALU.add, op1=ALU.mult), ["M0", "NRM"], ["ASCL%d" % l])
            for b in range(NB):
                for nh in range(2):
                    gps = ps[1 + (b * 2 + nh) % 2]
                    gkey = "psg%d" % ((b * 2 + nh) % 2)
                    for kc in range(8):
                        P.add("pe", lambda e, b=b, nh=nh, kc=kc, gps=gps: e.matmul(
                            gps[:], lhsT=SCB[:, kc * NB + b, :], rhs=W0[:, kc, 2 * D + nh * 512:2 * D + (nh + 1) * 512],
                            start=(kc == 0), stop=(kc == 7)), [("W0", kc), "SCB"], [gkey])
                    P.add("dve", lambda e, l=l, b=b, nh=nh, gps=gps: e.tensor_tensor(
                        out=GBC[l][:, b * D + nh * 512:b * D + (nh + 1) * 512], in0=gps[:],
                        in1=BG0[:, l * D + nh * 512:l * D + (nh + 1) * 512], op=ALU.add), [gkey, "BG0"], ["GBC%d" % l])
        if debug:
            P.dma("pool", dbg["d_mod"][:, 0:16], ASCL[0][:], ["ASCL0"], [], "s0")
            P.dma("pool", dbg["d_mod"][:, 16:32], ASFT[0][:], ["ASFT0"], [], "s0")
            P.dma("pool", dbg["d_mod"][:, 32:48], ASCL[1][:], ["ASCL1"], [], "s0")
            P.dma("pool", dbg["d_mod"][:, 48:64], GBC[0][:, 0:16], ["GBC0"], [], "s0")
        P.emit()
    if stop_after == 0:
        return nc, dbg, P

    def load_weights(Wd, Wsb, ncols, stg, wkey):
        npc = 4
        pw = ncols // npc
        engs = ("dve", "act")
        i = 0
        for kc in range(8):
            for pc in range(npc):
                st, sk = stg.next()
                P.dma("sp", st[:, 0:pw], Wd[kc * 128:(kc + 1) * 128, pc * pw:(pc + 1) * pw], [], [sk], "ws%d" % (i % 2))
                en = engs[i % 2]
                if en == "act":
                    P.add("act", lambda e, st=st, kc=kc, pc=pc: e.copy(out=Wsb[:, kc, pc * pw:(pc + 1) * pw], in_=st[:, 0:pw]),
                          [sk], [(wkey, kc, pc)])
                else:
                    P.add(en, lambda e, st=st, kc=kc, pc=pc: e.tensor_copy(out=Wsb[:, kc, pc * pw:(pc + 1) * pw], in_=st[:, 0:pw]),
                          [sk], [(wkey, kc, pc)])
                i += 1
        return [[(wkey, kc, pc) for pc in range(npc)] for kc in range(8)]

    def rope_tables(b, t0, specs, POSF, tabs, tmps):
        import os as _os
        lvl = int(_os.environ.get("K_TABLVL", 9))
        for (icol, ti) in specs:
            ct, st_, tk = tabs[ti]
            for which, tile_, off in ((0, ct, 0.25), (1, st_, 0.0)):
                ki, kf, kk = tmps.next()
                P.add("pool", lambda e, ki=ki, icol=icol, off=off: e.tensor_scalar(
                    out=ki[:], in0=POSF[:], scalar1=INV[:, icol:icol + 1], scalar2=off, op0=ALU.mult, op1=ALU.add),
                    ["POSF", "CST"], [kk + "i"])
                if lvl < 2:
                    continue
                P.add("pool", lambda e, ki=ki, kf=kf: e.tensor_copy(out=kf[:], in_=ki[:]), [kk + "i"], [kk + "f"])
                if lvl < 3:
                    continue
                P.add("dve", lambda e, kf=kf, icol=icol: e.scalar_tensor_tensor(
                    out=kf[:], in0=POSF[:], scalar=INV[:, icol:icol + 1], in1=kf[:], op0=ALU.mult, op1=ALU.subtract),
                    ["POSF", "CST", kk + "f"], [kk + "f"])
                if lvl < 4:
                    continue
                _sb = _os.environ.get("K_SINB", "")
                if _sb == "zero":
                    P.add("act", lambda e, kf=kf, tile_=tile_, off=off: e.activation(
                        out=tile_[:], in_=kf[:], func=AF.Sin, scale=2 * math.pi), [kk + "f"], [(tk, which)])
                elif _sb == "noscale":
                    P.add("act", lambda e, kf=kf, tile_=tile_, off=off: e.activation(
                        out=tile_[:], in_=kf[:], func=AF.Sin), [kk + "f"], [(tk, which)])
                elif off == 0.0:
                    P.add("act", lambda e, kf=kf, tile_=tile_, off=off: e.activation(
                        out=tile_[:], in_=kf[:], func=AF.Sin, scale=2 * math.pi), [kk + "f"], [(tk, which)])
                else:
                    P.add("act", lambda e, kf=kf, tile_=tile_, off=off: e.activation(
                        out=tile_[:], in_=kf[:], func=AF.Sin, scale=2 * math.pi, bias=HALFPI[:]),
                        [kk + "f", "HALFPI"], [(tk, which)])

    def norm_transpose(src_rows, XT, xkey, sub, SS, evacs, psT_pair, junk):
        import os as _os
        nlvl = int(_os.environ.get("K_NTLVL", 9))
        ssk = xkey + "ss"
        P.add("act", lambda e: e.activation(out=junk[:], in_=XT[:], func=AF.Square, accum_out=SS[:, 0:1]),
              [xkey], ["junk", ssk])
        P.add("act", lambda e: e.activation(out=SS[:, 1:2], in_=SS[:, 0:1], func=AF.Sqrt, scale=1.0 / D, bias=EPS),
              [ssk], [ssk + "b"])
        P.add("dve", lambda e: e.reciprocal(out=SS[:, 2:3], in_=SS[:, 1:2]), [ssk + "b"], [ssk + "c"])
        P.add("dve", lambda e: e.tensor_scalar(out=XT[:], in0=XT[:], scalar1=SS[:, 2:3], scalar2=None, op0=ALU.mult),
              [xkey, ssk + "c"], [xkey])
        if nlvl < 2:
            return
        for half in range(2):
            pt, pk = psT_pair[half]
            for q in range(4):
                kc = half * 4 + q
                P.add("pe", lambda e, pt=pt, q=q, kc=kc: e.transpose(
                    out=pt[:, q * 128:(q + 1) * 128], in_=XT[:, kc * 128:(kc + 1) * 128], identity=IDF),
                    [xkey, "CST"], [pk])
            for q in range(4):
                kc = half * 4 + q
                if nlvl >= 3:
                    evacs(kc, pt[:, q * 128:(q + 1) * 128], pk)

    with ExitStack() as es:
        W1 = es.enter_context(nc.sbuf_tensor("p1_w", [128, 8, WA_COLS], BF16))
        STG = es.enter_context(nc.sbuf_tensor("p1_stg", [128, 2, WA_COLS // 4], F32))
        X1 = es.enter_context(nc.sbuf_tensor("p1_x", [128, 2, D], F32))
        JUNK = es.enter_context(nc.sbuf_tensor("p1_junk", [128, D], BF16))
        SS1 = es.enter_context(nc.sbuf_tensor("p1_ss", [128, 2, 4], F32))
        HN = es.enter_context(nc.sbuf_tensor("p1_hn", [128, 2, 8, 512], BF16))
        POSI = es.enter_context(nc.sbuf_tensor("p1_posi", [128, 512], I32))
        POSF = es.enter_context(nc.sbuf_tensor("p1_posf", [128, 512], F32))
        TAB = es.enter_context(nc.sbuf_tensor("p1_tab", [128, 4, 512], F32))
        TKI = es.enter_context(nc.sbuf_tensor("p1_ki", [128, 1, 512], I32))
        TKF = es.enter_context(nc.sbuf_tensor("p1_kf", [128, 1, 512], F32))
        STQ = es.enter_context(nc.sbuf_tensor("p1_sq", [128, 2, 4, 8, 128], BF16))
        RO = es.enter_context(nc.sbuf_tensor("p1_ro", [128, 4, 512], BF16))
        RT = es.enter_context(nc.sbuf_tensor("p1_rt", [128, 4, 512], F32))
        VST = es.enter_context(nc.sbuf_tensor("p1_v", [128, 2, 4, 128], BF16))
        KLST = es.enter_context(nc.sbuf_tensor("p1_kl", [128, 2, 512], BF16))
        WIST = es.enter_context(nc.sbuf_tensor("p1_wi", [128, 2, 4, 8], F32))
        KVS = es.enter_context(nc.sbuf_tensor("p1_kv", [128, 8], F32))
        stg = Rot([(STG[:, i, :], "stg%d" % i) for i in range(2)])
        wkeys = load_weights(WA, W1, WA_COLS, stg, "W1")
        allw = [k for kc in range(8) for k in wkeys[kc]]
        tabs = [(TAB[:, 0, :], TAB[:, 1, :], "tab32"), (TAB[:, 2, :], TAB[:, 3, :], "tab64")]
        tmps = Rot([(TKI[:, i, :], TKF[:, i, :], "tk%d" % i) for i in range(1)])
        fm = Rot([(ps[i], "ps%d" % i) for i in range(4)])
        psT_pair = [(ps[4], "ps4"), (ps[5], "ps5")]
        psTok = (ps[6], "ps6")
        psVT = ps[7][:].bitcast(BF16)
        stq = Rot([(STQ[:, i], "stq%d" % i) for i in range(2)])
        ro = Rot([(RO[:, i, :], "ro%d" % i) for i in range(4)])
        rt = Rot([(RT[:, i, :], "rt%d" % i) for i in range(4)])
        import os as _os
        ntile = int(_os.environ.get('K_NT', NB * S // 512))
        _skip = _os.environ.get('K_SKIP', '')
        for tt in range(ntile):
            b = tt // 8
            t0 = (tt % 8) * 512
            qb0 = t0 // 128
            hn = HN[:, tt % 2]
            hk = "hn%d" % (tt % 2)
            P.dma("sp", POSI[:], pos[b:b + 1, t0:t0 + 512].partition_broadcast(128), [], ["POSI"], "POSI")
            P.add("pool", lambda e: e.tensor_copy(out=POSF[:], in_=POSI[:]), ["POSI"], ["POSF"])
            if 'tab' not in _skip:
                rope_tables(b, t0, [(0, 0), (1, 1)], POSF, tabs, tmps)
            vst = VST[:, tt % 2]; vk = "vst%d" % (tt % 2)
            klst = KLST[:, tt % 2, :]; klk = "klst%d" % (tt % 2)
            wist = WIST[:, tt % 2]; wik = "wist%d" % (tt % 2)
            for sub in range(4):
                xi = (tt * 4 + sub) % 2
                XT = X1[:, xi, :]
                xkey = "x1_%d" % xi
                r0 = b * S + t0 + sub * 128
                P.dma("sp", XT, x[r0:r0 + 128, :], [], [xkey], "lx%d" % xi)

                def evacs(kc, pap, pk, sub=sub, b=b, hn=hn, hk=hk):
                    dst = hn[:, kc, sub * 128:(sub + 1) * 128]
                    _ev = _os.environ.get('K_EV', '')
                    if (kc < 4 and _ev != 'dve') or _ev == 'act':
                        P.add("act", lambda e: e.activation(
                            out=dst, in_=pap, func=AF.Identity, scale=ASCL[0][:, kc * NB + b:kc * NB + b + 1],
                            bias=ASFT[0][:, kc * NB + b:kc * NB + b + 1]), [pk, "ASCL0", "ASFT0"], [(hk, kc)])
                    else:
                        P.add("dve", lambda e: e.tensor_scalar(
                            out=dst, in0=pap, scalar1=ASCL[0][:, kc * NB + b:kc * NB + b + 1],
                            scalar2=ASFT[0][:, kc * NB + b:kc * NB + b + 1], op0=ALU.mult, op1=ALU.add),
                            [pk, "ASCL0", "ASFT0"], [(hk, kc)])
                if 'nt' not in _skip:
                    norm_transpose(None, XT, xkey, sub, SS1[:, xi, :], evacs, psT_pair, JUNK)
            hkeys = [(hk, kc) for kc in range(8)]
            for sub in (range(0) if 'tok' in _skip else range(4)):
                pt, pk = psTok
                for kc in range(8):
                    P.add("pe", lambda e, kc=kc, sub=sub, pt=pt, hn=hn: e.matmul(
                        pt[:, 0:136], lhsT=hn[:, kc, sub * 128:(sub + 1) * 128], rhs=W1[:, kc, 5632:5768],
                        start=(kc == 0), stop=(kc == 7)), [(hk, kc)] + wkeys[kc], [pk])
                P.add("act", lambda e, pt=pt: e.activation(out=JUNK[:, 0:128], in_=pt[:, 0:128], func=AF.Square,
                                                           accum_out=KVS[:, 0:1]), [pk], ["junk", "kvs0"])
                P.add("act", lambda e: e.activation(out=KVS[:, 1:2], in_=KVS[:, 0:1], func=AF.Sqrt, scale=1.0 / 128, bias=EPS),
                      ["kvs0"], ["kvs1"])
                P.add("dve", lambda e: e.reciprocal(out=KVS[:, 2:3], in_=KVS[:, 1:2]), ["kvs1"], ["kvs2"])
                P.add("dve", lambda e, pt=pt, sub=sub, vst=vst: e.scalar_tensor_tensor(
                    out=vst[:, sub, :], in0=pt[:, 0:128], scalar=KVS[:, 2:3], in1=AKVG[:], op0=ALU.mult, op1=ALU.mult),
                    [pk, "kvs2", "AKVG"], [(vk, sub)])
                P.add("act", lambda e, pt=pt, sub=sub, wist=wist: e.mul(out=wist[:, sub, :], in_=pt[:, 128:136], mul=8.0 ** -0.5),
                      [pk], [(wik, sub)])
                P.add("pe", lambda e, sub=sub, vst=vst: e.transpose(out=psVT[:, sub * 128:(sub + 1) * 128], in_=vst[:, sub, :],
                                                                    identity=IDB[:]), [(vk, sub), "IDB"], ["psvt"])
            if 'tail' not in _skip:
                P.add("act", lambda e, klst=klst: e.copy(out=klst, in_=psVT[:, 0:512]), ["psvt"], [klk])
                P.dma("pool", VA[b, t0:t0 + 512, :].rearrange("(s p) c -> p s c", p=128), vst[:], [(vk, s_) for s_ in range(4)], [], vk)
                P.dma("pool", WI[b, t0:t0 + 512, :].rearrange("(s p) c -> p s c", p=128), wist[:], [(wik, s_) for s_ in range(4)], [], wik)
                P.dma("pool", KL[b, :, t0:t0 + 512], klst, [klk], [], klk)
            if debug and tt == 0:
                P.dma("pool", dbg["d_KL"][:, 0:512], klst, [klk], [], klk)

            def fm_chunk(ci, hn=hn, hk=hk):
                pt, pk = fm.next()
                for kc in range(8):
                    P.add("pe", lambda e, kc=kc, pt=pt, ci=ci, hn=hn: e.matmul(
                        pt[:], lhsT=W1[:, kc, ci * 128:(ci + 1) * 128], rhs=hn[:, kc, :], start=(kc == 0), stop=(kc == 7)),
                        [(hk, kc)] + wkeys[kc], [pk])
                return pt, pk
            for grp, func, dst in (() if 'fm' in _skip else ((0, None, QL), (1, AF.Silu, GT))):
              for hg in range(2):
                sq, sqk = stq.next()
                for hl in range(8):
                    h = hg * 8 + hl
                    pt, pk = fm_chunk(grp * 16 + h)
                    o = sq[:, :, hl, :]
                    i_ = pt[:].rearrange("p (a q) -> p a q", q=128)
                    if func is None:
                        if h % 2 == 0:
                            P.add("act", lambda e, o=o, i_=i_: e.copy(out=o, in_=i_), [pk], [(sqk, hl)])
                        else:
                            P.add("dve", lambda e, o=o, i_=i_: e.tensor_copy(out=o, in_=i_), [pk], [(sqk, hl)])
                    else:
                        P.add("act", lambda e, o=o, i_=i_: e.activation(out=o, in_=i_, func=AF.Silu), [pk], [(sqk, hl)])
                P.dma("pool", dst[b, qb0:qb0 + 4, :, hg * 1024:(hg + 1) * 1024].rearrange("a p f -> p a f"),
                      sq.rearrange("p a h q -> p a (h q)"), [(sqk, hl) for hl in range(8)], [], sqk)
                if debug and tt == 0 and grp == 0:
                    P.dma("pool", dbg["d_QL"][:, hg * 1024:(hg + 1) * 1024], sq[:, 0].rearrange("p h q -> p (h q)"),
                          [(sqk, hl) for hl in range(8)], [], sqk)
            pairs = [(32, 0, "qr", 0), (34, 0, "qr", 1), (36, 1, "qi", 0), (38, 1, "qi", 1), (40, 0, "kr", 0), (42, 1, "ki", 0)]
            for (c0, ti, kind, half) in ([] if 'rope' in _skip else pairs):
                ct, st_, tk = tabs[ti]
                p1, k1 = fm_chunk(c0)
                p2, k2 = fm_chunk(c0 + 1)
                ta, tak = rt.next(); tb, tbk = rt.next()
                oa, oak = ro.next(); ob, obk = ro.next()
                P.add("dve", lambda e, ta=ta, p1=p1, ct=ct: e.tensor_tensor(out=ta, in0=p1[:], in1=ct, op=ALU.mult), [k1, (tk, 0)], [tak])
                P.add("dve", lambda e, tb=tb, p2=p2, st_=st_: e.tensor_tensor(out=tb, in0=p2[:], in1=st_, op=ALU.mult), [k2, (tk, 1)], [tbk])
                P.add("dve", lambda e, oa=oa, ta=ta, tb=tb: e.tensor_tensor(out=oa, in0=ta, in1=tb, op=ALU.subtract), [tak, tbk], [oak])
                tc_, tck = rt.next(); td, tdk = rt.next()
                P.add("dve", lambda e, tc_=tc_, p2=p2, ct=ct: e.tensor_tensor(out=tc_, in0=p2[:], in1=ct, op=ALU.mult), [k2, (tk, 0)], [tck])
                P.add("dve", lambda e, td=td, p1=p1, st_=st_: e.tensor_tensor(out=td, in0=p1[:], in1=st_, op=ALU.mult), [k1, (tk, 1)], [tdk])
                P.add("dve", lambda e, ob=ob, tc_=tc_, td=td: e.tensor_tensor(out=ob, in0=tc_, in1=td, op=ALU.add), [tck, tdk], [obk])
                if kind == "qr":
                    P.dma("pool", QR[b, qb0:qb0 + 4, half].rearrange("a p q -> p a q"), oa.rearrange("p (a q) -> p a q", q=128), [oak], [], oak)
                    P.dma("pool", QR2[b, qb0:qb0 + 4, half].rearrange("a p q -> p a q"), ob.rearrange("p (a q) -> p a q", q=128), [obk], [], obk)
                elif kind == "qi":
                    P.dma("pool", QI[b, qb0:qb0 + 4, half].rearrange("a p q -> p a q"), oa.rearrange("p (a q) -> p a q", q=128), [oak], [], oak)
                    P.dma("pool", QI2[b, qb0:qb0 + 4, half].rearrange("a p q -> p a q"), ob.rearrange("p (a q) -> p a q", q=128), [obk], [], obk)
                elif kind == "kr":
                    P.dma("pool", KR1[b, :, t0:t0 + 512], oa, [oak], [], oak)
                    P.dma("pool", KR2[b, :, t0:t0 + 512], ob, [obk], [], obk)
                else:
                    P.dma("pool", KI1[b, :, t0:t0 + 512], oa, [oak], [], oak)
                    P.dma("pool", KI2[b, :, t0:t0 + 512], ob, [obk], [], obk)
        P.emit()

    if stop_after == 1:
        return nc, dbg, P

    import os as _os
    SCALE_A = (128 + 32) ** -0.5
    BIGM = 30000.0
    with ExitStack() as es:
        def T(name, shape, dt=F32):
            return es.enter_context(nc.sbuf_tensor(name, list(shape), dt))
        WOA = T("p2_woa", [128, 16, D], BF16)
        WUV = T("p2_wuv", [128, 16 * 128], BF16)
        KIs = T("p2_ki", [128, S], BF16)
        KLs = T("p2_kl", [128, S], BF16)
        KRs = T("p2_kr", [128, S], BF16)
        Vs = T("p2_v", [128, NQB, 128], BF16)
        QIq = T("p2_qi", [128, 2, 4, 128], BF16)
        WIq = T("p2_wi", [128, 2, 8], F32)
        WAB = T("p2_wab", [128, 2, 16], F32)
        QLq = T("p2_ql", [128, 2, 16, 128], BF16)
        QRq = T("p2_qr", [128, 2, 16, 128], BF16)
        GTq = T("p2_gt", [128, 16, 128], BF16)
        Xq = T("p2_x", [128, D], F32)
        IS = T("p2_is", [128, S], F32)
        MS = T("p2_ms", [128, 2, S], BF16)
        MT = T("p2_mt", [128, 2, NQB, 128], BF16)
        TMPI = T("p2_tmpi", [128, 3, 512], F32)
        PT = T("p2_pt", [128, 4, 512], BF16)
        RD = T("p2_rd", [128, 1024], F32)
        ON = T("p2_on", [128, 1024], BF16)
        Y = T("p2_y", [128, 16, 128], BF16)
        H1q = T("p2_h1", [128, D], F32)
        OC = T("p2_oc", [128, 2, 512], F32)
        TMPH = T("p2_tmph", [128, D], F32)
        BS = T("p2_bs", [128, 9 + NIT], F32)
        NEG30 = T("p2_neg", [128, 1], F32)
        TS = T("p2_ts", [128, NIT], F32)

        P.op("dve", "memset", [], ["NEG30"], NEG30[:], -BIGM)
        P.op("dve", "memset", [], ["KRs"], KRs[:], 0.0)
        P.op("pool", "memset", [], ["QRq0", "QRq1"], QRq[:], 0.0)
        for h in range(16):
            st = IS[:, (h % 2) * 1024:(h % 2 + 1) * 1024]
            sk2 = [("IS", 2 * (h % 2)), ("IS", 2 * (h % 2) + 1)]
            P.dma("sp", st, woa[h * 128:(h + 1) * 128, :], [], sk2, "stg2_%d" % (h % 2))
            if h % 2 == 0:
                P.op("dve", "tensor_copy", sk2, [("WOA", h)], out=WOA[:, h, :], in_=st)
            else:
                P.op("act", "copy", sk2, [("WOA", h)], out=WOA[:, h, :], in_=st)
        for i in range(2):
            st = IS[:, i * 1024:(i + 1) * 1024]
            sk2 = [("IS", 2 * i), ("IS", 2 * i + 1)]
            P.dma("sp", st, wuv[:, i * 1024:(i + 1) * 1024], [], sk2, "stg2_%d" % i)
            P.op("dve", "tensor_copy", sk2, [("WUV", i)], out=WUV[:, i * 1024:(i + 1) * 1024], in_=st)
        woa_keys = [("WOA", h) for h in range(16)]
        wpool = Rot([(ps[i], "ps%d" % i) for i in range(4)])
        psO = [(ps[4], "ps4"), (ps[5], "ps5")]
        psD = [(ps[6], "ps6"), (ps[7], "ps7")]
        tmpi = Rot([(TMPI[:, i, :], "tmpi%d" % i) for i in range(3)])
        ptr = Rot([(PT[:, i, :], "pt%d" % i) for i in range(4)])
        ocr = Rot([(OC[:, i, :], "oc%d" % i) for i in range(2)])
        nblk = int(_os.environ.get("K_NBLK", NB * NQB))
        dbg_qb = int(_os.environ.get("K_DBGQB", 3))
        def idx_part(blk):
            b = blk // NQB
            qb = blk % NQB
            sl = blk % 2
            nk = (qb + 1) * 128
            nkc = qb + 1
            if qb == 0:
                ki1 = KI1[b].rearrange("(i r) t -> r i t", r=4)[0]
                ki2 = KI2[b].rearrange("(i r) t -> r i t", r=4)[0]
                P.dma("sp", KIs[0:32, :], ki1, [], ["KIs"], "KIs")
                P.dma("sp", KIs[32:64, :], ki2, [], ["KIs"], "KIs")
                P.dma("sp", KIs[64:96, :], ki1, [], ["KIs"], "KIs")
                P.dma("sp", KIs[96:128, :], ki2, [], ["KIs"], "KIs")
            qik = "QIq%d" % sl
            for h2 in range(2):
                for xp, src in ((0, QI), (1, QI2)):
                    for half in range(2):
                        sv = src[b, qb, half].rearrange("(i pp h2) q -> h2 i pp q", pp=2, h2=2)[h2]
                        dv = QIq[h2 * 64 + xp * 32:h2 * 64 + xp * 32 + 32, sl, half * 2:half * 2 + 2, :]
                        P.dma("sp", dv, sv, [], [qik], qik)
            P.dma("sp", WIq[:, sl, :], WI[b, qb * 128:(qb + 1) * 128, :], [], ["WIq%d" % sl], "WIq%d" % sl)
            P.dma("sp", QLq[:, sl].rearrange("p h q -> p (h q)"), QL[b, qb], [], ["QLq%d" % sl], "QLq%d" % sl)
            for half in range(2):
                P.dma("sp", QRq[0:16, sl, half * 8:(half + 1) * 8, :], QR[b, qb, half].rearrange("(i hh) q -> i hh q", hh=8),
                      [], ["QRq%d" % sl], "QRq%d" % sl)
                P.dma("sp", QRq[16:32, sl, half * 8:(half + 1) * 8, :], QR2[b, qb, half].rearrange("(i hh) q -> i hh q", hh=8),
                      [], ["QRq%d" % sl], "QRq%d" % sl)
            wabk = "WAB%d" % sl
            P.op("act", "activation", ["WIq%d" % sl], [wabk + "a"], out=WAB[:, sl, 0:8], in_=WIq[:, sl, :], func=AF.Abs)
            P.op("act", "activation", ["WIq%d" % sl], [wabk + "s"], out=WAB[:, sl, 8:16], in_=WIq[:, sl, :], func=AF.Sign)
            nc5 = (nk + 511) // 512
            iskeys = [("IS", c5) for c5 in range(nc5)]
            for c5 in range(nc5):
                w = min(512, nk - c5 * 512)
                cs = slice(c5 * 512, c5 * 512 + w)
                for h in range(8):
                    pt_, pk = wpool.next()
                    pb = (h % 2) * 64
                    P.op("pe", "matmul", [qik, "KIs"], [pk], pt_[:, 0:w], lhsT=QIq[pb:pb + 64, sl, h // 2, :], rhs=KIs[pb:pb + 64, cs],
                         start=True, stop=True)
                    tm, tmk = tmpi.next()
                    P.op("act", "activation", [pk, wabk + "a"], [tmk], out=tm[:, 0:w], in_=pt_[:, 0:w], func=AF.Relu, scale=WAB[:, sl, h:h + 1])
                    if h == 0:
                        P.op("dve", "tensor_scalar", [tmk, wabk + "s"], [("IS", c5)], out=IS[:, cs], in0=tm[:, 0:w],
                             scalar1=WAB[:, sl, 8:9], scalar2=None, op0=ALU.mult)
                    else:
                        P.op("dve", "scalar_tensor_tensor", [tmk, wabk + "s", ("IS", c5)], [("IS", c5)], out=IS[:, cs], in0=tm[:, 0:w],
                             scalar=WAB[:, sl, 8 + h:9 + h], in1=IS[:, cs], op0=ALU.mult, op1=ALU.add)
            P.op("dve", "tensor_reduce", iskeys, ["bs_mx"], out=BS[:, 0:1], in_=IS[:, 0:nk], axis=AX.X, op=ALU.max)
            P.op("dve", "tensor_reduce", iskeys, ["bs_mn"], out=BS[:, 1:2], in_=IS[:, 0:nk], axis=AX.X, op=ALU.min)
            P.op("dve", "tensor_tensor", ["bs_mx", "bs_mn"], ["bs_w0"], out=BS[:, 2:3], in0=BS[:, 0:1], in1=BS[:, 1:2], op=ALU.subtract)
            P.op("dve", "tensor_scalar", ["bs_w0", "CST"], ["bs_steps"], out=BS[:, 8:9 + NIT], in0=POW2, scalar1=BS[:, 2:3], scalar2=None, op0=ALU.mult)
            P.op("dve", "tensor_copy", ["bs_mn"], ["bs_lo"], out=BS[:, 3:4], in_=BS[:, 1:2])
            dk = ("IS", (nk - 128) // 512)
            P.op("dve", "tensor_tensor", [dk, "CST"], [dk], out=IS[:, nk - 128:nk], in0=IS[:, nk - 128:nk], in1=CAUS, op=ALU.add)
            P.op("dve", "tensor_tensor", ["bs_lo", "bs_steps"], ["bs_mid"], out=BS[:, 4:5], in0=BS[:, 3:4], in1=BS[:, 8:9], op=ALU.add)
            for it in range(NIT):
                P.op("dve", "tensor_scalar", iskeys + ["bs_mid"], ["MS%d" % sl, "bs_cnt"], out=MS[:, sl, 0:nk], in0=IS[:, 0:nk], scalar1=BS[:, 4:5],
                     scalar2=0.0, op0=ALU.is_ge, op1=ALU.add, accum_out=BS[:, 5:6])
                P.op("dve", "scalar_tensor_tensor", ["bs_cnt", "bs_steps"], [("bs_t", it)], out=TS[:, it:it + 1], in0=BS[:, 5:6], scalar=TOPK - 0.5,
                     in1=BS[:, 8 + it:9 + it], op0=ALU.is_ge, op1=ALU.mult)
                if it < NIT - 1:
                    P.op("dve", "scalar_tensor_tensor", [("bs_t", it), "bs_steps", "bs_mid"], ["bs_mid"], out=BS[:, 4:5], in0=TS[:, it:it + 1],
                         scalar=BS[:, 9 + it:10 + it], in1=BS[:, 4:5], op0=ALU.subtract, op1=ALU.add)
            P.op("dve", "tensor_reduce", [("bs_t", i_) for i_ in range(NIT)], ["bs_ts"], out=BS[:, 6:7], in_=TS[:, 0:NIT], axis=AX.X, op=ALU.add)
            P.op("dve", "tensor_tensor", ["bs_ts", "bs_mn"], ["bs_lo"], out=BS[:, 3:4], in0=BS[:, 6:7], in1=BS[:, 1:2], op=ALU.add)
            P.op("dve", "tensor_scalar", iskeys + ["bs_lo"], ["MS%d" % sl], out=MS[:, sl, 0:nk], in0=IS[:, 0:nk], scalar1=BS[:, 3:4], scalar2=None, op0=ALU.is_ge)
            if debug and b == 0 and qb == dbg_qb:
                P.dma("pool", dbg["d_IS"][:, 0:nk], IS[:, 0:nk], iskeys, [], "dbgis")
                P.dma("pool", dbg["d_lo"], BS[:, 0:8], ["bs_lo", "bs_cnt", "bs_mx", "bs_mn"], [], "dbglo")

        def att_part(blk):
            b = blk // NQB
            qb = blk % NQB
            sl = blk % 2
            nk = (qb + 1) * 128
            nkc = qb + 1
            r0 = b * S + qb * 128
            if qb == 0:
                P.dma("sp", KLs[:], KL[b], [], ["KLs"], "KLs")
                P.dma("sp", KRs[0:16, :], KR1[b].rearrange("(i r) t -> r i t", r=8)[0], [], ["KRs"], "KRs")
                P.dma("sp", KRs[16:32, :], KR2[b].rearrange("(i r) t -> r i t", r=8)[0], [], ["KRs"], "KRs")
                P.dma("sp", Vs[:], VA[b].rearrange("(c p) d -> p c d", p=128), [], ["Vs"], "Vs")
            P.dma("sp", GTq[:].rearrange("p h q -> p (h q)"), GT[b, qb], [], ["GTq"], "GTq")
            P.dma("sp", Xq[:], x[r0:r0 + 128, :], [], ["Xq"], "Xq")
            for kc0 in range(0, nkc, 4):
                n4 = min(4, nkc - kc0)
                pt_, pk = wpool.next()
                pv = pt_[:].bitcast(BF16)
                for j in range(n4):
                    kc = kc0 + j
                    P.op("pe", "transpose", ["MS%d" % sl, "IDB"], [pk], out=pv[:, j * 128:(j + 1) * 128], in_=MS[:, sl, kc * 128:(kc + 1) * 128], identity=IDB[:])
                P.op("act", "activation", [pk, "NEG30"], [("MT", sl, kc0 // 4)], out=MT[:, sl, kc0:kc0 + n4, :].rearrange("p c q -> p (c q)"),
                     in_=pv[:, 0:n4 * 128], func=AF.Identity, scale=BIGM, bias=NEG30[:])

            for hg in range(2):
                groups = [(kc, g) for kc in range(nkc) for g in range(2)]

                def qk(kc, g):
                    ks = slice(kc * 128, (kc + 1) * 128)
                    h0 = hg * 8 + g * 4
                    pt_, pk = wpool.next()
                    P.op("pe", "matmul", ["KLs", "QLq%d" % sl], [pk], pt_[:], lhsT=KLs[:, ks],
                         rhs=QLq[:, sl, h0:h0 + 4, :].rearrange("p h q -> p (h q)"), start=True, stop=False)
                    P.op("pe", "matmul", ["KRs", "QRq%d" % sl], [pk], pt_[:], lhsT=KRs[:, ks],
                         rhs=QRq[:, sl, h0:h0 + 4, :].rearrange("p h q -> p (h q)"), start=False, stop=False)
                    P.op("pe", "matmul", ["IDB", ("MT", sl, kc // 4)], [pk], pt_[:], lhsT=IDB[:],
                         rhs=MT[:, sl, kc, :].unsqueeze(1).to_broadcast([128, 4, 128]), start=False, stop=True)
                    return pt_, pk
                pend = [qk(*groups[0])]
                if len(groups) > 1:
                    pend.append(qk(*groups[1]))
                for gi, (kc, g) in enumerate(groups):
                    if gi + 2 < len(groups):
                        pend.append(qk(*groups[gi + 2]))
                    pt_, pk = pend.pop(0)
                    pr, prk = ptr.next()
                    P.op("act", "activation", [pk], [prk], out=pr, in_=pt_[:], func=AF.Exp, scale=SCALE_A)
                    P.op("pe", "matmul", ["Vs", prk], [psO[g][1]], psO[g][0][:], lhsT=Vs[:, kc, :], rhs=pr, start=(kc == 0), stop=(kc == nkc - 1))
                    P.op("pe", "matmul", ["ONESB", prk], [psD[g][1]], psD[g][0][:], lhsT=ONESB[:], rhs=pr, start=(kc == 0), stop=(kc == nkc - 1))
                for g in range(2):
                    h0 = hg * 8 + g * 4
                    gs = slice(g * 512, (g + 1) * 512)
                    oc, ock = ocr.next()
                    P.op("act", "activation", [psD[g][1]], [("RD", g)], out=RD[:, gs], in_=psD[g][0][:], func=AF.Ln)
                    P.op("act", "activation", [("RD", g)], [("RD", g)], out=RD[:, gs], in_=RD[:, gs], func=AF.Exp, scale=-1.0)
                    P.op("act", "copy", [psO[g][1]], [ock], out=oc, in_=psO[g][0][:])
                    P.op("pool", "tensor_tensor", [ock, ("RD", g)], [("ON", g)], out=ON[:, gs], in0=oc, in1=RD[:, gs], op=ALU.mult)
                    pt_, pk = wpool.next()
                    for hl in range(4):
                        h = h0 + hl
                        P.op("pe", "matmul", [("WUV", h // 8), ("ON", g)], [pk], pt_[:, hl * 128:(hl + 1) * 128], lhsT=WUV[:, h * 128:(h + 1) * 128],
                             rhs=ON[:, g * 512 + hl * 128:g * 512 + (hl + 1) * 128], start=True, stop=True)
                    oc2, ock2 = ocr.next()
                    P.op("act", "copy", [pk], [ock2], out=oc2, in_=pt_[:])
                    P.op("pool", "tensor_tensor", [ock2, "GTq"], [("Y", h0 // 4)], out=Y[:, h0:h0 + 4, :].rearrange("p h q -> p (h q)"),
                         in0=oc2, in1=GTq[:, h0:h0 + 4, :].rearrange("p h q -> p (h q)"), op=ALU.mult)
            ykeys = [("Y", i) for i in range(4)]
            for nh in range(2):
                pt_, pk = wpool.next()
                for h in range(16):
                    P.op("pe", "matmul", ykeys + [("WOA", h)], [pk], pt_[:], lhsT=Y[:, h, :], rhs=WOA[:, h, nh * 512:(nh + 1) * 512],
                         start=(h == 0), stop=(h == 15))
                P.op("act", "copy", [pk], [("TMPH", nh)], out=TMPH[:, nh * 512:(nh + 1) * 512], in_=pt_[:])
                P.op("pool", "tensor_tensor", [("TMPH", nh), "GBC0"], [("TMPH", nh)], out=TMPH[:, nh * 512:(nh + 1) * 512],
                     in0=TMPH[:, nh * 512:(nh + 1) * 512], in1=GBC[0][:, b * D + nh * 512:b * D + (nh + 1) * 512], op=ALU.mult)
            P.op("pool", "tensor_tensor", [("TMPH", 0), ("TMPH", 1), "Xq"], ["H1q"], out=H1q[:], in0=TMPH[:], in1=Xq[:], op=ALU.add)
            P.dma("pool", H1[r0:r0 + 128, :], H1q[:], ["H1q"], [], "H1q")
            if debug:
                P.dma("pool", dbg["d_H1"][r0:r0 + 128, :], H1q[:], ["H1q"], [], "H1q")

        idx_part(0)
        for blk in range(nblk):
            if blk + 1 < nblk:
                idx_part(blk + 1)
            att_part(blk)
        P.emit()

    if stop_after == 2:
        return nc, dbg, P

    with ExitStack() as es:
        def T(name, shape, dt=F32):
            return es.enter_context(nc.sbuf_tensor(name, list(shape), dt))
        W3 = T("p3_w", [128, 8, WB_COLS], BF16)
        STG3 = T("p3_stg", [128, 2, WB_COLS // 4], F32)
        X3 = T("p3_x", [128, 2, D], F32)
        JUNK3 = T("p3_junk", [128, D], BF16)
        SS3 = T("p3_ss", [128, 2, 4], F32)
        KVT = T("p3_kvt", [128, 2, 8, 512], BF16)
        HBT = T("p3_hbt", [128, 2, 8, 512], BF16)
        POSI3 = T("p3_posi", [128, 512], I32)
        POSF3 = T("p3_posf", [128, 512], F32)
        TAB3 = T("p3_tab", [128, 2, 512], F32)
        TKI3 = T("p3_ki", [128, 1, 512], I32)
        TKF3 = T("p3_kf", [128, 1, 512], F32)
        RT3 = T("p3_rt", [128, 4, 512], F32)
        RO3 = T("p3_ro", [128, 4, 512], BF16)
        VSTG = T("p3_vst", [128, 2, D], BF16)
        GST = T("p3_gst", [128, 2, 512], BF16)
        stg = Rot([(STG3[:, i, :], "stg3_%d" % i) for i in range(2)])
        wkeys = load_weights(WB, W3, WB_COLS, stg, "W3")
        tabs = [(TAB3[:, 0, :], TAB3[:, 1, :], "tab128")]
        tmps = Rot([(TKI3[:, 0, :], TKF3[:, 0, :], "tk3")])
        fm = Rot([(ps[i], "ps%d" % i) for i in range(4)])
        psT_pair = [(ps[4], "ps4"), (ps[5], "ps5")]
        vps = Rot([(ps[6], "ps6"), (ps[7], "ps7")])
        ro = Rot([(RO3[:, i, :], "ro3_%d" % i) for i in range(4)])
        rt = Rot([(RT3[:, i, :], "rt3_%d" % i) for i in range(4)])
        gst = Rot([(GST[:, i, :], "gst%d" % i) for i in range(2)])
        ntile3 = int(_os.environ.get("K_NT3", NB * S // 512))
        for tt in range(ntile3):
            b = tt // 8
            t0 = (tt % 8) * 512
            kvt = KVT[:, tt % 2]; kvk = "kvt%d" % (tt % 2)
            hbt = HBT[:, tt % 2]; hbk = "hbt%d" % (tt % 2)
            P.dma("sp", POSI3[:], pos[b:b + 1, t0:t0 + 512].partition_broadcast(128), [], ["POSI"], "POSI")
            P.op("pool", "tensor_copy", ["POSI"], ["POSF"], out=POSF3[:], in_=POSI3[:])
            rope_tables(b, t0, [(2, 0)], POSF3, tabs, tmps)
            for sub in range(4):
                xi = (tt * 4 + sub) % 2
                XT = X3[:, xi, :]
                xkey = "x3_%d" % xi
                r0 = b * S + t0 + sub * 128
                P.dma("sp", XT, H1[r0:r0 + 128, :], [], [xkey], xkey)

                def evacs(kc, pap, pk, sub=sub, b=b, kvt=kvt, hbt=hbt, kvk=kvk, hbk=hbk):
                    d1 = kvt[:, kc, sub * 128:(sub + 1) * 128]
                    d2 = hbt[:, kc, sub * 128:(sub + 1) * 128]
                    if kc < 4:
                        P.op("act", "activation", [pk, "NRM"], [(kvk, kc)], out=d1, in_=pap, func=AF.Copy, scale=NRM[:, 16 + kc:17 + kc])
                        P.op("act", "activation", [pk, "ASCL1", "ASFT1"], [(hbk, kc)], out=d2, in_=pap, func=AF.Identity,
                             scale=ASCL[1][:, kc * NB + b:kc * NB + b + 1], bias=ASFT[1][:, kc * NB + b:kc * NB + b + 1])
                    else:
                        P.op("dve", "tensor_scalar", [pk, "NRM"], [(kvk, kc)], out=d1, in0=pap, scalar1=NRM[:, 16 + kc:17 + kc], scalar2=None, op0=ALU.mult)
                        P.op("dve", "tensor_scalar", [pk, "ASCL1", "ASFT1"], [(hbk, kc)], out=d2, in0=pap,
                             scalar1=ASCL[1][:, kc * NB + b:kc * NB + b + 1], scalar2=ASFT[1][:, kc * NB + b:kc * NB + b + 1], op0=ALU.mult, op1=ALU.add)
                norm_transpose(None, XT, xkey, sub, SS3[:, xi, :], evacs, psT_pair, JUNK3)

            def fm3(ci, src, sk):
                pt_, pk = fm.next()
                for kc in range(8):
                    P.op("pe", "matmul", [(sk, kc)] + wkeys[kc], [pk], pt_[:], lhsT=W3[:, kc, ci * 128:(ci + 1) * 128], rhs=src[:, kc, :],
                         start=(kc == 0), stop=(kc == 7))
                return pt_, pk
            ct, st_, tk = tabs[0]
            pair_list = [(2 * j, kvt, kvk, KT2[b, j]) for j in range(4)]
            pair_list += [(8 + g * 8 + 2 * j, hbt, hbk, QT2[b, g, j]) for g in range(3) for j in range(4)]
            for (c0, src, sk, dst) in pair_list:
                p1, k1 = fm3(c0, src, sk)
                p2_, k2 = fm3(c0 + 1, src, sk)
                ta, tak = rt.next(); tb, tbk = rt.next()
                oa, oak = ro.next(); ob, obk = ro.next()
                P.op("dve", "tensor_tensor", [k1, (tk, 0)], [tak], out=ta, in0=p1[:], in1=ct, op=ALU.mult)
                P.op("dve", "tensor_tensor", [k2, (tk, 1)], [tbk], out=tb, in0=p2_[:], in1=st_, op=ALU.mult)
                P.op("pool", "tensor_tensor", [tak, tbk], [oak], out=oa, in0=ta, in1=tb, op=ALU.subtract)
                tc_, tck = rt.next(); td, tdk = rt.next()
                P.op("dve", "tensor_tensor", [k2, (tk, 0)], [tck], out=tc_, in0=p2_[:], in1=ct, op=ALU.mult)
                P.op("dve", "tensor_tensor", [k1, (tk, 1)], [tdk], out=td, in0=p1[:], in1=st_, op=ALU.mult)
                P.op("pool", "tensor_tensor", [tck, tdk], [obk], out=ob, in0=tc_, in1=td, op=ALU.add)
                P.dma("pool", dst[0, :, t0:t0 + 512], oa, [oak], [], oak)
                P.dma("pool", dst[1, :, t0:t0 + 512], ob, [obk], [], obk)
                if debug and tt == 0 and c0 == 0:
                    P.dma("pool", dbg["d_KT"][:, 0:512], oa, [oak], [], oak)
            for h in range(8):
                p1, k1 = fm3(32 + h, hbt, hbk)
                g_, gk = gst.next()
                P.op("act", "activation", [k1], [gk], out=g_, in_=p1[:], func=AF.Silu)
                P.dma("pool", GB[b, h, :, t0:t0 + 512], g_, [gk], [], gk)
            for sub in range(4):
                vs_ = VSTG[:, sub % 2, :]
                vk_ = "vstg%d" % (sub % 2)
                for nh in range(2):
                    pt_, pk = vps.next()
                    for kc in range(8):
                        P.op("pe", "matmul", [(kvk, kc)] + wkeys[kc], [pk], pt_[:], lhsT=kvt[:, kc, sub * 128:(sub + 1) * 128],
                             rhs=W3[:, kc, 5120 + nh * 512:5120 + (nh + 1) * 512], start=(kc == 0), stop=(kc == 7))
                    if nh == 0:
                        P.op("act", "copy", [pk], [(vk_, nh)], out=vs_[:, nh * 512:(nh + 1) * 512], in_=pt_[:])
                    else:
                        P.op("dve", "tensor_copy", [pk], [(vk_, nh)], out=vs_[:, nh * 512:(nh + 1) * 512], in_=pt_[:])
                r0 = t0 + sub * 128
                P.dma("pool", VB[b, r0:r0 + 128, :], vs_, [(vk_, 0), (vk_, 1)], [], vk_)
        P.emit()
    if stop_after == 3:
        return nc, dbg, P

    SCALE_B = 128 ** -0.5
    with ExitStack() as es:
        def T(name, shape, dt=F32):
            return es.enter_context(nc.sbuf_tensor(name, list(shape), dt))
        KTh = T("p4_k", [128, 2, S], BF16)
        QTh = T("p4_q", [128, 2, 3, S], BF16)
        Vh = T("p4_v", [128, 2, 3, NQB, 128], BF16)
        GBh = T("p4_g", [128, 2, S], BF16)
        OD = T("p4_od", [128, 2, S], F32)
        PTb = T("p4_pt", [128, 4, 256], BF16)
        YTo = T("p4_y", [128, S], BF16)
        NEGMB = T("p4_negm", [128, 256], BF16)
        P.op("dve", "tensor_scalar", ["CST"], ["NEGMB"], out=NEGMB[:], in0=CST[:, 128:384], scalar1=30000.0, scalar2=-30000.0, op0=ALU.mult, op1=ALU.add)
        sp_ = Rot([(ps[i], "ps%d" % i) for i in range(4)])
        op_ = Rot([(ps[i], "ps%d" % i) for i in range(4, 8)])
        ptb = Rot([(PTb[:, i, :], "ptb%d" % i) for i in range(4)])
        nhead4 = int(_os.environ.get("K_NH4", NB * 8))
        for idx in range(nhead4):
            b = idx // 8
            h = idx % 8
            sl = idx % 2
            j = h // 2
            hh = h % 2
            kk_ = "KTh%d" % sl; qk_ = "QTh%d" % sl; vk_ = "Vh%d" % sl; gk_ = "GBh%d" % sl
            for xx in range(2):
                P.dma("sp", KTh[xx * 64:(xx + 1) * 64, sl, :], KT2[b, j, xx, hh * 64:(hh + 1) * 64, :], [], [kk_], kk_)
                for g in range(3):
                    P.dma("sp", QTh[xx * 64:(xx + 1) * 64, sl, g, :], QT2[b, g, j, xx, hh * 64:(hh + 1) * 64, :], [], [qk_], qk_)
            for g, d in enumerate((1, 4, 16)):
                nch = NQB // d
                vv = VB[b, :, h * 128:(h + 1) * 128].rearrange("(c a r) f -> r a c f", a=128, r=d)
                for r in range(d):
                    P.dma("sp", Vh[:, sl, g, r * nch:(r + 1) * nch, :], vv[r], [], [vk_], vk_)
            P.dma("sp", GBh[:, sl, :], GB[b, h], [], [gk_], gk_)
            units = [(g, d, r, c) for g, d in enumerate((1, 4, 16)) for r in range(d) for c in range(NQB // d)]

            def qk4(g, d, r, c):
                qv = QTh[:, sl, g, :].rearrange("p (c a r) -> p r c a", a=128, r=d)
                kv_ = KTh[:, sl, :].rearrange("p (c a r) -> p r c a", a=128, r=d)
                pS, pSk = sp_.next()
                if c > 0:
                    P.op("pe", "matmul", [kk_, qk_], [pSk], pS[:, 0:128], lhsT=kv_[:, r, c - 1, :], rhs=qv[:, r, c, :], start=True, stop=False)
                    P.op("pe", "matmul", ["IDB", "NEGMB"], [pSk], pS[:, 0:128], lhsT=IDB[:], rhs=NEGMB[:, 0:128], start=False, stop=True)
                P.op("pe", "matmul", [kk_, qk_], [pSk], pS[:, 128:256], lhsT=kv_[:, r, c, :], rhs=qv[:, r, c, :], start=True, stop=False)
                P.op("pe", "matmul", ["IDB", "NEGMB"], [pSk], pS[:, 128:256], lhsT=IDB[:], rhs=NEGMB[:, 128:256], start=False, stop=True)
                return pS, pSk
            pend = [qk4(*units[0]), qk4(*units[1])]
            for ui, (g, d, r, c) in enumerate(units):
                if ui + 2 < len(units):
                    pend.append(qk4(*units[ui + 2]))
                pS, pSk = pend.pop(0)
                nch = NQB // d
                ov = OD[:].rearrange("p x (c a r) -> p r c x a", a=128, r=d)
                ti = r * nch + c
                lo = 0 if c > 0 else 128
                pt_, ptk = ptb.next()
                P.op("act", "activation", [pSk], [ptk], out=pt_[:, lo:256], in_=pS[:, lo:256], func=AF.Exp, scale=SCALE_B)
                pO, pOk = op_.next()
                for half, lhs_of in ((0, lambda t: Vh[:, sl, g, t, :]), (1, lambda t: ONESB[:])):
                    oc = slice(half * 128, (half + 1) * 128)
                    if c > 0:
                        P.op("pe", "matmul", [vk_, "ONESB", ptk], [pOk], pO[:, oc], lhsT=lhs_of(ti - 1), rhs=pt_[:, 0:128], start=True, stop=False)
                        P.op("pe", "matmul", [vk_, "ONESB", ptk], [pOk], pO[:, oc], lhsT=lhs_of(ti), rhs=pt_[:, 128:256], start=False, stop=True)
                    else:
                        P.op("pe", "matmul", [vk_, "ONESB", ptk], [pOk], pO[:, oc], lhsT=lhs_of(ti), rhs=pt_[:, 128:256], start=True, stop=True)
                dst = ov[:, r, c, :, :]
                src = pO[:, 0:256].rearrange("p (x a) -> p x a", a=128)
                if g == 0:
                    P.op("act", "copy", [pOk], [("OD", c // 4)], out=dst, in_=src)
                else:
                    odk = [("OD", i) for i in range(8)]
                    P.op("dve", "tensor_tensor", [pOk] + odk, odk, out=dst, in0=src, in1=dst, op=ALU.add)
            odk = [("OD", i) for i in range(8)]
            P.op("dve", "reciprocal", odk, odk, out=OD[:, 1, :], in_=OD[:, 1, :])
            P.op("pool", "tensor_tensor", odk, odk, out=OD[:, 0, :], in0=OD[:, 0, :], in1=OD[:, 1, :], op=ALU.mult)
            P.op("dve", "tensor_tensor", odk + [gk_], ["YTo"], out=YTo[:], in0=OD[:, 0, :], in1=GBh[:, sl, :], op=ALU.mult)
            P.dma("pool", YT[b, h], YTo[:], ["YTo"], [], "YTo")
            if debug and idx == 0:
                P.dma("pool", dbg["d_YT"], YTo[:], ["YTo"], [], "YTo")
        P.emit()
    if stop_after == 4:
        return nc, dbg, P

    with ExitStack() as es:
        def T(name, shape, dt=F32):
            return es.enter_context(nc.sbuf_tensor(name, list(shape), dt))
        WOB = T("p5_w", [128, 8, D], BF16)
        STG5 = T("p5_stg", [128, 2, D], F32)
        Yq = T("p5_y", [128, 2, 8, 128], BF16)
        H1t = T("p5_h1", [128, 2, D], F32)
        TMP5 = T("p5_tmp", [128, D], F32)
        H2 = T("p5_h2", [128, D], F32)
        OUTt = T("p5_out", [128, 2, D], F32)
        JUNK5 = T("p5_junk", [128, D], BF16)
        SS5 = T("p5_ss", [128, 4], F32)
        for h in range(8):
            st = STG5[:, h % 2, :]
            P.dma("sp", st, wob[h * 128:(h + 1) * 128, :], [], ["stg5_%d" % (h % 2)], "stg5_%d" % (h % 2))
            if h % 2 == 0:
                P.op("dve", "tensor_copy", ["stg5_0"], [("WOB", h)], out=WOB[:, h, :], in_=st)
            else:
                P.op("act", "copy", ["stg5_1"], [("WOB", h)], out=WOB[:, h, :], in_=st)
        pp = Rot([(ps[i], "ps%d" % i) for i in range(4)])
        ntile5 = int(_os.environ.get("K_NT5", NB * NQB))
        for tt in range(ntile5):
            b = tt // NQB
            qb = tt % NQB
            sl = tt % 2
            r0 = b * S + qb * 128
            yk = "Yq%d" % sl; hk_ = "H1t%d" % sl; ok_ = "OUTt%d" % sl
            P.dma("sp", Yq[:, sl], YT[b, :, :, qb * 128:(qb + 1) * 128].rearrange("h p t -> p h t"), [], [yk], yk)
            P.dma("sp", H1t[:, sl, :], H1[r0:r0 + 128, :], [], [hk_], hk_)
            for nh in range(2):
                pt_, pk = pp.next()
                for h in range(8):
                    P.op("pe", "matmul", [yk, ("WOB", h)], [pk], pt_[:], lhsT=Yq[:, sl, h, :], rhs=WOB[:, h, nh * 512:(nh + 1) * 512],
                         start=(h == 0), stop=(h == 7))
                P.op("dve", "tensor_tensor", [pk, "GBC1"], [("TMP5", nh)], out=TMP5[:, nh * 512:(nh + 1) * 512], in0=pt_[:],
                     in1=GBC[1][:, b * D + nh * 512:b * D + (nh + 1) * 512], op=ALU.mult)
            P.op("pool", "tensor_tensor", [("TMP5", 0), ("TMP5", 1), hk_], ["H2"], out=H2[:], in0=TMP5[:], in1=H1t[:, sl, :], op=ALU.add)
            P.op("act", "activation", ["H2"], ["junk5", "ss5a"], out=JUNK5[:], in_=H2[:], func=AF.Square, accum_out=SS5[:, 0:1])
            P.op("act", "activation", ["ss5a"], ["ss5b"], out=SS5[:, 1:2], in_=SS5[:, 0:1], func=AF.Sqrt, scale=1.0 / D, bias=EPS)
            P.op("dve", "reciprocal", ["ss5b"], ["ss5c"], out=SS5[:, 2:3], in_=SS5[:, 1:2])
            P.op("dve", "scalar_tensor_tensor", ["H2", "ss5c", "FNG"], [ok_], out=OUTt[:, sl, :], in0=H2[:], scalar=SS5[:, 2:3], in1=FNG[:],
                 op0=ALU.mult, op1=ALU.mult)
            P.dma("pool", out[r0:r0 + 128, :], OUTt[:, sl, :], [ok_], [], ok_)
        P.emit()

    return nc, dbg, P


def _perm_a():
    idx = []
    idx += list(range(0, 2048))
    idx += list(range(2720, 4768))
    for hb in (0, 8):
        for part in (0, 16):
            idx += [2048 + (hb + (p % 8)) * 32 + part + p // 8 for p in range(128)]
    for hb in (0, 4):
        for part in (0, 32):
            idx += [4768 + (hb + (p % 4)) * 64 + part + p // 4 for p in range(128)]
    for part in (0, 16):
        idx += [2688 + part + p // 8 for p in range(128)]
    for part in (0, 32):
        idx += [5280 + part + p // 4 for p in range(128)]
    idx += list(range(2560, 2688))
    idx += list(range(5344, 5352))
    assert len(idx) == WA_COLS
    return np.array(idx)


def _consts():
    c = np.zeros((128, 640 + 4 + NIT), np.float32)
    c[:, 0:128] = np.eye(128, dtype=np.float32)
    a = np.arange(128)[:, None]
    q = np.arange(128)[None, :]
    c[:, 128:256] = (a >= q).astype(np.float32)
    c[:, 256:384] = (a <= q).astype(np.float32)
    qq = np.arange(128)[:, None]
    kk = np.arange(128)[None, :]
    c[:, 384:512] = np.where(kk <= qq, 0.0, NEG).astype(np.float32)
    p = np.arange(128)
    two_pi = 2 * math.pi
    inv32 = (np.float32(THETA) ** (-(np.arange(0, 32, 2, dtype=np.float32)) / np.float32(32))).astype(np.float32)
    inv64 = (np.float32(THETA) ** (-(np.arange(0, 64, 2, dtype=np.float32)) / np.float32(64))).astype(np.float32)
    inv128 = (np.float32(THETA) ** (-(np.arange(0, 128, 2, dtype=np.float32)) / np.float32(128))).astype(np.float32)
    c[:, 640] = inv32[p // 8].astype(np.float64) / two_pi
    c[:, 641] = inv64[p // 4].astype(np.float64) / two_pi
    c[:, 642] = inv128[p % 64].astype(np.float64) / two_pi
    c[:, 643:644 + NIT] = (0.5 ** np.arange(1, NIT + 2))[None, :]
    return c


def _fm(v, nch):
    return np.ascontiguousarray(np.asarray(v, np.float32).reshape(nch, 128).T)


def prepare_inputs(x, c, positions, a_norm, a_ada_w, a_ada_b, a_w_in, a_kv_norm, a_w_uv, a_w_out,
                   kv_norm, w_kv, b_norm, b_ada_w, b_ada_b, b_w_in, b_w_out, final_norm):
    f = lambda a: np.ascontiguousarray(np.asarray(a, np.float32))
    x = f(x); c = f(c)
    positions = np.ascontiguousarray(np.asarray(positions, np.int32))
    WA = np.ascontiguousarray(f(a_w_in)[0][:, _perm_a()])
    wkv = f(w_kv); bw = f(b_w_in)[0]
    cols = []
    def pair_cols(base):
        out_ = []
        for j in range(4):
            for part in (0, 64):
                out_.append([base + (2 * j + p // 64) * 128 + part + (p % 64) for p in range(128)])
        return out_
    kcols = np.concatenate([wkv[:, ci] for ci in pair_cols(0)], axis=1)
    qcols = np.concatenate([bw[:, ci] for g in range(3) for ci in pair_cols(g * 1024)], axis=1)
    WB = np.ascontiguousarray(np.concatenate([kcols, qcols, bw[:, 3072:4096], wkv[:, 1024:2048]], axis=1))
    assert WB.shape[1] == WB_COLS
    normsT = np.concatenate([_fm(f(a_norm)[0], 8), _fm(f(b_norm)[0], 8), _fm(f(kv_norm), 8)], axis=1)
    ada_bT = np.concatenate([_fm(f(a_ada_b)[0], 24), _fm(f(b_ada_b)[0], 24)], axis=1)
    ada_bg = np.concatenate([f(a_ada_b)[0][2 * D:], f(b_ada_b)[0][2 * D:]])[None, :]
    wuv = np.ascontiguousarray(f(a_w_uv)[0].transpose(1, 0, 2).reshape(128, 16 * 128))
    shared = {
        "normsT": np.ascontiguousarray(normsT), "a_ada_w": f(a_ada_w)[0], "b_ada_w": f(b_ada_w)[0],
        "ada_bT": np.ascontiguousarray(ada_bT), "ada_bg": np.ascontiguousarray(ada_bg),
        "WA": WA, "WB": WB, "akvg": f(a_kv_norm)[0][None, :], "wuv": wuv, "woa": f(a_w_out)[0], "wob": f(b_w_out)[0],
        "fng": f(final_norm)[None, :], "cst": _consts(),
    }
    in_maps = []
    for core in range(NCORES):
        bs = slice(core * NB, (core + 1) * NB)
        cc = c[bs]
        cTl = np.ascontiguousarray(cc.reshape(NB, 8, 128).transpose(2, 1, 0).reshape(128, 8 * NB))
        m = dict(shared)
        m["x"] = np.ascontiguousarray(x[bs].reshape(NB * S, D))
        m["pos"] = np.ascontiguousarray(positions[bs])
        m["cT"] = cTl
        in_maps.append(m)
    return in_maps


def kernel(**inputs):
    in_maps = prepare_inputs(**inputs)
    nc, _, _ = build_program(debug=False)
    res = run_bass_kernel_spmd(nc, in_maps, core_ids=list(range(NCORES)))
    outs = [np.asarray(r["out"]).reshape(NB, S, D) for r in res.results]
    return np.concatenate(outs, axis=0).astype(np.float32)
```

```python
import math
from contextlib import ExitStack
import numpy as np
import concourse.bass as bass
import concourse.mybir as mybir
from concourse.bass_utils import run_bass_kernel_spmd

F32 = mybir.dt.float32
BF16 = mybir.dt.bfloat16
I32 = mybir.dt.int32
AF = mybir.ActivationFunctionType
ALU = mybir.AluOpType
AX = mybir.AxisListType

ENGS = ("pe", "act", "dve", "pool", "sp")

D = 1024
S = 4096
NB = 2
NCORES = 8
NQB = S // 128
EPS = 1e-6
THETA = 10000.0
NIT = 16
TOPK = 256
NEG = -1.0e30
WA_COLS = 5768
WB_COLS = 6144


class Op:
    __slots__ = ("eng", "fn", "deps", "is_dma", "chan", "val", "flag")

    def __init__(self, eng, fn, is_dma=False, chan=None):
        self.eng = eng
        self.fn = fn
        self.deps = ()
        self.is_dma = is_dma
        self.chan = chan
        self.val = 0
        self.flag = False


class Prog:
    def __init__(self, nc):
        self.nc = nc
        self.chan_sem = {}
        self.chan_cnt = {}
        self.free_chan = []
        self.eng_sem = {}
        self.eng_cnt = {e: 0 for e in ENGS}
        self.phase_no = 0
        self.total_ops = 0
        self._reset()

    def _reset(self):
        self.ops = []
        self.last_w = {}
        self.readers = {}

    def add(self, eng, fn, reads=(), writes=(), chan=None):
        is_dma = chan is not None
        op = Op(eng, fn, is_dma, chan)
        psr = [k for k in reads if isinstance(k, str) and k.startswith("ps")]
        if psr:
            reads = [k for k in reads if k not in psr]
            writes = list(writes) + [k for k in psr if k not in writes]
        deps = {}
        for k in reads:
            w = self.last_w.get(k)
            if w is not None:
                deps[id(w)] = w
        for k in writes:
            w = self.last_w.get(k)
            if w is not None:
                deps[id(w)] = w
            for r in self.readers.get(k, ()):
                deps[id(r)] = r
        dl = []
        for d in deps.values():
            if d is op:
                continue
            if (not is_dma) and eng == "pe" and d.eng == "pe" and not d.is_dma:
                continue
            dl.append(d)
        op.deps = dl
        for k in reads:
            self.readers.setdefault(k, []).append(op)
        for k in writes:
            self.last_w[k] = op
            self.readers[k] = []
        self.ops.append(op)
        return op

    def op(self, eng, meth, reads, writes, *args, **kw):
        return self.add(eng, lambda e: getattr(e, meth)(*args, **kw), reads, writes)

    def dma(self, q, out, in_, reads, writes, chan):
        return self.add(q, lambda e: e.dma_start(out=out, in_=in_), reads, writes, chan=chan)

    def emit(self):
        nc = self.nc
        self.phase_no += 1
        ops = self.ops
        for op in ops:
            for d in op.deps:
                d.flag = True
        per_eng = {e: [] for e in ENGS}
        for op in ops:
            per_eng[op.eng].append(op)
        last_compute = {}
        for e in ENGS:
            for op in reversed(per_eng[e]):
                if not op.is_dma:
                    op.flag = True
                    last_compute[e] = op
                    break
        for e in last_compute:
            if e not in self.eng_sem:
                self.eng_sem[e] = nc.alloc_semaphore("eng_%s" % e)
        eng_sem = self.eng_sem
        cnt = self.eng_cnt
        for op in ops:
            if op.is_dma:
                if op.chan not in self.chan_sem:
                    if self.free_chan:
                        self.chan_sem[op.chan], self.chan_cnt[op.chan] = self.free_chan.pop()
                    else:
                        self.chan_sem[op.chan] = nc.alloc_semaphore("ch%d" % len(self.chan_sem))
                        self.chan_cnt[op.chan] = 0
                self.chan_cnt[op.chan] += 16
                op.val = self.chan_cnt[op.chan]
            elif op.flag:
                cnt[op.eng] += 1
                op.val = cnt[op.eng]
        final_eng = {e: (eng_sem[e], last_compute[e].val) for e in last_compute}
        final_chan = {c: (self.chan_sem[c], self.chan_cnt[c]) for c in self.chan_sem}

        def sem_of(d):
            return self.chan_sem[d.chan] if d.is_dma else eng_sem[d.eng]

        def run(e, engine):
            waited = {}
            for op in per_eng[e]:
                need = {}
                for d in op.deps:
                    s = sem_of(d)
                    k = id(s)
                    if k not in need or need[k][1] < d.val:
                        need[k] = (s, d.val)
                for k, (s, v) in need.items():
                    if waited.get(k, 0) >= v:
                        continue
                    engine.wait_ge(s, v)
                    waited[k] = v
                ins = op.fn(engine)
                if op.is_dma:
                    ins.then_inc(self.chan_sem[op.chan], 16)
                elif op.flag:
                    ins.then_inc(eng_sem[op.eng], 1)
            for e2, (s, v) in final_eng.items():
                if waited.get(id(s), 0) < v:
                    engine.wait_ge(s, v)
            for c, (s, v) in final_chan.items():
                if v > 0 and waited.get(id(s), 0) < v:
                    engine.wait_ge(s, v)

        with nc.Block() as block:
            @block.tensor
            def _(eng):
                run("pe", eng)

            @block.scalar
            def _(eng):
                run("act", eng)

            @block.vector
            def _(eng):
                run("dve", eng)

            @block.gpsimd
            def _(eng):
                run("pool", eng)

            @block.sync
            def _(eng):
                run("sp", eng)
        self.total_ops += len(ops)
        for c in list(self.chan_sem):
            self.free_chan.append((self.chan_sem.pop(c), self.chan_cnt.pop(c)))
        self._reset()


class Rot:
    def __init__(self, items):
        self.items = items
        self.i = 0

    def next(self):
        it = self.items[self.i % len(self.items)]
        self.i += 1
        return it


def build_program(debug=False, stop_after=99):
    nc = bass.Bass("TRN2", target_bir_lowering=False)
    P = Prog(nc)

    def din(name, shape, dt=F32):
        return nc.dram_tensor(name, list(shape), dt, kind="ExternalInput").ap()

    def dscr(name, shape, dt=BF16):
        return nc.dram_tensor(name, list(shape), dt).ap()

    x = din("x", [NB * S, D])
    pos = din("pos", [NB, S], I32)
    cT = din("cT", [128, 8 * NB])
    normsT = din("normsT", [128, 24])
    ada_w = [din("a_ada_w", [D, 3 * D]), din("b_ada_w", [D, 3 * D])]
    ada_bT = din("ada_bT", [128, 48])
    ada_bg = din("ada_bg", [1, 2 * D])
    WA = din("WA", [D, WA_COLS])
    WB = din("WB", [D, WB_COLS])
    akvg = din("akvg", [1, 128])
    wuv = din("wuv", [128, 16 * 128])
    woa = din("woa", [2 * D, D])
    wob = din("wob", [D, D])
    fng = din("fng", [1, D])
    cst = din("cst", [128, 640 + 4 + NIT])
    out = nc.dram_tensor("out", [NB * S, D], F32, kind="ExternalOutput").ap()

    QL = dscr("QL", [NB, NQB, 128, 2048])
    GT = dscr("GT", [NB, NQB, 128, 2048])
    QR = dscr("QR", [NB, NQB, 2, 128, 128])
    QI = dscr("QI", [NB, NQB, 2, 128, 128])
    QI2 = dscr("QI2", [NB, NQB, 2, 128, 128])
    QR2 = dscr("QR2", [NB, NQB, 2, 128, 128])
    KR1 = dscr("KR1", [NB, 128, S]); KR2 = dscr("KR2", [NB, 128, S])
    KI1 = dscr("KI1", [NB, 128, S]); KI2 = dscr("KI2", [NB, 128, S])
    VA = dscr("VA", [NB, S, 128])
    KL = dscr("KL", [NB, 128, S])
    WI = dscr("WI", [NB, S, 8], F32)
    H1 = dscr("H1", [NB * S, D], F32)
    KT2 = dscr("KT2", [NB, 4, 2, 128, S])
    QT2 = dscr("QT2", [NB, 3, 4, 2, 128, S])
    GB = dscr("GB", [NB, 8, 128, S])
    VB = dscr("VB", [NB, S, D])
    YT = dscr("YT", [NB, 8, 128, S])

    dbg = {}
    if debug:
        for nm, shp, dt in (("d_QL", [128, 2048], BF16), ("d_H1", [NB * S, D], F32), ("d_mod", [128, 64], F32),
                            ("d_KL", [128, S], BF16), ("d_IS", [128, S], F32), ("d_lo", [128, 8], F32),
                            ("d_YT", [128, S], BF16), ("d_KT", [128, S], BF16)):
            dbg[nm] = nc.dram_tensor(nm, shp, dt, kind="ExternalOutput").ap()

    def sb(name, shape, dt=F32):
        return nc.alloc_sbuf_tensor(name, list(shape), dt)

    CST = sb("CST", [128, 640 + 4 + NIT])
    IDF = CST[:, 0:128]
    MPREV_F = CST[:, 128:256]
    MCUR_F = CST[:, 256:384]
    CAUS = CST[:, 384:512]
    INV = CST[:, 640:643]
    POW2 = CST[:, 643:644 + NIT]
    IDB = sb("IDB", [128, 128], BF16)
    ONESB = sb("ONESB", [128, 128], BF16)
    ONESF = sb("ONESF", [1, 128], F32)
    HALFPI = sb("HALFPI", [128, 1], F32)
    MASKB = sb("MASKB", [128, 256], BF16)
    NRM = sb("NRM", [128, 24])
    ASCL = [sb("ASCL%d" % l, [128, 8 * NB]) for l in range(2)]
    ASFT = [sb("ASFT%d" % l, [128, 8 * NB]) for l in range(2)]
    GBC = [sb("GBC%d" % l, [128, NB * D]) for l in range(2)]
    AKVG = sb("AKVG", [128, 128])
    FNG = sb("FNG", [128, D])

    ps = [nc.alloc_psum_tensor("ps%d" % i, [128, 512], F32) for i in range(8)]

    P.dma("sp", CST[:], cst, [], ["CST"], "l0")
    P.dma("sp", NRM[:], normsT, [], ["NRM"], "l1")
    P.dma("sp", AKVG[:], akvg.partition_broadcast(128), [], ["AKVG"], "l2")
    P.dma("sp", FNG[:], fng.partition_broadcast(128), [], ["FNG"], "l3")
    P.add("dve", lambda e: e.tensor_copy(out=IDB[:], in_=IDF), ["CST"], ["IDB"])
    P.add("dve", lambda e: e.memset(ONESB[:], 1.0), [], ["ONESB"])
    P.add("dve", lambda e: e.memset(ONESF[:], 1.0), [], ["ONESF"])
    P.add("dve", lambda e: e.memset(HALFPI[:], math.pi / 2), [], ["HALFPI"])
    P.add("dve", lambda e: e.tensor_copy(out=MASKB[:], in_=CST[:, 128:384]), ["CST"], ["MASKB"])

    with ExitStack() as es:
        W0 = es.enter_context(nc.sbuf_tensor("p0_w", [128, 8, 3 * D], F32))
        C0 = es.enter_context(nc.sbuf_tensor("p0_c", [128, 8 * NB], F32))
        SC0 = es.enter_context(nc.sbuf_tensor("p0_sc", [128, 8 * NB], F32))
        SCB = es.enter_context(nc.sbuf_tensor("p0_scb", [128, 8 * NB, 128], F32))
        BT0 = es.enter_context(nc.sbuf_tensor("p0_bT", [128, 48], F32))
        BG0 = es.enter_context(nc.sbuf_tensor("p0_bg", [128, 2 * D], F32))
        M0 = es.enter_context(nc.sbuf_tensor("p0_m", [128, 16 * NB], F32))
        P.dma("sp", C0[:], cT, [], ["C0"], "l4")
        P.dma("sp", BT0[:], ada_bT, [], ["BT0"], "l5")
        P.dma("sp", BG0[:], ada_bg.partition_broadcast(128), [], ["BG0"], "l6")
        P.add("act", lambda e: e.activation(out=SC0[:], in_=C0[:], func=AF.Silu), ["C0"], ["SC0"])
        P.add("dve", lambda e: e.tensor_copy(out=SCB[:], in_=SC0[:].unsqueeze(2).to_broadcast([128, 8 * NB, 128])),
              ["SC0"], ["SCB"])
        for l in range(2):
            for kc in range(8):
                P.dma("sp", W0[:, kc, :], ada_w[l][kc * 128:(kc + 1) * 128, :], [], [("W0", kc)], "w%d" % kc)
            mps = ps[0]
            for j in range(16):
                for kc in range(8):
                    P.add("pe", lambda e, j=j, kc=kc: e.matmul(
                        mps[:, j * NB:(j + 1) * NB], lhsT=W0[:, kc, j * 128:(j + 1) * 128],
                        rhs=SC0[:, kc * NB:(kc + 1) * NB], start=(kc == 0), stop=(kc == 7)),
                        [("W0", kc), "SC0"], ["psmps"])
            P.add("dve", lambda e, l=l: e.tensor_tensor(
                out=M0[:].rearrange("p (j b) -> p j b", b=NB), in0=mps[:, 0:16 * NB].rearrange("p (j b) -> p j b", b=NB),
                in1=BT0[:, l * 24:l * 24 + 16].unsqueeze(2).to_broadcast([128, 16, NB]), op=ALU.add),
                ["psmps", "BT0"], ["M0"])
            P.add("dve", lambda e, l=l: e.tensor_copy(out=ASFT[l][:], in_=M0[:, 0:8 * NB]), ["M0"], ["ASFT%d" % l])
            P.add("dve", lambda e, l=l: e.scalar_tensor_tensor(
                out=ASCL[l][:].rearrange("p (j b) -> p j b", b=NB), in0=M0[:, 8 * NB:16 * NB].rearrange("p (j b) -> p j b", b=NB),
                scalar=1.0, in1=NRM[:, l * 8:(l + 1) * 8].unsqueeze(2).to_broadcast([128, 8, NB]),
                op0=ALU.add, op1=ALU.mult), ["M0", "NRM"], ["ASCL%d" % l])
            for b in range(NB):
                for nh in range(2):
                    gps = ps[1 + (b * 2 + nh) % 2]
                    gkey = "psg%d" % ((b * 2 + nh) % 2)
                    for kc in range(8):
                        P.add("pe", lambda e, b=b, nh=nh, kc=kc, gps=gps: e.matmul(
                            gps[:], lhsT=SCB[:, kc * NB + b, :], rhs=W0[:, kc, 2 * D + nh * 512:2 * D + (nh + 1) * 512],
                            start=(kc == 0), stop=(kc == 7)), [("W0", kc), "SCB"], [gkey])
                    P.add("dve", lambda e, l=l, b=b, nh=nh, gps=gps: e.tensor_tensor(
                        out=GBC[l][:, b * D + nh * 512:b * D + (nh + 1) * 512], in0=gps[:],
                        in1=BG0[:, l * D + nh * 512:l * D + (nh + 1) * 512], op=ALU.add), [gkey, "BG0"], ["GBC%d" % l])
        if debug:
            P.dma("act", dbg["d_mod"][:, 0:16], ASCL[0][:], ["ASCL0"], [], "s0")
            P.dma("act", dbg["d_mod"][:, 16:32], ASFT[0][:], ["ASFT0"], [], "s0")
            P.dma("act", dbg["d_mod"][:, 32:48], ASCL[1][:], ["ASCL1"], [], "s0")
            P.dma("act", dbg["d_mod"][:, 48:64], GBC[0][:, 0:16], ["GBC0"], [], "s0")
        P.emit()
    if stop_after == 0:
        return nc, dbg, P

    def load_weights(Wd, Wsb, ncols, stg, wkey):
        npc = 4
        pw = ncols // npc
        engs = ("dve", "act")
        i = 0
        for kc in range(8):
            for pc in range(npc):
                st, sk = stg.next()
                P.dma("sp", st[:, 0:pw], Wd[kc * 128:(kc + 1) * 128, pc * pw:(pc + 1) * pw], [], [sk], "ws%d" % (i % 2))
                en = engs[i % 2]
                if en == "act":
                    P.add("act", lambda e, st=st, kc=kc, pc=pc: e.copy(out=Wsb[:, kc, pc * pw:(pc + 1) * pw], in_=st[:, 0:pw]),
                          [sk], [(wkey, kc, pc)])
                else:
                    P.add(en, lambda e, st=st, kc=kc, pc=pc: e.tensor_copy(out=Wsb[:, kc, pc * pw:(pc + 1) * pw], in_=st[:, 0:pw]),
                          [sk], [(wkey, kc, pc)])
                i += 1
        return [[(wkey, kc, pc) for pc in range(npc)] for kc in range(8)]

    def rope_tables(b, t0, specs, POSF, tabs, tmps):
        import os as _os
        lvl = int(_os.environ.get("K_TABLVL", 9))
        for (icol, ti) in specs:
            ct, st_, tk = tabs[ti]
            for which, tile_, off in ((0, ct, 0.25), (1, st_, 0.0)):
                ki, kf, kk = tmps.next()
                P.add("pool", lambda e, ki=ki, icol=icol, off=off: e.tensor_scalar(
                    out=ki[:], in0=POSF[:], scalar1=INV[:, icol:icol + 1], scalar2=off, op0=ALU.mult, op1=ALU.add),
                    ["POSF", "CST"], [kk + "i"])
                if lvl < 2:
                    continue
                P.add("pool", lambda e, ki=ki, kf=kf: e.tensor_copy(out=kf[:], in_=ki[:]), [kk + "i"], [kk + "f"])
                if lvl < 3:
                    continue
                P.add("dve", lambda e, kf=kf, icol=icol: e.scalar_tensor_tensor(
                    out=kf[:], in0=POSF[:], scalar=INV[:, icol:icol + 1], in1=kf[:], op0=ALU.mult, op1=ALU.subtract),
                    ["POSF", "CST", kk + "f"], [kk + "f"])
                if lvl < 4:
                    continue
                _sb = _os.environ.get("K_SINB", "")
                if _sb == "zero":
                    P.add("act", lambda e, kf=kf, tile_=tile_, off=off: e.activation(
                        out=tile_[:], in_=kf[:], func=AF.Sin, scale=2 * math.pi), [kk + "f"], [(tk, which)])
                elif _sb == "noscale":
                    P.add("act", lambda e, kf=kf, tile_=tile_, off=off: e.activation(
                        out=tile_[:], in_=kf[:], func=AF.Sin), [kk + "f"], [(tk, which)])
                elif off == 0.0:
                    P.add("act", lambda e, kf=kf, tile_=tile_, off=off: e.activation(
                        out=tile_[:], in_=kf[:], func=AF.Sin, scale=2 * math.pi), [kk + "f"], [(tk, which)])
                else:
                    P.add("act", lambda e, kf=kf, tile_=tile_, off=off: e.activation(
                        out=tile_[:], in_=kf[:], func=AF.Sin, scale=2 * math.pi, bias=HALFPI[:]),
                        [kk + "f", "HALFPI"], [(tk, which)])

    def norm_transpose(src_rows, XT, xkey, sub, SS, evacs, psT_pair, junk):
        import os as _os
        nlvl = int(_os.environ.get("K_NTLVL", 9))
        ssk = xkey + "ss"
        P.add("act", lambda e: e.activation(out=junk[:], in_=XT[:], func=AF.Square, accum_out=SS[:, 0:1]),
              [xkey], ["junk", ssk])
        P.add("act", lambda e: e.activation(out=SS[:, 1:2], in_=SS[:, 0:1], func=AF.Sqrt, scale=1.0 / D, bias=EPS),
              [ssk], [ssk + "b"])
        P.add("dve", lambda e: e.reciprocal(out=SS[:, 2:3], in_=SS[:, 1:2]), [ssk + "b"], [ssk + "c"])
        P.add("dve", lambda e: e.tensor_scalar(out=XT[:], in0=XT[:], scalar1=SS[:, 2:3], scalar2=None, op0=ALU.mult),
              [xkey, ssk + "c"], [xkey])
        if nlvl < 2:
            return
        for half in range(2):
            pt, pk = psT_pair[half]
            for q in range(4):
                kc = half * 4 + q
                P.add("pe", lambda e, pt=pt, q=q, kc=kc: e.transpose(
                    out=pt[:, q * 128:(q + 1) * 128], in_=XT[:, kc * 128:(kc + 1) * 128], identity=IDF),
                    [xkey, "CST"], [pk])
            for q in range(4):
                kc = half * 4 + q
                if nlvl >= 3:
                    evacs(kc, pt[:, q * 128:(q + 1) * 128], pk)

    with ExitStack() as es:
        W1 = es.enter_context(nc.sbuf_tensor("p1_w", [128, 8, WA_COLS], BF16))
        STG = es.enter_context(nc.sbuf_tensor("p1_stg", [128, 2, WA_COLS // 4], F32))
        X1 = es.enter_context(nc.sbuf_tensor("p1_x", [128, 2, D], F32))
        JUNK = es.enter_context(nc.sbuf_tensor("p1_junk", [128, D], BF16))
        SS1 = es.enter_context(nc.sbuf_tensor("p1_ss", [128, 2, 4], F32))
        HN = es.enter_context(nc.sbuf_tensor("p1_hn", [128, 2, 8, 512], BF16))
        POSI = es.enter_context(nc.sbuf_tensor("p1_posi", [128, 512], I32))
        POSF = es.enter_context(nc.sbuf_tensor("p1_posf", [128, 512], F32))
        TAB = es.enter_context(nc.sbuf_tensor("p1_tab", [128, 4, 512], F32))
        TKI = es.enter_context(nc.sbuf_tensor("p1_ki", [128, 1, 512], I32))
        TKF = es.enter_context(nc.sbuf_tensor("p1_kf", [128, 1, 512], F32))
        STQ = es.enter_context(nc.sbuf_tensor("p1_sq", [128, 2, 4, 8, 128], BF16))
        RO = es.enter_context(nc.sbuf_tensor("p1_ro", [128, 4, 512], BF16))
        RT = es.enter_context(nc.sbuf_tensor("p1_rt", [128, 4, 512], F32))
        VST = es.enter_context(nc.sbuf_tensor("p1_v", [128, 2, 4, 128], BF16))
        KLST = es.enter_context(nc.sbuf_tensor("p1_kl", [128, 2, 512], BF16))
        WIST = es.enter_context(nc.sbuf_tensor("p1_wi", [128, 2, 4, 8], F32))
        KVS = es.enter_context(nc.sbuf_tensor("p1_kv", [128, 8], F32))
        stg = Rot([(STG[:, i, :], "stg%d" % i) for i in range(2)])
        wkeys = load_weights(WA, W1, WA_COLS, stg, "W1")
        allw = [k for kc in range(8) for k in wkeys[kc]]
        tabs = [(TAB[:, 0, :], TAB[:, 1, :], "tab32"), (TAB[:, 2, :], TAB[:, 3, :], "tab64")]
        tmps = Rot([(TKI[:, i, :], TKF[:, i, :], "tk%d" % i) for i in range(1)])
        fm = Rot([(ps[i], "ps%d" % i) for i in range(4)])
        psT_pair = [(ps[4], "ps4"), (ps[5], "ps5")]
        psTok = (ps[6], "ps6")
        psVT = ps[7][:].bitcast(BF16)
        stq = Rot([(STQ[:, i], "stq%d" % i) for i in range(2)])
        ro = Rot([(RO[:, i, :], "ro%d" % i) for i in range(4)])
        rt = Rot([(RT[:, i, :], "rt%d" % i) for i in range(4)])
        import os as _os
        ntile = int(_os.environ.get('K_NT', NB * S // 512))
        _skip = _os.environ.get('K_SKIP', '')
        def nt1(tt):
            b = tt // 8
            t0 = (tt % 8) * 512
            hn = HN[:, tt % 2]
            hk = "hn%d" % (tt % 2)
            for sub in range(4):
                xi = (tt * 4 + sub) % 2
                XT = X1[:, xi, :]
                xkey = "x1_%d" % xi
                r0 = b * S + t0 + sub * 128
                P.dma("sp", XT, x[r0:r0 + 128, :], [], [xkey], "lx%d" % xi)

                def evacs(kc, pap, pk, sub=sub, b=b, hn=hn, hk=hk):
                    dst = hn[:, kc, sub * 128:(sub + 1) * 128]
                    _ev = _os.environ.get('K_EV', '')
                    if (kc < 4 and _ev != 'dve') or _ev == 'act':
                        P.add("act", lambda e: e.activation(
                            out=dst, in_=pap, func=AF.Identity, scale=ASCL[0][:, kc * NB + b:kc * NB + b + 1],
                            bias=ASFT[0][:, kc * NB + b:kc * NB + b + 1]), [pk, "ASCL0", "ASFT0"], [(hk, kc)])
                    else:
                        P.add("dve", lambda e: e.tensor_scalar(
                            out=dst, in0=pap, scalar1=ASCL[0][:, kc * NB + b:kc * NB + b + 1],
                            scalar2=ASFT[0][:, kc * NB + b:kc * NB + b + 1], op0=ALU.mult, op1=ALU.add),
                            [pk, "ASCL0", "ASFT0"], [(hk, kc)])
                if 'nt' not in _skip:
                    norm_transpose(None, XT, xkey, sub, SS1[:, xi, :], evacs, psT_pair, JUNK)

        if ntile > 0:
            nt1(0)
        for tt in range(ntile):
            b = tt // 8
            t0 = (tt % 8) * 512
            qb0 = t0 // 128
            hn = HN[:, tt % 2]
            hk = "hn%d" % (tt % 2)
            P.dma("sp", POSI[:], pos[b:b + 1, t0:t0 + 512].partition_broadcast(128), [], ["POSI"], "POSI")
            P.add("pool", lambda e: e.tensor_copy(out=POSF[:], in_=POSI[:]), ["POSI"], ["POSF"])
            if 'tab' not in _skip:
                rope_tables(b, t0, [(0, 0), (1, 1)], POSF, tabs, tmps)
            vst = VST[:, tt % 2]; vk = "vst%d" % (tt % 2)
            klst = KLST[:, tt % 2, :]; klk = "klst%d" % (tt % 2)
            wist = WIST[:, tt % 2]; wik = "wist%d" % (tt % 2)
            hkeys = [(hk, kc) for kc in range(8)]
            for sub in (range(0) if 'tok' in _skip else range(4)):
                pt, pk = psTok
                for kc in range(8):
                    P.add("pe", lambda e, kc=kc, sub=sub, pt=pt, hn=hn: e.matmul(
                        pt[:, 0:136], lhsT=hn[:, kc, sub * 128:(sub + 1) * 128], rhs=W1[:, kc, 5632:5768],
                        start=(kc == 0), stop=(kc == 7)), [(hk, kc)] + wkeys[kc], [pk])
                P.add("act", lambda e, pt=pt: e.activation(out=JUNK[:, 0:128], in_=pt[:, 0:128], func=AF.Square,
                                                           accum_out=KVS[:, 0:1]), [pk], ["junk", "kvs0"])
                P.add("act", lambda e: e.activation(out=KVS[:, 1:2], in_=KVS[:, 0:1], func=AF.Sqrt, scale=1.0 / 128, bias=EPS),
                      ["kvs0"], ["kvs1"])
                P.add("dve", lambda e: e.reciprocal(out=KVS[:, 2:3], in_=KVS[:, 1:2]), ["kvs1"], ["kvs2"])
                P.add("dve", lambda e, pt=pt, sub=sub, vst=vst: e.scalar_tensor_tensor(
                    out=vst[:, sub, :], in0=pt[:, 0:128], scalar=KVS[:, 2:3], in1=AKVG[:], op0=ALU.mult, op1=ALU.mult),
                    [pk, "kvs2", "AKVG"], [(vk, sub)])
                P.add("act", lambda e, pt=pt, sub=sub, wist=wist: e.mul(out=wist[:, sub, :], in_=pt[:, 128:136], mul=8.0 ** -0.5),
                      [pk], [(wik, sub)])
                P.add("pe", lambda e, sub=sub, vst=vst: e.transpose(out=psVT[:, sub * 128:(sub + 1) * 128], in_=vst[:, sub, :],
                                                                    identity=IDB[:]), [(vk, sub), "IDB"], ["psvt"])
            if 'tail' not in _skip:
                P.add("act", lambda e, klst=klst: e.copy(out=klst, in_=psVT[:, 0:512]), ["psvt"], [klk])
                P.dma("sp", VA[b, t0:t0 + 512, :].rearrange("(s p) c -> p s c", p=128), vst[:], [(vk, s_) for s_ in range(4)], [], vk)
                P.dma("sp", WI[b, t0:t0 + 512, :].rearrange("(s p) c -> p s c", p=128), wist[:], [(wik, s_) for s_ in range(4)], [], wik)
                P.dma("sp", KL[b, :, t0:t0 + 512], klst, [klk], [], klk)
            if debug and tt == 0:
                P.dma("sp", dbg["d_KL"][:, 0:512], klst, [klk], [], klk)

            def fm_chunk(ci, hn=hn, hk=hk):
                pt, pk = fm.next()
                for kc in range(8):
                    P.add("pe", lambda e, kc=kc, pt=pt, ci=ci, hn=hn: e.matmul(
                        pt[:], lhsT=W1[:, kc, ci * 128:(ci + 1) * 128], rhs=hn[:, kc, :], start=(kc == 0), stop=(kc == 7)),
                        [(hk, kc)] + wkeys[kc], [pk])
                return pt, pk
            for grp, func, dst in (() if 'fm' in _skip else ((0, None, QL), (1, AF.Silu, GT))):
              for hg in range(2):
                sq, sqk = stq.next()
                for hl in range(8):
                    h = hg * 8 + hl
                    pt, pk = fm_chunk(grp * 16 + h)
                    o = sq[:, :, hl, :]
                    i_ = pt[:].rearrange("p (a q) -> p a q", q=128)
                    if func is None:
                        if h % 2 == 0:
                            P.add("act", lambda e, o=o, i_=i_: e.copy(out=o, in_=i_), [pk], [(sqk, hl)])
                        else:
                            P.add("dve", lambda e, o=o, i_=i_: e.tensor_copy(out=o, in_=i_), [pk], [(sqk, hl)])
                    else:
                        P.add("act", lambda e, o=o, i_=i_: e.activation(out=o, in_=i_, func=AF.Silu), [pk], [(sqk, hl)])
                P.dma("sp", dst[b, qb0:qb0 + 4, :, hg * 1024:(hg + 1) * 1024].rearrange("a p f -> p a f"),
                      sq.rearrange("p a h q -> p a (h q)"), [(sqk, hl) for hl in range(8)], [], sqk)
                if debug and tt == 0 and grp == 0:
                    P.dma("sp", dbg["d_QL"][:, hg * 1024:(hg + 1) * 1024], sq[:, 0].rearrange("p h q -> p (h q)"),
                          [(sqk, hl) for hl in range(8)], [], sqk)
            if tt + 1 < ntile:
                nt1(tt + 1)
            pairs = [(32, 0, "qr", 0), (34, 0, "qr", 1), (36, 1, "qi", 0), (38, 1, "qi", 1), (40, 0, "kr", 0), (42, 1, "ki", 0)]
            for (c0, ti, kind, half) in ([] if 'rope' in _skip else pairs):
                ct, st_, tk = tabs[ti]
                p1, k1 = fm_chunk(c0)
                p2, k2 = fm_chunk(c0 + 1)
                ta, tak = rt.next(); tb, tbk = rt.next()
                oa, oak = ro.next(); ob, obk = ro.next()
                P.add("dve", lambda e, ta=ta, p1=p1, ct=ct: e.tensor_tensor(out=ta, in0=p1[:], in1=ct, op=ALU.mult), [k1, (tk, 0)], [tak])
                P.add("dve", lambda e, tb=tb, p2=p2, st_=st_: e.tensor_tensor(out=tb, in0=p2[:], in1=st_, op=ALU.mult), [k2, (tk, 1)], [tbk])
                P.add("dve", lambda e, oa=oa, ta=ta, tb=tb: e.tensor_tensor(out=oa, in0=ta, in1=tb, op=ALU.subtract), [tak, tbk], [oak])
                tc_, tck = rt.next(); td, tdk = rt.next()
                P.add("dve", lambda e, tc_=tc_, p2=p2, ct=ct: e.tensor_tensor(out=tc_, in0=p2[:], in1=ct, op=ALU.mult), [k2, (tk, 0)], [tck])
                P.add("dve", lambda e, td=td, p1=p1, st_=st_: e.tensor_tensor(out=td, in0=p1[:], in1=st_, op=ALU.mult), [k1, (tk, 1)], [tdk])
                P.add("dve", lambda e, ob=ob, tc_=tc_, td=td: e.tensor_tensor(out=ob, in0=tc_, in1=td, op=ALU.add), [tck, tdk], [obk])
                if kind == "qr":
                    P.dma("sp", QR[b, qb0:qb0 + 4, half].rearrange("a p q -> p a q"), oa.rearrange("p (a q) -> p a q", q=128), [oak], [], oak)
                    P.dma("sp", QR2[b, qb0:qb0 + 4, half].rearrange("a p q -> p a q"), ob.rearrange("p (a q) -> p a q", q=128), [obk], [], obk)
                elif kind == "qi":
                    P.dma("sp", QI[b, qb0:qb0 + 4, half].rearrange("a p q -> p a q"), oa.rearrange("p (a q) -> p a q", q=128), [oak], [], oak)
                    P.dma("sp", QI2[b, qb0:qb0 + 4, half].rearrange("a p q -> p a q"), ob.rearrange("p (a q) -> p a q", q=128), [obk], [], obk)
                elif kind == "kr":
                    P.dma("sp", KR1[b, :, t0:t0 + 512], oa, [oak], [], oak)
                    P.dma("sp", KR2[b, :, t0:t0 + 512], ob, [obk], [], obk)
                else:
                    P.dma("sp", KI1[b, :, t0:t0 + 512], oa, [oak], [], oak)
                    P.dma("sp", KI2[b, :, t0:t0 + 512], ob, [obk], [], obk)
        P.emit()

    if stop_after == 1:
        return nc, dbg, P

    import os as _os
    SCALE_A = (128 + 32) ** -0.5
    BIGM = 30000.0
    with ExitStack() as es:
        def T(name, shape, dt=F32):
            return es.enter_context(nc.sbuf_tensor(name, list(shape), dt))
        WOA = T("p2_woa", [128, 16, D], BF16)
        WUV = T("p2_wuv", [128, 16 * 128], BF16)
        KIs = T("p2_ki", [128, S], BF16)
        KLs = T("p2_kl", [128, S], BF16)
        KRs = T("p2_kr", [128, S], BF16)
        Vs = T("p2_v", [128, NQB, 128], BF16)
        QIq = T("p2_qi", [128, 2, 4, 128], BF16)
        WIq = T("p2_wi", [128, 2, 8], F32)
        WAB = T("p2_wab", [128, 2, 16], F32)
        QLq = T("p2_ql", [128, 2, 16, 128], BF16)
        QRq = T("p2_qr", [128, 2, 16, 128], BF16)
        GTq = T("p2_gt", [128, 16, 128], BF16)
        Xq = T("p2_x", [128, D], F32)
        IS = T("p2_is", [128, S], F32)
        MS = T("p2_ms", [128, 2, S], BF16)
        MT = T("p2_mt", [128, 2, NQB, 128], BF16)
        TMPI = T("p2_tmpi", [128, 3, 512], F32)
        PT = T("p2_pt", [128, 4, 512], BF16)
        RD = T("p2_rd", [128, 1024], F32)
        ON = T("p2_on", [128, 1024], BF16)
        Y = T("p2_y", [128, 16, 128], BF16)
        H1q = T("p2_h1", [128, D], F32)
        OC = T("p2_oc", [128, 2, 512], F32)
        TMPH = T("p2_tmph", [128, D], F32)
        BS = T("p2_bs", [128, 9 + NIT], F32)
        NEG30 = T("p2_neg", [128, 1], F32)
        TS = T("p2_ts", [128, NIT], F32)

        P.op("dve", "memset", [], ["NEG30"], NEG30[:], -BIGM)
        P.op("dve", "memset", [], ["KRs"], KRs[:], 0.0)
        P.op("pool", "memset", [], ["QRq0", "QRq1"], QRq[:], 0.0)
        for h in range(16):
            st = IS[:, (h % 2) * 1024:(h % 2 + 1) * 1024]
            sk2 = [("IS", 2 * (h % 2)), ("IS", 2 * (h % 2) + 1)]
            P.dma("sp", st, woa[h * 128:(h + 1) * 128, :], [], sk2, "stg2_%d" % (h % 2))
            if h % 2 == 0:
                P.op("dve", "tensor_copy", sk2, [("WOA", h)], out=WOA[:, h, :], in_=st)
            else:
                P.op("act", "copy", sk2, [("WOA", h)], out=WOA[:, h, :], in_=st)
        for i in range(2):
            st = IS[:, i * 1024:(i + 1) * 1024]
            sk2 = [("IS", 2 * i), ("IS", 2 * i + 1)]
            P.dma("sp", st, wuv[:, i * 1024:(i + 1) * 1024], [], sk2, "stg2_%d" % i)
            P.op("dve", "tensor_copy", sk2, [("WUV", i)], out=WUV[:, i * 1024:(i + 1) * 1024], in_=st)
        woa_keys = [("WOA", h) for h in range(16)]
        wpool = Rot([(ps[i], "ps%d" % i) for i in range(4)])
        psO = [(ps[4], "ps4"), (ps[5], "ps5")]
        psD = [(ps[6], "ps6"), (ps[7], "ps7")]
        tmpi = Rot([(TMPI[:, i, :], "tmpi%d" % i) for i in range(3)])
        ptr = Rot([(PT[:, i, :], "pt%d" % i) for i in range(4)])
        ocr = Rot([(OC[:, i, :], "oc%d" % i) for i in range(2)])
        nblk = int(_os.environ.get("K_NBLK", NB * NQB))
        dbg_qb = int(_os.environ.get("K_DBGQB", 3))
        def idx_part(blk):
            b = blk // NQB
            qb = blk % NQB
            sl = blk % 2
            nk = (qb + 1) * 128
            nkc = qb + 1
            if qb == 0:
                ki1 = KI1[b].rearrange("(i r) t -> r i t", r=4)[0]
                ki2 = KI2[b].rearrange("(i r) t -> r i t", r=4)[0]
                P.dma("sp", KIs[0:32, :], ki1, [], ["KIs"], "KIs")
                P.dma("sp", KIs[32:64, :], ki2, [], ["KIs"], "KIs")
                P.dma("sp", KIs[64:96, :], ki1, [], ["KIs"], "KIs")
                P.dma("sp", KIs[96:128, :], ki2, [], ["KIs"], "KIs")
            qik = "QIq%d" % sl
            for h2 in range(2):
                for xp, src in ((0, QI), (1, QI2)):
                    for half in range(2):
                        sv = src[b, qb, half].rearrange("(i pp h2) q -> h2 i pp q", pp=2, h2=2)[h2]
                        dv = QIq[h2 * 64 + xp * 32:h2 * 64 + xp * 32 + 32, sl, half * 2:half * 2 + 2, :]
                        P.dma("sp", dv, sv, [], [qik], qik)
            P.dma("sp", WIq[:, sl, :], WI[b, qb * 128:(qb + 1) * 128, :], [], ["WIq%d" % sl], "WIq%d" % sl)
            P.dma("sp", QLq[:, sl].rearrange("p h q -> p (h q)"), QL[b, qb], [], ["QLq%d" % sl], "QLq%d" % sl)
            for half in range(2):
                P.dma("sp", QRq[0:16, sl, half * 8:(half + 1) * 8, :], QR[b, qb, half].rearrange("(i hh) q -> i hh q", hh=8),
                      [], ["QRq%d" % sl], "QRq%d" % sl)
                P.dma("sp", QRq[16:32, sl, half * 8:(half + 1) * 8, :], QR2[b, qb, half].rearrange("(i hh) q -> i hh q", hh=8),
                      [], ["QRq%d" % sl], "QRq%d" % sl)
            wabk = "WAB%d" % sl
            P.op("act", "activation", ["WIq%d" % sl], [wabk + "a"], out=WAB[:, sl, 0:8], in_=WIq[:, sl, :], func=AF.Abs)
            P.op("act", "activation", ["WIq%d" % sl], [wabk + "s"], out=WAB[:, sl, 8:16], in_=WIq[:, sl, :], func=AF.Sign)
            nc5 = (nk + 511) // 512
            iskeys = [("IS", c5) for c5 in range(nc5)]
            for c5 in range(nc5):
                w = min(512, nk - c5 * 512)
                cs = slice(c5 * 512, c5 * 512 + w)
                for h in range(8):
                    pt_, pk = wpool.next()
                    pb = (h % 2) * 64
                    P.op("pe", "matmul", [qik, "KIs"], [pk], pt_[:, 0:w], lhsT=QIq[pb:pb + 64, sl, h // 2, :], rhs=KIs[pb:pb + 64, cs],
                         start=True, stop=True)
                    tm, tmk = tmpi.next()
                    P.op("act", "activation", [pk, wabk + "a"], [tmk], out=tm[:, 0:w], in_=pt_[:, 0:w], func=AF.Relu, scale=WAB[:, sl, h:h + 1])
                    if h == 0:
                        P.op("dve", "tensor_scalar", [tmk, wabk + "s"], [("IS", c5)], out=IS[:, cs], in0=tm[:, 0:w],
                             scalar1=WAB[:, sl, 8:9], scalar2=None, op0=ALU.mult)
                    else:
                        P.op("dve", "scalar_tensor_tensor", [tmk, wabk + "s", ("IS", c5)], [("IS", c5)], out=IS[:, cs], in0=tm[:, 0:w],
                             scalar=WAB[:, sl, 8 + h:9 + h], in1=IS[:, cs], op0=ALU.mult, op1=ALU.add)
            P.op("dve", "tensor_reduce", iskeys, ["bs_mx"], out=BS[:, 0:1], in_=IS[:, 0:nk], axis=AX.X, op=ALU.max)
            P.op("dve", "tensor_reduce", iskeys, ["bs_mn"], out=BS[:, 1:2], in_=IS[:, 0:nk], axis=AX.X, op=ALU.min)
            P.op("dve", "tensor_tensor", ["bs_mx", "bs_mn"], ["bs_w0"], out=BS[:, 2:3], in0=BS[:, 0:1], in1=BS[:, 1:2], op=ALU.subtract)
            P.op("dve", "tensor_scalar", ["bs_w0", "CST"], ["bs_steps"], out=BS[:, 8:9 + NIT], in0=POW2, scalar1=BS[:, 2:3], scalar2=None, op0=ALU.mult)
            P.op("dve", "tensor_copy", ["bs_mn"], ["bs_lo"], out=BS[:, 3:4], in_=BS[:, 1:2])
            dk = ("IS", (nk - 128) // 512)
            P.op("dve", "tensor_tensor", [dk, "CST"], [dk], out=IS[:, nk - 128:nk], in0=IS[:, nk - 128:nk], in1=CAUS, op=ALU.add)
            P.op("dve", "tensor_tensor", ["bs_lo", "bs_steps"], ["bs_mid"], out=BS[:, 4:5], in0=BS[:, 3:4], in1=BS[:, 8:9], op=ALU.add)
            for it in range(NIT):
                P.op("dve", "tensor_scalar", iskeys + ["bs_mid"], ["MS%d" % sl, "bs_cnt"], out=MS[:, sl, 0:nk], in0=IS[:, 0:nk], scalar1=BS[:, 4:5],
                     scalar2=0.0, op0=ALU.is_ge, op1=ALU.add, accum_out=BS[:, 5:6])
                P.op("dve", "scalar_tensor_tensor", ["bs_cnt", "bs_steps"], [("bs_t", it)], out=TS[:, it:it + 1], in0=BS[:, 5:6], scalar=TOPK - 0.5,
                     in1=BS[:, 8 + it:9 + it], op0=ALU.is_ge, op1=ALU.mult)
                if it < NIT - 1:
                    P.op("dve", "scalar_tensor_tensor", [("bs_t", it), "bs_steps", "bs_mid"], ["bs_mid"], out=BS[:, 4:5], in0=TS[:, it:it + 1],
                         scalar=BS[:, 9 + it:10 + it], in1=BS[:, 4:5], op0=ALU.subtract, op1=ALU.add)
            P.op("dve", "tensor_reduce", [("bs_t", i_) for i_ in range(NIT)], ["bs_ts"], out=BS[:, 6:7], in_=TS[:, 0:NIT], axis=AX.X, op=ALU.add)
            P.op("dve", "tensor_tensor", ["bs_ts", "bs_mn"], ["bs_lo"], out=BS[:, 3:4], in0=BS[:, 6:7], in1=BS[:, 1:2], op=ALU.add)
            P.op("dve", "tensor_scalar", iskeys + ["bs_lo"], ["MS%d" % sl], out=MS[:, sl, 0:nk], in0=IS[:, 0:nk], scalar1=BS[:, 3:4], scalar2=None, op0=ALU.is_ge)
            if debug and b == 0 and qb == dbg_qb:
                P.dma("act", dbg["d_IS"][:, 0:nk], IS[:, 0:nk], iskeys, [], "dbgis")
                P.dma("act", dbg["d_lo"], BS[:, 0:8], ["bs_lo", "bs_cnt", "bs_mx", "bs_mn"], [], "dbglo")

        def att_part(blk):
            b = blk // NQB
            qb = blk % NQB
            sl = blk % 2
            nk = (qb + 1) * 128
            nkc = qb + 1
            r0 = b * S + qb * 128
            if qb == 0:
                P.dma("sp", KLs[:], KL[b], [], ["KLs"], "KLs")
                P.dma("sp", KRs[0:16, :], KR1[b].rearrange("(i r) t -> r i t", r=8)[0], [], ["KRs"], "KRs")
                P.dma("sp", KRs[16:32, :], KR2[b].rearrange("(i r) t -> r i t", r=8)[0], [], ["KRs"], "KRs")
                P.dma("sp", Vs[:], VA[b].rearrange("(c p) d -> p c d", p=128), [], ["Vs"], "Vs")
            P.dma("sp", GTq[:].rearrange("p h q -> p (h q)"), GT[b, qb], [], ["GTq"], "GTq")
            P.dma("sp", Xq[:], x[r0:r0 + 128, :], [], ["Xq"], "Xq")
            for kc0 in range(0, nkc, 4):
                n4 = min(4, nkc - kc0)
                pt_, pk = wpool.next()
                pv = pt_[:].bitcast(BF16)
                for j in range(n4):
                    kc = kc0 + j
                    P.op("pe", "transpose", ["MS%d" % sl, "IDB"], [pk], out=pv[:, j * 128:(j + 1) * 128], in_=MS[:, sl, kc * 128:(kc + 1) * 128], identity=IDB[:])
                P.op("act", "activation", [pk, "NEG30"], [("MT", sl, kc0 // 4)], out=MT[:, sl, kc0:kc0 + n4, :].rearrange("p c q -> p (c q)"),
                     in_=pv[:, 0:n4 * 128], func=AF.Identity, scale=BIGM, bias=NEG30[:])

            for hg in range(2):
                groups = [(kc, g) for kc in range(nkc) for g in range(2)]

                def qk(kc, g):
                    ks = slice(kc * 128, (kc + 1) * 128)
                    h0 = hg * 8 + g * 4
                    pt_, pk = wpool.next()
                    P.op("pe", "matmul", ["KLs", "QLq%d" % sl], [pk], pt_[:], lhsT=KLs[:, ks],
                         rhs=QLq[:, sl, h0:h0 + 4, :].rearrange("p h q -> p (h q)"), start=True, stop=False)
                    P.op("pe", "matmul", ["KRs", "QRq%d" % sl], [pk], pt_[:], lhsT=KRs[:, ks],
                         rhs=QRq[:, sl, h0:h0 + 4, :].rearrange("p h q -> p (h q)"), start=False, stop=False)
                    P.op("pe", "matmul", ["IDB", ("MT", sl, kc // 4)], [pk], pt_[:], lhsT=IDB[:],
                         rhs=MT[:, sl, kc, :].unsqueeze(1).to_broadcast([128, 4, 128]), start=False, stop=True)
                    return pt_, pk
                pend = [qk(*groups[0])]
                if len(groups) > 1:
                    pend.append(qk(*groups[1]))
                for gi, (kc, g) in enumerate(groups):
                    if gi + 2 < len(groups):
                        pend.append(qk(*groups[gi + 2]))
                    pt_, pk = pend.pop(0)
                    pr, prk = ptr.next()
                    P.op("act", "activation", [pk], [prk], out=pr, in_=pt_[:], func=AF.Exp, scale=SCALE_A)
                    P.op("pe", "matmul", ["Vs", prk], [psO[g][1]], psO[g][0][:], lhsT=Vs[:, kc, :], rhs=pr, start=(kc == 0), stop=(kc == nkc - 1))
                    P.op("pe", "matmul", ["ONESB", prk], [psD[g][1]], psD[g][0][:], lhsT=ONESB[:], rhs=pr, start=(kc == 0), stop=(kc == nkc - 1))
                for g in range(2):
                    h0 = hg * 8 + g * 4
                    gs = slice(g * 512, (g + 1) * 512)
                    oc, ock = ocr.next()
                    P.op("act", "activation", [psD[g][1]], [("RD", g)], out=RD[:, gs], in_=psD[g][0][:], func=AF.Ln)
                    P.op("act", "activation", [("RD", g)], [("RD", g)], out=RD[:, gs], in_=RD[:, gs], func=AF.Exp, scale=-1.0)
                    P.op("act", "copy", [psO[g][1]], [ock], out=oc, in_=psO[g][0][:])
                    P.op("pool", "tensor_tensor", [ock, ("RD", g)], [("ON", g)], out=ON[:, gs], in0=oc, in1=RD[:, gs], op=ALU.mult)
                    pt_, pk = wpool.next()
                    for hl in range(4):
                        h = h0 + hl
                        P.op("pe", "matmul", [("WUV", h // 8), ("ON", g)], [pk], pt_[:, hl * 128:(hl + 1) * 128], lhsT=WUV[:, h * 128:(h + 1) * 128],
                             rhs=ON[:, g * 512 + hl * 128:g * 512 + (hl + 1) * 128], start=True, stop=True)
                    oc2, ock2 = ocr.next()
                    P.op("act", "copy", [pk], [ock2], out=oc2, in_=pt_[:])
                    P.op("pool", "tensor_tensor", [ock2, "GTq"], [("Y", h0 // 4)], out=Y[:, h0:h0 + 4, :].rearrange("p h q -> p (h q)"),
                         in0=oc2, in1=GTq[:, h0:h0 + 4, :].rearrange("p h q -> p (h q)"), op=ALU.mult)
            ykeys = [("Y", i) for i in range(4)]
            for nh in range(2):
                pt_, pk = wpool.next()
                for h in range(16):
                    P.op("pe", "matmul", ykeys + [("WOA", h)], [pk], pt_[:], lhsT=Y[:, h, :], rhs=WOA[:, h, nh * 512:(nh + 1) * 512],
                         start=(h == 0), stop=(h == 15))
                P.op("act", "copy", [pk], [("TMPH", nh)], out=TMPH[:, nh * 512:(nh + 1) * 512], in_=pt_[:])
                P.op("pool", "tensor_tensor", [("TMPH", nh), "GBC0"], [("TMPH", nh)], out=TMPH[:, nh * 512:(nh + 1) * 512],
                     in0=TMPH[:, nh * 512:(nh + 1) * 512], in1=GBC[0][:, b * D + nh * 512:b * D + (nh + 1) * 512], op=ALU.mult)
            P.op("pool", "tensor_tensor", [("TMPH", 0), ("TMPH", 1), "Xq"], ["H1q"], out=H1q[:], in0=TMPH[:], in1=Xq[:], op=ALU.add)
            P.dma("act", H1[r0:r0 + 128, :], H1q[:], ["H1q"], [], "H1q")
            if debug:
                P.dma("act", dbg["d_H1"][r0:r0 + 128, :], H1q[:], ["H1q"], [], "H1q")

        idx_part(0)
        for blk in range(nblk):
            if blk + 1 < nblk:
                idx_part(blk + 1)
            att_part(blk)
        P.emit()

    if stop_after == 2:
        return nc, dbg, P

    with ExitStack() as es:
        def T(name, shape, dt=F32):
            return es.enter_context(nc.sbuf_tensor(name, list(shape), dt))
        W3 = T("p3_w", [128, 8, WB_COLS], BF16)
        STG3 = T("p3_stg", [128, 2, WB_COLS // 4], F32)
        X3 = T("p3_x", [128, 2, D], F32)
        JUNK3 = T("p3_junk", [128, D], BF16)
        SS3 = T("p3_ss", [128, 2, 4], F32)
        KVT = T("p3_kvt", [128, 2, 8, 512], BF16)
        HBT = T("p3_hbt", [128, 2, 8, 512], BF16)
        POSI3 = T("p3_posi", [128, 512], I32)
        POSF3 = T("p3_posf", [128, 512], F32)
        TAB3 = T("p3_tab", [128, 2, 512], F32)
        TKI3 = T("p3_ki", [128, 1, 512], I32)
        TKF3 = T("p3_kf", [128, 1, 512], F32)
        RT3 = T("p3_rt", [128, 4, 512], F32)
        RO3 = T("p3_ro", [128, 4, 512], BF16)
        VSTG = T("p3_vst", [128, 2, D], BF16)
        GST = T("p3_gst", [128, 2, 512], BF16)
        stg = Rot([(STG3[:, i, :], "stg3_%d" % i) for i in range(2)])
        wkeys = load_weights(WB, W3, WB_COLS, stg, "W3")
        tabs = [(TAB3[:, 0, :], TAB3[:, 1, :], "tab128")]
        tmps = Rot([(TKI3[:, 0, :], TKF3[:, 0, :], "tk3")])
        fm = Rot([(ps[i], "ps%d" % i) for i in range(4)])
        psT_pair = [(ps[4], "ps4"), (ps[5], "ps5")]
        vps = Rot([(ps[6], "ps6"), (ps[7], "ps7")])
        ro = Rot([(RO3[:, i, :], "ro3_%d" % i) for i in range(4)])
        rt = Rot([(RT3[:, i, :], "rt3_%d" % i) for i in range(4)])
        gst = Rot([(GST[:, i, :], "gst%d" % i) for i in range(2)])
        ntile3 = int(_os.environ.get("K_NT3", NB * S // 512))
        def nt3(tt):
            b = tt // 8
            t0 = (tt % 8) * 512
            kvt = KVT[:, tt % 2]; kvk = "kvt%d" % (tt % 2)
            hbt = HBT[:, tt % 2]; hbk = "hbt%d" % (tt % 2)
            for sub in range(4):
                xi = (tt * 4 + sub) % 2
                XT = X3[:, xi, :]
                xkey = "x3_%d" % xi
                r0 = b * S + t0 + sub * 128
                P.dma("sp", XT, H1[r0:r0 + 128, :], [], [xkey], xkey)

                def evacs(kc, pap, pk, sub=sub, b=b, kvt=kvt, hbt=hbt, kvk=kvk, hbk=hbk):
                    d1 = kvt[:, kc, sub * 128:(sub + 1) * 128]
                    d2 = hbt[:, kc, sub * 128:(sub + 1) * 128]
                    if kc < 4:
                        P.op("act", "activation", [pk, "NRM"], [(kvk, kc)], out=d1, in_=pap, func=AF.Copy, scale=NRM[:, 16 + kc:17 + kc])
                        P.op("act", "activation", [pk, "ASCL1", "ASFT1"], [(hbk, kc)], out=d2, in_=pap, func=AF.Identity,
                             scale=ASCL[1][:, kc * NB + b:kc * NB + b + 1], bias=ASFT[1][:, kc * NB + b:kc * NB + b + 1])
                    else:
                        P.op("dve", "tensor_scalar", [pk, "NRM"], [(kvk, kc)], out=d1, in0=pap, scalar1=NRM[:, 16 + kc:17 + kc], scalar2=None, op0=ALU.mult)
                        P.op("dve", "tensor_scalar", [pk, "ASCL1", "ASFT1"], [(hbk, kc)], out=d2, in0=pap,
                             scalar1=ASCL[1][:, kc * NB + b:kc * NB + b + 1], scalar2=ASFT[1][:, kc * NB + b:kc * NB + b + 1], op0=ALU.mult, op1=ALU.add)
                norm_transpose(None, XT, xkey, sub, SS3[:, xi, :], evacs, psT_pair, JUNK3)


        if ntile3 > 0:
            nt3(0)
        for tt in range(ntile3):
            b = tt // 8
            t0 = (tt % 8) * 512
            kvt = KVT[:, tt % 2]; kvk = "kvt%d" % (tt % 2)
            hbt = HBT[:, tt % 2]; hbk = "hbt%d" % (tt % 2)
            P.dma("sp", POSI3[:], pos[b:b + 1, t0:t0 + 512].partition_broadcast(128), [], ["POSI"], "POSI")
            P.op("pool", "tensor_copy", ["POSI"], ["POSF"], out=POSF3[:], in_=POSI3[:])
            rope_tables(b, t0, [(2, 0)], POSF3, tabs, tmps)
            def fm3(ci, src, sk):
                pt_, pk = fm.next()
                for kc in range(8):
                    P.op("pe", "matmul", [(sk, kc)] + wkeys[kc], [pk], pt_[:], lhsT=W3[:, kc, ci * 128:(ci + 1) * 128], rhs=src[:, kc, :],
                         start=(kc == 0), stop=(kc == 7))
                return pt_, pk
            ct, st_, tk = tabs[0]
            pair_list = [(2 * j, kvt, kvk, KT2[b, j]) for j in range(4)]
            pair_list += [(8 + g * 8 + 2 * j, hbt, hbk, QT2[b, g, j]) for g in range(3) for j in range(4)]
            for (c0, src, sk, dst) in pair_list:
                p1, k1 = fm3(c0, src, sk)
                p2_, k2 = fm3(c0 + 1, src, sk)
                ta, tak = rt.next(); tb, tbk = rt.next()
                oa, oak = ro.next(); ob, obk = ro.next()
                P.op("dve", "tensor_tensor", [k1, (tk, 0)], [tak], out=ta, in0=p1[:], in1=ct, op=ALU.mult)
                P.op("dve", "tensor_tensor", [k2, (tk, 1)], [tbk], out=tb, in0=p2_[:], in1=st_, op=ALU.mult)
                P.op("pool", "tensor_tensor", [tak, tbk], [oak], out=oa, in0=ta, in1=tb, op=ALU.subtract)
                tc_, tck = rt.next(); td, tdk = rt.next()
                P.op("dve", "tensor_tensor", [k2, (tk, 0)], [tck], out=tc_, in0=p2_[:], in1=ct, op=ALU.mult)
                P.op("dve", "tensor_tensor", [k1, (tk, 1)], [tdk], out=td, in0=p1[:], in1=st_, op=ALU.mult)
                P.op("pool", "tensor_tensor", [tck, tdk], [obk], out=ob, in0=tc_, in1=td, op=ALU.add)
                P.dma("sp", dst[0, :, t0:t0 + 512], oa, [oak], [], oak)
                P.dma("sp", dst[1, :, t0:t0 + 512], ob, [obk], [], obk)
                if debug and tt == 0 and c0 == 0:
                    P.dma("sp", dbg["d_KT"][:, 0:512], oa, [oak], [], oak)
            if tt + 1 < ntile3:
                nt3(tt + 1)
            for h in range(8):
                p1, k1 = fm3(32 + h, hbt, hbk)
                g_, gk = gst.next()
                P.op("act", "activation", [k1], [gk], out=g_, in_=p1[:], func=AF.Silu)
                P.dma("sp", GB[b, h, :, t0:t0 + 512], g_, [gk], [], gk)
            for sub in range(4):
                vs_ = VSTG[:, sub % 2, :]
                vk_ = "vstg%d" % (sub % 2)
                for nh in range(2):
                    pt_, pk = vps.next()
                    for kc in range(8):
                        P.op("pe", "matmul", [(kvk, kc)] + wkeys[kc], [pk], pt_[:], lhsT=kvt[:, kc, sub * 128:(sub + 1) * 128],
                             rhs=W3[:, kc, 5120 + nh * 512:5120 + (nh + 1) * 512], start=(kc == 0), stop=(kc == 7))
                    if nh == 0:
                        P.op("act", "copy", [pk], [(vk_, nh)], out=vs_[:, nh * 512:(nh + 1) * 512], in_=pt_[:])
                    else:
                        P.op("dve", "tensor_copy", [pk], [(vk_, nh)], out=vs_[:, nh * 512:(nh + 1) * 512], in_=pt_[:])
                r0 = t0 + sub * 128
                P.dma("sp", VB[b, r0:r0 + 128, :], vs_, [(vk_, 0), (vk_, 1)], [], vk_)
        P.emit()
    if stop_after == 3:
        return nc, dbg, P

    SCALE_B = 128 ** -0.5
    with ExitStack() as es:
        def T(name, shape, dt=F32):
            return es.enter_context(nc.sbuf_tensor(name, list(shape), dt))
        KTh = T("p4_k", [128, 2, S], BF16)
        QTh = T("p4_q", [128, 2, 3, S], BF16)
        Vh = T("p4_v", [128, 2, 3, NQB, 128], BF16)
        GBh = T("p4_g", [128, 2, S], BF16)
        OD = T("p4_od", [128, 2, S], F32)
        PTb = T("p4_pt", [128, 4, 256], BF16)
        YTo = T("p4_y", [128, S], BF16)
        NEGMB = T("p4_negm", [128, 256], BF16)
        P.op("dve", "tensor_scalar", ["CST"], ["NEGMB"], out=NEGMB[:], in0=CST[:, 128:384], scalar1=30000.0, scalar2=-30000.0, op0=ALU.mult, op1=ALU.add)
        sp_ = Rot([(ps[i], "ps%d" % i) for i in range(4)])
        op_ = Rot([(ps[i], "ps%d" % i) for i in range(4, 8)])
        ptb = Rot([(PTb[:, i, :], "ptb%d" % i) for i in range(4)])
        nhead4 = int(_os.environ.get("K_NH4", NB * 8))
        for idx in range(nhead4):
            b = idx // 8
            h = idx % 8
            sl = idx % 2
            j = h // 2
            hh = h % 2
            kk_ = "KTh%d" % sl; qk_ = "QTh%d" % sl; vk_ = "Vh%d" % sl; gk_ = "GBh%d" % sl
            for xx in range(2):
                P.dma("sp", KTh[xx * 64:(xx + 1) * 64, sl, :], KT2[b, j, xx, hh * 64:(hh + 1) * 64, :], [], [kk_], kk_)
                for g in range(3):
                    P.dma("sp", QTh[xx * 64:(xx + 1) * 64, sl, g, :], QT2[b, g, j, xx, hh * 64:(hh + 1) * 64, :], [], [qk_], qk_)
            for g, d in enumerate((1, 4, 16)):
                nch = NQB // d
                vv = VB[b, :, h * 128:(h + 1) * 128].rearrange("(c a r) f -> r a c f", a=128, r=d)
                for r in range(d):
                    P.dma("sp", Vh[:, sl, g, r * nch:(r + 1) * nch, :], vv[r], [], [vk_], vk_)
            P.dma("sp", GBh[:, sl, :], GB[b, h], [], [gk_], gk_)
            units = [(g, d, r, c) for g, d in enumerate((1, 4, 16)) for r in range(d) for c in range(NQB // d)]

            def qk4(g, d, r, c):
                qv = QTh[:, sl, g, :].rearrange("p (c a r) -> p r c a", a=128, r=d)
                kv_ = KTh[:, sl, :].rearrange("p (c a r) -> p r c a", a=128, r=d)
                pS, pSk = sp_.next()
                if c > 0:
                    P.op("pe", "matmul", [kk_, qk_], [pSk], pS[:, 0:128], lhsT=kv_[:, r, c - 1, :], rhs=qv[:, r, c, :], start=True, stop=False)
                    P.op("pe", "matmul", ["IDB", "NEGMB"], [pSk], pS[:, 0:128], lhsT=IDB[:], rhs=NEGMB[:, 0:128], start=False, stop=True)
                P.op("pe", "matmul", [kk_, qk_], [pSk], pS[:, 128:256], lhsT=kv_[:, r, c, :], rhs=qv[:, r, c, :], start=True, stop=False)
                P.op("pe", "matmul", ["IDB", "NEGMB"], [pSk], pS[:, 128:256], lhsT=IDB[:], rhs=NEGMB[:, 128:256], start=False, stop=True)
                return pS, pSk
            pend = [qk4(*units[0]), qk4(*units[1])]
            for ui, (g, d, r, c) in enumerate(units):
                if ui + 2 < len(units):
                    pend.append(qk4(*units[ui + 2]))
                pS, pSk = pend.pop(0)
                nch = NQB // d
                ov = OD[:].rearrange("p x (c a r) -> p r c x a", a=128, r=d)
                ti = r * nch + c
                lo = 0 if c > 0 else 128
                pt_, ptk = ptb.next()
                P.op("act", "activation", [pSk], [ptk], out=pt_[:, lo:256], in_=pS[:, lo:256], func=AF.Exp, scale=SCALE_B)
                pO, pOk = op_.next()
                for half, lhs_of in ((0, lambda t: Vh[:, sl, g, t, :]), (1, lambda t: ONESB[:])):
                    oc = slice(half * 128, (half + 1) * 128)
                    if c > 0:
                        P.op("pe", "matmul", [vk_, "ONESB", ptk], [pOk], pO[:, oc], lhsT=lhs_of(ti - 1), rhs=pt_[:, 0:128], start=True, stop=False)
                        P.op("pe", "matmul", [vk_, "ONESB", ptk], [pOk], pO[:, oc], lhsT=lhs_of(ti), rhs=pt_[:, 128:256], start=False, stop=True)
                    else:
                        P.op("pe", "matmul", [vk_, "ONESB", ptk], [pOk], pO[:, oc], lhsT=lhs_of(ti), rhs=pt_[:, 128:256], start=True, stop=True)
                dst = ov[:, r, c, :, :]
                src = pO[:, 0:256].rearrange("p (x a) -> p x a", a=128)
                if g == 0:
                    P.op("act", "copy", [pOk], [("OD", c // 4)], out=dst, in_=src)
                else:
                    odk = [("OD", i) for i in range(8)]
                    P.op("dve", "tensor_tensor", [pOk] + odk, odk, out=dst, in0=src, in1=dst, op=ALU.add)
            odk = [("OD", i) for i in range(8)]
            P.op("act", "activation", odk, odk, out=OD[:, 1, :], in_=OD[:, 1, :], func=AF.Ln)
            P.op("act", "activation", odk, odk, out=OD[:, 1, :], in_=OD[:, 1, :], func=AF.Exp, scale=-1.0)
            P.op("pool", "tensor_tensor", odk, odk, out=OD[:, 0, :], in0=OD[:, 0, :], in1=OD[:, 1, :], op=ALU.mult)
            P.op("dve", "tensor_tensor", odk + [gk_], ["YTo"], out=YTo[:], in0=OD[:, 0, :], in1=GBh[:, sl, :], op=ALU.mult)
            P.dma("act", YT[b, h], YTo[:], ["YTo"], [], "YTo")
            if debug and idx == 0:
                P.dma("act", dbg["d_YT"], YTo[:], ["YTo"], [], "YTo")
        P.emit()
    if stop_after == 4:
        return nc, dbg, P

    with ExitStack() as es:
        def T(name, shape, dt=F32):
            return es.enter_context(nc.sbuf_tensor(name, list(shape), dt))
        WOB = T("p5_w", [128, 8, D], BF16)
        STG5 = T("p5_stg", [128, 2, D], F32)
        Yq = T("p5_y", [128, 2, 8, 128], BF16)
        H1t = T("p5_h1", [128, 2, D], F32)
        TMP5 = T("p5_tmp", [128, D], F32)
        H2 = T("p5_h2", [128, D], F32)
        OUTt = T("p5_out", [128, 2, D], F32)
        JUNK5 = T("p5_junk", [128, D], BF16)
        SS5 = T("p5_ss", [128, 4], F32)
        for h in range(8):
            st = STG5[:, h % 2, :]
            P.dma("sp", st, wob[h * 128:(h + 1) * 128, :], [], ["stg5_%d" % (h % 2)], "stg5_%d" % (h % 2))
            if h % 2 == 0:
                P.op("dve", "tensor_copy", ["stg5_0"], [("WOB", h)], out=WOB[:, h, :], in_=st)
            else:
                P.op("act", "copy", ["stg5_1"], [("WOB", h)], out=WOB[:, h, :], in_=st)
        pp = Rot([(ps[i], "ps%d" % i) for i in range(4)])
        ntile5 = int(_os.environ.get("K_NT5", NB * NQB))
        for tt in range(ntile5):
            b = tt // NQB
            qb = tt % NQB
            sl = tt % 2
            r0 = b * S + qb * 128
            yk = "Yq%d" % sl; hk_ = "H1t%d" % sl; ok_ = "OUTt%d" % sl
            P.dma("sp", Yq[:, sl], YT[b, :, :, qb * 128:(qb + 1) * 128].rearrange("h p t -> p h t"), [], [yk], yk)
            P.dma("sp", H1t[:, sl, :], H1[r0:r0 + 128, :], [], [hk_], hk_)
            for nh in range(2):
                pt_, pk = pp.next()
                for h in range(8):
                    P.op("pe", "matmul", [yk, ("WOB", h)], [pk], pt_[:], lhsT=Yq[:, sl, h, :], rhs=WOB[:, h, nh * 512:(nh + 1) * 512],
                         start=(h == 0), stop=(h == 7))
                P.op("dve", "tensor_tensor", [pk, "GBC1"], [("TMP5", nh)], out=TMP5[:, nh * 512:(nh + 1) * 512], in0=pt_[:],
                     in1=GBC[1][:, b * D + nh * 512:b * D + (nh + 1) * 512], op=ALU.mult)
            P.op("pool", "tensor_tensor", [("TMP5", 0), ("TMP5", 1), hk_], ["H2"], out=H2[:], in0=TMP5[:], in1=H1t[:, sl, :], op=ALU.add)
            P.op("act", "activation", ["H2"], ["junk5", "ss5a"], out=JUNK5[:], in_=H2[:], func=AF.Square, accum_out=SS5[:, 0:1])
            P.op("act", "activation", ["ss5a"], ["ss5b"], out=SS5[:, 1:2], in_=SS5[:, 0:1], func=AF.Sqrt, scale=1.0 / D, bias=EPS)
            P.op("dve", "reciprocal", ["ss5b"], ["ss5c"], out=SS5[:, 2:3], in_=SS5[:, 1:2])
            P.op("dve", "scalar_tensor_tensor", ["H2", "ss5c", "FNG"], [ok_], out=OUTt[:, sl, :], in0=H2[:], scalar=SS5[:, 2:3], in1=FNG[:],
                 op0=ALU.mult, op1=ALU.mult)
            P.dma("act", out[r0:r0 + 128, :], OUTt[:, sl, :], [ok_], [], ok_)
        P.emit()

    return nc, dbg, P


def _perm_a():
    idx = []
    idx += list(range(0, 2048))
    idx += list(range(2720, 4768))
    for hb in (0, 8):
        for part in (0, 16):
            idx += [2048 + (hb + (p % 8)) * 32 + part + p // 8 for p in range(128)]
    for hb in (0, 4):
        for part in (0, 32):
            idx += [4768 + (hb + (p % 4)) * 64 + part + p // 4 for p in range(128)]
    for part in (0, 16):
        idx += [2688 + part + p // 8 for p in range(128)]
    for part in (0, 32):
        idx += [5280 + part + p // 4 for p in range(128)]
    idx += list(range(2560, 2688))
    idx += list(range(5344, 5352))
    assert len(idx) == WA_COLS
    return np.array(idx)


def _consts():
    c = np.zeros((128, 640 + 4 + NIT), np.float32)
    c[:, 0:128] = np.eye(128, dtype=np.float32)
    a = np.arange(128)[:, None]
    q = np.arange(128)[None, :]
    c[:, 128:256] = (a >= q).astype(np.float32)
    c[:, 256:384] = (a <= q).astype(np.float32)
    qq = np.arange(128)[:, None]
    kk = np.arange(128)[None, :]
    c[:, 384:512] = np.where(kk <= qq, 0.0, NEG).astype(np.float32)
    p = np.arange(128)
    two_pi = 2 * math.pi
    inv32 = (np.float32(THETA) ** (-(np.arange(0, 32, 2, dtype=np.float32)) / np.float32(32))).astype(np.float32)
    inv64 = (np.float32(THETA) ** (-(np.arange(0, 64, 2, dtype=np.float32)) / np.float32(64))).astype(np.float32)
    inv128 = (np.float32(THETA) ** (-(np.arange(0, 128, 2, dtype=np.float32)) / np.float32(128))).astype(np.float32)
    c[:, 640] = inv32[p // 8].astype(np.float64) / two_pi
    c[:, 641] = inv64[p // 4].astype(np.float64) / two_pi
    c[:, 642] = inv128[p % 64].astype(np.float64) / two_pi
    c[:, 643:644 + NIT] = (0.5 ** np.arange(1, NIT + 2))[None, :]
    return c


def _fm(v, nch):
    return np.ascontiguousarray(np.asarray(v, np.float32).reshape(nch, 128).T)


def prepare_inputs(x, c, positions, a_norm, a_ada_w, a_ada_b, a_w_in, a_kv_norm, a_w_uv, a_w_out,
                   kv_norm, w_kv, b_norm, b_ada_w, b_ada_b, b_w_in, b_w_out, final_norm):
    f = lambda a: np.ascontiguousarray(np.asarray(a, np.float32))
    x = f(x); c = f(c)
    positions = np.ascontiguousarray(np.asarray(positions, np.int32))
    WA = np.ascontiguousarray(f(a_w_in)[0][:, _perm_a()])
    wkv = f(w_kv); bw = f(b_w_in)[0]
    cols = []
    def pair_cols(base):
        out_ = []
        for j in range(4):
            for part in (0, 64):
                out_.append([base + (2 * j + p // 64) * 128 + part + (p % 64) for p in range(128)])
        return out_
    kcols = np.concatenate([wkv[:, ci] for ci in pair_cols(0)], axis=1)
    qcols = np.concatenate([bw[:, ci] for g in range(3) for ci in pair_cols(g * 1024)], axis=1)
    WB = np.ascontiguousarray(np.concatenate([kcols, qcols, bw[:, 3072:4096], wkv[:, 1024:2048]], axis=1))
    assert WB.shape[1] == WB_COLS
    normsT = np.concatenate([_fm(f(a_norm)[0], 8), _fm(f(b_norm)[0], 8), _fm(f(kv_norm), 8)], axis=1)
    ada_bT = np.concatenate([_fm(f(a_ada_b)[0], 24), _fm(f(b_ada_b)[0], 24)], axis=1)
    ada_bg = np.concatenate([f(a_ada_b)[0][2 * D:], f(b_ada_b)[0][2 * D:]])[None, :]
    wuv = np.ascontiguousarray(f(a_w_uv)[0].transpose(1, 0, 2).reshape(128, 16 * 128))
    shared = {
        "normsT": np.ascontiguousarray(normsT), "a_ada_w": f(a_ada_w)[0], "b_ada_w": f(b_ada_w)[0],
        "ada_bT": np.ascontiguousarray(ada_bT), "ada_bg": np.ascontiguousarray(ada_bg),
        "WA": WA, "WB": WB, "akvg": f(a_kv_norm)[0][None, :], "wuv": wuv, "woa": f(a_w_out)[0], "wob": f(b_w_out)[0],
        "fng": f(final_norm)[None, :], "cst": _consts(),
    }
    in_maps = []
    for core in range(NCORES):
        bs = slice(core * NB, (core + 1) * NB)
        cc = c[bs]
        cTl = np.ascontiguousarray(cc.reshape(NB, 8, 128).transpose(2, 1, 0).reshape(128, 8 * NB))
        m = dict(shared)
        m["x"] = np.ascontiguousarray(x[bs].reshape(NB * S, D))
        m["pos"] = np.ascontiguousarray(positions[bs])
        m["cT"] = cTl
        in_maps.append(m)
    return in_maps


def kernel(**inputs):
    in_maps = prepare_inputs(**inputs)
    nc, _, _ = build_program(debug=False)
    res = run_bass_kernel_spmd(nc, in_maps, core_ids=list(range(NCORES)))
    outs = [np.asarray(r["out"]).reshape(NB, S, D) for r in res.results]
    return np.concatenate(outs, axis=0).astype(np.float32)
```

```python
import math
from contextlib import ExitStack
import numpy as np
import concourse.bass as bass
import concourse.mybir as mybir
from concourse.bass_utils import run_bass_kernel_spmd

F32 = mybir.dt.float32
BF16 = mybir.dt.bfloat16
I32 = mybir.dt.int32
AF = mybir.ActivationFunctionType
ALU = mybir.AluOpType
AX = mybir.AxisListType

ENGS = ("pe", "act", "dve", "pool", "sp")

D = 1024
S = 4096
NB = 2
NCORES = 8
NQB = S // 128
EPS = 1e-6
THETA = 10000.0
NIT = 16
TOPK = 256
NEG = -1.0e30
WA_COLS = 5768
WB_COLS = 6144


class Op:
    __slots__ = ("eng", "fn", "deps", "is_dma", "chan", "val", "flag")

    def __init__(self, eng, fn, is_dma=False, chan=None):
        self.eng = eng
        self.fn = fn
        self.deps = ()
        self.is_dma = is_dma
        self.chan = chan
        self.val = 0
        self.flag = False


class Prog:
    def __init__(self, nc):
        self.nc = nc
        self.chan_sem = {}
        self.chan_cnt = {}
        self.free_chan = []
        self.sw_chans = set()
        self.eng_sem = {}
        self.eng_cnt = {e: 0 for e in ENGS}
        self.phase_no = 0
        self.total_ops = 0
        self._reset()

    def _reset(self):
        self.ops = []
        self.last_w = {}
        self.readers = {}

    def add(self, eng, fn, reads=(), writes=(), chan=None):
        is_dma = chan is not None
        op = Op(eng, fn, is_dma, chan)
        psr = [k for k in reads if isinstance(k, str) and k.startswith("ps")]
        if psr:
            reads = [k for k in reads if k not in psr]
            writes = list(writes) + [k for k in psr if k not in writes]
        deps = {}
        for k in reads:
            w = self.last_w.get(k)
            if w is not None:
                deps[id(w)] = w
        for k in writes:
            w = self.last_w.get(k)
            if w is not None:
                deps[id(w)] = w
            for r in self.readers.get(k, ()):
                deps[id(r)] = r
        dl = []
        for d in deps.values():
            if d is op:
                continue
            if (not is_dma) and eng == "pe" and d.eng == "pe" and not d.is_dma:
                continue
            dl.append(d)
        op.deps = dl
        for k in reads:
            self.readers.setdefault(k, []).append(op)
        for k in writes:
            self.last_w[k] = op
            self.readers[k] = []
        self.ops.append(op)
        return op

    def op(self, eng, meth, reads, writes, *args, **kw):
        return self.add(eng, lambda e: getattr(e, meth)(*args, **kw), reads, writes)

    def dma(self, q, out, in_, reads, writes, chan):
        return self.add(q, lambda e: e.dma_start(out=out, in_=in_), reads, writes, chan=chan)

    def emit(self):
        nc = self.nc
        self.phase_no += 1
        ops = self.ops
        for op in ops:
            for d in op.deps:
                d.flag = True
        per_eng = {e: [] for e in ENGS}
        for op in ops:
            per_eng[op.eng].append(op)
        last_compute = {}
        for e in ENGS:
            for op in reversed(per_eng[e]):
                if not op.is_dma:
                    op.flag = True
                    last_compute[e] = op
                    break
        for e in last_compute:
            if e not in self.eng_sem:
                self.eng_sem[e] = nc.alloc_semaphore("eng_%s" % e)
        eng_sem = self.eng_sem
        cnt = self.eng_cnt
        for op in ops:
            if op.is_dma:
                if op.chan not in self.chan_sem:
                    if op.eng == "pool":
                        self.chan_sem[op.chan] = nc.alloc_semaphore("sw%d" % len(self.sw_chans))
                        self.chan_cnt[op.chan] = 0
                        self.sw_chans.add(op.chan)
                    elif self.free_chan:
                        self.chan_sem[op.chan], self.chan_cnt[op.chan] = self.free_chan.pop()
                    else:
                        self.chan_sem[op.chan] = nc.alloc_semaphore("ch%d" % len(self.chan_sem))
                        self.chan_cnt[op.chan] = 0
                self.chan_cnt[op.chan] += 16
                op.val = self.chan_cnt[op.chan]
            elif op.flag:
                cnt[op.eng] += 1
                op.val = cnt[op.eng]
        final_eng = {e: (eng_sem[e], last_compute[e].val) for e in last_compute}
        final_chan = {c: (self.chan_sem[c], self.chan_cnt[c]) for c in self.chan_sem}

        def sem_of(d):
            return self.chan_sem[d.chan] if d.is_dma else eng_sem[d.eng]

        def run(e, engine):
            waited = {}
            for op in per_eng[e]:
                need = {}
                for d in op.deps:
                    s = sem_of(d)
                    k = id(s)
                    if k not in need or need[k][1] < d.val:
                        need[k] = (s, d.val)
                for k, (s, v) in need.items():
                    if waited.get(k, 0) >= v:
                        continue
                    engine.wait_ge(s, v)
                    waited[k] = v
                ins = op.fn(engine)
                if op.is_dma:
                    ins.then_inc(self.chan_sem[op.chan], 16)
                elif op.flag:
                    ins.then_inc(eng_sem[op.eng], 1)
            for e2, (s, v) in final_eng.items():
                if waited.get(id(s), 0) < v:
                    engine.wait_ge(s, v)
            for c, (s, v) in final_chan.items():
                if v > 0 and waited.get(id(s), 0) < v:
                    engine.wait_ge(s, v)

        with nc.Block() as block:
            @block.tensor
            def _(eng):
                run("pe", eng)

            @block.scalar
            def _(eng):
                run("act", eng)

            @block.vector
            def _(eng):
                run("dve", eng)

            @block.gpsimd
            def _(eng):
                run("pool", eng)

            @block.sync
            def _(eng):
                run("sp", eng)
        self.total_ops += len(ops)
        for c in list(self.chan_sem):
            if c in self.sw_chans:
                continue
            self.free_chan.append((self.chan_sem.pop(c), self.chan_cnt.pop(c)))
        self._reset()


class Rot:
    def __init__(self, items):
        self.items = items
        self.i = 0

    def next(self):
        it = self.items[self.i % len(self.items)]
        self.i += 1
        return it


def build_program(debug=False, stop_after=99):
    nc = bass.Bass("TRN2", target_bir_lowering=False)
    P = Prog(nc)

    def din(name, shape, dt=F32):
        return nc.dram_tensor(name, list(shape), dt, kind="ExternalInput").ap()

    def dscr(name, shape, dt=BF16):
        return nc.dram_tensor(name, list(shape), dt).ap()

    x = din("x", [NB * S, D])
    pos = din("pos", [NB, S], I32)
    cT = din("cT", [128, 8 * NB])
    normsT = din("normsT", [128, 24])
    ada_w = [din("a_ada_w", [D, 3 * D]), din("b_ada_w", [D, 3 * D])]
    ada_bT = din("ada_bT", [128, 48])
    ada_bg = din("ada_bg", [1, 2 * D])
    WA = din("WA", [D, WA_COLS])
    WB = din("WB", [D, WB_COLS])
    akvg = din("akvg", [1, 128])
    wuv = din("wuv", [128, 16 * 128])
    woa = din("woa", [2 * D, D])
    wob = din("wob", [D, D])
    fng = din("fng", [1, D])
    cst = din("cst", [128, 640 + 4 + NIT])
    out = nc.dram_tensor("out", [NB * S, D], F32, kind="ExternalOutput").ap()

    QL = dscr("QL", [NB, NQB, 128, 2048])
    GT = dscr("GT", [NB, NQB, 128, 2048])
    QR = dscr("QR", [NB, NQB, 2, 128, 128])
    QI = dscr("QI", [NB, NQB, 2, 128, 128])
    QI2 = dscr("QI2", [NB, NQB, 2, 128, 128])
    QR2 = dscr("QR2", [NB, NQB, 2, 128, 128])
    KR1 = dscr("KR1", [NB, 128, S]); KR2 = dscr("KR2", [NB, 128, S])
    KI1 = dscr("KI1", [NB, 128, S]); KI2 = dscr("KI2", [NB, 128, S])
    VA = dscr("VA", [NB, S, 128])
    KL = dscr("KL", [NB, 128, S])
    WI = dscr("WI", [NB, S, 8], F32)
    H1 = dscr("H1", [NB * S, D], F32)
    KT2 = dscr("KT2", [NB, 4, 2, 128, S])
    QT2 = dscr("QT2", [NB, 3, 4, 2, 128, S])
    GB = dscr("GB", [NB, 8, 128, S])
    VB = dscr("VB", [NB, S, D])
    YT = dscr("YT", [NB, 8, 128, S])

    dbg = {}
    if debug:
        for nm, shp, dt in (("d_QL", [128, 2048], BF16), ("d_H1", [NB * S, D], F32), ("d_mod", [128, 64], F32),
                            ("d_KL", [128, S], BF16), ("d_IS", [128, S], F32), ("d_lo", [128, 8], F32),
                            ("d_YT", [128, S], BF16), ("d_KT", [128, S], BF16)):
            dbg[nm] = nc.dram_tensor(nm, shp, dt, kind="ExternalOutput").ap()

    def sb(name, shape, dt=F32):
        return nc.alloc_sbuf_tensor(name, list(shape), dt)

    CST = sb("CST", [128, 640 + 4 + NIT])
    IDF = CST[:, 0:128]
    MPREV_F = CST[:, 128:256]
    MCUR_F = CST[:, 256:384]
    CAUS = CST[:, 384:512]
    INV = CST[:, 640:643]
    POW2 = CST[:, 643:644 + NIT]
    IDB = sb("IDB", [128, 128], BF16)
    ONESB = sb("ONESB", [128, 128], BF16)
    ONESF = sb("ONESF", [1, 128], F32)
    HALFPI = sb("HALFPI", [128, 1], F32)
    MASKB = sb("MASKB", [128, 256], BF16)
    NRM = sb("NRM", [128, 24])
    ASCL = [sb("ASCL%d" % l, [128, 8 * NB]) for l in range(2)]
    ASFT = [sb("ASFT%d" % l, [128, 8 * NB]) for l in range(2)]
    GBC = [sb("GBC%d" % l, [128, NB * D]) for l in range(2)]
    AKVG = sb("AKVG", [128, 128])
    FNG = sb("FNG", [128, D])

    ps = [nc.alloc_psum_tensor("ps%d" % i, [128, 512], F32) for i in range(8)]

    P.dma("sp", CST[:], cst, [], ["CST"], "l0")
    P.dma("sp", NRM[:], normsT, [], ["NRM"], "l1")
    P.dma("sp", AKVG[:], akvg.partition_broadcast(128), [], ["AKVG"], "l2")
    P.dma("sp", FNG[:], fng.partition_broadcast(128), [], ["FNG"], "l3")
    P.add("dve", lambda e: e.tensor_copy(out=IDB[:], in_=IDF), ["CST"], ["IDB"])
    P.add("dve", lambda e: e.memset(ONESB[:], 1.0), [], ["ONESB"])
    P.add("dve", lambda e: e.memset(ONESF[:], 1.0), [], ["ONESF"])
    P.add("dve", lambda e: e.memset(HALFPI[:], math.pi / 2), [], ["HALFPI"])
    P.add("dve", lambda e: e.tensor_copy(out=MASKB[:], in_=CST[:, 128:384]), ["CST"], ["MASKB"])

    with ExitStack() as es:
        W0 = es.enter_context(nc.sbuf_tensor("p0_w", [128, 8, 3 * D], F32))
        C0 = es.enter_context(nc.sbuf_tensor("p0_c", [128, 8 * NB], F32))
        SC0 = es.enter_context(nc.sbuf_tensor("p0_sc", [128, 8 * NB], F32))
        SCB = es.enter_context(nc.sbuf_tensor("p0_scb", [128, 8 * NB, 128], F32))
        BT0 = es.enter_context(nc.sbuf_tensor("p0_bT", [128, 48], F32))
        BG0 = es.enter_context(nc.sbuf_tensor("p0_bg", [128, 2 * D], F32))
        M0 = es.enter_context(nc.sbuf_tensor("p0_m", [128, 16 * NB], F32))
        P.dma("sp", C0[:], cT, [], ["C0"], "l4")
        P.dma("sp", BT0[:], ada_bT, [], ["BT0"], "l5")
        P.dma("sp", BG0[:], ada_bg.partition_broadcast(128), [], ["BG0"], "l6")
        P.add("act", lambda e: e.activation(out=SC0[:], in_=C0[:], func=AF.Silu), ["C0"], ["SC0"])
        P.add("dve", lambda e: e.tensor_copy(out=SCB[:], in_=SC0[:].unsqueeze(2).to_broadcast([128, 8 * NB, 128])),
              ["SC0"], ["SCB"])
        for l in range(2):
            for kc in range(8):
                P.dma("sp", W0[:, kc, :], ada_w[l][kc * 128:(kc + 1) * 128, :], [], [("W0", kc)], "w%d" % kc)
            mps = ps[0]
            for j in range(16):
                for kc in range(8):
                    P.add("pe", lambda e, j=j, kc=kc: e.matmul(
                        mps[:, j * NB:(j + 1) * NB], lhsT=W0[:, kc, j * 128:(j + 1) * 128],
                        rhs=SC0[:, kc * NB:(kc + 1) * NB], start=(kc == 0), stop=(kc == 7)),
                        [("W0", kc), "SC0"], ["psmps"])
            P.add("dve", lambda e, l=l: e.tensor_tensor(
                out=M0[:].rearrange("p (j b) -> p j b", b=NB), in0=mps[:, 0:16 * NB].rearrange("p (j b) -> p j b", b=NB),
                in1=BT0[:, l * 24:l * 24 + 16].unsqueeze(2).to_broadcast([128, 16, NB]), op=ALU.add),
                ["psmps", "BT0"], ["M0"])
            P.add("dve", lambda e, l=l: e.tensor_copy(out=ASFT[l][:], in_=M0[:, 0:8 * NB]), ["M0"], ["ASFT%d" % l])
            P.add("dve", lambda e, l=l: e.scalar_tensor_tensor(
                out=ASCL[l][:].rearrange("p (j b) -> p j b", b=NB), in0=M0[:, 8 * NB:16 * NB].rearrange("p (j b) -> p j b", b=NB),
                scalar=1.0, in1=NRM[:, l * 8:(l + 1) * 8].unsqueeze(2).to_broadcast([128, 8, NB]),
                op0=ALU.add, op1=ALU.mult), ["M0", "NRM"], ["ASCL%d" % l])
            for b in range(NB):
                for nh in range(2):
                    gps = ps[1 + (b * 2 + nh) % 2]
                    gkey = "psg%d" % ((b * 2 + nh) % 2)
                    for kc in range(8):
                        P.add("pe", lambda e, b=b, nh=nh, kc=kc, gps=gps: e.matmul(
                            gps[:], lhsT=SCB[:, kc * NB + b, :], rhs=W0[:, kc, 2 * D + nh * 512:2 * D + (nh + 1) * 512],
                            start=(kc == 0), stop=(kc == 7)), [("W0", kc), "SCB"], [gkey])
                    P.add("dve", lambda e, l=l, b=b, nh=nh, gps=gps: e.tensor_tensor(
                        out=GBC[l][:, b * D + nh * 512:b * D + (nh + 1) * 512], in0=gps[:],
                        in1=BG0[:, l * D + nh * 512:l * D + (nh + 1) * 512], op=ALU.add), [gkey, "BG0"], ["GBC%d" % l])
        if debug:
            P.dma("pool", dbg["d_mod"][:, 0:16], ASCL[0][:], ["ASCL0"], [], "s0")
            P.dma("pool", dbg["d_mod"][:, 16:32], ASFT[0][:], ["ASFT0"], [], "s0")
            P.dma("pool", dbg["d_mod"][:, 32:48], ASCL[1][:], ["ASCL1"], [], "s0")
            P.dma("pool", dbg["d_mod"][:, 48:64], GBC[0][:, 0:16], ["GBC0"], [], "s0")
        P.emit()
    if stop_after == 0:
        return nc, dbg, P

    def load_weights(Wd, Wsb, ncols, stg, wkey):
        npc = 4
        pw = ncols // npc
        engs = ("dve", "act")
        i = 0
        for kc in range(8):
            for pc in range(npc):
                st, sk = stg.next()
                P.dma("sp", st[:, 0:pw], Wd[kc * 128:(kc + 1) * 128, pc * pw:(pc + 1) * pw], [], [sk], "ws%d" % (i % 2))
                en = engs[i % 2]
                if en == "act":
                    P.add("act", lambda e, st=st, kc=kc, pc=pc: e.copy(out=Wsb[:, kc, pc * pw:(pc + 1) * pw], in_=st[:, 0:pw]),
                          [sk], [(wkey, kc, pc)])
                else:
                    P.add(en, lambda e, st=st, kc=kc, pc=pc: e.tensor_copy(out=Wsb[:, kc, pc * pw:(pc + 1) * pw], in_=st[:, 0:pw]),
                          [sk], [(wkey, kc, pc)])
                i += 1
        return [[(wkey, kc, pc) for pc in range(npc)] for kc in range(8)]

    def rope_tables(b, t0, specs, POSF, tabs, tmps):
        import os as _os
        lvl = int(_os.environ.get("K_TABLVL", 9))
        for (icol, ti) in specs:
            ct, st_, tk = tabs[ti]
            for which, tile_, off in ((0, ct, 0.25), (1, st_, 0.0)):
                ki, kf, kk = tmps.next()
                P.add("pool", lambda e, ki=ki, icol=icol, off=off: e.tensor_scalar(
                    out=ki[:], in0=POSF[:], scalar1=INV[:, icol:icol + 1], scalar2=off, op0=ALU.mult, op1=ALU.add),
                    ["POSF", "CST"], [kk + "i"])
                if lvl < 2:
                    continue
                P.add("pool", lambda e, ki=ki, kf=kf: e.tensor_copy(out=kf[:], in_=ki[:]), [kk + "i"], [kk + "f"])
                if lvl < 3:
                    continue
                P.add("dve", lambda e, kf=kf, icol=icol: e.scalar_tensor_tensor(
                    out=kf[:], in0=POSF[:], scalar=INV[:, icol:icol + 1], in1=kf[:], op0=ALU.mult, op1=ALU.subtract),
                    ["POSF", "CST", kk + "f"], [kk + "f"])
                if lvl < 4:
                    continue
                _sb = _os.environ.get("K_SINB", "")
                if _sb == "zero":
                    P.add("act", lambda e, kf=kf, tile_=tile_, off=off: e.activation(
                        out=tile_[:], in_=kf[:], func=AF.Sin, scale=2 * math.pi), [kk + "f"], [(tk, which)])
                elif _sb == "noscale":
                    P.add("act", lambda e, kf=kf, tile_=tile_, off=off: e.activation(
                        out=tile_[:], in_=kf[:], func=AF.Sin), [kk + "f"], [(tk, which)])
                elif off == 0.0:
                    P.add("act", lambda e, kf=kf, tile_=tile_, off=off: e.activation(
                        out=tile_[:], in_=kf[:], func=AF.Sin, scale=2 * math.pi), [kk + "f"], [(tk, which)])
                else:
                    P.add("act", lambda e, kf=kf, tile_=tile_, off=off: e.activation(
                        out=tile_[:], in_=kf[:], func=AF.Sin, scale=2 * math.pi, bias=HALFPI[:]),
                        [kk + "f", "HALFPI"], [(tk, which)])

    def norm_transpose(src_rows, XT, xkey, sub, SS, evacs, psT_pair, junk):
        import os as _os
        nlvl = int(_os.environ.get("K_NTLVL", 9))
        ssk = xkey + "ss"
        P.add("act", lambda e: e.activation(out=junk[:], in_=XT[:], func=AF.Square, accum_out=SS[:, 0:1]),
              [xkey], ["junk", ssk])
        P.add("act", lambda e: e.activation(out=SS[:, 1:2], in_=SS[:, 0:1], func=AF.Sqrt, scale=1.0 / D, bias=EPS),
              [ssk], [ssk + "b"])
        P.add("dve", lambda e: e.reciprocal(out=SS[:, 2:3], in_=SS[:, 1:2]), [ssk + "b"], [ssk + "c"])
        P.add("dve", lambda e: e.tensor_scalar(out=XT[:], in0=XT[:], scalar1=SS[:, 2:3], scalar2=None, op0=ALU.mult),
              [xkey, ssk + "c"], [xkey])
        if nlvl < 2:
            return
        for half in range(2):
            pt, pk = psT_pair[half]
            for q in range(4):
                kc = half * 4 + q
                P.add("pe", lambda e, pt=pt, q=q, kc=kc: e.transpose(
                    out=pt[:, q * 128:(q + 1) * 128], in_=XT[:, kc * 128:(kc + 1) * 128], identity=IDF),
                    [xkey, "CST"], [pk])
            for q in range(4):
                kc = half * 4 + q
                if nlvl >= 3:
                    evacs(kc, pt[:, q * 128:(q + 1) * 128], pk)

    with ExitStack() as es:
        W1 = es.enter_context(nc.sbuf_tensor("p1_w", [128, 8, WA_COLS], BF16))
        STG = es.enter_context(nc.sbuf_tensor("p1_stg", [128, 2, WA_COLS // 4], F32))
        X1 = es.enter_context(nc.sbuf_tensor("p1_x", [128, 2, D], F32))
        JUNK = es.enter_context(nc.sbuf_tensor("p1_junk", [128, D], BF16))
        SS1 = es.enter_context(nc.sbuf_tensor("p1_ss", [128, 2, 4], F32))
        HN = es.enter_context(nc.sbuf_tensor("p1_hn", [128, 2, 8, 512], BF16))
        POSI = es.enter_context(nc.sbuf_tensor("p1_posi", [128, 512], I32))
        POSF = es.enter_context(nc.sbuf_tensor("p1_posf", [128, 512], F32))
        TAB = es.enter_context(nc.sbuf_tensor("p1_tab", [128, 4, 512], F32))
        TKI = es.enter_context(nc.sbuf_tensor("p1_ki", [128, 1, 512], I32))
        TKF = es.enter_context(nc.sbuf_tensor("p1_kf", [128, 1, 512], F32))
        STQ = es.enter_context(nc.sbuf_tensor("p1_sq", [128, 2, 4, 8, 128], BF16))
        RO = es.enter_context(nc.sbuf_tensor("p1_ro", [128, 4, 512], BF16))
        RT = es.enter_context(nc.sbuf_tensor("p1_rt", [128, 4, 512], F32))
        VST = es.enter_context(nc.sbuf_tensor("p1_v", [128, 2, 4, 128], BF16))
        KLST = es.enter_context(nc.sbuf_tensor("p1_kl", [128, 2, 512], BF16))
        WIST = es.enter_context(nc.sbuf_tensor("p1_wi", [128, 2, 4, 8], F32))
        KVS = es.enter_context(nc.sbuf_tensor("p1_kv", [128, 8], F32))
        stg = Rot([(STG[:, i, :], "stg%d" % i) for i in range(2)])
        wkeys = load_weights(WA, W1, WA_COLS, stg, "W1")
        allw = [k for kc in range(8) for k in wkeys[kc]]
        tabs = [(TAB[:, 0, :], TAB[:, 1, :], "tab32"), (TAB[:, 2, :], TAB[:, 3, :], "tab64")]
        tmps = Rot([(TKI[:, i, :], TKF[:, i, :], "tk%d" % i) for i in range(1)])
        fm = Rot([(ps[i], "ps%d" % i) for i in range(4)])
        psT_pair = [(ps[4], "ps4"), (ps[5], "ps5")]
        psTok = (ps[6], "ps6")
        psVT = ps[7][:].bitcast(BF16)
        stq = Rot([(STQ[:, i], "stq%d" % i) for i in range(2)])
        ro = Rot([(RO[:, i, :], "ro%d" % i) for i in range(4)])
        rt = Rot([(RT[:, i, :], "rt%d" % i) for i in range(4)])
        import os as _os
        ntile = int(_os.environ.get('K_NT', NB * S // 512))
        _skip = _os.environ.get('K_SKIP', '')
        def nt1(tt):
            b = tt // 8
            t0 = (tt % 8) * 512
            hn = HN[:, tt % 2]
            hk = "hn%d" % (tt % 2)
            for sub in range(4):
                xi = (tt * 4 + sub) % 2
                XT = X1[:, xi, :]
                xkey = "x1_%d" % xi
                r0 = b * S + t0 + sub * 128
                P.dma("sp", XT, x[r0:r0 + 128, :], [], [xkey], "lx%d" % xi)

                def evacs(kc, pap, pk, sub=sub, b=b, hn=hn, hk=hk):
                    dst = hn[:, kc, sub * 128:(sub + 1) * 128]
                    _ev = _os.environ.get('K_EV', '')
                    if (kc < 4 and _ev != 'dve') or _ev == 'act':
                        P.add("act", lambda e: e.activation(
                            out=dst, in_=pap, func=AF.Identity, scale=ASCL[0][:, kc * NB + b:kc * NB + b + 1],
                            bias=ASFT[0][:, kc * NB + b:kc * NB + b + 1]), [pk, "ASCL0", "ASFT0"], [(hk, kc)])
                    else:
                        P.add("dve", lambda e: e.tensor_scalar(
                            out=dst, in0=pap, scalar1=ASCL[0][:, kc * NB + b:kc * NB + b + 1],
                            scalar2=ASFT[0][:, kc * NB + b:kc * NB + b + 1], op0=ALU.mult, op1=ALU.add),
                            [pk, "ASCL0", "ASFT0"], [(hk, kc)])
                if 'nt' not in _skip:
                    norm_transpose(None, XT, xkey, sub, SS1[:, xi, :], evacs, psT_pair, JUNK)

        if ntile > 0:
            nt1(0)
        for tt in range(ntile):
            b = tt // 8
            t0 = (tt % 8) * 512
            qb0 = t0 // 128
            hn = HN[:, tt % 2]
            hk = "hn%d" % (tt % 2)
            P.dma("sp", POSI[:], pos[b:b + 1, t0:t0 + 512].partition_broadcast(128), [], ["POSI"], "POSI")
            P.add("pool", lambda e: e.tensor_copy(out=POSF[:], in_=POSI[:]), ["POSI"], ["POSF"])
            if 'tab' not in _skip:
                rope_tables(b, t0, [(0, 0), (1, 1)], POSF, tabs, tmps)
            vst = VST[:, tt % 2]; vk = "vst%d" % (tt % 2)
            klst = KLST[:, tt % 2, :]; klk = "klst%d" % (tt % 2)
            wist = WIST[:, tt % 2]; wik = "wist%d" % (tt % 2)
            hkeys = [(hk, kc) for kc in range(8)]
            for sub in (range(0) if 'tok' in _skip else range(4)):
                pt, pk = psTok
                for kc in range(8):
                    P.add("pe", lambda e, kc=kc, sub=sub, pt=pt, hn=hn: e.matmul(
                        pt[:, 0:136], lhsT=hn[:, kc, sub * 128:(sub + 1) * 128], rhs=W1[:, kc, 5632:5768],
                        start=(kc == 0), stop=(kc == 7)), [(hk, kc)] + wkeys[kc], [pk])
                P.add("act", lambda e, pt=pt: e.activation(out=JUNK[:, 0:128], in_=pt[:, 0:128], func=AF.Square,
                                                           accum_out=KVS[:, 0:1]), [pk], ["junk", "kvs0"])
                P.add("act", lambda e: e.activation(out=KVS[:, 1:2], in_=KVS[:, 0:1], func=AF.Sqrt, scale=1.0 / 128, bias=EPS),
                      ["kvs0"], ["kvs1"])
                P.add("dve", lambda e: e.reciprocal(out=KVS[:, 2:3], in_=KVS[:, 1:2]), ["kvs1"], ["kvs2"])
                P.add("dve", lambda e, pt=pt, sub=sub, vst=vst: e.scalar_tensor_tensor(
                    out=vst[:, sub, :], in0=pt[:, 0:128], scalar=KVS[:, 2:3], in1=AKVG[:], op0=ALU.mult, op1=ALU.mult),
                    [pk, "kvs2", "AKVG"], [(vk, sub)])
                P.add("act", lambda e, pt=pt, sub=sub, wist=wist: e.mul(out=wist[:, sub, :], in_=pt[:, 128:136], mul=8.0 ** -0.5),
                      [pk], [(wik, sub)])
                P.add("pe", lambda e, sub=sub, vst=vst: e.transpose(out=psVT[:, sub * 128:(sub + 1) * 128], in_=vst[:, sub, :],
                                                                    identity=IDB[:]), [(vk, sub), "IDB"], ["psvt"])
            if 'tail' not in _skip:
                P.add("act", lambda e, klst=klst: e.copy(out=klst, in_=psVT[:, 0:512]), ["psvt"], [klk])
                P.dma("pool", VA[b, t0:t0 + 512, :].rearrange("(s p) c -> p s c", p=128), vst[:], [(vk, s_) for s_ in range(4)], [], vk)
                P.dma("pool", WI[b, t0:t0 + 512, :].rearrange("(s p) c -> p s c", p=128), wist[:], [(wik, s_) for s_ in range(4)], [], wik)
                P.dma("pool", KL[b, :, t0:t0 + 512], klst, [klk], [], klk)
            if debug and tt == 0:
                P.dma("pool", dbg["d_KL"][:, 0:512], klst, [klk], [], klk)

            def fm_chunk(ci, hn=hn, hk=hk):
                pt, pk = fm.next()
                for kc in range(8):
                    P.add("pe", lambda e, kc=kc, pt=pt, ci=ci, hn=hn: e.matmul(
                        pt[:], lhsT=W1[:, kc, ci * 128:(ci + 1) * 128], rhs=hn[:, kc, :], start=(kc == 0), stop=(kc == 7)),
                        [(hk, kc)] + wkeys[kc], [pk])
                return pt, pk
            for grp, func, dst in (() if 'fm' in _skip else ((0, None, QL), (1, AF.Silu, GT))):
              for hg in range(2):
                sq, sqk = stq.next()
                for hl in range(8):
                    h = hg * 8 + hl
                    pt, pk = fm_chunk(grp * 16 + h)
                    o = sq[:, :, hl, :]
                    i_ = pt[:].rearrange("p (a q) -> p a q", q=128)
                    if func is None:
                        if h % 2 == 0:
                            P.add("act", lambda e, o=o, i_=i_: e.copy(out=o, in_=i_), [pk], [(sqk, hl)])
                        else:
                            P.add("dve", lambda e, o=o, i_=i_: e.tensor_copy(out=o, in_=i_), [pk], [(sqk, hl)])
                    else:
                        P.add("act", lambda e, o=o, i_=i_: e.activation(out=o, in_=i_, func=AF.Silu), [pk], [(sqk, hl)])
                P.dma("pool", dst[b, qb0:qb0 + 4, :, hg * 1024:(hg + 1) * 1024].rearrange("a p f -> p a f"),
                      sq.rearrange("p a h q -> p a (h q)"), [(sqk, hl) for hl in range(8)], [], sqk)
                if debug and tt == 0 and grp == 0:
                    P.dma("pool", dbg["d_QL"][:, hg * 1024:(hg + 1) * 1024], sq[:, 0].rearrange("p h q -> p (h q)"),
                          [(sqk, hl) for hl in range(8)], [], sqk)
            if tt + 1 < ntile:
                nt1(tt + 1)
            pairs = [(32, 0, "qr", 0), (34, 0, "qr", 1), (36, 1, "qi", 0), (38, 1, "qi", 1), (40, 0, "kr", 0), (42, 1, "ki", 0)]
            for (c0, ti, kind, half) in ([] if 'rope' in _skip else pairs):
                ct, st_, tk = tabs[ti]
                p1, k1 = fm_chunk(c0)
                p2, k2 = fm_chunk(c0 + 1)
                ta, tak = rt.next(); tb, tbk = rt.next()
                oa, oak = ro.next(); ob, obk = ro.next()
                P.add("dve", lambda e, ta=ta, p1=p1, ct=ct: e.tensor_tensor(out=ta, in0=p1[:], in1=ct, op=ALU.mult), [k1, (tk, 0)], [tak])
                P.add("dve", lambda e, tb=tb, p2=p2, st_=st_: e.tensor_tensor(out=tb, in0=p2[:], in1=st_, op=ALU.mult), [k2, (tk, 1)], [tbk])
                P.add("dve", lambda e, oa=oa, ta=ta, tb=tb: e.tensor_tensor(out=oa, in0=ta, in1=tb, op=ALU.subtract), [tak, tbk], [oak])
                tc_, tck = rt.next(); td, tdk = rt.next()
                P.add("dve", lambda e, tc_=tc_, p2=p2, ct=ct: e.tensor_tensor(out=tc_, in0=p2[:], in1=ct, op=ALU.mult), [k2, (tk, 0)], [tck])
                P.add("dve", lambda e, td=td, p1=p1, st_=st_: e.tensor_tensor(out=td, in0=p1[:], in1=st_, op=ALU.mult), [k1, (tk, 1)], [tdk])
                P.add("dve", lambda e, ob=ob, tc_=tc_, td=td: e.tensor_tensor(out=ob, in0=tc_, in1=td, op=ALU.add), [tck, tdk], [obk])
                if kind == "qr":
                    P.dma("pool", QR[b, qb0:qb0 + 4, half].rearrange("a p q -> p a q"), oa.rearrange("p (a q) -> p a q", q=128), [oak], [], oak)
                    P.dma("pool", QR2[b, qb0:qb0 + 4, half].rearrange("a p q -> p a q"), ob.rearrange("p (a q) -> p a q", q=128), [obk], [], obk)
                elif kind == "qi":
                    P.dma("pool", QI[b, qb0:qb0 + 4, half].rearrange("a p q -> p a q"), oa.rearrange("p (a q) -> p a q", q=128), [oak], [], oak)
                    P.dma("pool", QI2[b, qb0:qb0 + 4, half].rearrange("a p q -> p a q"), ob.rearrange("p (a q) -> p a q", q=128), [obk], [], obk)
                elif kind == "kr":
                    P.dma("pool", KR1[b, :, t0:t0 + 512], oa, [oak], [], oak)
                    P.dma("pool", KR2[b, :, t0:t0 + 512], ob, [obk], [], obk)
                else:
                    P.dma("pool", KI1[b, :, t0:t0 + 512], oa, [oak], [], oak)
                    P.dma("pool", KI2[b, :, t0:t0 + 512], ob, [obk], [], obk)
        P.emit()

    if stop_after == 1:
        return nc, dbg, P

    import os as _os
    SCALE_A = (128 + 32) ** -0.5
    BIGM = 30000.0
    with ExitStack() as es:
        def T(name, shape, dt=F32):
            return es.enter_context(nc.sbuf_tensor(name, list(shape), dt))
        WOA = T("p2_woa", [128, 16, D], BF16)
        WUV = T("p2_wuv", [128, 16 * 128], BF16)
        KIs = T("p2_ki", [128, S], BF16)
        KLs = T("p2_kl", [128, S], BF16)
        KRs = T("p2_kr", [128, S], BF16)
        Vs = T("p2_v", [128, NQB, 128], BF16)
        QIq = T("p2_qi", [128, 2, 4, 128], BF16)
        WIq = T("p2_wi", [128, 2, 8], F32)
        WAB = T("p2_wab", [128, 2, 16], F32)
        QLq = T("p2_ql", [128, 2, 16, 128], BF16)
        QRq = T("p2_qr", [128, 2, 16, 128], BF16)
        GTq = T("p2_gt", [128, 16, 128], BF16)
        Xq = T("p2_x", [128, D], F32)
        IS = T("p2_is", [128, S], F32)
        MS = T("p2_ms", [128, 2, S], BF16)
        MT = T("p2_mt", [128, 2, NQB, 128], BF16)
        TMPI = T("p2_tmpi", [128, 3, 512], F32)
        PT = T("p2_pt", [128, 4, 512], BF16)
        RD = T("p2_rd", [128, 1024], F32)
        ON = T("p2_on", [128, 1024], BF16)
        Y = T("p2_y", [128, 16, 128], BF16)
        H1q = T("p2_h1", [128, D], F32)
        OC = T("p2_oc", [128, 2, 512], F32)
        TMPH = T("p2_tmph", [128, D], F32)
        BS = T("p2_bs", [128, 9 + NIT], F32)
        NEG30 = T("p2_neg", [128, 1], F32)
        TS = T("p2_ts", [128, NIT], F32)

        P.op("dve", "memset", [], ["NEG30"], NEG30[:], -BIGM)
        P.op("dve", "memset", [], ["KRs"], KRs[:], 0.0)
        P.op("pool", "memset", [], ["QRq0", "QRq1"], QRq[:], 0.0)
        for h in range(16):
            st = IS[:, (h % 2) * 1024:(h % 2 + 1) * 1024]
            sk2 = [("IS", 2 * (h % 2)), ("IS", 2 * (h % 2) + 1)]
            P.dma("sp", st, woa[h * 128:(h + 1) * 128, :], [], sk2, "stg2_%d" % (h % 2))
            if h % 2 == 0:
                P.op("dve", "tensor_copy", sk2, [("WOA", h)], out=WOA[:, h, :], in_=st)
            else:
                P.op("act", "copy", sk2, [("WOA", h)], out=WOA[:, h, :], in_=st)
        for i in range(2):
            st = IS[:, i * 1024:(i + 1) * 1024]
            sk2 = [("IS", 2 * i), ("IS", 2 * i + 1)]
            P.dma("sp", st, wuv[:, i * 1024:(i + 1) * 1024], [], sk2, "stg2_%d" % i)
            P.op("dve", "tensor_copy", sk2, [("WUV", i)], out=WUV[:, i * 1024:(i + 1) * 1024], in_=st)
        woa_keys = [("WOA", h) for h in range(16)]
        wpool = Rot([(ps[i], "ps%d" % i) for i in range(4)])
        psO = [(ps[4], "ps4"), (ps[5], "ps5")]
        psD = [(ps[6], "ps6"), (ps[7], "ps7")]
        tmpi = Rot([(TMPI[:, i, :], "tmpi%d" % i) for i in range(3)])
        ptr = Rot([(PT[:, i, :], "pt%d" % i) for i in range(4)])
        ocr = Rot([(OC[:, i, :], "oc%d" % i) for i in range(2)])
        nblk = int(_os.environ.get("K_NBLK", NB * NQB))
        dbg_qb = int(_os.environ.get("K_DBGQB", 3))
        def idx_part(blk):
            b = blk // NQB
            qb = blk % NQB
            sl = blk % 2
            nk = (qb + 1) * 128
            nkc = qb + 1
            if qb == 0:
                ki1 = KI1[b].rearrange("(i r) t -> r i t", r=4)[0]
                ki2 = KI2[b].rearrange("(i r) t -> r i t", r=4)[0]
                P.dma("sp", KIs[0:32, :], ki1, [], ["KIs"], "KIs")
                P.dma("sp", KIs[32:64, :], ki2, [], ["KIs"], "KIs")
                P.dma("sp", KIs[64:96, :], ki1, [], ["KIs"], "KIs")
                P.dma("sp", KIs[96:128, :], ki2, [], ["KIs"], "KIs")
            qik = "QIq%d" % sl
            for h2 in range(2):
                for xp, src in ((0, QI), (1, QI2)):
                    for half in range(2):
                        sv = src[b, qb, half].rearrange("(i pp h2) q -> h2 i pp q", pp=2, h2=2)[h2]
                        dv = QIq[h2 * 64 + xp * 32:h2 * 64 + xp * 32 + 32, sl, half * 2:half * 2 + 2, :]
                        P.dma("sp", dv, sv, [], [qik], qik)
            P.dma("sp", WIq[:, sl, :], WI[b, qb * 128:(qb + 1) * 128, :], [], ["WIq%d" % sl], "WIq%d" % sl)
            P.dma("sp", QLq[:, sl].rearrange("p h q -> p (h q)"), QL[b, qb], [], ["QLq%d" % sl], "QLq%d" % sl)
            for half in range(2):
                P.dma("sp", QRq[0:16, sl, half * 8:(half + 1) * 8, :], QR[b, qb, half].rearrange("(i hh) q -> i hh q", hh=8),
                      [], ["QRq%d" % sl], "QRq%d" % sl)
                P.dma("sp", QRq[16:32, sl, half * 8:(half + 1) * 8, :], QR2[b, qb, half].rearrange("(i hh) q -> i hh q", hh=8),
                      [], ["QRq%d" % sl], "QRq%d" % sl)
            wabk = "WAB%d" % sl
            P.op("act", "activation", ["WIq%d" % sl], [wabk + "a"], out=WAB[:, sl, 0:8], in_=WIq[:, sl, :], func=AF.Abs)
            P.op("act", "activation", ["WIq%d" % sl], [wabk + "s"], out=WAB[:, sl, 8:16], in_=WIq[:, sl, :], func=AF.Sign)
            nc5 = (nk + 511) // 512
            iskeys = [("IS", c5) for c5 in range(nc5)]
            for c5 in range(nc5):
                w = min(512, nk - c5 * 512)
                cs = slice(c5 * 512, c5 * 512 + w)
                for h in range(8):
                    pt_, pk = wpool.next()
                    pb = (h % 2) * 64
                    P.op("pe", "matmul", [qik, "KIs"], [pk], pt_[:, 0:w], lhsT=QIq[pb:pb + 64, sl, h // 2, :], rhs=KIs[pb:pb + 64, cs],
                         start=True, stop=True)
                    tm, tmk = tmpi.next()
                    P.op("act", "activation", [pk, wabk + "a"], [tmk], out=tm[:, 0:w], in_=pt_[:, 0:w], func=AF.Relu, scale=WAB[:, sl, h:h + 1])
                    if h == 0:
                        P.op("dve", "tensor_scalar", [tmk, wabk + "s"], [("IS", c5)], out=IS[:, cs], in0=tm[:, 0:w],
                             scalar1=WAB[:, sl, 8:9], scalar2=None, op0=ALU.mult)
                    else:
                        P.op("dve", "scalar_tensor_tensor", [tmk, wabk + "s", ("IS", c5)], [("IS", c5)], out=IS[:, cs], in0=tm[:, 0:w],
                             scalar=WAB[:, sl, 8 + h:9 + h], in1=IS[:, cs], op0=ALU.mult, op1=ALU.add)
            P.op("dve", "tensor_reduce", iskeys, ["bs_mx"], out=BS[:, 0:1], in_=IS[:, 0:nk], axis=AX.X, op=ALU.max)
            P.op("dve", "tensor_reduce", iskeys, ["bs_mn"], out=BS[:, 1:2], in_=IS[:, 0:nk], axis=AX.X, op=ALU.min)
            P.op("dve", "tensor_tensor", ["bs_mx", "bs_mn"], ["bs_w0"], out=BS[:, 2:3], in0=BS[:, 0:1], in1=BS[:, 1:2], op=ALU.subtract)
            P.op("dve", "tensor_scalar", ["bs_w0", "CST"], ["bs_steps"], out=BS[:, 8:9 + NIT], in0=POW2, scalar1=BS[:, 2:3], scalar2=None, op0=ALU.mult)
            P.op("dve", "tensor_copy", ["bs_mn"], ["bs_lo"], out=BS[:, 3:4], in_=BS[:, 1:2])
            dk = ("IS", (nk - 128) // 512)
            P.op("dve", "tensor_tensor", [dk, "CST"], [dk], out=IS[:, nk - 128:nk], in0=IS[:, nk - 128:nk], in1=CAUS, op=ALU.add)
            P.op("dve", "tensor_tensor", ["bs_lo", "bs_steps"], ["bs_mid"], out=BS[:, 4:5], in0=BS[:, 3:4], in1=BS[:, 8:9], op=ALU.add)
            for it in range(NIT):
                P.op("dve", "tensor_scalar", iskeys + ["bs_mid"], ["MS%d" % sl, "bs_cnt"], out=MS[:, sl, 0:nk], in0=IS[:, 0:nk], scalar1=BS[:, 4:5],
                     scalar2=0.0, op0=ALU.is_ge, op1=ALU.add, accum_out=BS[:, 5:6])
                P.op("dve", "scalar_tensor_tensor", ["bs_cnt", "bs_steps"], [("bs_t", it)], out=TS[:, it:it + 1], in0=BS[:, 5:6], scalar=TOPK - 0.5,
                     in1=BS[:, 8 + it:9 + it], op0=ALU.is_ge, op1=ALU.mult)
                if it < NIT - 1:
                    P.op("dve", "scalar_tensor_tensor", [("bs_t", it), "bs_steps", "bs_mid"], ["bs_mid"], out=BS[:, 4:5], in0=TS[:, it:it + 1],
                         scalar=BS[:, 9 + it:10 + it], in1=BS[:, 4:5], op0=ALU.subtract, op1=ALU.add)
            P.op("dve", "tensor_reduce", [("bs_t", i_) for i_ in range(NIT)], ["bs_ts"], out=BS[:, 6:7], in_=TS[:, 0:NIT], axis=AX.X, op=ALU.add)
            P.op("dve", "tensor_tensor", ["bs_ts", "bs_mn"], ["bs_lo"], out=BS[:, 3:4], in0=BS[:, 6:7], in1=BS[:, 1:2], op=ALU.add)
            P.op("dve", "tensor_scalar", iskeys + ["bs_lo"], ["MS%d" % sl], out=MS[:, sl, 0:nk], in0=IS[:, 0:nk], scalar1=BS[:, 3:4], scalar2=None, op0=ALU.is_ge)
            if debug and b == 0 and qb == dbg_qb:
                P.dma("pool", dbg["d_IS"][:, 0:nk], IS[:, 0:nk], iskeys, [], "dbgis")
                P.dma("pool", dbg["d_lo"], BS[:, 0:8], ["bs_lo", "bs_cnt", "bs_mx", "bs_mn"], [], "dbglo")

        def att_part(blk):
            b = blk // NQB
            qb = blk % NQB
            sl = blk % 2
            nk = (qb + 1) * 128
            nkc = qb + 1
            r0 = b * S + qb * 128
            if qb == 0:
                P.dma("sp", KLs[:], KL[b], [], ["KLs"], "KLs")
                P.dma("sp", KRs[0:16, :], KR1[b].rearrange("(i r) t -> r i t", r=8)[0], [], ["KRs"], "KRs")
                P.dma("sp", KRs[16:32, :], KR2[b].rearrange("(i r) t -> r i t", r=8)[0], [], ["KRs"], "KRs")
                P.dma("sp", Vs[:], VA[b].rearrange("(c p) d -> p c d", p=128), [], ["Vs"], "Vs")
            P.dma("sp", GTq[:].rearrange("p h q -> p (h q)"), GT[b, qb], [], ["GTq"], "GTq")
            P.dma("sp", Xq[:], x[r0:r0 + 128, :], [], ["Xq"], "Xq")
            for kc0 in range(0, nkc, 4):
                n4 = min(4, nkc - kc0)
                pt_, pk = wpool.next()
                pv = pt_[:].bitcast(BF16)
                for j in range(n4):
                    kc = kc0 + j
                    P.op("pe", "transpose", ["MS%d" % sl, "IDB"], [pk], out=pv[:, j * 128:(j + 1) * 128], in_=MS[:, sl, kc * 128:(kc + 1) * 128], identity=IDB[:])
                P.op("act", "activation", [pk, "NEG30"], [("MT", sl, kc0 // 4)], out=MT[:, sl, kc0:kc0 + n4, :].rearrange("p c q -> p (c q)"),
                     in_=pv[:, 0:n4 * 128], func=AF.Identity, scale=BIGM, bias=NEG30[:])

            for hg in range(2):
                groups = [(kc, g) for kc in range(nkc) for g in range(2)]

                def qk(kc, g):
                    ks = slice(kc * 128, (kc + 1) * 128)
                    h0 = hg * 8 + g * 4
                    pt_, pk = wpool.next()
                    P.op("pe", "matmul", ["KLs", "QLq%d" % sl], [pk], pt_[:], lhsT=KLs[:, ks],
                         rhs=QLq[:, sl, h0:h0 + 4, :].rearrange("p h q -> p (h q)"), start=True, stop=False)
                    P.op("pe", "matmul", ["KRs", "QRq%d" % sl], [pk], pt_[:], lhsT=KRs[:, ks],
                         rhs=QRq[:, sl, h0:h0 + 4, :].rearrange("p h q -> p (h q)"), start=False, stop=False)
                    P.op("pe", "matmul", ["IDB", ("MT", sl, kc // 4)], [pk], pt_[:], lhsT=IDB[:],
                         rhs=MT[:, sl, kc, :].unsqueeze(1).to_broadcast([128, 4, 128]), start=False, stop=True)
                    return pt_, pk
                pend = [qk(*groups[0])]
                if len(groups) > 1:
                    pend.append(qk(*groups[1]))
                for gi, (kc, g) in enumerate(groups):
                    if gi + 2 < len(groups):
                        pend.append(qk(*groups[gi + 2]))
                    pt_, pk = pend.pop(0)
                    pr, prk = ptr.next()
                    P.op("act", "activation", [pk], [prk], out=pr, in_=pt_[:], func=AF.Exp, scale=SCALE_A)
                    P.op("pe", "matmul", ["Vs", prk], [psO[g][1]], psO[g][0][:], lhsT=Vs[:, kc, :], rhs=pr, start=(kc == 0), stop=(kc == nkc - 1))
                    P.op("pe", "matmul", ["ONESB", prk], [psD[g][1]], psD[g][0][:], lhsT=ONESB[:], rhs=pr, start=(kc == 0), stop=(kc == nkc - 1))
                for g in range(2):
                    h0 = hg * 8 + g * 4
                    gs = slice(g * 512, (g + 1) * 512)
                    oc, ock = ocr.next()
                    P.op("act", "activation", [psD[g][1]], [("RD", g)], out=RD[:, gs], in_=psD[g][0][:], func=AF.Ln)
                    P.op("act", "activation", [("RD", g)], [("RD", g)], out=RD[:, gs], in_=RD[:, gs], func=AF.Exp, scale=-1.0)
                    P.op("act", "copy", [psO[g][1]], [ock], out=oc, in_=psO[g][0][:])
                    P.op("pool", "tensor_tensor", [ock, ("RD", g)], [("ON", g)], out=ON[:, gs], in0=oc, in1=RD[:, gs], op=ALU.mult)
                    pt_, pk = wpool.next()
                    for hl in range(4):
                        h = h0 + hl
                        P.op("pe", "matmul", [("WUV", h // 8), ("ON", g)], [pk], pt_[:, hl * 128:(hl + 1) * 128], lhsT=WUV[:, h * 128:(h + 1) * 128],
                             rhs=ON[:, g * 512 + hl * 128:g * 512 + (hl + 1) * 128], start=True, stop=True)
                    oc2, ock2 = ocr.next()
                    P.op("act", "copy", [pk], [ock2], out=oc2, in_=pt_[:])
                    P.op("pool", "tensor_tensor", [ock2, "GTq"], [("Y", h0 // 4)], out=Y[:, h0:h0 + 4, :].rearrange("p h q -> p (h q)"),
                         in0=oc2, in1=GTq[:, h0:h0 + 4, :].rearrange("p h q -> p (h q)"), op=ALU.mult)
            ykeys = [("Y", i) for i in range(4)]
            for nh in range(2):
                pt_, pk = wpool.next()
                for h in range(16):
                    P.op("pe", "matmul", ykeys + [("WOA", h)], [pk], pt_[:], lhsT=Y[:, h, :], rhs=WOA[:, h, nh * 512:(nh + 1) * 512],
                         start=(h == 0), stop=(h == 15))
                P.op("act", "copy", [pk], [("TMPH", nh)], out=TMPH[:, nh * 512:(nh + 1) * 512], in_=pt_[:])
                P.op("pool", "tensor_tensor", [("TMPH", nh), "GBC0"], [("TMPH", nh)], out=TMPH[:, nh * 512:(nh + 1) * 512],
                     in0=TMPH[:, nh * 512:(nh + 1) * 512], in1=GBC[0][:, b * D + nh * 512:b * D + (nh + 1) * 512], op=ALU.mult)
            P.op("pool", "tensor_tensor", [("TMPH", 0), ("TMPH", 1), "Xq"], ["H1q"], out=H1q[:], in0=TMPH[:], in1=Xq[:], op=ALU.add)
            P.dma("pool", H1[r0:r0 + 128, :], H1q[:], ["H1q"], [], "H1q")
            if debug:
                P.dma("pool", dbg["d_H1"][r0:r0 + 128, :], H1q[:], ["H1q"], [], "H1q")

        idx_part(0)
        for blk in range(nblk):
            if blk + 1 < nblk:
                idx_part(blk + 1)
            att_part(blk)
        P.emit()

    if stop_after == 2:
        return nc, dbg, P

    with ExitStack() as es:
        def T(name, shape, dt=F32):
            return es.enter_context(nc.sbuf_tensor(name, list(shape), dt))
        W3 = T("p3_w", [128, 8, WB_COLS], BF16)
        STG3 = T("p3_stg", [128, 2, WB_COLS // 4], F32)
        X3 = T("p3_x", [128, 2, D], F32)
        JUNK3 = T("p3_junk", [128, D], BF16)
        SS3 = T("p3_ss", [128, 2, 4], F32)
        KVT = T("p3_kvt", [128, 2, 8, 512], BF16)
        HBT = T("p3_hbt", [128, 2, 8, 512], BF16)
        POSI3 = T("p3_posi", [128, 512], I32)
        POSF3 = T("p3_posf", [128, 512], F32)
        TAB3 = T("p3_tab", [128, 2, 512], F32)
        TKI3 = T("p3_ki", [128, 1, 512], I32)
        TKF3 = T("p3_kf", [128, 1, 512], F32)
        RT3 = T("p3_rt", [128, 4, 512], F32)
        RO3 = T("p3_ro", [128, 4, 512], BF16)
        VSTG = T("p3_vst", [128, 2, D], BF16)
        GST = T("p3_gst", [128, 2, 512], BF16)
        stg = Rot([(STG3[:, i, :], "stg3_%d" % i) for i in range(2)])
        wkeys = load_weights(WB, W3, WB_COLS, stg, "W3")
        tabs = [(TAB3[:, 0, :], TAB3[:, 1, :], "tab128")]
        tmps = Rot([(TKI3[:, 0, :], TKF3[:, 0, :], "tk3")])
        fm = Rot([(ps[i], "ps%d" % i) for i in range(4)])
        psT_pair = [(ps[4], "ps4"), (ps[5], "ps5")]
        vps = Rot([(ps[6], "ps6"), (ps[7], "ps7")])
        ro = Rot([(RO3[:, i, :], "ro3_%d" % i) for i in range(4)])
        rt = Rot([(RT3[:, i, :], "rt3_%d" % i) for i in range(4)])
        gst = Rot([(GST[:, i, :], "gst%d" % i) for i in range(2)])
        ntile3 = int(_os.environ.get("K_NT3", NB * S // 512))
        def nt3(tt):
            b = tt // 8
            t0 = (tt % 8) * 512
            kvt = KVT[:, tt % 2]; kvk = "kvt%d" % (tt % 2)
            hbt = HBT[:, tt % 2]; hbk = "hbt%d" % (tt % 2)
            for sub in range(4):
                xi = (tt * 4 + sub) % 2
                XT = X3[:, xi, :]
                xkey = "x3_%d" % xi
                r0 = b * S + t0 + sub * 128
                P.dma("sp", XT, H1[r0:r0 + 128, :], [], [xkey], xkey)

                def evacs(kc, pap, pk, sub=sub, b=b, kvt=kvt, hbt=hbt, kvk=kvk, hbk=hbk):
                    d1 = kvt[:, kc, sub * 128:(sub + 1) * 128]
                    d2 = hbt[:, kc, sub * 128:(sub + 1) * 128]
                    if kc < 4:
                        P.op("act", "activation", [pk, "NRM"], [(kvk, kc)], out=d1, in_=pap, func=AF.Copy, scale=NRM[:, 16 + kc:17 + kc])
                        P.op("act", "activation", [pk, "ASCL1", "ASFT1"], [(hbk, kc)], out=d2, in_=pap, func=AF.Identity,
                             scale=ASCL[1][:, kc * NB + b:kc * NB + b + 1], bias=ASFT[1][:, kc * NB + b:kc * NB + b + 1])
                    else:
                        P.op("dve", "tensor_scalar", [pk, "NRM"], [(kvk, kc)], out=d1, in0=pap, scalar1=NRM[:, 16 + kc:17 + kc], scalar2=None, op0=ALU.mult)
                        P.op("dve", "tensor_scalar", [pk, "ASCL1", "ASFT1"], [(hbk, kc)], out=d2, in0=pap,
                             scalar1=ASCL[1][:, kc * NB + b:kc * NB + b + 1], scalar2=ASFT[1][:, kc * NB + b:kc * NB + b + 1], op0=ALU.mult, op1=ALU.add)
                norm_transpose(None, XT, xkey, sub, SS3[:, xi, :], evacs, psT_pair, JUNK3)


        if ntile3 > 0:
            nt3(0)
        for tt in range(ntile3):
            b = tt // 8
            t0 = (tt % 8) * 512
            kvt = KVT[:, tt % 2]; kvk = "kvt%d" % (tt % 2)
            hbt = HBT[:, tt % 2]; hbk = "hbt%d" % (tt % 2)
            P.dma("sp", POSI3[:], pos[b:b + 1, t0:t0 + 512].partition_broadcast(128), [], ["POSI"], "POSI")
            P.op("pool", "tensor_copy", ["POSI"], ["POSF"], out=POSF3[:], in_=POSI3[:])
            rope_tables(b, t0, [(2, 0)], POSF3, tabs, tmps)
            def fm3(ci, src, sk):
                pt_, pk = fm.next()
                for kc in range(8):
                    P.op("pe", "matmul", [(sk, kc)] + wkeys[kc], [pk], pt_[:], lhsT=W3[:, kc, ci * 128:(ci + 1) * 128], rhs=src[:, kc, :],
                         start=(kc == 0), stop=(kc == 7))
                return pt_, pk
            ct, st_, tk = tabs[0]
            pair_list = [(2 * j, kvt, kvk, KT2[b, j]) for j in range(4)]
            pair_list += [(8 + g * 8 + 2 * j, hbt, hbk, QT2[b, g, j]) for g in range(3) for j in range(4)]
            for (c0, src, sk, dst) in pair_list:
                p1, k1 = fm3(c0, src, sk)
                p2_, k2 = fm3(c0 + 1, src, sk)
                ta, tak = rt.next(); tb, tbk = rt.next()
                oa, oak = ro.next(); ob, obk = ro.next()
                P.op("dve", "tensor_tensor", [k1, (tk, 0)], [tak], out=ta, in0=p1[:], in1=ct, op=ALU.mult)
                P.op("dve", "tensor_tensor", [k2, (tk, 1)], [tbk], out=tb, in0=p2_[:], in1=st_, op=ALU.mult)
                P.op("pool", "tensor_tensor", [tak, tbk], [oak], out=oa, in0=ta, in1=tb, op=ALU.subtract)
                tc_, tck = rt.next(); td, tdk = rt.next()
                P.op("dve", "tensor_tensor", [k2, (tk, 0)], [tck], out=tc_, in0=p2_[:], in1=ct, op=ALU.mult)
                P.op("dve", "tensor_tensor", [k1, (tk, 1)], [tdk], out=td, in0=p1[:], in1=st_, op=ALU.mult)
                P.op("pool", "tensor_tensor", [tck, tdk], [obk], out=ob, in0=tc_, in1=td, op=ALU.add)
                P.dma("pool", dst[0, :, t0:t0 + 512], oa, [oak], [], oak)
                P.dma("pool", dst[1, :, t0:t0 + 512], ob, [obk], [], obk)
                if debug and tt == 0 and c0 == 0:
                    P.dma("pool", dbg["d_KT"][:, 0:512], oa, [oak], [], oak)
            if tt + 1 < ntile3:
                nt3(tt + 1)
            for h in range(8):
                p1, k1 = fm3(32 + h, hbt, hbk)
                g_, gk = gst.next()
                P.op("act", "activation", [k1], [gk], out=g_, in_=p1[:], func=AF.Silu)
                P.dma("pool", GB[b, h, :, t0:t0 + 512], g_, [gk], [], gk)
            for sub in range(4):
                vs_ = VSTG[:, sub % 2, :]
                vk_ = "vstg%d" % (sub % 2)
                for nh in range(2):
                    pt_, pk = vps.next()
                    for kc in range(8):
                        P.op("pe", "matmul", [(kvk, kc)] + wkeys[kc], [pk], pt_[:], lhsT=kvt[:, kc, sub * 128:(sub + 1) * 128],
                             rhs=W3[:, kc, 5120 + nh * 512:5120 + (nh + 1) * 512], start=(kc == 0), stop=(kc == 7))
                    if nh == 0:
                        P.op("act", "copy", [pk], [(vk_, nh)], out=vs_[:, nh * 512:(nh + 1) * 512], in_=pt_[:])
                    else:
                        P.op("dve", "tensor_copy", [pk], [(vk_, nh)], out=vs_[:, nh * 512:(nh + 1) * 512], in_=pt_[:])
                r0 = t0 + sub * 128
                P.dma("pool", VB[b, r0:r0 + 128, :], vs_, [(vk_, 0), (vk_, 1)], [], vk_)
        P.emit()
    if stop_after == 3:
        return nc, dbg, P

    SCALE_B = 128 ** -0.5
    with ExitStack() as es:
        def T(name, shape, dt=F32):
            return es.enter_context(nc.sbuf_tensor(name, list(shape), dt))
        KTh = T("p4_k", [128, 2, S], BF16)
        QTh = T("p4_q", [128, 2, 3, S], BF16)
        Vh = T("p4_v", [128, 2, 3, NQB, 128], BF16)
        GBh = T("p4_g", [128, 2, S], BF16)
        OD = T("p4_od", [128, 2, S], F32)
        PTb = T("p4_pt", [128, 4, 256], BF16)
        YTo = T("p4_y", [128, S], BF16)
        NEGMB = T("p4_negm", [128, 256], BF16)
        P.op("dve", "tensor_scalar", ["CST"], ["NEGMB"], out=NEGMB[:], in0=CST[:, 128:384], scalar1=30000.0, scalar2=-30000.0, op0=ALU.mult, op1=ALU.add)
        sp_ = Rot([(ps[i], "ps%d" % i) for i in range(4)])
        op_ = Rot([(ps[i], "ps%d" % i) for i in range(4, 8)])
        ptb = Rot([(PTb[:, i, :], "ptb%d" % i) for i in range(4)])
        nhead4 = int(_os.environ.get("K_NH4", NB * 8))
        for idx in range(nhead4):
            b = idx // 8
            h = idx % 8
            sl = idx % 2
            j = h // 2
            hh = h % 2
            kk_ = "KTh%d" % sl; qk_ = "QTh%d" % sl; vk_ = "Vh%d" % sl; gk_ = "GBh%d" % sl
            for xx in range(2):
                P.dma("sp", KTh[xx * 64:(xx + 1) * 64, sl, :], KT2[b, j, xx, hh * 64:(hh + 1) * 64, :], [], [kk_], kk_)
                for g in range(3):
                    P.dma("sp", QTh[xx * 64:(xx + 1) * 64, sl, g, :], QT2[b, g, j, xx, hh * 64:(hh + 1) * 64, :], [], [qk_], qk_)
            for g, d in enumerate((1, 4, 16)):
                nch = NQB // d
                vv = VB[b, :, h * 128:(h + 1) * 128].rearrange("(c a r) f -> r a c f", a=128, r=d)
                for r in range(d):
                    P.dma("sp", Vh[:, sl, g, r * nch:(r + 1) * nch, :], vv[r], [], [vk_], vk_)
            P.dma("sp", GBh[:, sl, :], GB[b, h], [], [gk_], gk_)
            units = [(g, d, r, c) for g, d in enumerate((1, 4, 16)) for r in range(d) for c in range(NQB // d)]

            def qk4(g, d, r, c):
                qv = QTh[:, sl, g, :].rearrange("p (c a r) -> p r c a", a=128, r=d)
                kv_ = KTh[:, sl, :].rearrange("p (c a r) -> p r c a", a=128, r=d)
                pS, pSk = sp_.next()
                if c > 0:
                    P.op("pe", "matmul", [kk_, qk_], [pSk], pS[:, 0:128], lhsT=kv_[:, r, c - 1, :], rhs=qv[:, r, c, :], start=True, stop=False)
                    P.op("pe", "matmul", ["IDB", "NEGMB"], [pSk], pS[:, 0:128], lhsT=IDB[:], rhs=NEGMB[:, 0:128], start=False, stop=True)
                P.op("pe", "matmul", [kk_, qk_], [pSk], pS[:, 128:256], lhsT=kv_[:, r, c, :], rhs=qv[:, r, c, :], start=True, stop=False)
                P.op("pe", "matmul", ["IDB", "NEGMB"], [pSk], pS[:, 128:256], lhsT=IDB[:], rhs=NEGMB[:, 128:256], start=False, stop=True)
                return pS, pSk
            pend = [qk4(*units[0]), qk4(*units[1])]
            for ui, (g, d, r, c) in enumerate(units):
                if ui + 2 < len(units):
                    pend.append(qk4(*units[ui + 2]))
                pS, pSk = pend.pop(0)
                nch = NQB // d
                ov = OD[:].rearrange("p x (c a r) -> p r c x a", a=128, r=d)
                ti = r * nch + c
                lo = 0 if c > 0 else 128
                pt_, ptk = ptb.next()
                P.op("act", "activation", [pSk], [ptk], out=pt_[:, lo:256], in_=pS[:, lo:256], func=AF.Exp, scale=SCALE_B)
                pO, pOk = op_.next()
                for half, lhs_of in ((0, lambda t: Vh[:, sl, g, t, :]), (1, lambda t: ONESB[:])):
                    oc = slice(half * 128, (half + 1) * 128)
                    if c > 0:
                        P.op("pe", "matmul", [vk_, "ONESB", ptk], [pOk], pO[:, oc], lhsT=lhs_of(ti - 1), rhs=pt_[:, 0:128], start=True, stop=False)
                        P.op("pe", "matmul", [vk_, "ONESB", ptk], [pOk], pO[:, oc], lhsT=lhs_of(ti), rhs=pt_[:, 128:256], start=False, stop=True)
                    else:
                        P.op("pe", "matmul", [vk_, "ONESB", ptk], [pOk], pO[:, oc], lhsT=lhs_of(ti), rhs=pt_[:, 128:256], start=True, stop=True)
                dst = ov[:, r, c, :, :]
                src = pO[:, 0:256].rearrange("p (x a) -> p x a", a=128)
                if g == 0:
                    P.op("act", "copy", [pOk], [("OD", c // 4)], out=dst, in_=src)
                else:
                    odk = [("OD", i) for i in range(8)]
                    P.op("dve", "tensor_tensor", [pOk] + odk, odk, out=dst, in0=src, in1=dst, op=ALU.add)
            odk = [("OD", i) for i in range(8)]
            P.op("act", "activation", odk, odk, out=OD[:, 1, :], in_=OD[:, 1, :], func=AF.Ln)
            P.op("act", "activation", odk, odk, out=OD[:, 1, :], in_=OD[:, 1, :], func=AF.Exp, scale=-1.0)
            P.op("pool", "tensor_tensor", odk, odk, out=OD[:, 0, :], in0=OD[:, 0, :], in1=OD[:, 1, :], op=ALU.mult)
            P.op("dve", "tensor_tensor", odk + [gk_], ["YTo"], out=YTo[:], in0=OD[:, 0, :], in1=GBh[:, sl, :], op=ALU.mult)
            P.dma("pool", YT[b, h], YTo[:], ["YTo"], [], "YTo")
            if debug and idx == 0:
                P.dma("pool", dbg["d_YT"], YTo[:], ["YTo"], [], "YTo")
        P.emit()
    if stop_after == 4:
        return nc, dbg, P

    with ExitStack() as es:
        def T(name, shape, dt=F32):
            return es.enter_context(nc.sbuf_tensor(name, list(shape), dt))
        WOB = T("p5_w", [128, 8, D], BF16)
        STG5 = T("p5_stg", [128, 2, D], F32)
        Yq = T("p5_y", [128, 2, 8, 128], BF16)
        H1t = T("p5_h1", [128, 2, D], F32)
        TMP5 = T("p5_tmp", [128, D], F32)
        H2 = T("p5_h2", [128, D], F32)
        OUTt = T("p5_out", [128, 2, D], F32)
        JUNK5 = T("p5_junk", [128, D], BF16)
        SS5 = T("p5_ss", [128, 4], F32)
        for h in range(8):
            st = STG5[:, h % 2, :]
            P.dma("sp", st, wob[h * 128:(h + 1) * 128, :], [], ["stg5_%d" % (h % 2)], "stg5_%d" % (h % 2))
            if h % 2 == 0:
                P.op("dve", "tensor_copy", ["stg5_0"], [("WOB", h)], out=WOB[:, h, :], in_=st)
            else:
                P.op("act", "copy", ["stg5_1"], [("WOB", h)], out=WOB[:, h, :], in_=st)
        pp = Rot([(ps[i], "ps%d" % i) for i in range(4)])
        ntile5 = int(_os.environ.get("K_NT5", NB * NQB))
        for tt in range(ntile5):
            b = tt // NQB
            qb = tt % NQB
            sl = tt % 2
            r0 = b * S + qb * 128
            yk = "Yq%d" % sl; hk_ = "H1t%d" % sl; ok_ = "OUTt%d" % sl
            P.dma("sp", Yq[:, sl], YT[b, :, :, qb * 128:(qb + 1) * 128].rearrange("h p t -> p h t"), [], [yk], yk)
            P.dma("sp", H1t[:, sl, :], H1[r0:r0 + 128, :], [], [hk_], hk_)
            for nh in range(2):
                pt_, pk = pp.next()
                for h in range(8):
                    P.op("pe", "matmul", [yk, ("WOB", h)], [pk], pt_[:], lhsT=Yq[:, sl, h, :], rhs=WOB[:, h, nh * 512:(nh + 1) * 512],
                         start=(h == 0), stop=(h == 7))
                P.op("dve", "tensor_tensor", [pk, "GBC1"], [("TMP5", nh)], out=TMP5[:, nh * 512:(nh + 1) * 512], in0=pt_[:],
                     in1=GBC[1][:, b * D + nh * 512:b * D + (nh + 1) * 512], op=ALU.mult)
            P.op("pool", "tensor_tensor", [("TMP5", 0), ("TMP5", 1), hk_], ["H2"], out=H2[:], in0=TMP5[:], in1=H1t[:, sl, :], op=ALU.add)
            P.op("act", "activation", ["H2"], ["junk5", "ss5a"], out=JUNK5[:], in_=H2[:], func=AF.Square, accum_out=SS5[:, 0:1])
            P.op("act", "activation", ["ss5a"], ["ss5b"], out=SS5[:, 1:2], in_=SS5[:, 0:1], func=AF.Sqrt, scale=1.0 / D, bias=EPS)
            P.op("dve", "reciprocal", ["ss5b"], ["ss5c"], out=SS5[:, 2:3], in_=SS5[:, 1:2])
            P.op("dve", "scalar_tensor_tensor", ["H2", "ss5c", "FNG"], [ok_], out=OUTt[:, sl, :], in0=H2[:], scalar=SS5[:, 2:3], in1=FNG[:],
                 op0=ALU.mult, op1=ALU.mult)
            P.dma("pool", out[r0:r0 + 128, :], OUTt[:, sl, :], [ok_], [], ok_)
        P.emit()

    return nc, dbg, P


def _perm_a():
    idx = []
    idx += list(range(0, 2048))
    idx += list(range(2720, 4768))
    for hb in (0, 8):
        for part in (0, 16):
            idx += [2048 + (hb + (p % 8)) * 32 + part + p // 8 for p in range(128)]
    for hb in (0, 4):
        for part in (0, 32):
            idx += [4768 + (hb + (p % 4)) * 64 + part + p // 4 for p in range(128)]
    for part in (0, 16):
        idx += [2688 + part + p // 8 for p in range(128)]
    for part in (0, 32):
        idx += [5280 + part + p // 4 for p in range(128)]
    idx += list(range(2560, 2688))
    idx += list(range(5344, 5352))
    assert len(idx) == WA_COLS
    return np.array(idx)


def _consts():
    c = np.zeros((128, 640 + 4 + NIT), np.float32)
    c[:, 0:128] = np.eye(128, dtype=np.float32)
    a = np.arange(128)[:, None]
    q = np.arange(128)[None, :]
    c[:, 128:256] = (a >= q).astype(np.float32)
    c[:, 256:384] = (a <= q).astype(np.float32)
    qq = np.arange(128)[:, None]
    kk = np.arange(128)[None, :]
    c[:, 384:512] = np.where(kk <= qq, 0.0, NEG).astype(np.float32)
    p = np.arange(128)
    two_pi = 2 * math.pi
    inv32 = (np.float32(THETA) ** (-(np.arange(0, 32, 2, dtype=np.float32)) / np.float32(32))).astype(np.float32)
    inv64 = (np.float32(THETA) ** (-(np.arange(0, 64, 2, dtype=np.float32)) / np.float32(64))).astype(np.float32)
    inv128 = (np.float32(THETA) ** (-(np.arange(0, 128, 2, dtype=np.float32)) / np.float32(128))).astype(np.float32)
    c[:, 640] = inv32[p // 8].astype(np.float64) / two_pi
    c[:, 641] = inv64[p // 4].astype(np.float64) / two_pi
    c[:, 642] = inv128[p % 64].astype(np.float64) / two_pi
    c[:, 643:644 + NIT] = (0.5 ** np.arange(1, NIT + 2))[None, :]
    return c


def _fm(v, nch):
    return np.ascontiguousarray(np.asarray(v, np.float32).reshape(nch, 128).T)


def prepare_inputs(x, c, positions, a_norm, a_ada_w, a_ada_b, a_w_in, a_kv_norm, a_w_uv, a_w_out,
                   kv_norm, w_kv, b_norm, b_ada_w, b_ada_b, b_w_in, b_w_out, final_norm):
    f = lambda a: np.ascontiguousarray(np.asarray(a, np.float32))
    x = f(x); c = f(c)
    positions = np.ascontiguousarray(np.asarray(positions, np.int32))
    WA = np.ascontiguousarray(f(a_w_in)[0][:, _perm_a()])
    wkv = f(w_kv); bw = f(b_w_in)[0]
    cols = []
    def pair_cols(base):
        out_ = []
        for j in range(4):
            for part in (0, 64):
                out_.append([base + (2 * j + p // 64) * 128 + part + (p % 64) for p in range(128)])
        return out_
    kcols = np.concatenate([wkv[:, ci] for ci in pair_cols(0)], axis=1)
    qcols = np.concatenate([bw[:, ci] for g in range(3) for ci in pair_cols(g * 1024)], axis=1)
    WB = np.ascontiguousarray(np.concatenate([kcols, qcols, bw[:, 3072:4096], wkv[:, 1024:2048]], axis=1))
    assert WB.shape[1] == WB_COLS
    normsT = np.concatenate([_fm(f(a_norm)[0], 8), _fm(f(b_norm)[0], 8), _fm(f(kv_norm), 8)], axis=1)
    ada_bT = np.concatenate([_fm(f(a_ada_b)[0], 24), _fm(f(b_ada_b)[0], 24)], axis=1)
    ada_bg = np.concatenate([f(a_ada_b)[0][2 * D:], f(b_ada_b)[0][2 * D:]])[None, :]
    wuv = np.ascontiguousarray(f(a_w_uv)[0].transpose(1, 0, 2).reshape(128, 16 * 128))
    shared = {
        "normsT": np.ascontiguousarray(normsT), "a_ada_w": f(a_ada_w)[0], "b_ada_w": f(b_ada_w)[0],
        "ada_bT": np.ascontiguousarray(ada_bT), "ada_bg": np.ascontiguousarray(ada_bg),
        "WA": WA, "WB": WB, "akvg": f(a_kv_norm)[0][None, :], "wuv": wuv, "woa": f(a_w_out)[0], "wob": f(b_w_out)[0],
        "fng": f(final_norm)[None, :], "cst": _consts(),
    }
    in_maps = []
    for core in range(NCORES):
        bs = slice(core * NB, (core + 1) * NB)
        cc = c[bs]
        cTl = np.ascontiguousarray(cc.reshape(NB, 8, 128).transpose(2, 1, 0).reshape(128, 8 * NB))
        m = dict(shared)
        m["x"] = np.ascontiguousarray(x[bs].reshape(NB * S, D))
        m["pos"] = np.ascontiguousarray(positions[bs])
        m["cT"] = cTl
        in_maps.append(m)
    return in_maps


def kernel(**inputs):
    in_maps = prepare_inputs(**inputs)
    nc, _, _ = build_program(debug=False)
    res = run_bass_kernel_spmd(nc, in_maps, core_ids=list(range(NCORES)))
    outs = [np.asarray(r["out"]).reshape(NB, S, D) for r in res.results]
    return np.concatenate(outs, axis=0).astype(np.float32)
```
